# Optimizing a Trainium2 kernel written in Bass

```python
import jax, jax.numpy as jnp
from jax import lax
import numpy as np

D_MODEL = 1024
BATCH = 4
SEQ = 8192
DEPTH = 4

A_HEADS = 16
A_KV_HEADS = 4
A_HEAD_DIM = 64
A_REP = A_HEADS // A_KV_HEADS
IDX_HEADS = 8
IDX_DIM = 64
IDX_TOPK_MAX = 256
Q_BLOCK = 128
A_Q = A_HEADS * A_HEAD_DIM
A_KV = A_KV_HEADS * A_HEAD_DIM
A_IN = A_Q + 2 * A_KV + IDX_HEADS * IDX_DIM + IDX_DIM + IDX_HEADS
A_SPLITS = (A_Q, A_Q + A_KV, A_Q + 2 * A_KV, A_Q + 2 * A_KV + IDX_HEADS * IDX_DIM,
            A_Q + 2 * A_KV + IDX_HEADS * IDX_DIM + IDX_DIM)

B_HEADS = 4
B_DK = D_MODEL // 2 // B_HEADS
B_DV = D_MODEL // B_HEADS
B_GATE_RANK = 16
B_GATE_TAU = 16.0
B_CHUNK = 64
B_QK = B_HEADS * B_DK
B_V = B_HEADS * B_DV
B_IN = 2 * B_QK + 2 * B_V + B_GATE_RANK
B_SPLITS = (B_QK, 2 * B_QK, 2 * B_QK + B_V, 2 * B_QK + 2 * B_V)

D_FF = 4 * D_MODEL

ROPE_THETA = 500000.0
ROT_DIM = A_HEAD_DIM // 4
LN_EPS = 1e-5
RMS_EPS = 1e-6
DN_ALPHA = (2 * DEPTH) ** 0.25
DN_BETA = (8 * DEPTH) ** -0.25
N_A_LAYERS = (DEPTH + 1) // 2
N_B_LAYERS = DEPTH // 2

kernel_name = 'hybrid_dsa_gla_deepnorm'


def layer_norm(x, g, b):
    xf = x.astype(jnp.float32)
    mu = jnp.mean(xf, axis=-1, keepdims=True)
    var = jnp.mean(jnp.square(xf - mu), axis=-1, keepdims=True)
    return ((xf - mu) * lax.rsqrt(var + LN_EPS) * g + b).astype(x.dtype)


def rotary_tables(positions):
    inv = ROPE_THETA ** (-jnp.arange(0, ROT_DIM, 2, dtype=jnp.float32) / ROT_DIM)
    ang = positions.astype(jnp.float32)[..., None] * inv
    return jnp.cos(ang), jnp.sin(ang)


def partial_rotary(t, cos, sin):
    half = cos.shape[-1]
    shape = cos.shape[:2] + (1,) * (t.ndim - 3) + (half,)
    c = cos.reshape(shape)
    s = sin.reshape(shape)
    t1 = t[..., :half].astype(jnp.float32)
    t2 = t[..., half:2 * half].astype(jnp.float32)
    return jnp.concatenate([(t1 * c - t2 * s).astype(t.dtype),
                            (t2 * c + t1 * s).astype(t.dtype),
                            t[..., 2 * half:]], axis=-1)


def dsa_mixer(x, cos, sin, w_in, w_o):
    bsz, L, _ = x.shape
    top_k = min(IDX_TOPK_MAX, L // 4)
    proj = x @ w_in
    q, k, v, iq, ik, iw = jnp.split(proj, A_SPLITS, axis=-1)
    q = partial_rotary(q.reshape(bsz, L, A_HEADS, A_HEAD_DIM), cos, sin)
    k = partial_rotary(k.reshape(bsz, L, A_KV_HEADS, A_HEAD_DIM), cos, sin)
    v = v.reshape(bsz, L, A_KV_HEADS, A_HEAD_DIM)
    iq = partial_rotary(iq.reshape(bsz, L, IDX_HEADS, IDX_DIM), cos, sin)
    ik = partial_rotary(ik, cos, sin)
    iw = iw.astype(jnp.float32) * IDX_HEADS ** -0.5
    key_idx = jnp.arange(L)

    def block(i):
        start = i * Q_BLOCK
        q_idx = start + jnp.arange(Q_BLOCK)
        iq_b = lax.dynamic_slice_in_dim(iq, start, Q_BLOCK, axis=1)
        iw_b = lax.dynamic_slice_in_dim(iw, start, Q_BLOCK, axis=1)
        s = jnp.einsum('bqhd,bsd->bqhs', iq_b, ik, preferred_element_type=jnp.float32) * IDX_DIM ** -0.5
        score = jnp.einsum('bqh,bqhs->bqs', iw_b, jax.nn.relu(s))
        causal = key_idx[None, :] <= q_idx[:, None]
        score = jnp.where(causal[None], score, -jnp.inf)
        _, sel = lax.top_k(score, top_k)
        valid = sel <= q_idx[None, :, None]
        k_sel = jax.vmap(lambda kk, ii: kk[ii])(k, sel)
        v_sel = jax.vmap(lambda vv, ii: vv[ii])(v, sel)
        q_b = lax.dynamic_slice_in_dim(q, start, Q_BLOCK, axis=1).reshape(bsz, Q_BLOCK, A_KV_HEADS, A_REP, A_HEAD_DIM)
        logits = jnp.einsum('bqgrd,bqkgd->bqgrk', q_b, k_sel, preferred_element_type=jnp.float32) * A_HEAD_DIM ** -0.5
        logits = jnp.where(valid[:, :, None, None, :], logits, -jnp.inf)
        p = jax.nn.softmax(logits, axis=-1)
        o = jnp.einsum('bqgrk,bqkgd->bqgrd', p.astype(v.dtype), v_sel)
        return o.reshape(bsz, Q_BLOCK, A_Q)

    out = lax.map(block, jnp.arange(L // Q_BLOCK))
    out = out.transpose(1, 0, 2, 3).reshape(bsz, L, A_Q)
    return out @ w_o


def gla_mixer(x, w_in, w_a2, b_a, g_norm, w_o):
    bsz, L, _ = x.shape
    n_c = L // B_CHUNK
    proj = x @ w_in
    q, k, v, r, a_low = jnp.split(proj, B_SPLITS, axis=-1)
    log_a = jax.nn.log_sigmoid((a_low @ w_a2 + b_a).astype(jnp.float32)) / B_GATE_TAU

    def to_chunks(t, d):
        return t.astype(jnp.float32).reshape(bsz, n_c, B_CHUNK, B_HEADS, d).transpose(1, 0, 3, 2, 4)

    qc = to_chunks(q, B_DK) * B_DK ** -0.5
    kc = to_chunks(k, B_DK)
    vc = to_chunks(v, B_DV)
    gc = to_chunks(log_a, B_DK)
    mask = jnp.tril(jnp.ones((B_CHUNK, B_CHUNK), dtype=bool))

    def step(state, inp):
        qi, ki, vi, gi = inp
        b = jnp.cumsum(gi, axis=2)
        b_last = b[:, :, -1:, :]
        diff = jnp.where(mask[None, None, :, :, None], b[:, :, :, None, :] - b[:, :, None, :, :], -jnp.inf)
        attn = jnp.einsum('bhid,bhjd,bhijd->bhij', qi, ki, jnp.exp(diff))
        intra = jnp.einsum('bhij,bhje->bhie', attn, vi)
        inter = jnp.einsum('bhid,bhde->bhie', qi * jnp.exp(b), state)
        new_state = jnp.exp(b_last)[:, :, 0, :, None] * state + jnp.einsum('bhjd,bhje->bhde', ki * jnp.exp(b_last - b), vi)
        return new_state, intra + inter

    state0 = jnp.zeros((bsz, B_HEADS, B_DK, B_DV), jnp.float32)
    _, outs = lax.scan(step, state0, (qc, kc, vc, gc))
    o = outs.transpose(1, 0, 3, 2, 4).reshape(bsz, L, B_HEADS, B_DV)
    o = o * lax.rsqrt(jnp.mean(jnp.square(o), axis=-1, keepdims=True) + RMS_EPS) * g_norm
    gate = jax.nn.silu(r.astype(jnp.float32).reshape(bsz, L, B_HEADS, B_DV))
    y = (o * gate).reshape(bsz, L, B_V).astype(x.dtype)
    return y @ w_o


def squared_relu_mlp(x, w_up, w_down):
    return jnp.square(jax.nn.relu(x @ w_up)) @ w_down


def setup_inputs(seed: int = 0) -> dict:
    key = jax.random.key(seed)
    ks = jax.random.split(key, 16)
    f32 = jnp.float32

    def nrm(k, shape, fan_in, scale=1.0):
        return jax.random.normal(k, shape, f32) * (scale * fan_in ** -0.5)

    x = jax.random.normal(ks[0], (BATCH, SEQ, D_MODEL), f32)
    positions = jnp.broadcast_to(jnp.arange(SEQ, dtype=jnp.int32), (BATCH, SEQ))
    a_w_in = nrm(ks[1], (N_A_LAYERS, D_MODEL, A_IN), D_MODEL)
    a_w_o = nrm(ks[2], (N_A_LAYERS, A_Q, D_MODEL), A_Q, DN_BETA)
    b_w_in = nrm(ks[3], (N_B_LAYERS, D_MODEL, B_IN), D_MODEL)
    b_w_a2 = nrm(ks[4], (N_B_LAYERS, B_GATE_RANK, B_QK), B_GATE_RANK)
    b_b_a = 0.1 * jax.random.normal(ks[5], (N_B_LAYERS, B_QK), f32)
    b_g_norm = 1.0 + 0.02 * jax.random.normal(ks[6], (N_B_LAYERS, B_DV), f32)
    b_w_o = nrm(ks[7], (N_B_LAYERS, B_V, D_MODEL), B_V, DN_BETA)
    ln_mix_g = 1.0 + 0.02 * jax.random.normal(ks[8], (DEPTH, D_MODEL), f32)
    ln_mix_b = 0.02 * jax.random.normal(ks[9], (DEPTH, D_MODEL), f32)
    mlp_w_up = nrm(ks[10], (DEPTH, D_MODEL, D_FF), D_MODEL)
    mlp_w_down = nrm(ks[11], (DEPTH, D_FF, D_MODEL), D_FF, DN_BETA)
    ln_mlp_g = 1.0 + 0.02 * jax.random.normal(ks[12], (DEPTH, D_MODEL), f32)
    ln_mlp_b = 0.02 * jax.random.normal(ks[13], (DEPTH, D_MODEL), f32)
    return {'x': x, 'positions': positions, 'a_w_in': a_w_in, 'a_w_o': a_w_o,
            'b_w_in': b_w_in, 'b_w_a2': b_w_a2, 'b_b_a': b_b_a, 'b_g_norm': b_g_norm, 'b_w_o': b_w_o,
            'ln_mix_g': ln_mix_g, 'ln_mix_b': ln_mix_b, 'mlp_w_up': mlp_w_up, 'mlp_w_down': mlp_w_down,
            'ln_mlp_g': ln_mlp_g, 'ln_mlp_b': ln_mlp_b}


def reference(x, positions, a_w_in, a_w_o, b_w_in, b_w_a2, b_b_a, b_g_norm, b_w_o,
              ln_mix_g, ln_mix_b, mlp_w_up, mlp_w_down, ln_mlp_g, ln_mlp_b):
    cos, sin = rotary_tables(positions)
    h = x
    for i in range(DEPTH):
        j = i // 2
        if i % 2 == 0:
            mix = dsa_mixer(h, cos, sin, a_w_in[j], a_w_o[j])
        else:
            mix = gla_mixer(h, b_w_in[j], b_w_a2[j], b_b_a[j], b_g_norm[j], b_w_o[j])
        h = layer_norm(DN_ALPHA * h + mix, ln_mix_g[i], ln_mix_b[i])
        h = layer_norm(DN_ALPHA * h + squared_relu_mlp(h, mlp_w_up[i], mlp_w_down[i]), ln_mlp_g[i], ln_mlp_b[i])
    return h
```

```python
import contextlib
import numpy as np
import concourse.bass as bass
import concourse.mybir as mybir
from concourse.ap import AP
from concourse.bass_utils import run_bass_kernel_spmd

F32 = mybir.dt.float32
BF16 = mybir.dt.bfloat16
I32 = mybir.dt.int32
AF = mybir.ActivationFunctionType
ALU = mybir.AluOpType
AX = mybir.AxisListType

D = 1024
DFF = 4096
DEPTH = 4
ALPHA = (2 * DEPTH) ** 0.25
LN_EPS = 1e-5
NCORES = 8


class Buf:
    __slots__ = ("name", "last_w", "readers")

    def __init__(self, name):
        self.name = name
        self.last_w = None
        self.readers = {}


class Sched:
    ENGS = ("sp", "act", "dve", "pool", "pe")
    SAME_WIN = 3
    NDMA = 8

    def __init__(self, nc):
        self.nc = nc
        self.stack = contextlib.ExitStack()
        self.ops = {e: [] for e in self.ENGS}
        self.count = {e: 0 for e in self.ENGS}
        self.seen = {e: {} for e in self.ENGS}
        self.esem = {}
        for e in ("act", "dve", "pool", "pe"):
            self.esem[e] = self.stack.enter_context(nc.semaphore("s_" + e))
        self.dsem = {}
        self.dma_i = {}
        for q in ("sp", "act", "pool"):
            self.dsem[q] = [self.stack.enter_context(nc.semaphore("d_%s%d" % (q, i))) for i in range(self.NDMA)]
            self.dma_i[q] = 0
        self.nbuf = 0
        self.stage_stack = None
        self.stage_no = 0
        self.bar = self.stack.enter_context(nc.semaphore("s_bar"))

    def sbuf(self, name, shape, dt):
        st = self.stage_stack if self.stage_stack is not None else self.stack
        return st.enter_context(self.nc.sbuf_tensor("sb%d_" % self.stage_no + name, list(shape), dt))

    def psum(self, name, shape, dt):
        st = self.stage_stack if self.stage_stack is not None else self.stack
        return st.enter_context(self.nc.psum_tensor("pp%d_" % self.stage_no + name, list(shape), dt))

    def stage_begin(self):
        self.stage_stack = contextlib.ExitStack()

    def stage_end(self, last=False):
        self.finish()
        if not last:
            self.stage_no += 1
            n = self.stage_no
            bar = self.bar
            self.ops["sp"].append(([], (lambda e, bar=bar: e.sem_inc(bar, 1)), None, 0))
            for eng in ("act", "dve", "pool", "pe"):
                self.ops[eng].append(([(bar, n)], None, None, 0))
        self._emit_block()
        self.stage_stack.close()
        self.stage_stack = None

    def buf(self, name=None):
        self.nbuf += 1
        return Buf(name or ("b%d" % self.nbuf))

    def bufs(self, n, name="b"):
        return [self.buf("%s%d" % (name, i)) for i in range(n)]

    def _deps(self, reads, writes):
        raw = {}
        other = {}
        for b in reads:
            if b.last_w is not None:
                k, v = b.last_w
                raw[k] = max(raw.get(k, 0), v)
        for b in writes:
            if b.last_w is not None:
                k, v = b.last_w
                other[k] = max(other.get(k, 0), v)
            for k, v in b.readers.items():
                other[k] = max(other.get(k, 0), v)
        return raw, other

    def _commit(self, ev, reads, writes):
        k, v = ev
        for b in reads:
            b.readers[k] = max(b.readers.get(k, 0), v)
        for b in writes:
            b.last_w = ev
            b.readers = {}

    def op(self, eng, fn, reads=(), writes=()):
        raw, other = self._deps(reads, writes)
        own = self.esem[eng]
        waits = {}
        seen = self.seen[eng]
        for d, is_raw in ((raw, True), (other, False)):
            for k, v in d.items():
                if k is own:
                    if eng == "pe" or not is_raw:
                        continue
                    if v <= self.count[eng] - self.SAME_WIN:
                        continue
                if seen.get(k, 0) >= v:
                    continue
                waits[k] = max(waits.get(k, 0), v)
        for k, v in waits.items():
            seen[k] = v
        self.count[eng] += 1
        ev = (own, self.count[eng])
        self._commit(ev, reads, writes)
        self.ops[eng].append((list(waits.items()), fn, own, 1))

    def dma(self, q, out, in_, reads=(), writes=(), fn=None, **kw):
        raw, other = self._deps(reads, writes)
        waits = {}
        seen = self.seen[q]
        for d in (raw, other):
            for k, v in d.items():
                if seen.get(k, 0) >= v:
                    continue
                waits[k] = max(waits.get(k, 0), v)
        i = self.dma_i[q]
        self.dma_i[q] = i + 1
        slot = self.dsem[q][i % self.NDMA]
        target = 16 * (i // self.NDMA + 1)
        if target > 16 and seen.get(slot, 0) < target - 16:
            waits[slot] = max(waits.get(slot, 0), target - 16)
        for k, v in waits.items():
            seen[k] = v
        ev = (slot, target)
        self._commit(ev, reads, writes)
        if fn is None:
            fn = lambda e, out=out, in_=in_, kw=kw: e.dma_start(out=out, in_=in_, **kw)
        self.ops[q].append((list(waits.items()), fn, slot, 16))

    def allgather(self, out, in_, groups, reads=(), writes=()):
        fn = lambda e: e.collective_compute("AllGather", ALU.bypass, replica_groups=groups, ins=[in_], outs=[out])
        self.dma("pool", None, None, reads=reads, writes=writes, fn=fn)

    def finish(self):
        waits = []
        for q in ("sp", "act", "pool"):
            n = self.dma_i[q]
            for s in range(min(n, self.NDMA)):
                cnt = (n - 1 - s) // self.NDMA + 1
                waits.append((self.dsem[q][s], 16 * cnt))
        for e in ("act", "dve", "pool", "pe"):
            if self.count[e]:
                waits.append((self.esem[e], self.count[e]))
        self.ops["sp"].append((waits, None, None, 0))

    def emit(self):
        self.finish()
        self._emit_block()
        self.stack.close()

    def _emit_block(self):
        nc = self.nc
        with nc.Block() as block:
            deco = {"sp": block.sync, "act": block.scalar, "dve": block.vector,
                    "pool": block.gpsimd, "pe": block.tensor}
            for eng in self.ENGS:
                ops = self.ops[eng]
                if not ops:
                    continue

                def body(e, ops=ops):
                    for waits, fn, sem, inc in ops:
                        for s, v in waits:
                            e.wait_ge(s, v)
                        if fn is not None:
                            ins = fn(e)
                            if sem is not None:
                                ins.then_inc(sem, inc)

                deco[eng](body)
        self.ops = {e: [] for e in self.ENGS}


def bcast_rows(ap_row, nparts):
    return ap_row.partition_broadcast(nparts)


class Consts:
    def __init__(self, S, ident_dram):
        self.ident = S.sbuf("ident", [128, 128], BF16)
        self.b_ident = S.buf("ident")
        S.dma("pool", self.ident[:], ident_dram[:, :], writes=[self.b_ident])


def load_bcast(S, name, row_ap, n):
    t = S.sbuf(name, [128, n], F32)
    b = S.buf(name)
    S.dma("sp", t[:], row_ap.partition_broadcast(128), writes=[b])
    return t, b


def layer_norm_tile(S, u, b_u, out, b_out, g_t, b_g, bt_t, b_bt, scr, eng2="pool"):
    st, b_st = scr["st"], scr["b_st"]
    mv, b_mv = scr["mv"], scr["b_mv"]
    for c in range(2):
        S.op("dve", lambda e, c=c: e.bn_stats(out=st[:, c, :], in_=u[:, c * 512:(c + 1) * 512]),
             reads=[b_u], writes=[b_st[c]])
    S.op("dve", lambda e: e.bn_aggr(out=mv[:, 0:2], in_=st[:, :, :]), reads=b_st, writes=[b_mv[0]])
    S.op("dve", lambda e: e.tensor_scalar(out=mv[:, 2:3], in0=mv[:, 1:2], scalar1=LN_EPS, scalar2=None,
                                           op0=ALU.add), reads=[b_mv[0]], writes=[b_mv[1]])
    S.op("act", lambda e: e.activation(out=mv[:, 2:3], in_=mv[:, 2:3], func=AF.Sqrt), reads=[b_mv[1]], writes=[b_mv[1]])
    S.op("dve", lambda e: e.reciprocal(out=mv[:, 2:3], in_=mv[:, 2:3]), reads=[b_mv[1]], writes=[b_mv[1]])
    S.op("dve", lambda e: e.scalar_tensor_tensor(out=mv[:, 3:4], in0=mv[:, 0:1], scalar=-1.0, in1=mv[:, 2:3],
                                                  op0=ALU.mult, op1=ALU.mult), reads=[b_mv[0], b_mv[1]], writes=[b_mv[2]])
    S.op("act", lambda e: e.activation(out=u[:, :], in_=u[:, :], func=AF.Identity, bias=mv[:, 3:4], scale=mv[:, 2:3]),
         reads=[b_u, b_mv[1], b_mv[2]], writes=[b_u])
    S.op(eng2, lambda e: e.tensor_tensor(out=u[:, :], in0=u[:, :], in1=g_t[:, :], op=ALU.mult),
         reads=[b_u, b_g], writes=[b_u])
    S.op(eng2, lambda e: e.tensor_tensor(out=out[:, :], in0=u[:, :], in1=bt_t[:, :], op=ALU.add),
         reads=[b_u, b_bt], writes=[b_out])


def ln_scratch(S, name):
    return {"st": S.sbuf(name + "_st", [128, 2, 6], F32), "b_st": S.bufs(2, name + "st"),
            "mv": S.sbuf(name + "_mv", [128, 4], F32), "b_mv": S.bufs(3, name + "mv")}


def stage_mlp(S, C, T, h1, w_up, w_down, g, b, h2):
    NT = 256
    NS = NT // 128
    wup = S.sbuf("wup", [128, 8, DFF], BF16)
    wdn = S.sbuf("wdn", [128, 32, D], BF16)
    b_wup = S.bufs(8, "wup")
    b_wdn = S.bufs(8, "wdn")
    wu_v = w_up.rearrange("(c p) f -> p c f", p=128)
    wd_v = w_down.rearrange("(c p) n -> p c n", p=128)
    for c in range(8):
        S.dma("pool", wup[:, c, :], wu_v[:, c, :], writes=[b_wup[c]])
    for c in range(8):
        S.dma("pool", wdn[:, 4 * c:4 * c + 4, :], wd_v[:, 4 * c:4 * c + 4, :], writes=[b_wdn[c]])
    g_t, b_g = load_bcast(S, "mlp_g", g, D)
    bt_t, b_bt = load_bcast(S, "mlp_b", b, D)

    NB = 2
    x_sb = [S.sbuf("x_sb%d" % i, [128, NS, D], F32) for i in range(NB)]
    b_x = [S.bufs(NS, "x%d_" % i) for i in range(NB)]
    xb = S.sbuf("xb", [128, NS, D], BF16)
    b_xb = S.bufs(NS, "xb")
    xT = S.sbuf("xT", [128, 8, NT], BF16)
    b_xT = S.bufs(8, "xT")
    h2T = S.sbuf("h2T", [128, 32, NT], BF16)
    b_h2T = S.bufs(32, "h2T")
    r_sb = [S.sbuf("r_sb%d" % i, [128, NT], F32) for i in range(2)]
    b_r = S.bufs(2, "r")
    y_sb = [S.sbuf("y_sb%d" % i, [128, D], F32) for i in range(2)]
    b_y = S.bufs(2, "y")
    tp_ps = [S.psum("tp_ps%d" % i, [128, 1024], BF16) for i in range(2)]
    b_tp = S.bufs(2, "tp")
    up_ps = [S.psum("up_ps%d" % i, [128, 512], F32) for i in range(2)]
    b_up = S.bufs(2, "up")
    dn_ps = [S.psum("dn_ps%d" % i, [128, 512], F32) for i in range(2)]
    b_dn = S.bufs(2, "dn")
    scr = ln_scratch(S, "mlp")

    h1v = h1.rearrange("(t s p) d -> t p s d", p=128, s=NS)
    h2v = h2.rearrange("(t s p) d -> t s p d", p=128, s=NS)
    n_up = 0
    n_dn = 0
    n_tp = 0
    n_y = 0
    for t in range(T // NT):
        xs, bx = x_sb[t % NB], b_x[t % NB]
        S.dma("sp", xs[:, :, :], h1v[t], writes=bx)
        for s in range(NS):
            S.op("act", lambda e, xs=xs, s=s: e.activation(out=xb[:, s, :], in_=xs[:, s, :], func=AF.Copy),
                 reads=[bx[s]], writes=[b_xb[s]])
        for c in range(8):
            tp, btp = tp_ps[n_tp % 2], b_tp[n_tp % 2]
            n_tp += 1
            for s in range(NS):
                S.op("pe", lambda e, tp=tp, s=s, c=c: e.transpose(out=tp[:, s * 128:(s + 1) * 128],
                                                                  in_=xb[:, s, c * 128:(c + 1) * 128],
                                                                  identity=C.ident[:]),
                     reads=[b_xb[s], C.b_ident], writes=[btp])
            S.op("dve", lambda e, tp=tp, c=c: e.tensor_copy(out=xT[:, c, :], in_=tp[:, 0:NT]),
                 reads=[btp], writes=[b_xT[c]])
        for fc in range(32):
            ps, bps = up_ps[n_up % 2], b_up[n_up % 2]
            r, br = r_sb[n_up % 2], b_r[n_up % 2]
            n_up += 1
            for c in range(8):
                S.op("pe", lambda e, ps=ps, c=c, fc=fc: e.matmul(ps[:, 0:NT], lhsT=wup[:, c, fc * 128:(fc + 1) * 128],
                                                                  rhs=xT[:, c, :], start=(c == 0), stop=(c == 7)),
                     reads=[b_wup[c], b_xT[c]], writes=[bps])
            S.op("act", lambda e, ps=ps, r=r: e.activation(out=r[:, :], in_=ps[:, 0:NT], func=AF.Relu),
                 reads=[bps], writes=[br])
            S.op("dve", lambda e, r=r, fc=fc: e.tensor_tensor(out=h2T[:, fc, :], in0=r[:, :], in1=r[:, :], op=ALU.mult),
                 reads=[br], writes=[b_h2T[fc]])
        for s in range(NS):
            for nh in range(2):
                ps, bps = dn_ps[n_dn % 2], b_dn[n_dn % 2]
                n_dn += 1
                for fc in range(32):
                    S.op("pe", lambda e, ps=ps, fc=fc, s=s, nh=nh: e.matmul(
                        ps[:, :], lhsT=h2T[:, fc, s * 128:(s + 1) * 128], rhs=wdn[:, fc, nh * 512:(nh + 1) * 512],
                        start=(fc == 0), stop=(fc == 31)),
                         reads=[b_h2T[fc], b_wdn[fc // 4]], writes=[bps])
                S.op("dve", lambda e, ps=ps, xs=xs, s=s, nh=nh: e.scalar_tensor_tensor(
                    out=xs[:, s, nh * 512:(nh + 1) * 512], in0=xs[:, s, nh * 512:(nh + 1) * 512], scalar=ALPHA,
                    in1=ps[:, :], op0=ALU.mult, op1=ALU.add),
                     reads=[bps, bx[s]], writes=[bx[s]])
            y, by = y_sb[n_y % 2], b_y[n_y % 2]
            n_y += 1
            layer_norm_tile(S, xs[:, s, :], bx[s], y, by, g_t, b_g, bt_t, b_bt, scr)
            S.dma("sp", h2v[t, s], y[:, :], reads=[by])


TWO_PI = float(2 * np.pi)
PI = float(np.pi)


def _range_reduce(S, x, bx, tmp_f, btf, tmp_i, bti):
    S.op("dve", lambda e: e.tensor_scalar(out=tmp_f, in0=x, scalar1=1.0 / TWO_PI, scalar2=None, op0=ALU.mult),
         reads=[bx], writes=[btf])
    S.op("dve", lambda e: e.tensor_copy(out=tmp_i, in_=tmp_f), reads=[btf], writes=[bti])
    S.op("dve", lambda e: e.tensor_copy(out=tmp_f, in_=tmp_i), reads=[bti], writes=[btf])
    S.op("dve", lambda e: e.scalar_tensor_tensor(out=x, in0=tmp_f, scalar=-TWO_PI, in1=x, op0=ALU.mult, op1=ALU.add),
         reads=[btf, bx], writes=[bx])
    S.op("dve", lambda e: e.tensor_scalar(out=tmp_f, in0=x, scalar1=PI, scalar2=-TWO_PI, op0=ALU.is_gt, op1=ALU.mult),
         reads=[bx], writes=[btf])
    S.op("dve", lambda e: e.tensor_tensor(out=x, in0=x, in1=tmp_f, op=ALU.add), reads=[btf, bx], writes=[bx])
    S.op("dve", lambda e: e.tensor_scalar(out=tmp_f, in0=x, scalar1=-PI, scalar2=TWO_PI, op0=ALU.is_lt, op1=ALU.mult),
         reads=[bx], writes=[btf])
    S.op("dve", lambda e: e.tensor_tensor(out=x, in0=x, in1=tmp_f, op=ALU.add), reads=[btf, bx], writes=[bx])


def rotary_tables(S, NTL, pos_pt, invf):
    posi = S.sbuf("posi", [128, NTL], I32)
    posf = S.sbuf("posf", [128, NTL], F32)
    invt = S.sbuf("invt", [128, 8], F32)
    cos_t = S.sbuf("cos_t", [128, NTL, 8], F32)
    sin_t = S.sbuf("sin_t", [128, NTL, 8], F32)
    tmpf = S.sbuf("rr_tf", [128, NTL, 8], F32)
    tmpi = S.sbuf("rr_ti", [128, NTL, 8], I32)
    b_pi, b_pf, b_inv, b_cos, b_sin, b_tf, b_ti = S.bufs(7, "rot")
    S.dma("sp", posi[:], pos_pt[:, :], writes=[b_pi])
    S.dma("sp", invt[:], invf.partition_broadcast(128), writes=[b_inv])
    S.op("dve", lambda e: e.tensor_copy(out=posf[:], in_=posi[:]), reads=[b_pi], writes=[b_pf])
    S.op("dve", lambda e: e.tensor_tensor(out=sin_t[:, :, :], in0=posf[:, :].unsqueeze(2).to_broadcast([128, NTL, 8]),
                                          in1=invt[:, :].unsqueeze(1).to_broadcast([128, NTL, 8]), op=ALU.mult),
         reads=[b_pf, b_inv], writes=[b_sin])
    S.op("dve", lambda e: e.tensor_scalar(out=cos_t[:, :, :], in0=sin_t[:, :, :], scalar1=PI / 2, scalar2=None, op0=ALU.add),
         reads=[b_sin], writes=[b_cos])
    _range_reduce(S, sin_t[:, :, :], b_sin, tmpf[:, :, :], b_tf, tmpi[:, :, :], b_ti)
    _range_reduce(S, cos_t[:, :, :], b_cos, tmpf[:, :, :], b_tf, tmpi[:, :, :], b_ti)
    S.op("act", lambda e: e.activation(out=sin_t[:, :, :], in_=sin_t[:, :, :], func=AF.Sin), reads=[b_sin], writes=[b_sin])
    S.op("act", lambda e: e.activation(out=cos_t[:, :, :], in_=cos_t[:, :, :], func=AF.Sin), reads=[b_cos], writes=[b_cos])
    return cos_t, b_cos, sin_t, b_sin


def apply_rotary(S, P, bP, h0, nh, cos_t, b_cos, sin_t, b_sin, t, tmp, btmp):
    Pv = P[:, h0 * 64:(h0 + nh) * 64].rearrange("p (h d) -> p h d", d=64)
    x1 = Pv[:, :, 0:8]
    x2 = Pv[:, :, 8:16]
    c = cos_t[:, t, :].unsqueeze(1).to_broadcast([128, nh, 8])
    s = sin_t[:, t, :].unsqueeze(1).to_broadcast([128, nh, 8])
    t1, t2, t3, t4 = (tmp[:, i, 0:nh, :] for i in range(4))
    bP = list(bP)
    rd = bP + [b_cos, b_sin]
    S.op("dve", lambda e: e.tensor_tensor(out=t1, in0=x1, in1=c, op=ALU.mult), reads=rd, writes=[btmp[0]])
    S.op("dve", lambda e: e.tensor_tensor(out=t2, in0=x2, in1=s, op=ALU.mult), reads=rd, writes=[btmp[1]])
    S.op("dve", lambda e: e.tensor_tensor(out=t3, in0=x2, in1=c, op=ALU.mult), reads=rd, writes=[btmp[2]])
    S.op("dve", lambda e: e.tensor_tensor(out=t4, in0=x1, in1=s, op=ALU.mult), reads=rd, writes=[btmp[3]])
    S.op("dve", lambda e: e.tensor_tensor(out=x1, in0=t1, in1=t2, op=ALU.subtract), reads=[btmp[0], btmp[1], btmp[3]], writes=bP)
    S.op("dve", lambda e: e.tensor_tensor(out=x2, in0=t3, in1=t4, op=ALU.add), reads=[btmp[2], btmp[3]], writes=bP)


A_IN = 2120
IW_SCALE = float(8 ** -0.5 * 64 ** -0.5)


def stage_proj_A(S, C, T, h, pos_pt, w_in, invf, qT, kT, v, iqT, ikT, iw):
    NTL = T // 128
    win = S.sbuf("win", [128, 8, A_IN], BF16)
    b_win = S.bufs(8, "win")
    wv = w_in.rearrange("(c p) n -> p c n", p=128)
    for c in range(8):
        S.dma("pool", win[:, c, :], wv[:, c, :], writes=[b_win[c]])
    cos_t, b_cos, sin_t, b_sin = rotary_tables(S, NTL, pos_pt, invf)

    hs = [S.sbuf("pa_hs%d" % i, [128, D], F32) for i in range(2)]
    b_hs = S.bufs(2, "pa_hs")
    hb = S.sbuf("pa_hb", [128, D], BF16)
    b_hb = S.buf("pa_hb")
    hT = S.sbuf("pa_hT", [128, 8, 128], BF16)
    b_hT = S.buf("pa_hT")
    P = S.sbuf("pa_P", [128, A_IN], F32)
    b_P = S.bufs(5, "pa_P")
    Pb = S.sbuf("pa_Pb", [128, A_IN], BF16)
    b_Pq, b_Pk, b_Pv, b_Pi = S.bufs(4, "pa_Pb")
    iws = [S.sbuf("pa_iw%d" % i, [128, 8], F32) for i in range(2)]
    b_iws = S.bufs(2, "pa_iw")
    rt = S.sbuf("pa_rt", [128, 4, 20, 8], F32)
    b_rt = S.bufs(4, "pa_rt")
    qTs = [S.sbuf("pa_qT%d" % i, [128, 8, 128], BF16) for i in range(2)]
    b_qTs = S.bufs(2, "pa_qT")
    kiTs = [S.sbuf("pa_kiT%d" % i, [128, 7, 128], BF16) for i in range(2)]
    b_kiTs = S.bufs(2, "pa_kiT")
    pj = [S.psum("pa_pj%d" % i, [128, 512], F32) for i in range(5)]
    b_pj = S.bufs(5, "pa_pj")
    tpA = S.psum("pa_tpA", [128, 1024], BF16)
    tpB = S.psum("pa_tpB", [128, 1024], BF16)
    b_tpA, b_tpB = S.bufs(2, "pa_tp")
    chunks = [(0, 512), (512, 1024), (1024, 1536), (1536, 2048), (2048, A_IN)]

    hv = h.rearrange("(t p) d -> t p d", p=128)
    qTv = qT.rearrange("(c p) t -> p c t", p=128)
    kTv = kT.rearrange("(c p) t -> p c t", p=128)
    iqTv = iqT.rearrange("(c p) t -> p c t", p=128)
    for t in range(NTL):
        x, bx = hs[t % 2], b_hs[t % 2]
        S.dma("sp", x[:, :], hv[t], writes=[bx])
        S.op("act", lambda e, x=x: e.activation(out=hb[:, :], in_=x[:, :], func=AF.Copy), reads=[bx], writes=[b_hb])
        for c in range(8):
            S.op("pe", lambda e, c=c: e.transpose(out=tpA[:, c * 128:(c + 1) * 128], in_=hb[:, c * 128:(c + 1) * 128],
                                                  identity=C.ident[:]), reads=[b_hb, C.b_ident], writes=[b_tpA])
        S.op("dve", lambda e: e.tensor_copy(out=hT[:, :, :], in_=tpA[:, :].rearrange("p (c t) -> p c t", t=128)),
             reads=[b_tpA], writes=[b_hT])
        for i, (n0, n1) in enumerate(chunks):
            for c in range(8):
                S.op("pe", lambda e, i=i, c=c, n0=n0, n1=n1: e.matmul(pj[i][:, 0:n1 - n0], lhsT=hT[:, c, :], rhs=win[:, c, n0:n1],
                                                                      start=(c == 0), stop=(c == 7)),
                     reads=[b_hT, b_win[c]], writes=[b_pj[i]])
            S.op("act", lambda e, i=i, n0=n0, n1=n1: e.activation(out=P[:, n0:n1], in_=pj[i][:, 0:n1 - n0], func=AF.Copy),
                 reads=[b_pj[i]], writes=[b_P[i]])
        apply_rotary(S, P, b_P[0:3], 0, 20, cos_t, b_cos, sin_t, b_sin, t, rt, b_rt)
        apply_rotary(S, P, b_P[3:5], 24, 9, cos_t, b_cos, sin_t, b_sin, t, rt, b_rt)
        S.op("act", lambda e: e.activation(out=Pb[:, 0:1024], in_=P[:, 0:1024], func=AF.Copy, scale=0.125),
             reads=b_P[0:2], writes=[b_Pq])
        S.op("act", lambda e: e.activation(out=Pb[:, 1024:1536], in_=P[:, 1024:1536], func=AF.Copy),
             reads=[b_P[2]], writes=[b_Pk])
        S.op("act", lambda e: e.activation(out=Pb[:, 1536:2112], in_=P[:, 1536:2112], func=AF.Copy),
             reads=b_P[3:5], writes=[b_Pi])
        iwt, biw = iws[t % 2], b_iws[t % 2]
        S.op("act", lambda e, iwt=iwt: e.activation(out=iwt[:, :], in_=P[:, 2112:2120], func=AF.Copy, scale=IW_SCALE),
             reads=[b_P[4]], writes=[biw])
        S.dma("sp", iw[t * 128:(t + 1) * 128, :], iwt[:, :], reads=[biw])
        S.dma("sp", v[t * 128:(t + 1) * 128, :], Pb[:, 1280:1536], reads=[b_Pk])
        qs, bqs = qTs[t % 2], b_qTs[t % 2]
        ks, bks = kiTs[t % 2], b_kiTs[t % 2]
        for c in range(8):
            S.op("pe", lambda e, c=c: e.transpose(out=tpB[:, c * 128:(c + 1) * 128], in_=Pb[:, c * 128:(c + 1) * 128],
                                                  identity=C.ident[:]), reads=[b_Pq, C.b_ident], writes=[b_tpB])
        S.op("dve", lambda e, qs=qs: e.tensor_copy(out=qs[:, :, :], in_=tpB[:, :].rearrange("p (c t) -> p c t", t=128)),
             reads=[b_tpB], writes=[bqs])
        S.dma("sp", qTv[:, :, t * 128:(t + 1) * 128], qs[:, :, :], reads=[bqs])
        srcs = [(1024, 128, b_Pk), (1152, 128, b_Pk), (1536, 128, b_Pi), (1664, 128, b_Pi), (1792, 128, b_Pi),
                (1920, 128, b_Pi), (2048, 64, b_Pi)]
        for j, (c0, w, bsrc) in enumerate(srcs):
            S.op("pe", lambda e, j=j, c0=c0, w=w: e.transpose(out=tpA[0:w, j * 128:(j + 1) * 128], in_=Pb[:, c0:c0 + w],
                                                              identity=C.ident[:]), reads=[bsrc, C.b_ident], writes=[b_tpA])
        S.op("dve", lambda e, ks=ks: e.tensor_copy(out=ks[:, 0:6, :], in_=tpA[:, 0:768].rearrange("p (c t) -> p c t", t=128)),
             reads=[b_tpA], writes=[bks])
        S.op("dve", lambda e, ks=ks: e.tensor_copy(out=ks[0:64, 6, :], in_=tpA[0:64, 768:896]),
             reads=[b_tpA], writes=[bks])
        S.dma("sp", kTv[:, :, t * 128:(t + 1) * 128], ks[:, 0:2, :], reads=[bks])
        S.dma("sp", iqTv[:, :, t * 128:(t + 1) * 128], ks[:, 2:6, :], reads=[bks])
        S.dma("sp", ikT[:, t * 128:(t + 1) * 128], ks[0:64, 6, :], reads=[bks])


TOPK = 256
NEG_MASK = -3.0e38
NEG_LO = -1.0e30
N_BISECT = 22
MASK_BIG = 32768.0


def stage_attn(S, C, NQ, kT, v, ikT, qT, iqT, iw, maskc, negI, o, dbg=None, solo=False):
    NKTT = NQ if solo else 2 * NQ
    SK = NKTT * 128
    nkt_of = (lambda j: j + 1) if solo else (lambda j: 2 * j + 2)
    kT_sb = S.sbuf("at_kT", [128, 2, SK], BF16)
    v_sb = S.sbuf("at_v", [128, NKTT, 4, 65], BF16)
    ik_sb = S.sbuf("at_ik", [64, SK], BF16)
    b_kT, b_v, b_ik = S.bufs(3, "at_res")
    for hh in range(2):
        S.dma("sp", kT_sb[hh * 64:(hh + 1) * 64, :, :], kT[2 * hh:2 * hh + 2, :, :].rearrange("g d s -> d g s"), writes=[b_kT])
    S.op("pool", lambda e: e.memset(v_sb[:, :, :, 64:65], 1.0), writes=[b_v])
    vv = v.rearrange("(k p) (g d) -> p k g d", p=128, d=64)
    KCH = 16
    for k0 in range(0, NKTT, KCH):
        k1 = min(NKTT, k0 + KCH)
        for g in range(4):
            S.dma("sp", v_sb[:, k0:k1, g, 0:64], vv[:, k0:k1, g, :], writes=[b_v])
    S.dma("sp", ik_sb[:, :], ikT[:, :], writes=[b_ik])
    mc = S.sbuf("at_mc", [128, 256], F32)
    nI = S.sbuf("at_nI", [128, 4, 128], BF16)
    b_mc, b_nI = S.bufs(2, "at_c")
    S.dma("sp", mc[:, :], maskc[:, :], writes=[b_mc])
    for r in range(4):
        S.dma("pool", nI[:, r, :], negI[:, :], writes=[b_nI])

    sc = S.sbuf("at_sc", [128, SK], F32)
    b_sc = S.buf("at_sc")
    junk = S.sbuf("at_junk", [128, SK], BF16)
    b_junk = S.buf("at_junk")
    mb = [S.sbuf("at_mb%d" % i, [128, SK], BF16) for i in range(2)]
    b_mb = S.bufs(2, "at_mb")
    t_sb = [S.sbuf("at_t%d" % i, [128, 512], F32) for i in range(2)]
    b_t = S.bufs(2, "at_t")
    PT = [S.sbuf("at_PT%d" % i, [128, 512], BF16) for i in range(3)]
    b_PT = S.bufs(3, "at_PT")
    q_sb = [S.sbuf("at_q%d" % i, [128, 2, 4, 128], BF16) for i in range(2)]
    b_q = S.bufs(2, "at_q")
    iq_sb = [S.sbuf("at_iq%d" % i, [64, 8, 128], BF16) for i in range(2)]
    b_iq = S.bufs(2, "at_iq")
    iw_sb = [S.sbuf("at_iw%d" % i, [128, 8], F32) for i in range(2)]
    b_iw = S.bufs(2, "at_iw")
    o_sb = [S.sbuf("at_o%d" % i, [128, 16, 64], BF16) for i in range(2)]
    b_o = S.bufs(2, "at_o")
    sm = S.sbuf("at_sm", [128, 16], F32)
    b_sm = S.bufs(8, "at_sm")
    rc = S.sbuf("at_rc", [128, 16], F32)
    b_rc = S.buf("at_rc")
    s_ps = [S.psum("at_sps%d" % i, [128, 512], F32) for i in range(2)]
    b_sps = S.bufs(2, "at_sps")
    l_ps = [S.psum("at_lps%d" % i, [128, 512], F32) for i in range(2)]
    b_lps = S.bufs(2, "at_lps")
    o_ps = [S.psum("at_ops%d" % i, [128, 7, 65], F32) for i in range(3)]
    b_ops = S.bufs(3, "at_ops")
    LO, HI, MID, CNT, GE, DD, MM = range(7)
    col = lambda i: sm[:, i:i + 1]

    n_s = 0
    n_l = 0
    n_pt = 0

    def load_q(j):
        qs, bq = q_sb[j % 2], b_q[j % 2]
        for hh in range(2):
            S.dma("sp", qs[hh * 64:(hh + 1) * 64, :, :, :].rearrange("d a r q -> d (a r) q"),
                  qT[8 * hh:8 * hh + 8, :, j * 128:(j + 1) * 128].rearrange("h d q -> d h q"), writes=[bq])
        S.dma("sp", iq_sb[j % 2][:, :, :], iqT[:, :, j * 128:(j + 1) * 128].rearrange("h d q -> d h q"), writes=[b_iq[j % 2]])
        S.dma("sp", iw_sb[j % 2][:, :], iw[j * 128:(j + 1) * 128, :], writes=[b_iw[j % 2]])

    def phase12(j):
        nonlocal n_s
        nk = nkt_of(j) * 128
        iqs, biq = iq_sb[j % 2], b_iq[j % 2]
        iws, biw = iw_sb[j % 2], b_iw[j % 2]
        for hd in range(8):
            for k0 in range(0, nk, 512):
                w = min(512, nk - k0)
                ps, bps = s_ps[n_s % 2], b_sps[n_s % 2]
                tt, bt = t_sb[n_s % 2], b_t[n_s % 2]
                n_s += 1
                S.op("pe", lambda e, ps=ps, hd=hd, k0=k0, w=w, iqs=iqs: e.matmul(
                    ps[:, 0:w], lhsT=iqs[:, hd, :], rhs=ik_sb[:, k0:k0 + w], start=True, stop=True),
                     reads=[biq, b_ik], writes=[bps])
                S.op("act", lambda e, ps=ps, tt=tt, w=w: e.activation(out=tt[:, 0:w], in_=ps[:, 0:w], func=AF.Relu),
                     reads=[bps], writes=[bt])
                if hd == 0:
                    S.op("dve", lambda e, tt=tt, k0=k0, w=w, iws=iws: e.tensor_scalar(
                        out=sc[:, k0:k0 + w], in0=tt[:, 0:w], scalar1=iws[:, 0:1], scalar2=None, op0=ALU.mult),
                         reads=[bt, biw], writes=[b_sc])
                else:
                    S.op("dve", lambda e, tt=tt, k0=k0, w=w, iws=iws, hd=hd: e.scalar_tensor_tensor(
                        out=sc[:, k0:k0 + w], in0=tt[:, 0:w], scalar=iws[:, hd:hd + 1], in1=sc[:, k0:k0 + w],
                        op0=ALU.mult, op1=ALU.add), reads=[bt, biw, b_sc], writes=[b_sc])
        S.op("dve", lambda e: e.tensor_reduce(out=col(MM), in_=sc[:, 0:nk], axis=AX.X, op=ALU.max, apply_absolute_value=True),
             reads=[b_sc], writes=[b_sm[MM]])
        S.op("dve", lambda e: e.tensor_scalar(out=col(HI), in0=col(MM), scalar1=1.0, scalar2=None, op0=ALU.add),
             reads=[b_sm[MM]], writes=[b_sm[HI]])
        S.op("dve", lambda e: e.tensor_scalar(out=col(LO), in0=col(HI), scalar1=-1.0, scalar2=None, op0=ALU.mult),
             reads=[b_sm[HI]], writes=[b_sm[LO]])
        if solo:
            S.op("dve", lambda e: e.tensor_tensor(out=sc[:, nk - 128:nk], in0=sc[:, nk - 128:nk], in1=mc[:, 128:256], op=ALU.add),
                 reads=[b_sc, b_mc], writes=[b_sc])
        else:
            S.op("dve", lambda e: e.tensor_tensor(out=sc[:, nk - 256:nk], in0=sc[:, nk - 256:nk], in1=mc[:, :], op=ALU.add),
                 reads=[b_sc, b_mc], writes=[b_sc])
        for it in range(N_BISECT):
            S.op("dve", lambda e: e.tensor_tensor(out=col(MID), in0=col(LO), in1=col(HI), op=ALU.add),
                 reads=[b_sm[LO], b_sm[HI]], writes=[b_sm[MID]])
            S.op("dve", lambda e: e.tensor_scalar(out=col(MID), in0=col(MID), scalar1=0.5, scalar2=None, op0=ALU.mult),
                 reads=[b_sm[MID]], writes=[b_sm[MID]])
            S.op("dve", lambda e: e.tensor_scalar(out=junk[:, 0:nk], in0=sc[:, 0:nk], scalar1=col(MID), scalar2=0.0,
                                                   op0=ALU.is_ge, op1=ALU.add, accum_out=col(CNT)),
                 reads=[b_sc, b_sm[MID]], writes=[b_junk, b_sm[CNT]])
            S.op("dve", lambda e: e.tensor_scalar(out=col(GE), in0=col(CNT), scalar1=TOPK - 0.5, scalar2=None, op0=ALU.is_ge),
                 reads=[b_sm[CNT]], writes=[b_sm[GE]])
            S.op("dve", lambda e: e.tensor_tensor(out=col(DD), in0=col(MID), in1=col(LO), op=ALU.subtract),
                 reads=[b_sm[MID], b_sm[LO]], writes=[b_sm[DD]])
            S.op("dve", lambda e: e.scalar_tensor_tensor(out=col(LO), in0=col(DD), scalar=col(GE), in1=col(LO),
                                                          op0=ALU.mult, op1=ALU.add),
                 reads=[b_sm[DD], b_sm[GE], b_sm[LO]], writes=[b_sm[LO]])
            S.op("dve", lambda e: e.tensor_tensor(out=col(DD), in0=col(HI), in1=col(MID), op=ALU.subtract),
                 reads=[b_sm[MID], b_sm[HI]], writes=[b_sm[DD]])
            S.op("dve", lambda e: e.scalar_tensor_tensor(out=col(HI), in0=col(DD), scalar=col(GE), in1=col(MID),
                                                          op0=ALU.mult, op1=ALU.add),
                 reads=[b_sm[DD], b_sm[GE], b_sm[MID]], writes=[b_sm[HI]])
        m, bm = mb[j % 2], b_mb[j % 2]
        S.op("dve", lambda e, m=m: e.tensor_scalar(out=m[:, 0:nk], in0=sc[:, 0:nk], scalar1=col(LO), scalar2=None, op0=ALU.is_lt),
             reads=[b_sc, b_sm[LO]], writes=[bm])
        if dbg is not None and j == dbg["j"]:
            S.dma("sp", dbg["sc"][:, 0:nk], sc[:, 0:nk], reads=[b_sc])
            S.dma("sp", dbg["mb"][:, 0:nk], m[:, 0:nk], reads=[bm])
            S.dma("sp", dbg["sm"][:, :], sm[:, :], reads=b_sm)

    def phase3(j):
        nonlocal n_l, n_pt
        NKT = nkt_of(j)
        qs, bq = q_sb[j % 2], b_q[j % 2]
        m, bm = mb[j % 2], b_mb[j % 2]
        for kt in range(NKT):
            for g in range(4):
                hh, a = g // 2, g % 2
                lp, blp = l_ps[n_l % 2], b_lps[n_l % 2]
                n_l += 1
                pt, bpt = PT[n_pt % 3], b_PT[n_pt % 3]
                n_pt += 1
                S.op("pe", lambda e, lp=lp, hh=hh, a=a, kt=kt, qs=qs: e.matmul(
                    lp[:, :], lhsT=kT_sb[hh * 64:(hh + 1) * 64, a, kt * 128:(kt + 1) * 128],
                    rhs=qs[hh * 64:(hh + 1) * 64, a, :, :].rearrange("d r q -> d (r q)"), start=True, stop=False),
                     reads=[b_kT, bq], writes=[blp])
                S.op("pe", lambda e, lp=lp, kt=kt, m=m: e.matmul(
                    lp[:, :], lhsT=m[:, kt * 128:(kt + 1) * 128], rhs=nI[:, :, :].rearrange("p r q -> p (r q)"),
                    start=False, stop=True), reads=[bm, b_nI], writes=[blp])
                S.op("act", lambda e, lp=lp, pt=pt: e.activation(out=pt[:, :], in_=lp[:, :], func=AF.Exp),
                     reads=[blp], writes=[bpt])
                for r in range(4):
                    hd = 4 * g + r
                    bank, slot = hd // 7, hd % 7
                    S.op("pe", lambda e, pt=pt, r=r, kt=kt, g=g, bank=bank, slot=slot: e.matmul(
                        o_ps[bank][:, slot, :], lhsT=pt[:, r * 128:(r + 1) * 128], rhs=v_sb[:, kt, g, :],
                        start=(kt == 0 and slot == 0), stop=(kt == NKT - 1), skip_group_check=True),
                         reads=[bpt, b_v], writes=[b_ops[bank]])
        ob, bo = o_sb[j % 2], b_o[j % 2]
        for bank in range(3):
            nh = min(7, 16 - 7 * bank)
            S.op("dve", lambda e, bank=bank, nh=nh: e.reciprocal(out=rc[:, 0:nh], in_=o_ps[bank][:, 0:nh, 64]),
                 reads=[b_ops[bank]], writes=[b_rc])
            S.op("dve", lambda e, bank=bank, nh=nh, ob=ob: e.tensor_tensor(
                out=ob[:, 7 * bank:7 * bank + nh, :], in0=o_ps[bank][:, 0:nh, 0:64],
                in1=rc[:, 0:nh].unsqueeze(2).to_broadcast([128, nh, 64]), op=ALU.mult),
                 reads=[b_ops[bank], b_rc], writes=[bo])
        S.dma("sp", o[j * 128:(j + 1) * 128, :], ob[:, :, :].rearrange("p h d -> p (h d)"), reads=[bo])

    load_q(0)
    phase12(0)
    for j in range(NQ):
        if j + 1 < NQ:
            load_q(j + 1)
            phase12(j + 1)
        phase3(j)


def stage_wo_ln(S, C, T, o, h, w_o, g, b, h1, gate=None):
    wo = S.sbuf("wo", [128, 8, D], BF16)
    b_wo = S.bufs(8, "wo")
    wv = w_o.rearrange("(c p) n -> p c n", p=128)
    for c in range(8):
        S.dma("pool", wo[:, c, :], wv[:, c, :], writes=[b_wo[c]])
    g_t, b_g = load_bcast(S, "wo_g", g, D)
    bt_t, b_bt = load_bcast(S, "wo_b", b, D)
    ob = [S.sbuf("wo_ob%d" % i, [128, D], BF16) for i in range(2)]
    b_ob = S.bufs(2, "wo_ob")
    hs = [S.sbuf("wo_hs%d" % i, [128, D], F32) for i in range(2)]
    b_hs = S.bufs(2, "wo_hs")
    oT = S.sbuf("wo_oT", [128, 8, 128], BF16)
    b_oT = S.buf("wo_oT")
    y_sb = [S.sbuf("wo_y%d" % i, [128, D], F32) for i in range(2)]
    b_y = S.bufs(2, "wo_y")
    tp = S.psum("wo_tp", [128, 1024], BF16)
    b_tp = S.buf("wo_tp")
    mx = [S.psum("wo_mx%d" % i, [128, 512], F32) for i in range(4)]
    b_mx = S.bufs(4, "wo_mx")
    scr = ln_scratch(S, "wo")
    ov = o.rearrange("(t p) d -> t p d", p=128) if gate is None else None
    if gate is not None:
        gate_a = [S.sbuf("wo_ga%d" % i, [128, D], F32) for i in range(2)]
        gate_b = [S.sbuf("wo_gb%d" % i, [128, D], F32) for i in range(2)]
        b_ga = S.bufs(2, "wo_ga")
        b_gb = S.bufs(2, "wo_gb")
    hv = h.rearrange("(t p) d -> t p d", p=128)
    h1v = h1.rearrange("(t p) d -> t p d", p=128)
    for t in range(T // 128):
        x, bx = ob[t % 2], b_ob[t % 2]
        hh, bh = hs[t % 2], b_hs[t % 2]
        if gate is None:
            S.dma("sp", x[:, :], ov[t], writes=[bx])
        else:
            ga, gb_ = gate_a[t % 2], gate_b[t % 2]
            S.dma("sp", ga[:, :], gate[0].rearrange("(t p) d -> t p d", p=128)[t], writes=[b_ga[t % 2]])
            S.dma("sp", gb_[:, :], gate[1].rearrange("(t p) d -> t p d", p=128)[t], writes=[b_gb[t % 2]])
            S.op("pool", lambda e, x=x, ga=ga, gb_=gb_: e.tensor_tensor(out=x[:, :], in0=ga[:, :], in1=gb_[:, :], op=ALU.mult),
                 reads=[b_ga[t % 2], b_gb[t % 2]], writes=[bx])
        S.dma("sp", hh[:, :], hv[t], writes=[bh])
        for c in range(8):
            S.op("pe", lambda e, c=c, x=x: e.transpose(out=tp[:, c * 128:(c + 1) * 128], in_=x[:, c * 128:(c + 1) * 128],
                                                       identity=C.ident[:]), reads=[bx, C.b_ident], writes=[b_tp])
        S.op("act", lambda e: e.activation(out=oT[:, :, :], in_=tp[:, :].rearrange("p (c t) -> p c t", t=128), func=AF.Copy),
             reads=[b_tp], writes=[b_oT])
        for nh in range(2):
            ps, bps = mx[(2 * t + nh) % 4], b_mx[(2 * t + nh) % 4]
            for c in range(8):
                S.op("pe", lambda e, ps=ps, c=c, nh=nh: e.matmul(ps[:, :], lhsT=oT[:, c, :], rhs=wo[:, c, nh * 512:(nh + 1) * 512],
                                                                  start=(c == 0), stop=(c == 7)),
                     reads=[b_oT, b_wo[c]], writes=[bps])
            S.op("dve", lambda e, ps=ps, hh=hh, nh=nh: e.scalar_tensor_tensor(
                out=hh[:, nh * 512:(nh + 1) * 512], in0=hh[:, nh * 512:(nh + 1) * 512], scalar=ALPHA, in1=ps[:, :],
                op0=ALU.mult, op1=ALU.add), reads=[bps, bh], writes=[bh])
        y, by = y_sb[t % 2], b_y[t % 2]
        layer_norm_tile(S, hh, bh, y, by, g_t, b_g, bt_t, b_bt, scr)
        S.dma("sp", h1v[t], y[:, :], reads=[by])


B_IN = 3088
GATE_TAU = 16.0


def stage_proj_B(S, C, T, h, w_in, w_a2, b_a, cmU, cmW, qgT, kgT, kh, vb, dec, sr):
    NTL = T // 128
    win = S.sbuf("pb_win", [128, 8, B_IN], BF16)
    b_win = S.bufs(8, "pb_win")
    wv = w_in.rearrange("(c p) n -> p c n", p=128)
    for c in range(8):
        S.dma("pool", win[:, c, :], wv[:, c, :], writes=[b_win[c]])
    wa2 = S.sbuf("pb_wa2", [16, 512], BF16)
    U = S.sbuf("pb_U", [128, 128], BF16)
    W = S.sbuf("pb_W", [128, 128], BF16)
    b_wa2, b_U, b_W = S.bufs(3, "pb_c")
    S.dma("pool", wa2[:, :], w_a2[:, :], writes=[b_wa2])
    S.dma("pool", U[:, :], cmU[:, :], writes=[b_U])
    S.dma("pool", W[:, :], cmW[:, :], writes=[b_W])
    ba_t, b_ba = load_bcast(S, "pb_ba", b_a, 512)

    hs = [S.sbuf("pb_hs%d" % i, [128, D], F32) for i in range(2)]
    b_hs = S.bufs(2, "pb_hs")
    hb = S.sbuf("pb_hb", [128, D], BF16)
    b_hb = S.buf("pb_hb")
    hT = S.sbuf("pb_hT", [128, 8, 128], BF16)
    b_hT = S.buf("pb_hT")
    alT = S.sbuf("pb_alT", [16, 128], BF16)
    b_alT = S.buf("pb_alT")
    gg = S.sbuf("pb_g", [128, 512], F32)
    b_gg = S.buf("pb_g")
    ghi = S.sbuf("pb_ghi", [128, 512], BF16)
    glo = S.sbuf("pb_glo", [128, 512], BF16)
    b_ghi, b_glo = S.bufs(2, "pb_gs")
    EbT = S.sbuf("pb_EbT", [128, 4, 128], F32)
    EnbT = S.sbuf("pb_EnbT", [128, 4, 128], F32)
    Ebl = S.sbuf("pb_Ebl", [128, 512], F32)
    b_EbT, b_EnbT, b_Ebl = S.bufs(3, "pb_E")
    qg_s = [S.sbuf("pb_qg%d" % i, [128, 4, 128], BF16) for i in range(2)]
    kg_s = [S.sbuf("pb_kg%d" % i, [128, 4, 128], BF16) for i in range(2)]
    kh_s = [S.sbuf("pb_kh%d" % i, [128, 512], BF16) for i in range(2)]
    vb_s = [S.sbuf("pb_vb%d" % i, [128, 1024], BF16) for i in range(2)]
    sr_s = [S.sbuf("pb_sr%d" % i, [128, 1024], F32) for i in range(2)]
    dc_s = [S.sbuf("pb_dc%d" % i, [128, 4, 2], F32) for i in range(2)]
    b_qg, b_kg, b_kh, b_vb, b_sr, b_dc = (S.bufs(2, "pb_o%d" % i) for i in range(6))
    tpT = S.psum("pb_tp", [128, 1024], BF16)
    b_tpT = S.buf("pb_tp")
    pk = [S.psum("pb_pk%d" % i, [128, 512], F32) for i in range(7)]
    b_pk = S.bufs(7, "pb_pk")
    QSCALE = float(128 ** -0.5)

    hv = h.rearrange("(t p) d -> t p d", p=128)
    for t in range(NTL):
        x, bx = hs[t % 2], b_hs[t % 2]
        i2 = t % 2
        S.dma("sp", x[:, :], hv[t], writes=[bx])
        S.op("act", lambda e, x=x: e.activation(out=hb[:, :], in_=x[:, :], func=AF.Copy), reads=[bx], writes=[b_hb])
        for c in range(8):
            S.op("pe", lambda e, c=c: e.transpose(out=tpT[:, c * 128:(c + 1) * 128], in_=hb[:, c * 128:(c + 1) * 128],
                                                  identity=C.ident[:]), reads=[b_hb, C.b_ident], writes=[b_tpT])
        S.op("dve", lambda e: e.tensor_copy(out=hT[:, :, :], in_=tpT[:, :].rearrange("p (c t) -> p c t", t=128)),
             reads=[b_tpT], writes=[b_hT])

        def tok_mm(bank, n0, n1):
            for c in range(8):
                S.op("pe", lambda e, c=c: e.matmul(pk[bank][:, 0:n1 - n0], lhsT=hT[:, c, :], rhs=win[:, c, n0:n1],
                                                   start=(c == 0), stop=(c == 7)), reads=[b_hT, b_win[c]], writes=[b_pk[bank]])

        def feat_mm(bank, slot, n0, m):
            for c in range(8):
                S.op("pe", lambda e, c=c: e.matmul(pk[bank][0:m, slot * 128:(slot + 1) * 128], lhsT=win[:, c, n0:n0 + m],
                                                   rhs=hT[:, c, :], start=(c == 0 and slot == 0), stop=(c == 7),
                                                   skip_group_check=True), reads=[b_hT, b_win[c]], writes=[b_pk[bank]])

        feat_mm(6, 0, 3072, 16)
        S.op("act", lambda e: e.activation(out=alT[:, :], in_=pk[6][0:16, 0:128], func=AF.Copy), reads=[b_pk[6]], writes=[b_alT])
        S.op("pe", lambda e: e.matmul(pk[5][:, :], lhsT=alT[:, :], rhs=wa2[:, :], start=True, stop=True),
             reads=[b_alT, b_wa2], writes=[b_pk[5]])
        S.op("dve", lambda e: e.tensor_tensor(out=gg[:, :], in0=pk[5][:, :], in1=ba_t[:, :], op=ALU.add),
             reads=[b_pk[5], b_ba], writes=[b_gg])
        S.op("act", lambda e: e.activation(out=gg[:, :], in_=gg[:, :], func=AF.Exp, scale=-1.0), reads=[b_gg], writes=[b_gg])
        S.op("dve", lambda e: e.tensor_scalar(out=gg[:, :], in0=gg[:, :], scalar1=1.0, scalar2=None, op0=ALU.add),
             reads=[b_gg], writes=[b_gg])
        S.op("act", lambda e: e.activation(out=gg[:, :], in_=gg[:, :], func=AF.Ln), reads=[b_gg], writes=[b_gg])
        S.op("dve", lambda e: e.tensor_scalar(out=gg[:, :], in0=gg[:, :], scalar1=-1.0 / GATE_TAU, scalar2=None, op0=ALU.mult),
             reads=[b_gg], writes=[b_gg])
        S.op("dve", lambda e: e.tensor_copy(out=ghi[:, :], in_=gg[:, :]), reads=[b_gg], writes=[b_ghi])
        S.op("dve", lambda e: e.tensor_tensor(out=glo[:, :], in0=gg[:, :], in1=ghi[:, :], op=ALU.subtract),
             reads=[b_gg, b_ghi], writes=[b_glo])
        for hd in range(4):
            for part, (gs, bgs) in enumerate(((ghi, b_ghi), (glo, b_glo))):
                S.op("pe", lambda e, hd=hd, gs=gs, part=part: e.matmul(
                    pk[4][:, hd * 128:(hd + 1) * 128], lhsT=gs[:, hd * 128:(hd + 1) * 128], rhs=U[:, :],
                    start=(hd == 0 and part == 0), stop=(part == 1), skip_group_check=True),
                     reads=[bgs, b_U], writes=[b_pk[4]])
        for part, (gs, bgs) in enumerate(((ghi, b_ghi), (glo, b_glo))):
            S.op("pe", lambda e, gs=gs, part=part: e.matmul(pk[5][:, :], lhsT=W[:, :], rhs=gs[:, :], start=(part == 0), stop=(part == 1)),
                 reads=[bgs, b_W], writes=[b_pk[5]])
        S.op("act", lambda e: e.activation(out=EbT[:, :, :], in_=pk[4][:, :].rearrange("p (h i) -> p h i", i=128), func=AF.Exp),
             reads=[b_pk[4]], writes=[b_EbT])
        S.op("act", lambda e: e.activation(out=EnbT[:, :, :], in_=pk[4][:, :].rearrange("p (h i) -> p h i", i=128), func=AF.Exp, scale=-1.0),
             reads=[b_pk[4]], writes=[b_EnbT])
        S.op("act", lambda e: e.activation(out=Ebl[:, :], in_=pk[5][:, :], func=AF.Exp), reads=[b_pk[5]], writes=[b_Ebl])
        dcs, bdc = dc_s[i2], b_dc[i2]
        S.op("pool", lambda e, dcs=dcs: e.tensor_copy(out=dcs[:, :, :], in_=EbT[:, :, :].rearrange("p h (c j) -> p h c j", j=64)[:, :, :, 63]),
             reads=[b_EbT], writes=[bdc])
        S.dma("sp", dec[:, :, 2 * t:2 * t + 2].rearrange("h d c -> d h c"), dcs[:, :, :], reads=[bdc])
        for hd in range(4):
            feat_mm(6, hd, hd * 128, 128)
        qgs, bqg = qg_s[i2], b_qg[i2]
        S.op("dve", lambda e, qgs=qgs: e.scalar_tensor_tensor(out=qgs[:, :, :], in0=pk[6][:, :].rearrange("p (h i) -> p h i", i=128),
                                                              scalar=QSCALE, in1=EbT[:, :, :], op0=ALU.mult, op1=ALU.mult),
             reads=[b_pk[6], b_EbT], writes=[bqg])
        S.dma("sp", qgT[:, :, t * 128:(t + 1) * 128].rearrange("h d i -> d h i"), qgs[:, :, :], reads=[bqg])
        for hd in range(4):
            feat_mm(3, hd, 512 + hd * 128, 128)
        kgs, bkg = kg_s[i2], b_kg[i2]
        S.op("dve", lambda e, kgs=kgs: e.tensor_tensor(out=kgs[:, :, :], in0=pk[3][:, :].rearrange("p (h i) -> p h i", i=128),
                                                       in1=EnbT[:, :, :], op=ALU.mult), reads=[b_pk[3], b_EnbT], writes=[bkg])
        S.dma("sp", kgT[:, :, t * 128:(t + 1) * 128].rearrange("h d i -> d h i"), kgs[:, :, :], reads=[bkg])
        tok_mm(2, 512, 1024)
        khs, bkh = kh_s[i2], b_kh[i2]
        S.op("dve", lambda e, khs=khs: e.tensor_tensor(out=khs[:, :], in0=pk[2][:, :], in1=Ebl[:, :], op=ALU.mult),
             reads=[b_pk[2], b_Ebl], writes=[bkh])
        S.dma("sp", kh[t * 128:(t + 1) * 128, :], khs[:, :], reads=[bkh])
        vbs, bvb = vb_s[i2], b_vb[i2]
        for half in range(2):
            tok_mm(half, 1024 + half * 512, 1536 + half * 512)
            S.op("act", lambda e, half=half, vbs=vbs: e.activation(out=vbs[:, half * 512:(half + 1) * 512], in_=pk[half][:, :], func=AF.Copy),
                 reads=[b_pk[half]], writes=[bvb])
        S.dma("sp", vb[t * 128:(t + 1) * 128, :], vbs[:, :], reads=[bvb])
        srs, bsr = sr_s[i2], b_sr[i2]
        for half in range(2):
            tok_mm(half, 2048 + half * 512, 2560 + half * 512)
            S.op("act", lambda e, half=half, srs=srs: e.activation(out=srs[:, half * 512:(half + 1) * 512], in_=pk[half][:, :], func=AF.Silu),
                 reads=[b_pk[half]], writes=[bsr])
        S.dma("sp", sr[t * 128:(t + 1) * 128, :], srs[:, :], reads=[bsr])


RMS_EPS = 1e-6
GLA_CB = 16


def stage_gla(S, C, SEQ, NH, acc, dec_ap, g_norm, tri, CB=GLA_CB):
    NBLK = SEQ // (64 * CB)
    NCH = SEQ // 64
    tri_f = S.sbuf("gl_trif", [64, 64], F32)
    b_tri = S.buf("gl_tri")
    S.dma("sp", tri_f[:, :], tri[:, :], writes=[b_tri])
    gn = S.sbuf("gl_gn", [64, 256], F32)
    b_gn = S.buf("gl_gn")
    S.dma("sp", gn[:, :], g_norm.partition_broadcast(64), writes=[b_gn])
    dec_sb = S.sbuf("gl_dec", [128, NH, NCH], F32)
    b_dec = S.buf("gl_dec")
    for hd in range(NH):
        S.dma("sp", dec_sb[:, hd, :], dec_ap(hd), writes=[b_dec])
    st = S.sbuf("gl_st", [128, NH, 256], F32)
    b_st = S.bufs(NH, "gl_st")
    stb = [S.sbuf("gl_stb%d" % i, [128, NH, 256], BF16) for i in range(2)]
    b_stb = [S.bufs(NH, "gl_stb%d_" % i) for i in range(2)]
    S.op("dve", lambda e: e.memset(st[:, :, :], 0.0), writes=b_st)
    S.op("dve", lambda e: e.memset(stb[0][:, :, :], 0.0), writes=b_stb[0])
    qg_sb = [[S.sbuf("gl_qg%d_%d" % (i, hd), [128, 64 * CB], BF16) for hd in range(NH)] for i in range(2)]
    kg_sb = [[S.sbuf("gl_kg%d_%d" % (i, hd), [128, 64 * CB], BF16) for hd in range(NH)] for i in range(2)]
    kh_sb = [[S.sbuf("gl_kh%d_%d" % (i, hd), [64, CB, 128], BF16) for hd in range(NH)] for i in range(2)]
    v_sb = [[S.sbuf("gl_v%d_%d" % (i, hd), [64, CB, 256], BF16) for hd in range(NH)] for i in range(2)]
    b_in = [[S.bufs(4, "gl_in%d_%d_" % (i, hd)) for hd in range(NH)] for i in range(2)]
    o_st = [[S.sbuf("gl_ost%d_%d" % (i, hd), [64, CB, 256], F32) for hd in range(NH)] for i in range(2)]
    b_ost = [[S.buf("gl_ost%d_%d" % (i, hd)) for hd in range(NH)] for i in range(2)]
    ss = [[S.sbuf("gl_ss%d_%d" % (i, hd), [64, CB], F32) for hd in range(NH)] for i in range(2)]
    b_ss = [[S.buf("gl_ss%d_%d" % (i, hd)) for hd in range(NH)] for i in range(2)]
    junk = S.sbuf("gl_junk", [64, 256], F32)
    b_junk = S.buf("gl_junk")
    A_sb = [S.sbuf("gl_A%d" % i, [64, 64], BF16) for i in range(2)]
    b_A = S.bufs(2, "gl_A")
    a_ps = [S.psum("gl_aps%d" % i, [64, 512], F32) for i in range(2)]
    b_aps = S.bufs(2, "gl_aps")
    o_ps = [S.psum("gl_ops%d" % i, [64, 512], F32) for i in range(2)]
    b_ops = S.bufs(2, "gl_ops")
    s_ps = [S.psum("gl_sps%d" % i, [128, 512], F32) for i in range(2)]
    b_sps = S.bufs(2, "gl_sps")
    n = 0
    for blk in range(NBLK):
        i2 = blk % 2
        for hd in range(NH):
            bi = b_in[i2][hd]
            S.dma("sp", qg_sb[i2][hd][:, :], acc("qg", hd, blk), writes=[bi[0]])
            S.dma("sp", kg_sb[i2][hd][:, :], acc("kg", hd, blk), writes=[bi[1]])
            S.dma("sp", kh_sb[i2][hd][:, :, :], acc("kh", hd, blk), writes=[bi[2]])
            S.dma("sp", v_sb[i2][hd][:, :, :], acc("v", hd, blk), writes=[bi[3]])
        for cc in range(CB):
            c = blk * CB + cc
            cur, nxt = c % 2, (c + 1) % 2
            for hd in range(NH):
                bi = b_in[i2][hd]
                qg = qg_sb[i2][hd][:, cc * 64:(cc + 1) * 64]
                kg = kg_sb[i2][hd][:, cc * 64:(cc + 1) * 64]
                khc = kh_sb[i2][hd][:, cc, :]
                vc = v_sb[i2][hd][:, cc, :]
                aps, baps = a_ps[n % 2], b_aps[n % 2]
                ops, bops = o_ps[n % 2], b_ops[n % 2]
                sps, bsps = s_ps[n % 2], b_sps[n % 2]
                A, bA = A_sb[n % 2], b_A[n % 2]
                n += 1
                S.op("pe", lambda e, aps=aps, kg=kg, qg=qg: e.matmul(aps[:, 0:64], lhsT=kg, rhs=qg, start=True, stop=True),
                     reads=[bi[0], bi[1]], writes=[baps])
                S.op("dve", lambda e, aps=aps, A=A: e.tensor_tensor(out=A[:, :], in0=aps[:, 0:64], in1=tri_f[:, :], op=ALU.mult),
                     reads=[baps, b_tri], writes=[bA])
                S.op("pe", lambda e, ops=ops, A=A, vc=vc: e.matmul(ops[:, 0:256], lhsT=A[:, :], rhs=vc, start=True, stop=False),
                     reads=[bA, bi[3]], writes=[bops])
                S.op("pe", lambda e, ops=ops, qg=qg, cur=cur, hd=hd: e.matmul(ops[:, 0:256], lhsT=qg, rhs=stb[cur][:, hd, :],
                                                                             start=False, stop=True),
                     reads=[bi[0], b_stb[cur][hd]], writes=[bops])
                S.op("pe", lambda e, sps=sps, khc=khc, vc=vc: e.matmul(sps[:, 0:256], lhsT=khc, rhs=vc, start=True, stop=True),
                     reads=[bi[2], bi[3]], writes=[bsps])
                S.op("dve", lambda e, sps=sps, hd=hd, c=c: e.scalar_tensor_tensor(
                    out=st[:, hd, :], in0=st[:, hd, :], scalar=dec_sb[:, hd, c:c + 1], in1=sps[:, 0:256],
                    op0=ALU.mult, op1=ALU.add), reads=[bsps, b_st[hd], b_dec], writes=[b_st[hd]])
                S.op("act", lambda e, hd=hd, nxt=nxt: e.activation(out=stb[nxt][:, hd, :], in_=st[:, hd, :], func=AF.Copy),
                     reads=[b_st[hd]], writes=[b_stb[nxt][hd]])
                ssc = ss[i2][hd][:, cc:cc + 1]
                ostc = o_st[i2][hd][:, cc, :]
                S.op("act", lambda e, ops=ops, ssc=ssc: e.activation(out=junk[:, :], in_=ops[:, 0:256], func=AF.Square, accum_out=ssc),
                     reads=[bops], writes=[b_junk, b_ss[i2][hd]])
                S.op("act", lambda e, ops=ops, ostc=ostc: e.activation(out=ostc, in_=ops[:, 0:256], func=AF.Copy),
                     reads=[bops], writes=[b_ost[i2][hd]])
        for hd in range(NH):
            s_, bs_ = ss[i2][hd], b_ss[i2][hd]
            S.op("dve", lambda e, s_=s_: e.tensor_scalar(out=s_[:, :], in0=s_[:, :], scalar1=1.0 / 256, scalar2=RMS_EPS,
                                                         op0=ALU.mult, op1=ALU.add), reads=[bs_], writes=[bs_])
            S.op("act", lambda e, s_=s_: e.activation(out=s_[:, :], in_=s_[:, :], func=AF.Sqrt), reads=[bs_], writes=[bs_])
            S.op("dve", lambda e, s_=s_: e.reciprocal(out=s_[:, :], in_=s_[:, :]), reads=[bs_], writes=[bs_])
            ot, bo = o_st[i2][hd], b_ost[i2][hd]
            S.op("dve", lambda e, ot=ot, s_=s_: e.tensor_tensor(out=ot[:, :, :], in0=ot[:, :, :],
                                                               in1=s_[:, :].unsqueeze(2).to_broadcast([64, CB, 256]), op=ALU.mult),
                 reads=[bo, bs_], writes=[bo])
            S.op("pool", lambda e, ot=ot: e.tensor_tensor(out=ot[:, :, :], in0=ot[:, :, :],
                                                         in1=gn[:, :].unsqueeze(1).to_broadcast([64, CB, 256]), op=ALU.mult),
                 reads=[bo, b_gn], writes=[bo])
            S.dma("sp", acc("on", hd, blk), ot[:, :, :], reads=[bo])


_NP2DT = {np.dtype("float32"): F32, np.dtype("int32"): I32}
try:
    import ml_dtypes
    _NP2DT[np.dtype(ml_dtypes.bfloat16)] = BF16
    NPBF16 = ml_dtypes.bfloat16
except Exception:
    NPBF16 = None

_PROG_CACHE = {}


def launch(key, prog, in_list, outs):
    sig = (key, tuple((k, v.shape, str(v.dtype)) for k, v in in_list[0].items()), tuple((k, tuple(sh), str(dt)) for k, (sh, dt) in outs.items()))
    if sig not in _PROG_CACHE:
        nc = bass.Bass("TRN2", target_bir_lowering=False)
        aps = {}
        for k, v in in_list[0].items():
            aps[k] = nc.dram_tensor(k, list(v.shape), _NP2DT[v.dtype], kind="ExternalInput").ap()
        for k, (shape, dt) in outs.items():
            aps[k] = nc.dram_tensor(k, list(shape), dt, kind="ExternalOutput").ap()
        S = Sched(nc)
        prog(S, aps)
        S.emit()
        _PROG_CACHE[sig] = nc
    nc = _PROG_CACHE[sig]
    res = run_bass_kernel_spmd(nc, in_list, core_ids=list(range(len(in_list))))
    return res.results


def _consts():
    ident = np.eye(128, dtype=np.float32)
    tri = np.where(np.arange(128)[None, :] <= np.arange(128)[:, None], 0.0, NEG_MASK).astype(np.float32)
    allm = np.full((128, 128), NEG_MASK, np.float32)
    none = np.zeros((128, 128), np.float32)
    maskc = [np.concatenate([tri, allm], 1), np.concatenate([none, tri], 1)]
    negI = (-MASK_BIG * np.eye(128)).astype(np.float32)
    invf = (500000.0 ** (-np.arange(0, 16, 2, dtype=np.float32) / 16)).astype(np.float32)
    j = np.arange(128)[:, None]
    i = np.arange(128)[None, :]
    same = (j // 64) == (i // 64)
    cmU = (same & (j <= i)).astype(np.float32)
    cmW = (same & (j > i)).astype(np.float32)
    tri64 = (np.arange(64)[:, None] <= np.arange(64)[None, :]).astype(np.float32)
    return dict(ident=ident, maskc=maskc, negI=negI, invf=invf, cmU=cmU, cmW=cmW, tri64=tri64)


def kernel_unfused(x, positions, a_w_in, a_w_o, b_w_in, b_w_a2, b_b_a, b_g_norm, b_w_o,
           ln_mix_g, ln_mix_b, mlp_w_up, mlp_w_down, ln_mlp_g, ln_mlp_b):
    x = np.asarray(x, np.float32)
    B, SEQ, _ = x.shape
    NC = 2 * B
    T = SEQ // 2
    NTL = T // 128
    NQ = NTL
    cst = _consts()
    f32 = lambda a: np.ascontiguousarray(np.asarray(a, np.float32))
    tok = []
    for c in range(NC):
        r = c % 2
        tiles = np.arange(r, SEQ // 128, 2)
        tok.append((tiles[:, None] * 128 + np.arange(128)[None, :]).reshape(-1))
    h = [np.ascontiguousarray(x[c // 2][tok[c]]) for c in range(NC)]
    pos_pt = [np.ascontiguousarray(np.asarray(positions)[c // 2][tok[c]].astype(np.int32).reshape(NTL, 128).T) for c in range(NC)]
    depth = ln_mix_g.shape[0]
    for i in range(depth):
        j = i // 2
        if i % 2 == 0:
            w_in = f32(a_w_in[j])
            ins = [dict(h=h[c], pos=pos_pt[c], w=w_in, invf=cst["invf"], ident=cst["ident"]) for c in range(NC)]
            outs = {"qT": ([1024, T], BF16), "kT": ([256, T], BF16), "v": ([T, 256], BF16), "iqT": ([512, T], BF16),
                    "ikT": ([64, T], BF16), "iw": ([T, 8], F32)}

            def prog(S, a):
                C = Consts(S, a["ident"])
                stage_proj_A(S, C, T, a["h"], a["pos"], a["w"], a["invf"], a["qT"], a["kT"], a["v"], a["iqT"], a["ikT"], a["iw"])
            pr = launch("projA", prog, ins, outs)
            ins = []
            for c in range(NC):
                b0 = (c // 2) * 2
                kT = np.empty((256, SEQ), NPBF16)
                ikT = np.empty((64, SEQ), NPBF16)
                vf = np.empty((SEQ, 256), NPBF16)
                for r in range(2):
                    kT[:, tok[b0 + r]] = pr[b0 + r]["kT"]
                    ikT[:, tok[b0 + r]] = pr[b0 + r]["ikT"]
                    vf[tok[b0 + r]] = pr[b0 + r]["v"]
                ins.append(dict(ident=cst["ident"], kT=kT, v=vf, ikT=ikT, qT=pr[c]["qT"], iqT=pr[c]["iqT"], iw=pr[c]["iw"],
                                maskc=cst["maskc"][c % 2], negI=cst["negI"]))

            def prog(S, a):
                C = Consts(S, a["ident"])
                stage_attn(S, C, NQ, a["kT"].rearrange("(g d) s -> g d s", d=64), a["v"], a["ikT"],
                           a["qT"].rearrange("(h d) t -> h d t", d=64), a["iqT"].rearrange("(h d) t -> h d t", d=64),
                           a["iw"], a["maskc"], a["negI"], a["o"])
            ar = launch("attn", prog, ins, {"o": ([T, 1024], BF16)})
            w_o = f32(a_w_o[j])
            ins = [dict(ident=cst["ident"], o=ar[c]["o"], h=h[c], w=w_o, g=f32(ln_mix_g[i]), b=f32(ln_mix_b[i])) for c in range(NC)]

            def prog(S, a):
                C = Consts(S, a["ident"])
                stage_wo_ln(S, C, T, a["o"], a["h"], a["w"], a["g"], a["b"], a["h1"])
            wr = launch("wo", prog, ins, {"h1": ([T, D], F32)})
        else:
            ins = [dict(h=h[c], w=f32(b_w_in[j]), wa2=f32(b_w_a2[j]), ba=f32(b_b_a[j]), U=cst["cmU"], W=cst["cmW"], ident=cst["ident"])
                   for c in range(NC)]
            outs = {"qgT": ([512, T], BF16), "kgT": ([512, T], BF16), "kh": ([T, 512], BF16), "vb": ([T, 1024], BF16),
                    "dec": ([512, T // 64], F32), "sr": ([T, 1024], F32)}

            def prog(S, a):
                C = Consts(S, a["ident"])
                stage_proj_B(S, C, T, a["h"], a["w"], a["wa2"], a["ba"], a["U"], a["W"],
                             a["qgT"].rearrange("(h d) t -> h d t", d=128), a["kgT"].rearrange("(h d) t -> h d t", d=128),
                             a["kh"], a["vb"], a["dec"].rearrange("(h d) c -> h d c", d=128), a["sr"])
            pr = launch("projB", prog, ins, outs)
            ins = []
            ctok = [t_.reshape(-1, 128)[:, ::64].reshape(-1) // 64 for t_ in tok]
            for c in range(NC):
                b0 = (c // 2) * 2
                hs = slice((c % 2) * 256, (c % 2) * 256 + 256)
                vs = slice((c % 2) * 512, (c % 2) * 512 + 512)
                qg = np.empty((256, SEQ), NPBF16)
                kg = np.empty((256, SEQ), NPBF16)
                khh = np.empty((SEQ, 256), NPBF16)
                vv = np.empty((SEQ, 512), NPBF16)
                dd = np.empty((256, SEQ // 64), np.float32)
                for r in range(2):
                    qg[:, tok[b0 + r]] = pr[b0 + r]["qgT"][hs]
                    kg[:, tok[b0 + r]] = pr[b0 + r]["kgT"][hs]
                    khh[tok[b0 + r]] = pr[b0 + r]["kh"][:, hs]
                    vv[tok[b0 + r]] = pr[b0 + r]["vb"][:, vs]
                    dd[:, ctok[b0 + r]] = pr[b0 + r]["dec"][hs]
                ins.append(dict(ident=cst["ident"], qgT=qg, kgT=kg, kh=khh, vb=vv, dec=dd, gn=f32(b_g_norm[j]), tri=cst["tri64"]))

            def prog(S, a):
                C = Consts(S, a["ident"])

                def acc(kind, hd, blk):
                    t0, t1 = blk * 1024, (blk + 1) * 1024
                    if kind == "qg":
                        return a["qgT"][hd * 128:(hd + 1) * 128, t0:t1]
                    if kind == "kg":
                        return a["kgT"][hd * 128:(hd + 1) * 128, t0:t1]
                    if kind == "kh":
                        return a["kh"][t0:t1, hd * 128:(hd + 1) * 128].rearrange("(c j) d -> j c d", j=64)
                    if kind == "v":
                        return a["vb"][t0:t1, hd * 256:(hd + 1) * 256].rearrange("(c j) e -> j c e", j=64)
                    return a["on"][t0:t1, hd * 256:(hd + 1) * 256].rearrange("(c j) e -> j c e", j=64)
                stage_gla(S, C, SEQ, 2, acc, lambda hd: a["dec"][hd * 128:(hd + 1) * 128, :], a["gn"], a["tri"])
            gr = launch("gla", prog, ins, {"on": ([SEQ, 512], F32)})
            w_o = f32(b_w_o[j])
            ins = []
            for c in range(NC):
                b0 = (c // 2) * 2
                on = np.concatenate([gr[b0]["on"][tok[c]], gr[b0 + 1]["on"][tok[c]]], axis=1)
                ins.append(dict(ident=cst["ident"], on=np.ascontiguousarray(on), sr=pr[c]["sr"], h=h[c], w=w_o,
                                g=f32(ln_mix_g[i]), b=f32(ln_mix_b[i])))

            def prog(S, a):
                C = Consts(S, a["ident"])
                stage_wo_ln(S, C, T, None, a["h"], a["w"], a["g"], a["b"], a["h1"], gate=(a["on"], a["sr"]))
            wr = launch("wog", prog, ins, {"h1": ([T, D], F32)})
        ins = [dict(ident=cst["ident"], h1=wr[c]["h1"], wu=f32(mlp_w_up[i]), wd=f32(mlp_w_down[i]), g=f32(ln_mlp_g[i]), b=f32(ln_mlp_b[i]))
               for c in range(NC)]

        def prog(S, a):
            C = Consts(S, a["ident"])
            stage_mlp(S, C, T, a["h1"], a["wu"], a["wd"], a["g"], a["b"], a["h2"])
        mr = launch("mlp", prog, ins, {"h2": ([T, D], F32)})
        h = [mr[c]["h2"] for c in range(NC)]
    out = np.empty((B, SEQ, D), np.float32)
    for c in range(NC):
        out[c // 2][tok[c]] = h[c]
    return out


def build_fused(SEQ, depth):
    nc = bass.Bass("TRN2", target_bir_lowering=False)
    T = SEQ
    NTL = T // 128

    def din(name, shape, dt=F32):
        return nc.dram_tensor(name, list(shape), dt, kind="ExternalInput").ap()

    def scr(name, shape, dt):
        return nc.dram_tensor("scr_" + name, list(shape), dt).ap()

    NA, NB = (depth + 1) // 2, depth // 2
    a = dict(
        x=din("x", [T, D]), pos=din("pos", [128, NTL], I32), invf=din("invf", [8]), ident=din("ident", [128, 128]),
        maskc=din("maskc", [128, 256]), negI=din("negI", [128, 128]), cmU=din("cmU", [128, 128]), cmW=din("cmW", [128, 128]),
        tri64=din("tri64", [64, 64]),
        a_w_in=din("a_w_in", [NA, D, A_IN]), a_w_o=din("a_w_o", [NA, D, D]),
        b_w_in=din("b_w_in", [max(NB, 1), D, B_IN]), b_w_a2=din("b_w_a2", [max(NB, 1), 16, 512]), b_b_a=din("b_b_a", [max(NB, 1), 512]),
        b_g_norm=din("b_g_norm", [max(NB, 1), 256]), b_w_o=din("b_w_o", [max(NB, 1), D, D]),
        ln_mix_g=din("ln_mix_g", [depth, D]), ln_mix_b=din("ln_mix_b", [depth, D]),
        mlp_w_up=din("mlp_w_up", [depth, D, DFF]), mlp_w_down=din("mlp_w_down", [depth, DFF, D]),
        ln_mlp_g=din("ln_mlp_g", [depth, D]), ln_mlp_b=din("ln_mlp_b", [depth, D]),
    )
    out = nc.dram_tensor("out", [T, D], F32, kind="ExternalOutput").ap()
    hA = scr("hA", [T, D], F32)
    hB = scr("hB", [T, D], F32)
    qT = scr("qT", [1024, T], BF16)
    kT = scr("kT", [256, T], BF16)
    vv = scr("v", [T, 256], BF16)
    iqT = scr("iqT", [512, T], BF16)
    ikT = scr("ikT", [64, T], BF16)
    iw = scr("iw", [T, 8], F32)
    o = scr("o", [T, 1024], BF16)
    qgT = scr("qgT", [512, T], BF16)
    kgT = scr("kgT", [512, T], BF16)
    kh = scr("kh", [T, 512], BF16)
    vb = scr("vb", [T, 1024], BF16)
    dec = scr("dec", [512, T // 64], F32)
    sr = scr("sr", [T, 1024], F32)
    on = scr("on", [T, 1024], F32)

    S = Sched(nc)
    CB = 8
    h_in = a["x"]
    for i in range(depth):
        j = i // 2
        last = i == depth - 1
        if i % 2 == 0:
            S.stage_begin()
            C = Consts(S, a["ident"])
            stage_proj_A(S, C, T, h_in, a["pos"], a["a_w_in"][j], a["invf"], qT, kT, vv, iqT, ikT, iw)
            S.stage_end()
            S.stage_begin()
            C = Consts(S, a["ident"])
            stage_attn(S, C, NTL, kT.rearrange("(g d) s -> g d s", d=64), vv, ikT, qT.rearrange("(h d) t -> h d t", d=64),
                       iqT.rearrange("(h d) t -> h d t", d=64), iw, a["maskc"], a["negI"], o, solo=True)
            S.stage_end()
            S.stage_begin()
            C = Consts(S, a["ident"])
            stage_wo_ln(S, C, T, o, h_in, a["a_w_o"][j], a["ln_mix_g"][i], a["ln_mix_b"][i], hB)
            S.stage_end()
        else:
            S.stage_begin()
            C = Consts(S, a["ident"])
            stage_proj_B(S, C, T, h_in, a["b_w_in"][j], a["b_w_a2"][j], a["b_b_a"][j], a["cmU"], a["cmW"],
                         qgT.rearrange("(h d) t -> h d t", d=128), kgT.rearrange("(h d) t -> h d t", d=128), kh, vb,
                         dec.rearrange("(h d) c -> h d c", d=128), sr)
            S.stage_end()
            S.stage_begin()
            C = Consts(S, a["ident"])

            def acc(kind, hd, blk):
                t0, t1 = blk * 64 * CB, (blk + 1) * 64 * CB
                if kind == "qg":
                    return qgT[hd * 128:(hd + 1) * 128, t0:t1]
                if kind == "kg":
                    return kgT[hd * 128:(hd + 1) * 128, t0:t1]
                if kind == "kh":
                    return kh[t0:t1, hd * 128:(hd + 1) * 128].rearrange("(c j) d -> j c d", j=64)
                if kind == "v":
                    return vb[t0:t1, hd * 256:(hd + 1) * 256].rearrange("(c j) e -> j c e", j=64)
                return on[t0:t1, hd * 256:(hd + 1) * 256].rearrange("(c j) e -> j c e", j=64)
            stage_gla(S, C, SEQ, 4, acc, lambda hd: dec[hd * 128:(hd + 1) * 128, :], a["b_g_norm"][j], a["tri64"], CB=CB)
            S.stage_end()
            S.stage_begin()
            C = Consts(S, a["ident"])
            stage_wo_ln(S, C, T, None, h_in, a["b_w_o"][j], a["ln_mix_g"][i], a["ln_mix_b"][i], hB, gate=(on, sr))
            S.stage_end()
        S.stage_begin()
        C = Consts(S, a["ident"])
        h_out = out if last else hA
        stage_mlp(S, C, T, hB, a["mlp_w_up"][i], a["mlp_w_down"][i], a["ln_mlp_g"][i], a["ln_mlp_b"][i], h_out)
        S.stage_end(last=last)
        h_in = hA
    S.stack.close()
    return nc


_FUSED = {}


def kernel(x, positions, a_w_in, a_w_o, b_w_in, b_w_a2, b_b_a, b_g_norm, b_w_o,
           ln_mix_g, ln_mix_b, mlp_w_up, mlp_w_down, ln_mlp_g, ln_mlp_b):
    x = np.asarray(x, np.float32)
    B, SEQ, _ = x.shape
    depth = int(np.asarray(ln_mix_g).shape[0])
    key = (SEQ, depth)
    if key not in _FUSED:
        _FUSED[key] = build_fused(SEQ, depth)
    nc = _FUSED[key]
    cst = _consts()
    f32 = lambda t: np.ascontiguousarray(np.asarray(t, np.float32))
    shared = dict(invf=cst["invf"], ident=cst["ident"], maskc=cst["maskc"][1], negI=cst["negI"], cmU=cst["cmU"], cmW=cst["cmW"],
                  tri64=cst["tri64"], a_w_in=f32(a_w_in), a_w_o=f32(a_w_o), b_w_in=f32(b_w_in), b_w_a2=f32(b_w_a2), b_b_a=f32(b_b_a),
                  b_g_norm=f32(b_g_norm), b_w_o=f32(b_w_o), ln_mix_g=f32(ln_mix_g), ln_mix_b=f32(ln_mix_b),
                  mlp_w_up=f32(mlp_w_up), mlp_w_down=f32(mlp_w_down), ln_mlp_g=f32(ln_mlp_g), ln_mlp_b=f32(ln_mlp_b))
    in_maps = []
    for c in range(B):
        pos_pt = np.ascontiguousarray(np.asarray(positions)[c].astype(np.int32).reshape(SEQ // 128, 128).T)
        m = dict(shared)
        m["x"] = np.ascontiguousarray(x[c])
        m["pos"] = pos_pt
        in_maps.append(m)
    res = run_bass_kernel_spmd(nc, in_maps, core_ids=list(range(B)))
    return np.stack([res.results[c]["out"] for c in range(B)], axis=0)
```

```python
import contextlib
import numpy as np
import concourse.bass as bass
import concourse.mybir as mybir
from concourse.ap import AP
from concourse.bass_utils import run_bass_kernel_spmd

F32 = mybir.dt.float32
BF16 = mybir.dt.bfloat16
I32 = mybir.dt.int32
AF = mybir.ActivationFunctionType
ALU = mybir.AluOpType
AX = mybir.AxisListType

D = 1024
DFF = 4096
DEPTH = 4
ALPHA = (2 * DEPTH) ** 0.25
LN_EPS = 1e-5
NCORES = 8


class Buf:
    __slots__ = ("name", "last_w", "readers")

    def __init__(self, name):
        self.name = name
        self.last_w = None
        self.readers = {}


class Sched:
    ENGS = ("sp", "act", "dve", "pool", "pe")
    SAME_WIN = 3
    NDMA = 8

    def __init__(self, nc):
        self.nc = nc
        self.stack = contextlib.ExitStack()
        self.ops = {e: [] for e in self.ENGS}
        self.count = {e: 0 for e in self.ENGS}
        self.seen = {e: {} for e in self.ENGS}
        self.esem = {}
        for e in ("act", "dve", "pool", "pe"):
            self.esem[e] = self.stack.enter_context(nc.semaphore("s_" + e))
        self.dsem = {}
        self.dma_i = {}
        for q in ("sp", "act", "pool"):
            self.dsem[q] = [self.stack.enter_context(nc.semaphore("d_%s%d" % (q, i))) for i in range(self.NDMA)]
            self.dma_i[q] = 0
        self.nbuf = 0
        self.stage_stack = None
        self.stage_no = 0
        self.bar = self.stack.enter_context(nc.semaphore("s_bar"))

    def sbuf(self, name, shape, dt):
        st = self.stage_stack if self.stage_stack is not None else self.stack
        return st.enter_context(self.nc.sbuf_tensor("sb%d_" % self.stage_no + name, list(shape), dt))

    def psum(self, name, shape, dt):
        st = self.stage_stack if self.stage_stack is not None else self.stack
        return st.enter_context(self.nc.psum_tensor("pp%d_" % self.stage_no + name, list(shape), dt))

    def stage_begin(self):
        self.stage_stack = contextlib.ExitStack()

    def stage_end(self, last=False):
        self.finish()
        if not last:
            self.stage_no += 1
            n = self.stage_no
            bar = self.bar
            self.ops["sp"].append(([], (lambda e, bar=bar: e.sem_inc(bar, 1)), None, 0))
            for eng in ("act", "dve", "pool", "pe"):
                self.ops[eng].append(([(bar, n)], None, None, 0))
        self._emit_block()
        self.stage_stack.close()
        self.stage_stack = None

    def buf(self, name=None):
        self.nbuf += 1
        return Buf(name or ("b%d" % self.nbuf))

    def bufs(self, n, name="b"):
        return [self.buf("%s%d" % (name, i)) for i in range(n)]

    def _deps(self, reads, writes):
        raw = {}
        other = {}
        for b in reads:
            if b.last_w is not None:
                k, v = b.last_w
                raw[k] = max(raw.get(k, 0), v)
        for b in writes:
            if b.last_w is not None:
                k, v = b.last_w
                other[k] = max(other.get(k, 0), v)
            for k, v in b.readers.items():
                other[k] = max(other.get(k, 0), v)
        return raw, other

    def _commit(self, ev, reads, writes):
        k, v = ev
        for b in reads:
            b.readers[k] = max(b.readers.get(k, 0), v)
        for b in writes:
            b.last_w = ev
            b.readers = {}

    def op(self, eng, fn, reads=(), writes=()):
        raw, other = self._deps(reads, writes)
        own = self.esem[eng]
        waits = {}
        seen = self.seen[eng]
        for d, is_raw in ((raw, True), (other, False)):
            for k, v in d.items():
                if k is own:
                    if eng == "pe" or not is_raw:
                        continue
                    if v <= self.count[eng] - self.SAME_WIN:
                        continue
                if seen.get(k, 0) >= v:
                    continue
                waits[k] = max(waits.get(k, 0), v)
        for k, v in waits.items():
            seen[k] = v
        self.count[eng] += 1
        ev = (own, self.count[eng])
        self._commit(ev, reads, writes)
        self.ops[eng].append((list(waits.items()), fn, own, 1))

    def dma(self, q, out, in_, reads=(), writes=(), fn=None, **kw):
        raw, other = self._deps(reads, writes)
        waits = {}
        seen = self.seen[q]
        for d in (raw, other):
            for k, v in d.items():
                if seen.get(k, 0) >= v:
                    continue
                waits[k] = max(waits.get(k, 0), v)
        i = self.dma_i[q]
        self.dma_i[q] = i + 1
        slot = self.dsem[q][i % self.NDMA]
        target = 16 * (i // self.NDMA + 1)
        if target > 16 and seen.get(slot, 0) < target - 16:
            waits[slot] = max(waits.get(slot, 0), target - 16)
        for k, v in waits.items():
            seen[k] = v
        ev = (slot, target)
        self._commit(ev, reads, writes)
        if fn is None:
            fn = lambda e, out=out, in_=in_, kw=kw: e.dma_start(out=out, in_=in_, **kw)
        self.ops[q].append((list(waits.items()), fn, slot, 16))

    def allgather(self, out, in_, groups, reads=(), writes=()):
        fn = lambda e: e.collective_compute("AllGather", ALU.bypass, replica_groups=groups, ins=[in_], outs=[out])
        self.dma("pool", None, None, reads=reads, writes=writes, fn=fn)

    def finish(self):
        waits = []
        for q in ("sp", "act", "pool"):
            n = self.dma_i[q]
            for s in range(min(n, self.NDMA)):
                cnt = (n - 1 - s) // self.NDMA + 1
                waits.append((self.dsem[q][s], 16 * cnt))
        for e in ("act", "dve", "pool", "pe"):
            if self.count[e]:
                waits.append((self.esem[e], self.count[e]))
        self.ops["sp"].append((waits, None, None, 0))

    def emit(self):
        self.finish()
        self._emit_block()
        self.stack.close()

    def _emit_block(self):
        nc = self.nc
        with nc.Block() as block:
            deco = {"sp": block.sync, "act": block.scalar, "dve": block.vector,
                    "pool": block.gpsimd, "pe": block.tensor}
            for eng in self.ENGS:
                ops = self.ops[eng]
                if not ops:
                    continue

                def body(e, ops=ops):
                    for waits, fn, sem, inc in ops:
                        for s, v in waits:
                            e.wait_ge(s, v)
                        if fn is not None:
                            ins = fn(e)
                            if sem is not None:
                                ins.then_inc(sem, inc)

                deco[eng](body)
        self.ops = {e: [] for e in self.ENGS}


def bcast_rows(ap_row, nparts):
    return ap_row.partition_broadcast(nparts)


class Consts:
    def __init__(self, S, ident_dram):
        self.ident = S.sbuf("ident", [128, 128], BF16)
        self.b_ident = S.buf("ident")
        S.dma("pool", self.ident[:], ident_dram[:, :], writes=[self.b_ident])


def load_bcast(S, name, row_ap, n):
    t = S.sbuf(name, [128, n], F32)
    b = S.buf(name)
    S.dma("sp", t[:], row_ap.partition_broadcast(128), writes=[b])
    return t, b


def layer_norm_tile(S, u, b_u, out, b_out, g_t, b_g, bt_t, b_bt, scr, eng2="pool"):
    st, b_st = scr["st"], scr["b_st"]
    mv, b_mv = scr["mv"], scr["b_mv"]
    for c in range(2):
        S.op("dve", lambda e, c=c: e.bn_stats(out=st[:, c, :], in_=u[:, c * 512:(c + 1) * 512]),
             reads=[b_u], writes=[b_st[c]])
    S.op("dve", lambda e: e.bn_aggr(out=mv[:, 0:2], in_=st[:, :, :]), reads=b_st, writes=[b_mv[0]])
    S.op("dve", lambda e: e.tensor_scalar(out=mv[:, 2:3], in0=mv[:, 1:2], scalar1=LN_EPS, scalar2=None,
                                           op0=ALU.add), reads=[b_mv[0]], writes=[b_mv[1]])
    S.op("act", lambda e: e.activation(out=mv[:, 2:3], in_=mv[:, 2:3], func=AF.Sqrt), reads=[b_mv[1]], writes=[b_mv[1]])
    S.op("dve", lambda e: e.reciprocal(out=mv[:, 2:3], in_=mv[:, 2:3]), reads=[b_mv[1]], writes=[b_mv[1]])
    S.op("dve", lambda e: e.scalar_tensor_tensor(out=mv[:, 3:4], in0=mv[:, 0:1], scalar=-1.0, in1=mv[:, 2:3],
                                                  op0=ALU.mult, op1=ALU.mult), reads=[b_mv[0], b_mv[1]], writes=[b_mv[2]])
    S.op("act", lambda e: e.activation(out=u[:, :], in_=u[:, :], func=AF.Identity, bias=mv[:, 3:4], scale=mv[:, 2:3]),
         reads=[b_u, b_mv[1], b_mv[2]], writes=[b_u])
    S.op(eng2, lambda e: e.tensor_tensor(out=u[:, :], in0=u[:, :], in1=g_t[:, :], op=ALU.mult),
         reads=[b_u, b_g], writes=[b_u])
    S.op(eng2, lambda e: e.tensor_tensor(out=out[:, :], in0=u[:, :], in1=bt_t[:, :], op=ALU.add),
         reads=[b_u, b_bt], writes=[b_out])


def ln_scratch(S, name):
    return {"st": S.sbuf(name + "_st", [128, 2, 6], F32), "b_st": S.bufs(2, name + "st"),
            "mv": S.sbuf(name + "_mv", [128, 4], F32), "b_mv": S.bufs(3, name + "mv")}


def stage_mlp(S, C, T, h1, w_up, w_down, g, b, h2):
    NT = 256
    NS = NT // 128
    wup = S.sbuf("wup", [128, 8, DFF], BF16)
    wdn = S.sbuf("wdn", [128, 32, D], BF16)
    b_wup = S.bufs(8, "wup")
    b_wdn = S.bufs(8, "wdn")
    wu_v = w_up.rearrange("(c p) f -> p c f", p=128)
    wd_v = w_down.rearrange("(c p) n -> p c n", p=128)
    for c in range(8):
        S.dma("pool", wup[:, c, :], wu_v[:, c, :], writes=[b_wup[c]])
    for c in range(8):
        S.dma("pool", wdn[:, 4 * c:4 * c + 4, :], wd_v[:, 4 * c:4 * c + 4, :], writes=[b_wdn[c]])
    g_t, b_g = load_bcast(S, "mlp_g", g, D)
    bt_t, b_bt = load_bcast(S, "mlp_b", b, D)

    NB = 2
    x_sb = [S.sbuf("x_sb%d" % i, [128, NS, D], F32) for i in range(NB)]
    b_x = [S.bufs(NS, "x%d_" % i) for i in range(NB)]
    xb = S.sbuf("xb", [128, NS, D], BF16)
    b_xb = S.bufs(NS, "xb")
    xT = S.sbuf("xT", [128, 8, NT], BF16)
    b_xT = S.bufs(8, "xT")
    h2T = S.sbuf("h2T", [128, 32, NT], BF16)
    b_h2T = S.bufs(32, "h2T")
    r_sb = [S.sbuf("r_sb%d" % i, [128, NT], F32) for i in range(4)]
    b_r = S.bufs(4, "r")
    y_sb = [S.sbuf("y_sb%d" % i, [128, D], F32) for i in range(2)]
    b_y = S.bufs(2, "y")
    tp_ps = [S.psum("tp_ps%d" % i, [128, 1024], BF16) for i in range(2)]
    b_tp = S.bufs(2, "tp")
    up_ps = [S.psum("up_ps%d" % i, [128, 512], F32) for i in range(4)]
    b_up = S.bufs(4, "up")
    dn_ps = [S.psum("dn_ps%d" % i, [128, 512], F32) for i in range(2)]
    b_dn = S.bufs(2, "dn")
    scrs = [ln_scratch(S, "mlp%d" % i) for i in range(2)]

    h1v = h1.rearrange("(t s p) d -> t p s d", p=128, s=NS)
    h2v = h2.rearrange("(t s p) d -> t s p d", p=128, s=NS)
    n_up = 0
    n_dn = 0
    n_tp = 0
    n_y = 0
    for t in range(T // NT):
        xs, bx = x_sb[t % NB], b_x[t % NB]
        S.dma("sp", xs[:, :, :], h1v[t], writes=bx)
        for s in range(NS):
            S.op("act", lambda e, xs=xs, s=s: e.activation(out=xb[:, s, :], in_=xs[:, s, :], func=AF.Copy),
                 reads=[bx[s]], writes=[b_xb[s]])
        for c in range(8):
            tp, btp = tp_ps[n_tp % 2], b_tp[n_tp % 2]
            n_tp += 1
            for s in range(NS):
                S.op("pe", lambda e, tp=tp, s=s, c=c: e.transpose(out=tp[:, s * 128:(s + 1) * 128],
                                                                  in_=xb[:, s, c * 128:(c + 1) * 128],
                                                                  identity=C.ident[:]),
                     reads=[b_xb[s], C.b_ident], writes=[btp])
            S.op("dve", lambda e, tp=tp, c=c: e.tensor_copy(out=xT[:, c, :], in_=tp[:, 0:NT]),
                 reads=[btp], writes=[b_xT[c]])
        for fc in range(32):
            ps, bps = up_ps[n_up % 4], b_up[n_up % 4]
            r, br = r_sb[n_up % 4], b_r[n_up % 4]
            n_up += 1
            for c in range(8):
                S.op("pe", lambda e, ps=ps, c=c, fc=fc: e.matmul(ps[:, 0:NT], lhsT=wup[:, c, fc * 128:(fc + 1) * 128],
                                                                  rhs=xT[:, c, :], start=(c == 0), stop=(c == 7)),
                     reads=[b_wup[c], b_xT[c]], writes=[bps])
            S.op("act", lambda e, ps=ps, r=r: e.activation(out=r[:, :], in_=ps[:, 0:NT], func=AF.Relu),
                 reads=[bps], writes=[br])
            S.op("dve", lambda e, r=r, fc=fc: e.tensor_tensor(out=h2T[:, fc, :], in0=r[:, :], in1=r[:, :], op=ALU.mult),
                 reads=[br], writes=[b_h2T[fc]])
        for s in range(NS):
            for nh in range(2):
                ps, bps = dn_ps[n_dn % 2], b_dn[n_dn % 2]
                n_dn += 1
                for fc in range(32):
                    S.op("pe", lambda e, ps=ps, fc=fc, s=s, nh=nh: e.matmul(
                        ps[:, :], lhsT=h2T[:, fc, s * 128:(s + 1) * 128], rhs=wdn[:, fc, nh * 512:(nh + 1) * 512],
                        start=(fc == 0), stop=(fc == 31)),
                         reads=[b_h2T[fc], b_wdn[fc // 4]], writes=[bps])
                S.op("dve", lambda e, ps=ps, xs=xs, s=s, nh=nh: e.scalar_tensor_tensor(
                    out=xs[:, s, nh * 512:(nh + 1) * 512], in0=xs[:, s, nh * 512:(nh + 1) * 512], scalar=ALPHA,
                    in1=ps[:, :], op0=ALU.mult, op1=ALU.add),
                     reads=[bps, bx[s]], writes=[bx[s]])
            y, by = y_sb[n_y % 2], b_y[n_y % 2]
            n_y += 1
            layer_norm_tile(S, xs[:, s, :], bx[s], y, by, g_t, b_g, bt_t, b_bt, scrs[n_y % 2])
            S.dma("sp", h2v[t, s], y[:, :], reads=[by])


TWO_PI = float(2 * np.pi)
PI = float(np.pi)


def _range_reduce(S, x, bx, tmp_f, btf, tmp_i, bti):
    S.op("dve", lambda e: e.tensor_scalar(out=tmp_f, in0=x, scalar1=1.0 / TWO_PI, scalar2=None, op0=ALU.mult),
         reads=[bx], writes=[btf])
    S.op("dve", lambda e: e.tensor_copy(out=tmp_i, in_=tmp_f), reads=[btf], writes=[bti])
    S.op("dve", lambda e: e.tensor_copy(out=tmp_f, in_=tmp_i), reads=[bti], writes=[btf])
    S.op("dve", lambda e: e.scalar_tensor_tensor(out=x, in0=tmp_f, scalar=-TWO_PI, in1=x, op0=ALU.mult, op1=ALU.add),
         reads=[btf, bx], writes=[bx])
    S.op("dve", lambda e: e.tensor_scalar(out=tmp_f, in0=x, scalar1=PI, scalar2=-TWO_PI, op0=ALU.is_gt, op1=ALU.mult),
         reads=[bx], writes=[btf])
    S.op("dve", lambda e: e.tensor_tensor(out=x, in0=x, in1=tmp_f, op=ALU.add), reads=[btf, bx], writes=[bx])
    S.op("dve", lambda e: e.tensor_scalar(out=tmp_f, in0=x, scalar1=-PI, scalar2=TWO_PI, op0=ALU.is_lt, op1=ALU.mult),
         reads=[bx], writes=[btf])
    S.op("dve", lambda e: e.tensor_tensor(out=x, in0=x, in1=tmp_f, op=ALU.add), reads=[btf, bx], writes=[bx])


def rotary_tables(S, NTL, pos_pt, invf):
    posi = S.sbuf("posi", [128, NTL], I32)
    posf = S.sbuf("posf", [128, NTL], F32)
    invt = S.sbuf("invt", [128, 8], F32)
    cos_t = S.sbuf("cos_t", [128, NTL, 8], F32)
    sin_t = S.sbuf("sin_t", [128, NTL, 8], F32)
    tmpf = S.sbuf("rr_tf", [128, NTL, 8], F32)
    tmpi = S.sbuf("rr_ti", [128, NTL, 8], I32)
    b_pi, b_pf, b_inv, b_cos, b_sin, b_tf, b_ti = S.bufs(7, "rot")
    S.dma("sp", posi[:], pos_pt[:, :], writes=[b_pi])
    S.dma("sp", invt[:], invf.partition_broadcast(128), writes=[b_inv])
    S.op("dve", lambda e: e.tensor_copy(out=posf[:], in_=posi[:]), reads=[b_pi], writes=[b_pf])
    S.op("dve", lambda e: e.tensor_tensor(out=sin_t[:, :, :], in0=posf[:, :].unsqueeze(2).to_broadcast([128, NTL, 8]),
                                          in1=invt[:, :].unsqueeze(1).to_broadcast([128, NTL, 8]), op=ALU.mult),
         reads=[b_pf, b_inv], writes=[b_sin])
    S.op("dve", lambda e: e.tensor_scalar(out=cos_t[:, :, :], in0=sin_t[:, :, :], scalar1=PI / 2, scalar2=None, op0=ALU.add),
         reads=[b_sin], writes=[b_cos])
    _range_reduce(S, sin_t[:, :, :], b_sin, tmpf[:, :, :], b_tf, tmpi[:, :, :], b_ti)
    _range_reduce(S, cos_t[:, :, :], b_cos, tmpf[:, :, :], b_tf, tmpi[:, :, :], b_ti)
    S.op("act", lambda e: e.activation(out=sin_t[:, :, :], in_=sin_t[:, :, :], func=AF.Sin), reads=[b_sin], writes=[b_sin])
    S.op("act", lambda e: e.activation(out=cos_t[:, :, :], in_=cos_t[:, :, :], func=AF.Sin), reads=[b_cos], writes=[b_cos])
    return cos_t, b_cos, sin_t, b_sin


def apply_rotary(S, P, bP, h0, nh, cos_t, b_cos, sin_t, b_sin, t, tmp, btmp):
    Pv = P[:, h0 * 64:(h0 + nh) * 64].rearrange("p (h d) -> p h d", d=64)
    x1 = Pv[:, :, 0:8]
    x2 = Pv[:, :, 8:16]
    c = cos_t[:, t, :].unsqueeze(1).to_broadcast([128, nh, 8])
    s = sin_t[:, t, :].unsqueeze(1).to_broadcast([128, nh, 8])
    t1, t2, t3, t4 = (tmp[:, i, 0:nh, :] for i in range(4))
    bP = list(bP)
    rd = bP + [b_cos, b_sin]
    S.op("dve", lambda e: e.tensor_tensor(out=t1, in0=x1, in1=c, op=ALU.mult), reads=rd, writes=[btmp[0]])
    S.op("dve", lambda e: e.tensor_tensor(out=t2, in0=x2, in1=s, op=ALU.mult), reads=rd, writes=[btmp[1]])
    S.op("dve", lambda e: e.tensor_tensor(out=t3, in0=x2, in1=c, op=ALU.mult), reads=rd, writes=[btmp[2]])
    S.op("dve", lambda e: e.tensor_tensor(out=t4, in0=x1, in1=s, op=ALU.mult), reads=rd, writes=[btmp[3]])
    S.op("dve", lambda e: e.tensor_tensor(out=x1, in0=t1, in1=t2, op=ALU.subtract), reads=[btmp[0], btmp[1], btmp[3]], writes=bP)
    S.op("dve", lambda e: e.tensor_tensor(out=x2, in0=t3, in1=t4, op=ALU.add), reads=[btmp[2], btmp[3]], writes=bP)


A_IN = 2120
IW_SCALE = float(8 ** -0.5 * 64 ** -0.5)


def stage_proj_A(S, C, T, h, pos_pt, w_in, invf, qT, kT, v, iqT, ikT, iw):
    NTL = T // 128
    win = S.sbuf("win", [128, 8, A_IN], BF16)
    b_win = S.bufs(8, "win")
    wv = w_in.rearrange("(c p) n -> p c n", p=128)
    for c in range(8):
        S.dma("pool", win[:, c, :], wv[:, c, :], writes=[b_win[c]])
    cos_t, b_cos, sin_t, b_sin = rotary_tables(S, NTL, pos_pt, invf)

    hs = [S.sbuf("pa_hs%d" % i, [128, D], F32) for i in range(2)]
    b_hs = S.bufs(2, "pa_hs")
    hb_l = [S.sbuf("pa_hb%d" % i, [128, D], BF16) for i in range(2)]
    b_hb_l = S.bufs(2, "pa_hb")
    hT_l = [S.sbuf("pa_hT%d" % i, [128, 8, 128], BF16) for i in range(2)]
    b_hT_l = S.bufs(2, "pa_hT")
    P_l = [S.sbuf("pa_P%d" % i, [128, A_IN], F32) for i in range(2)]
    b_P_l = [S.bufs(5, "pa_P%d_" % i) for i in range(2)]
    Pb_l = [S.sbuf("pa_Pb%d" % i, [128, A_IN], BF16) for i in range(2)]
    b_Pb_l = [S.bufs(4, "pa_Pb%d_" % i) for i in range(2)]
    iws = [S.sbuf("pa_iw%d" % i, [128, 8], F32) for i in range(2)]
    b_iws = S.bufs(2, "pa_iw")
    rt_l = [S.sbuf("pa_rt%d" % i, [128, 4, 20, 8], F32) for i in range(2)]
    b_rt_l = [S.bufs(4, "pa_rt%d_" % i) for i in range(2)]
    qTs = [S.sbuf("pa_qT%d" % i, [128, 8, 128], BF16) for i in range(2)]
    b_qTs = S.bufs(2, "pa_qT")
    kiTs = [S.sbuf("pa_kiT%d" % i, [128, 7, 128], BF16) for i in range(2)]
    b_kiTs = S.bufs(2, "pa_kiT")
    pj = [S.psum("pa_pj%d" % i, [128, 512], F32) for i in range(5)]
    b_pj = S.bufs(5, "pa_pj")
    tpA = S.psum("pa_tpA", [128, 1024], BF16)
    tpB = S.psum("pa_tpB", [128, 1024], BF16)
    b_tpA, b_tpB = S.bufs(2, "pa_tp")
    chunks = [(0, 512), (512, 1024), (1024, 1536), (1536, 2048), (2048, A_IN)]

    hv = h.rearrange("(t p) d -> t p d", p=128)
    qTv = qT.rearrange("(c p) t -> p c t", p=128)
    kTv = kT.rearrange("(c p) t -> p c t", p=128)
    iqTv = iqT.rearrange("(c p) t -> p c t", p=128)
    def _tile(t, hb, b_hb, hT, b_hT, P, b_P, Pb, b_Pbs, rt, b_rt):
        b_Pq, b_Pk, b_Pv, b_Pi = b_Pbs
        x, bx = hs[t % 2], b_hs[t % 2]
        S.dma("sp", x[:, :], hv[t], writes=[bx])
        S.op("act", lambda e, x=x: e.activation(out=hb[:, :], in_=x[:, :], func=AF.Copy), reads=[bx], writes=[b_hb])
        for c in range(8):
            S.op("pe", lambda e, c=c: e.transpose(out=tpA[:, c * 128:(c + 1) * 128], in_=hb[:, c * 128:(c + 1) * 128],
                                                  identity=C.ident[:]), reads=[b_hb, C.b_ident], writes=[b_tpA])
        S.op("dve", lambda e: e.tensor_copy(out=hT[:, :, :], in_=tpA[:, :].rearrange("p (c t) -> p c t", t=128)),
             reads=[b_tpA], writes=[b_hT])
        for i, (n0, n1) in enumerate(chunks):
            for c in range(8):
                S.op("pe", lambda e, i=i, c=c, n0=n0, n1=n1: e.matmul(pj[i][:, 0:n1 - n0], lhsT=hT[:, c, :], rhs=win[:, c, n0:n1],
                                                                      start=(c == 0), stop=(c == 7)),
                     reads=[b_hT, b_win[c]], writes=[b_pj[i]])
            S.op("act", lambda e, i=i, n0=n0, n1=n1: e.activation(out=P[:, n0:n1], in_=pj[i][:, 0:n1 - n0], func=AF.Copy),
                 reads=[b_pj[i]], writes=[b_P[i]])
        apply_rotary(S, P, b_P[0:3], 0, 20, cos_t, b_cos, sin_t, b_sin, t, rt, b_rt)
        apply_rotary(S, P, b_P[3:5], 24, 9, cos_t, b_cos, sin_t, b_sin, t, rt, b_rt)
        S.op("act", lambda e: e.activation(out=Pb[:, 0:1024], in_=P[:, 0:1024], func=AF.Copy, scale=0.125),
             reads=b_P[0:2], writes=[b_Pq])
        S.op("act", lambda e: e.activation(out=Pb[:, 1024:1536], in_=P[:, 1024:1536], func=AF.Copy),
             reads=[b_P[2]], writes=[b_Pk])
        S.op("act", lambda e: e.activation(out=Pb[:, 1536:2112], in_=P[:, 1536:2112], func=AF.Copy),
             reads=b_P[3:5], writes=[b_Pi])
        iwt, biw = iws[t % 2], b_iws[t % 2]
        S.op("act", lambda e, iwt=iwt: e.activation(out=iwt[:, :], in_=P[:, 2112:2120], func=AF.Copy, scale=IW_SCALE),
             reads=[b_P[4]], writes=[biw])
        S.dma("sp", iw[t * 128:(t + 1) * 128, :], iwt[:, :], reads=[biw])
        S.dma("sp", v[t * 128:(t + 1) * 128, :], Pb[:, 1280:1536], reads=[b_Pk])
        qs, bqs = qTs[t % 2], b_qTs[t % 2]
        ks, bks = kiTs[t % 2], b_kiTs[t % 2]
        for c in range(8):
            S.op("pe", lambda e, c=c: e.transpose(out=tpB[:, c * 128:(c + 1) * 128], in_=Pb[:, c * 128:(c + 1) * 128],
                                                  identity=C.ident[:]), reads=[b_Pq, C.b_ident], writes=[b_tpB])
        S.op("dve", lambda e, qs=qs: e.tensor_copy(out=qs[:, :, :], in_=tpB[:, :].rearrange("p (c t) -> p c t", t=128)),
             reads=[b_tpB], writes=[bqs])
        S.dma("sp", qTv[:, :, t * 128:(t + 1) * 128], qs[:, :, :], reads=[bqs])
        srcs = [(1024, 128, b_Pk), (1152, 128, b_Pk), (1536, 128, b_Pi), (1664, 128, b_Pi), (1792, 128, b_Pi),
                (1920, 128, b_Pi), (2048, 64, b_Pi)]
        for j, (c0, w, bsrc) in enumerate(srcs):
            S.op("pe", lambda e, j=j, c0=c0, w=w: e.transpose(out=tpA[0:w, j * 128:(j + 1) * 128], in_=Pb[:, c0:c0 + w],
                                                              identity=C.ident[:]), reads=[bsrc, C.b_ident], writes=[b_tpA])
        S.op("dve", lambda e, ks=ks: e.tensor_copy(out=ks[:, 0:6, :], in_=tpA[:, 0:768].rearrange("p (c t) -> p c t", t=128)),
             reads=[b_tpA], writes=[bks])
        S.op("dve", lambda e, ks=ks: e.tensor_copy(out=ks[0:64, 6, :], in_=tpA[0:64, 768:896]),
             reads=[b_tpA], writes=[bks])
        S.dma("sp", kTv[:, :, t * 128:(t + 1) * 128], ks[:, 0:2, :], reads=[bks])
        S.dma("sp", iqTv[:, :, t * 128:(t + 1) * 128], ks[:, 2:6, :], reads=[bks])
        S.dma("sp", ikT[:, t * 128:(t + 1) * 128], ks[0:64, 6, :], reads=[bks])

    for t in range(NTL):
        _tile(t, hb_l[t % 2], b_hb_l[t % 2], hT_l[t % 2], b_hT_l[t % 2], P_l[t % 2], b_P_l[t % 2], Pb_l[t % 2], b_Pb_l[t % 2], rt_l[t % 2], b_rt_l[t % 2])

TOPK = 256
NEG_MASK = -3.0e38
NEG_LO = -1.0e30
N_BISECT = 18
MASK_BIG = 32768.0


def stage_attn(S, C, NQ, kT, v, ikT, qT, iqT, iw, maskc, negI, o, dbg=None, solo=False):
    NKTT = NQ if solo else 2 * NQ
    SK = NKTT * 128
    nkt_of = (lambda j: j + 1) if solo else (lambda j: 2 * j + 2)
    kT_sb = S.sbuf("at_kT", [128, 2, SK], BF16)
    v_sb = S.sbuf("at_v", [128, NKTT, 4, 65], BF16)
    ik_sb = S.sbuf("at_ik", [64, SK], BF16)
    b_kT, b_v, b_ik = S.bufs(3, "at_res")
    for hh in range(2):
        S.dma("sp", kT_sb[hh * 64:(hh + 1) * 64, :, :], kT[2 * hh:2 * hh + 2, :, :].rearrange("g d s -> d g s"), writes=[b_kT])
    S.op("pool", lambda e: e.memset(v_sb[:, :, :, 64:65], 1.0), writes=[b_v])
    vv = v.rearrange("(k p) (g d) -> p k g d", p=128, d=64)
    KCH = 16
    for k0 in range(0, NKTT, KCH):
        k1 = min(NKTT, k0 + KCH)
        for g in range(4):
            S.dma("sp", v_sb[:, k0:k1, g, 0:64], vv[:, k0:k1, g, :], writes=[b_v])
    S.dma("sp", ik_sb[:, :], ikT[:, :], writes=[b_ik])
    mc = S.sbuf("at_mc", [128, 256], F32)
    nI = S.sbuf("at_nI", [128, 4, 128], BF16)
    b_mc, b_nI = S.bufs(2, "at_c")
    S.dma("sp", mc[:, :], maskc[:, :], writes=[b_mc])
    for r in range(4):
        S.dma("pool", nI[:, r, :], negI[:, :], writes=[b_nI])

    sc = S.sbuf("at_sc", [128, SK], F32)
    b_sc = S.buf("at_sc")
    junk = S.sbuf("at_junk", [128, SK], BF16)
    b_junk = S.buf("at_junk")
    mb = [S.sbuf("at_mb%d" % i, [128, SK], BF16) for i in range(2)]
    b_mb = S.bufs(2, "at_mb")
    NT_SB = 4
    NS_PS = 3
    t_sb = [S.sbuf("at_t%d" % i, [128, 512], F32) for i in range(NT_SB)]
    b_t = S.bufs(NT_SB, "at_t")
    PT = [S.sbuf("at_PT%d" % i, [128, 512], BF16) for i in range(3)]
    b_PT = S.bufs(3, "at_PT")
    q_sb = [S.sbuf("at_q%d" % i, [128, 2, 4, 128], BF16) for i in range(2)]
    b_q = S.bufs(2, "at_q")
    iq_sb = [S.sbuf("at_iq%d" % i, [64, 8, 128], BF16) for i in range(2)]
    b_iq = S.bufs(2, "at_iq")
    iw_sb = [S.sbuf("at_iw%d" % i, [128, 8], F32) for i in range(2)]
    b_iw = S.bufs(2, "at_iw")
    o_sb = [S.sbuf("at_o%d" % i, [128, 16, 64], BF16) for i in range(2)]
    b_o = S.bufs(2, "at_o")
    sm = S.sbuf("at_sm", [128, 16], F32)
    b_sm = S.bufs(8, "at_sm")
    rc = S.sbuf("at_rc", [128, 16], F32)
    b_rc = S.buf("at_rc")
    s_ps = [S.psum("at_sps%d" % i, [128, 512], F32) for i in range(NS_PS)]
    b_sps = S.bufs(NS_PS, "at_sps")
    KB = N_BISECT
    pow2 = S.sbuf("at_pow2", [128, KB + 1], F32)
    wtab = S.sbuf("at_wtab", [128, KB + 1], F32)
    b_pow2, b_wtab = S.bufs(2, "at_w")
    for k in range(KB + 1):
        S.op("pool", lambda e, k=k: e.memset(pow2[:, k:k + 1], float(2.0 ** -k)), writes=[b_pow2])
    l_ps = [S.psum("at_lps%d" % i, [128, 512], F32) for i in range(2)]
    b_lps = S.bufs(2, "at_lps")
    o_ps = [S.psum("at_ops%d" % i, [128, 7, 65], F32) for i in range(3)]
    b_ops = S.bufs(3, "at_ops")
    LO, HI, MID, CNT, GE, DD, MM = range(7)
    col = lambda i: sm[:, i:i + 1]

    n_s = 0
    n_l = 0
    n_pt = 0

    def load_q(j):
        qs, bq = q_sb[j % 2], b_q[j % 2]
        for hh in range(2):
            S.dma("sp", qs[hh * 64:(hh + 1) * 64, :, :, :].rearrange("d a r q -> d (a r) q"),
                  qT[8 * hh:8 * hh + 8, :, j * 128:(j + 1) * 128].rearrange("h d q -> d h q"), writes=[bq])
        S.dma("sp", iq_sb[j % 2][:, :, :], iqT[:, :, j * 128:(j + 1) * 128].rearrange("h d q -> d h q"), writes=[b_iq[j % 2]])
        S.dma("sp", iw_sb[j % 2][:, :], iw[j * 128:(j + 1) * 128, :], writes=[b_iw[j % 2]])

    def phase12(j):
        nonlocal n_s
        nk = nkt_of(j) * 128
        iqs, biq = iq_sb[j % 2], b_iq[j % 2]
        iws, biw = iw_sb[j % 2], b_iw[j % 2]
        for hd in range(8):
            for k0 in range(0, nk, 512):
                w = min(512, nk - k0)
                ps, bps = s_ps[n_s % NS_PS], b_sps[n_s % NS_PS]
                tt, bt = t_sb[n_s % NT_SB], b_t[n_s % NT_SB]
                n_s += 1
                S.op("pe", lambda e, ps=ps, hd=hd, k0=k0, w=w, iqs=iqs: e.matmul(
                    ps[:, 0:w], lhsT=iqs[:, hd, :], rhs=ik_sb[:, k0:k0 + w], start=True, stop=True),
                     reads=[biq, b_ik], writes=[bps])
                S.op("act", lambda e, ps=ps, tt=tt, w=w: e.activation(out=tt[:, 0:w], in_=ps[:, 0:w], func=AF.Relu),
                     reads=[bps], writes=[bt])
                if hd == 0:
                    S.op("dve", lambda e, tt=tt, k0=k0, w=w, iws=iws: e.tensor_scalar(
                        out=sc[:, k0:k0 + w], in0=tt[:, 0:w], scalar1=iws[:, 0:1], scalar2=None, op0=ALU.mult),
                         reads=[bt, biw], writes=[b_sc])
                else:
                    S.op("dve", lambda e, tt=tt, k0=k0, w=w, iws=iws, hd=hd: e.scalar_tensor_tensor(
                        out=sc[:, k0:k0 + w], in0=tt[:, 0:w], scalar=iws[:, hd:hd + 1], in1=sc[:, k0:k0 + w],
                        op0=ALU.mult, op1=ALU.add), reads=[bt, biw, b_sc], writes=[b_sc])
        S.op("dve", lambda e: e.tensor_reduce(out=col(MM), in_=sc[:, 0:nk], axis=AX.X, op=ALU.max, apply_absolute_value=True),
             reads=[b_sc], writes=[b_sm[MM]])
        S.op("dve", lambda e: e.tensor_scalar(out=col(HI), in0=col(MM), scalar1=1.0, scalar2=None, op0=ALU.add),
             reads=[b_sm[MM]], writes=[b_sm[HI]])
        S.op("dve", lambda e: e.tensor_scalar(out=wtab[:, :], in0=pow2[:, :], scalar1=col(HI), scalar2=None, op0=ALU.mult),
             reads=[b_sm[HI], b_pow2], writes=[b_wtab])
        if solo:
            S.op("dve", lambda e: e.tensor_tensor(out=sc[:, nk - 128:nk], in0=sc[:, nk - 128:nk], in1=mc[:, 128:256], op=ALU.add),
                 reads=[b_sc, b_mc], writes=[b_sc])
        else:
            S.op("dve", lambda e: e.tensor_tensor(out=sc[:, nk - 256:nk], in0=sc[:, nk - 256:nk], in1=mc[:, :], op=ALU.add),
                 reads=[b_sc, b_mc], writes=[b_sc])
        S.op("dve", lambda e: e.memset(col(MID), 0.0), writes=[b_sm[MID]])
        for it in range(KB):
            S.op("dve", lambda e: e.tensor_scalar(out=junk[:, 0:nk], in0=sc[:, 0:nk], scalar1=col(MID), scalar2=0.0,
                                                   op0=ALU.is_ge, op1=ALU.add, accum_out=col(CNT)),
                 reads=[b_sc, b_sm[MID]], writes=[b_junk, b_sm[CNT]])
            S.op("dve", lambda e, it=it: e.scalar_tensor_tensor(out=col(GE), in0=col(CNT), scalar=TOPK - 0.5, in1=wtab[:, it:it + 1],
                                                                op0=ALU.is_ge, op1=ALU.mult),
                 reads=[b_sm[CNT], b_wtab], writes=[b_sm[GE]])
            S.op("dve", lambda e, it=it: e.scalar_tensor_tensor(out=col(MID), in0=col(MID), scalar=wtab[:, it + 1:it + 2], in1=col(GE),
                                                                op0=ALU.subtract, op1=ALU.add),
                 reads=[b_sm[MID], b_sm[GE], b_wtab], writes=[b_sm[MID]])
        S.op("dve", lambda e: e.tensor_tensor(out=col(LO), in0=col(MID), in1=wtab[:, KB:KB + 1], op=ALU.subtract),
             reads=[b_sm[MID], b_wtab], writes=[b_sm[LO]])
        m, bm = mb[j % 2], b_mb[j % 2]
        S.op("dve", lambda e, m=m: e.tensor_scalar(out=m[:, 0:nk], in0=sc[:, 0:nk], scalar1=col(LO), scalar2=None, op0=ALU.is_lt),
             reads=[b_sc, b_sm[LO]], writes=[bm])
        if dbg is not None and j == dbg["j"]:
            S.dma("sp", dbg["sc"][:, 0:nk], sc[:, 0:nk], reads=[b_sc])
            S.dma("sp", dbg["mb"][:, 0:nk], m[:, 0:nk], reads=[bm])
            S.dma("sp", dbg["sm"][:, :], sm[:, :], reads=b_sm)

    def phase3(j):
        nonlocal n_l, n_pt
        NKT = nkt_of(j)
        qs, bq = q_sb[j % 2], b_q[j % 2]
        m, bm = mb[j % 2], b_mb[j % 2]
        for kt in range(NKT):
            for g in range(4):
                hh, a = g // 2, g % 2
                lp, blp = l_ps[n_l % 2], b_lps[n_l % 2]
                n_l += 1
                pt, bpt = PT[n_pt % 3], b_PT[n_pt % 3]
                n_pt += 1
                S.op("pe", lambda e, lp=lp, hh=hh, a=a, kt=kt, qs=qs: e.matmul(
                    lp[:, :], lhsT=kT_sb[hh * 64:(hh + 1) * 64, a, kt * 128:(kt + 1) * 128],
                    rhs=qs[hh * 64:(hh + 1) * 64, a, :, :].rearrange("d r q -> d (r q)"), start=True, stop=False),
                     reads=[b_kT, bq], writes=[blp])
                S.op("pe", lambda e, lp=lp, kt=kt, m=m: e.matmul(
                    lp[:, :], lhsT=m[:, kt * 128:(kt + 1) * 128], rhs=nI[:, :, :].rearrange("p r q -> p (r q)"),
                    start=False, stop=True), reads=[bm, b_nI], writes=[blp])
                S.op("act", lambda e, lp=lp, pt=pt: e.activation(out=pt[:, :], in_=lp[:, :], func=AF.Exp),
                     reads=[blp], writes=[bpt])
                for r in range(4):
                    hd = 4 * g + r
                    bank, slot = hd // 7, hd % 7
                    S.op("pe", lambda e, pt=pt, r=r, kt=kt, g=g, bank=bank, slot=slot: e.matmul(
                        o_ps[bank][:, slot, :], lhsT=pt[:, r * 128:(r + 1) * 128], rhs=v_sb[:, kt, g, :],
                        start=(kt == 0 and slot == 0), stop=(kt == NKT - 1), skip_group_check=True),
                         reads=[bpt, b_v], writes=[b_ops[bank]])
        ob, bo = o_sb[j % 2], b_o[j % 2]
        for bank in range(3):
            nh = min(7, 16 - 7 * bank)
            S.op("dve", lambda e, bank=bank, nh=nh: e.reciprocal(out=rc[:, 0:nh], in_=o_ps[bank][:, 0:nh, 64]),
                 reads=[b_ops[bank]], writes=[b_rc])
            S.op("dve", lambda e, bank=bank, nh=nh, ob=ob: e.tensor_tensor(
                out=ob[:, 7 * bank:7 * bank + nh, :], in0=o_ps[bank][:, 0:nh, 0:64],
                in1=rc[:, 0:nh].unsqueeze(2).to_broadcast([128, nh, 64]), op=ALU.mult),
                 reads=[b_ops[bank], b_rc], writes=[bo])
        S.dma("sp", o[j * 128:(j + 1) * 128, :], ob[:, :, :].rearrange("p h d -> p (h d)"), reads=[bo])

    load_q(0)
    phase12(0)
    for j in range(NQ):
        if j + 1 < NQ:
            load_q(j + 1)
            phase12(j + 1)
        phase3(j)


def stage_wo_ln(S, C, T, o, h, w_o, g, b, h1, gate=None):
    wo = S.sbuf("wo", [128, 8, D], BF16)
    b_wo = S.bufs(8, "wo")
    wv = w_o.rearrange("(c p) n -> p c n", p=128)
    for c in range(8):
        S.dma("pool", wo[:, c, :], wv[:, c, :], writes=[b_wo[c]])
    g_t, b_g = load_bcast(S, "wo_g", g, D)
    bt_t, b_bt = load_bcast(S, "wo_b", b, D)
    ob = [S.sbuf("wo_ob%d" % i, [128, D], BF16) for i in range(2)]
    b_ob = S.bufs(2, "wo_ob")
    hs = [S.sbuf("wo_hs%d" % i, [128, D], F32) for i in range(2)]
    b_hs = S.bufs(2, "wo_hs")
    oTs = [S.sbuf("wo_oT%d" % i, [128, 8, 128], BF16) for i in range(2)]
    b_oTs = S.bufs(2, "wo_oT")
    y_sb = [S.sbuf("wo_y%d" % i, [128, D], F32) for i in range(2)]
    b_y = S.bufs(2, "wo_y")
    tps = [S.psum("wo_tp%d" % i, [128, 1024], BF16) for i in range(2)]
    b_tps = S.bufs(2, "wo_tp")
    mx = [S.psum("wo_mx%d" % i, [128, 512], F32) for i in range(4)]
    b_mx = S.bufs(4, "wo_mx")
    scrs = [ln_scratch(S, "wo%d" % i) for i in range(2)]
    ov = o.rearrange("(t p) d -> t p d", p=128) if gate is None else None
    if gate is not None:
        gate_a = [S.sbuf("wo_ga%d" % i, [128, D], F32) for i in range(2)]
        gate_b = [S.sbuf("wo_gb%d" % i, [128, D], F32) for i in range(2)]
        b_ga = S.bufs(2, "wo_ga")
        b_gb = S.bufs(2, "wo_gb")
    hv = h.rearrange("(t p) d -> t p d", p=128)
    h1v = h1.rearrange("(t p) d -> t p d", p=128)
    for t in range(T // 128):
        x, bx = ob[t % 2], b_ob[t % 2]
        hh, bh = hs[t % 2], b_hs[t % 2]
        if gate is None:
            S.dma("sp", x[:, :], ov[t], writes=[bx])
        else:
            ga, gb_ = gate_a[t % 2], gate_b[t % 2]
            S.dma("sp", ga[:, :], gate[0].rearrange("(t p) d -> t p d", p=128)[t], writes=[b_ga[t % 2]])
            S.dma("sp", gb_[:, :], gate[1].rearrange("(t p) d -> t p d", p=128)[t], writes=[b_gb[t % 2]])
            S.op("pool", lambda e, x=x, ga=ga, gb_=gb_: e.tensor_tensor(out=x[:, :], in0=ga[:, :], in1=gb_[:, :], op=ALU.mult),
                 reads=[b_ga[t % 2], b_gb[t % 2]], writes=[bx])
        S.dma("sp", hh[:, :], hv[t], writes=[bh])
        tp, b_tp = tps[t % 2], b_tps[t % 2]
        oT, b_oT = oTs[t % 2], b_oTs[t % 2]
        scr = scrs[t % 2]
        for c in range(8):
            S.op("pe", lambda e, c=c, x=x, tp=tp: e.transpose(out=tp[:, c * 128:(c + 1) * 128], in_=x[:, c * 128:(c + 1) * 128],
                                                       identity=C.ident[:]), reads=[bx, C.b_ident], writes=[b_tp])
        S.op("act", lambda e, oT=oT, tp=tp: e.activation(out=oT[:, :, :], in_=tp[:, :].rearrange("p (c t) -> p c t", t=128), func=AF.Copy),
             reads=[b_tp], writes=[b_oT])
        for nh in range(2):
            ps, bps = mx[(2 * t + nh) % 4], b_mx[(2 * t + nh) % 4]
            for c in range(8):
                S.op("pe", lambda e, ps=ps, c=c, nh=nh, oT=oT: e.matmul(ps[:, :], lhsT=oT[:, c, :], rhs=wo[:, c, nh * 512:(nh + 1) * 512],
                                                                  start=(c == 0), stop=(c == 7)),
                     reads=[b_oT, b_wo[c]], writes=[bps])
            S.op("dve", lambda e, ps=ps, hh=hh, nh=nh: e.scalar_tensor_tensor(
                out=hh[:, nh * 512:(nh + 1) * 512], in0=hh[:, nh * 512:(nh + 1) * 512], scalar=ALPHA, in1=ps[:, :],
                op0=ALU.mult, op1=ALU.add), reads=[bps, bh], writes=[bh])
        y, by = y_sb[t % 2], b_y[t % 2]
        layer_norm_tile(S, hh, bh, y, by, g_t, b_g, bt_t, b_bt, scr)
        S.dma("sp", h1v[t], y[:, :], reads=[by])


B_IN = 3088
GATE_TAU = 16.0


def stage_proj_B(S, C, T, h, w_in, w_a2, b_a, cmU, cmW, qgT, kgT, kh, vb, dec, sr):
    NTL = T // 128
    win = S.sbuf("pb_win", [128, 8, B_IN], BF16)
    b_win = S.bufs(8, "pb_win")
    wv = w_in.rearrange("(c p) n -> p c n", p=128)
    for c in range(8):
        S.dma("pool", win[:, c, :], wv[:, c, :], writes=[b_win[c]])
    wa2 = S.sbuf("pb_wa2", [16, 512], BF16)
    U = S.sbuf("pb_U", [128, 128], BF16)
    W = S.sbuf("pb_W", [128, 128], BF16)
    b_wa2, b_U, b_W = S.bufs(3, "pb_c")
    S.dma("pool", wa2[:, :], w_a2[:, :], writes=[b_wa2])
    S.dma("pool", U[:, :], cmU[:, :], writes=[b_U])
    S.dma("pool", W[:, :], cmW[:, :], writes=[b_W])
    ba_t, b_ba = load_bcast(S, "pb_ba", b_a, 512)

    hs = [S.sbuf("pb_hs%d" % i, [128, D], F32) for i in range(2)]
    b_hs = S.bufs(2, "pb_hs")
    def _mk(i):
        return dict(hb=S.sbuf("pb_hb%d" % i, [128, D], BF16), b_hb=S.buf("pb_hb"), hT=S.sbuf("pb_hT%d" % i, [128, 8, 128], BF16), b_hT=S.buf("pb_hT"),
                    alT=S.sbuf("pb_alT%d" % i, [16, 128], BF16), b_alT=S.buf("pb_alT"), gg=S.sbuf("pb_g%d" % i, [128, 512], F32), b_gg=S.buf("pb_g"),
                    ghi=S.sbuf("pb_ghi%d" % i, [128, 512], BF16), glo=S.sbuf("pb_glo%d" % i, [128, 512], BF16), b_ghi=S.buf("pb_ghi"), b_glo=S.buf("pb_glo"),
                    EbT=S.sbuf("pb_EbT%d" % i, [128, 4, 128], F32), EnbT=S.sbuf("pb_EnbT%d" % i, [128, 4, 128], F32), Ebl=S.sbuf("pb_Ebl%d" % i, [128, 512], F32),
                    b_EbT=S.buf("pb_EbT"), b_EnbT=S.buf("pb_EnbT"), b_Ebl=S.buf("pb_Ebl"))
    sets = [_mk(0), _mk(1)]
    qg_s = [S.sbuf("pb_qg%d" % i, [128, 4, 128], BF16) for i in range(2)]
    kg_s = [S.sbuf("pb_kg%d" % i, [128, 4, 128], BF16) for i in range(2)]
    kh_s = [S.sbuf("pb_kh%d" % i, [128, 512], BF16) for i in range(2)]
    vb_s = [S.sbuf("pb_vb%d" % i, [128, 1024], BF16) for i in range(2)]
    sr_s = [S.sbuf("pb_sr%d" % i, [128, 1024], F32) for i in range(2)]
    dc_s = [S.sbuf("pb_dc%d" % i, [128, 4, 2], F32) for i in range(2)]
    b_qg, b_kg, b_kh, b_vb, b_sr, b_dc = (S.bufs(2, "pb_o%d" % i) for i in range(6))
    tpT = S.psum("pb_tp", [128, 1024], BF16)
    b_tpT = S.buf("pb_tp")
    pk = [S.psum("pb_pk%d" % i, [128, 512], F32) for i in range(7)]
    b_pk = S.bufs(7, "pb_pk")
    QSCALE = float(128 ** -0.5)

    hv = h.rearrange("(t p) d -> t p d", p=128)
    def _tile(t, hb, b_hb, hT, b_hT, alT, b_alT, gg, b_gg, ghi, glo, b_ghi, b_glo, EbT, EnbT, Ebl, b_EbT, b_EnbT, b_Ebl):
        x, bx = hs[t % 2], b_hs[t % 2]
        i2 = t % 2
        S.dma("sp", x[:, :], hv[t], writes=[bx])
        S.op("act", lambda e, x=x: e.activation(out=hb[:, :], in_=x[:, :], func=AF.Copy), reads=[bx], writes=[b_hb])
        for c in range(8):
            S.op("pe", lambda e, c=c: e.transpose(out=tpT[:, c * 128:(c + 1) * 128], in_=hb[:, c * 128:(c + 1) * 128],
                                                  identity=C.ident[:]), reads=[b_hb, C.b_ident], writes=[b_tpT])
        S.op("dve", lambda e: e.tensor_copy(out=hT[:, :, :], in_=tpT[:, :].rearrange("p (c t) -> p c t", t=128)),
             reads=[b_tpT], writes=[b_hT])

        def tok_mm(bank, n0, n1):
            for c in range(8):
                S.op("pe", lambda e, c=c: e.matmul(pk[bank][:, 0:n1 - n0], lhsT=hT[:, c, :], rhs=win[:, c, n0:n1],
                                                   start=(c == 0), stop=(c == 7)), reads=[b_hT, b_win[c]], writes=[b_pk[bank]])

        def feat_mm(bank, slot, n0, m):
            for c in range(8):
                S.op("pe", lambda e, c=c: e.matmul(pk[bank][0:m, slot * 128:(slot + 1) * 128], lhsT=win[:, c, n0:n0 + m],
                                                   rhs=hT[:, c, :], start=(c == 0 and slot == 0), stop=(c == 7),
                                                   skip_group_check=True), reads=[b_hT, b_win[c]], writes=[b_pk[bank]])

        feat_mm(6, 0, 3072, 16)
        S.op("act", lambda e: e.activation(out=alT[:, :], in_=pk[6][0:16, 0:128], func=AF.Copy), reads=[b_pk[6]], writes=[b_alT])
        S.op("pe", lambda e: e.matmul(pk[5][:, :], lhsT=alT[:, :], rhs=wa2[:, :], start=True, stop=True),
             reads=[b_alT, b_wa2], writes=[b_pk[5]])
        S.op("dve", lambda e: e.tensor_tensor(out=gg[:, :], in0=pk[5][:, :], in1=ba_t[:, :], op=ALU.add),
             reads=[b_pk[5], b_ba], writes=[b_gg])
        S.op("act", lambda e: e.activation(out=gg[:, :], in_=gg[:, :], func=AF.Exp, scale=-1.0), reads=[b_gg], writes=[b_gg])
        S.op("dve", lambda e: e.tensor_scalar(out=gg[:, :], in0=gg[:, :], scalar1=1.0, scalar2=None, op0=ALU.add),
             reads=[b_gg], writes=[b_gg])
        S.op("act", lambda e: e.activation(out=gg[:, :], in_=gg[:, :], func=AF.Ln), reads=[b_gg], writes=[b_gg])
        S.op("dve", lambda e: e.tensor_scalar(out=gg[:, :], in0=gg[:, :], scalar1=-1.0 / GATE_TAU, scalar2=None, op0=ALU.mult),
             reads=[b_gg], writes=[b_gg])
        S.op("dve", lambda e: e.tensor_copy(out=ghi[:, :], in_=gg[:, :]), reads=[b_gg], writes=[b_ghi])
        S.op("dve", lambda e: e.tensor_tensor(out=glo[:, :], in0=gg[:, :], in1=ghi[:, :], op=ALU.subtract),
             reads=[b_gg, b_ghi], writes=[b_glo])
        for hd in range(4):
            for part, (gs, bgs) in enumerate(((ghi, b_ghi), (glo, b_glo))):
                S.op("pe", lambda e, hd=hd, gs=gs, part=part: e.matmul(
                    pk[4][:, hd * 128:(hd + 1) * 128], lhsT=gs[:, hd * 128:(hd + 1) * 128], rhs=U[:, :],
                    start=(hd == 0 and part == 0), stop=(part == 1), skip_group_check=True),
                     reads=[bgs, b_U], writes=[b_pk[4]])
        for part, (gs, bgs) in enumerate(((ghi, b_ghi), (glo, b_glo))):
            S.op("pe", lambda e, gs=gs, part=part: e.matmul(pk[5][:, :], lhsT=W[:, :], rhs=gs[:, :], start=(part == 0), stop=(part == 1)),
                 reads=[bgs, b_W], writes=[b_pk[5]])
        S.op("act", lambda e: e.activation(out=EbT[:, :, :], in_=pk[4][:, :].rearrange("p (h i) -> p h i", i=128), func=AF.Exp),
             reads=[b_pk[4]], writes=[b_EbT])
        S.op("act", lambda e: e.activation(out=EnbT[:, :, :], in_=pk[4][:, :].rearrange("p (h i) -> p h i", i=128), func=AF.Exp, scale=-1.0),
             reads=[b_pk[4]], writes=[b_EnbT])
        S.op("act", lambda e: e.activation(out=Ebl[:, :], in_=pk[5][:, :], func=AF.Exp), reads=[b_pk[5]], writes=[b_Ebl])
        dcs, bdc = dc_s[i2], b_dc[i2]
        S.op("pool", lambda e, dcs=dcs: e.tensor_copy(out=dcs[:, :, :], in_=EbT[:, :, :].rearrange("p h (c j) -> p h c j", j=64)[:, :, :, 63]),
             reads=[b_EbT], writes=[bdc])
        S.dma("sp", dec[:, :, 2 * t:2 * t + 2].rearrange("h d c -> d h c"), dcs[:, :, :], reads=[bdc])
        for hd in range(4):
            feat_mm(6, hd, hd * 128, 128)
        qgs, bqg = qg_s[i2], b_qg[i2]
        S.op("dve", lambda e, qgs=qgs: e.scalar_tensor_tensor(out=qgs[:, :, :], in0=pk[6][:, :].rearrange("p (h i) -> p h i", i=128),
                                                              scalar=QSCALE, in1=EbT[:, :, :], op0=ALU.mult, op1=ALU.mult),
             reads=[b_pk[6], b_EbT], writes=[bqg])
        S.dma("sp", qgT[:, :, t * 128:(t + 1) * 128].rearrange("h d i -> d h i"), qgs[:, :, :], reads=[bqg])
        for hd in range(4):
            feat_mm(3, hd, 512 + hd * 128, 128)
        kgs, bkg = kg_s[i2], b_kg[i2]
        S.op("dve", lambda e, kgs=kgs: e.tensor_tensor(out=kgs[:, :, :], in0=pk[3][:, :].rearrange("p (h i) -> p h i", i=128),
                                                       in1=EnbT[:, :, :], op=ALU.mult), reads=[b_pk[3], b_EnbT], writes=[bkg])
        S.dma("sp", kgT[:, :, t * 128:(t + 1) * 128].rearrange("h d i -> d h i"), kgs[:, :, :], reads=[bkg])
        tok_mm(2, 512, 1024)
        khs, bkh = kh_s[i2], b_kh[i2]
        S.op("dve", lambda e, khs=khs: e.tensor_tensor(out=khs[:, :], in0=pk[2][:, :], in1=Ebl[:, :], op=ALU.mult),
             reads=[b_pk[2], b_Ebl], writes=[bkh])
        S.dma("sp", kh[t * 128:(t + 1) * 128, :], khs[:, :], reads=[bkh])
        vbs, bvb = vb_s[i2], b_vb[i2]
        for half in range(2):
            tok_mm(half, 1024 + half * 512, 1536 + half * 512)
            S.op("act", lambda e, half=half, vbs=vbs: e.activation(out=vbs[:, half * 512:(half + 1) * 512], in_=pk[half][:, :], func=AF.Copy),
                 reads=[b_pk[half]], writes=[bvb])
        S.dma("sp", vb[t * 128:(t + 1) * 128, :], vbs[:, :], reads=[bvb])
        srs, bsr = sr_s[i2], b_sr[i2]
        for half in range(2):
            tok_mm(half, 2048 + half * 512, 2560 + half * 512)
            S.op("act", lambda e, half=half, srs=srs: e.activation(out=srs[:, half * 512:(half + 1) * 512], in_=pk[half][:, :], func=AF.Silu),
                 reads=[b_pk[half]], writes=[bsr])
        S.dma("sp", sr[t * 128:(t + 1) * 128, :], srs[:, :], reads=[bsr])

    for t in range(NTL):
        _tile(t, **sets[t % 2])

RMS_EPS = 1e-6
GLA_CB = 16


def stage_gla(S, C, SEQ, NH, acc, dec_ap, g_norm, tri, CB=GLA_CB):
    NBLK = SEQ // (64 * CB)
    NCH = SEQ // 64
    tri_f = S.sbuf("gl_trif", [64, 64], F32)
    b_tri = S.buf("gl_tri")
    S.dma("sp", tri_f[:, :], tri[:, :], writes=[b_tri])
    gn = S.sbuf("gl_gn", [64, 256], F32)
    b_gn = S.buf("gl_gn")
    S.dma("sp", gn[:, :], g_norm.partition_broadcast(64), writes=[b_gn])
    dec_sb = S.sbuf("gl_dec", [128, NH, NCH], F32)
    b_dec = S.buf("gl_dec")
    for hd in range(NH):
        S.dma("sp", dec_sb[:, hd, :], dec_ap(hd), writes=[b_dec])
    st = S.sbuf("gl_st", [128, NH, 256], F32)
    b_st = S.bufs(NH, "gl_st")
    stb = [S.sbuf("gl_stb%d" % i, [128, NH, 256], BF16) for i in range(2)]
    b_stb = [S.bufs(NH, "gl_stb%d_" % i) for i in range(2)]
    S.op("dve", lambda e: e.memset(st[:, :, :], 0.0), writes=b_st)
    S.op("dve", lambda e: e.memset(stb[0][:, :, :], 0.0), writes=b_stb[0])
    qg_sb = [[S.sbuf("gl_qg%d_%d" % (i, hd), [128, 64 * CB], BF16) for hd in range(NH)] for i in range(2)]
    kg_sb = [[S.sbuf("gl_kg%d_%d" % (i, hd), [128, 64 * CB], BF16) for hd in range(NH)] for i in range(2)]
    kh_sb = [[S.sbuf("gl_kh%d_%d" % (i, hd), [64, CB, 128], BF16) for hd in range(NH)] for i in range(2)]
    v_sb = [[S.sbuf("gl_v%d_%d" % (i, hd), [64, CB, 256], BF16) for hd in range(NH)] for i in range(2)]
    b_in = [[S.bufs(4, "gl_in%d_%d_" % (i, hd)) for hd in range(NH)] for i in range(2)]
    o_st = [[S.sbuf("gl_ost%d_%d" % (i, hd), [64, CB, 256], F32) for hd in range(NH)] for i in range(2)]
    b_ost = [[S.buf("gl_ost%d_%d" % (i, hd)) for hd in range(NH)] for i in range(2)]
    ss = [[S.sbuf("gl_ss%d_%d" % (i, hd), [64, CB], F32) for hd in range(NH)] for i in range(2)]
    b_ss = [[S.buf("gl_ss%d_%d" % (i, hd)) for hd in range(NH)] for i in range(2)]
    junk = S.sbuf("gl_junk", [64, 256], F32)
    b_junk = S.buf("gl_junk")
    A_sb = [S.sbuf("gl_A%d" % i, [64, 64], BF16) for i in range(4)]
    b_A = S.bufs(4, "gl_A")
    a_bank = S.psum("gl_aps", [64, 512], F32)
    a_ps = [a_bank[:, i * 64:(i + 1) * 64] for i in range(4)]
    b_aps = S.bufs(4, "gl_aps")
    o_bank = [S.psum("gl_ops%d" % i, [64, 512], F32) for i in range(4)]
    o_ps = [o_bank[i][:, 0:256] for i in range(4)]
    b_ops = S.bufs(4, "gl_ops")
    s_bank = [S.psum("gl_sps%d" % i, [128, 512], F32) for i in range(2)]
    s_ps = [s_bank[i // 2][:, (i % 2) * 256:(i % 2) * 256 + 256] for i in range(4)]
    b_sps = S.bufs(4, "gl_sps")
    n = 0
    for blk in range(NBLK):
        i2 = blk % 2
        for hd in range(NH):
            bi = b_in[i2][hd]
            S.dma("sp", qg_sb[i2][hd][:, :], acc("qg", hd, blk), writes=[bi[0]])
            S.dma("sp", kg_sb[i2][hd][:, :], acc("kg", hd, blk), writes=[bi[1]])
            S.dma("sp", kh_sb[i2][hd][:, :, :], acc("kh", hd, blk), writes=[bi[2]])
            S.dma("sp", v_sb[i2][hd][:, :, :], acc("v", hd, blk), writes=[bi[3]])
        for cc in range(CB):
            c = blk * CB + cc
            cur, nxt = c % 2, (c + 1) % 2
            for hd in range(NH):
                bi = b_in[i2][hd]
                qg = qg_sb[i2][hd][:, cc * 64:(cc + 1) * 64]
                kg = kg_sb[i2][hd][:, cc * 64:(cc + 1) * 64]
                khc = kh_sb[i2][hd][:, cc, :]
                vc = v_sb[i2][hd][:, cc, :]
                aps, baps = a_ps[n % 4], b_aps[n % 4]
                ops, bops = o_ps[n % 4], b_ops[n % 4]
                sps, bsps = s_ps[n % 4], b_sps[n % 4]
                A, bA = A_sb[n % 4], b_A[n % 4]
                n += 1
                S.op("pe", lambda e, aps=aps, kg=kg, qg=qg: e.matmul(aps, lhsT=kg, rhs=qg, start=True, stop=True, skip_group_check=True),
                     reads=[bi[0], bi[1]], writes=[baps])
                S.op("dve", lambda e, aps=aps, A=A: e.tensor_tensor(out=A[:, :], in0=aps, in1=tri_f[:, :], op=ALU.mult),
                     reads=[baps, b_tri], writes=[bA])
                S.op("pe", lambda e, ops=ops, A=A, vc=vc: e.matmul(ops, lhsT=A[:, :], rhs=vc, start=True, stop=False),
                     reads=[bA, bi[3]], writes=[bops])
                S.op("pe", lambda e, ops=ops, qg=qg, cur=cur, hd=hd: e.matmul(ops, lhsT=qg, rhs=stb[cur][:, hd, :],
                                                                             start=False, stop=True),
                     reads=[bi[0], b_stb[cur][hd]], writes=[bops])
                S.op("pe", lambda e, sps=sps, khc=khc, vc=vc: e.matmul(sps, lhsT=khc, rhs=vc, start=True, stop=True, skip_group_check=True),
                     reads=[bi[2], bi[3]], writes=[bsps])
                S.op("dve", lambda e, sps=sps, hd=hd, c=c: e.scalar_tensor_tensor(
                    out=st[:, hd, :], in0=st[:, hd, :], scalar=dec_sb[:, hd, c:c + 1], in1=sps,
                    op0=ALU.mult, op1=ALU.add), reads=[bsps, b_st[hd], b_dec], writes=[b_st[hd]])
                S.op("act", lambda e, hd=hd, nxt=nxt: e.activation(out=stb[nxt][:, hd, :], in_=st[:, hd, :], func=AF.Copy),
                     reads=[b_st[hd]], writes=[b_stb[nxt][hd]])
                ssc = ss[i2][hd][:, cc:cc + 1]
                ostc = o_st[i2][hd][:, cc, :]
                S.op("act", lambda e, ops=ops, ssc=ssc: e.activation(out=junk[:, :], in_=ops, func=AF.Square, accum_out=ssc),
                     reads=[bops], writes=[b_junk, b_ss[i2][hd]])
                S.op("act", lambda e, ops=ops, ostc=ostc: e.activation(out=ostc, in_=ops, func=AF.Copy),
                     reads=[bops], writes=[b_ost[i2][hd]])
        for hd in range(NH):
            s_, bs_ = ss[i2][hd], b_ss[i2][hd]
            S.op("dve", lambda e, s_=s_: e.tensor_scalar(out=s_[:, :], in0=s_[:, :], scalar1=1.0 / 256, scalar2=RMS_EPS,
                                                         op0=ALU.mult, op1=ALU.add), reads=[bs_], writes=[bs_])
            S.op("act", lambda e, s_=s_: e.activation(out=s_[:, :], in_=s_[:, :], func=AF.Sqrt), reads=[bs_], writes=[bs_])
            S.op("dve", lambda e, s_=s_: e.reciprocal(out=s_[:, :], in_=s_[:, :]), reads=[bs_], writes=[bs_])
            ot, bo = o_st[i2][hd], b_ost[i2][hd]
            S.op("dve", lambda e, ot=ot, s_=s_: e.tensor_tensor(out=ot[:, :, :], in0=ot[:, :, :],
                                                               in1=s_[:, :].unsqueeze(2).to_broadcast([64, CB, 256]), op=ALU.mult),
                 reads=[bo, bs_], writes=[bo])
            S.op("pool", lambda e, ot=ot: e.tensor_tensor(out=ot[:, :, :], in0=ot[:, :, :],
                                                         in1=gn[:, :].unsqueeze(1).to_broadcast([64, CB, 256]), op=ALU.mult),
                 reads=[bo, b_gn], writes=[bo])
            S.dma("sp", acc("on", hd, blk), ot[:, :, :], reads=[bo])


_NP2DT = {np.dtype("float32"): F32, np.dtype("int32"): I32}
try:
    import ml_dtypes
    _NP2DT[np.dtype(ml_dtypes.bfloat16)] = BF16
    NPBF16 = ml_dtypes.bfloat16
except Exception:
    NPBF16 = None

_PROG_CACHE = {}


def launch(key, prog, in_list, outs):
    sig = (key, tuple((k, v.shape, str(v.dtype)) for k, v in in_list[0].items()), tuple((k, tuple(sh), str(dt)) for k, (sh, dt) in outs.items()))
    if sig not in _PROG_CACHE:
        nc = bass.Bass("TRN2", target_bir_lowering=False)
        aps = {}
        for k, v in in_list[0].items():
            aps[k] = nc.dram_tensor(k, list(v.shape), _NP2DT[v.dtype], kind="ExternalInput").ap()
        for k, (shape, dt) in outs.items():
            aps[k] = nc.dram_tensor(k, list(shape), dt, kind="ExternalOutput").ap()
        S = Sched(nc)
        prog(S, aps)
        S.emit()
        _PROG_CACHE[sig] = nc
    nc = _PROG_CACHE[sig]
    res = run_bass_kernel_spmd(nc, in_list, core_ids=list(range(len(in_list))))
    return res.results


def _consts():
    ident = np.eye(128, dtype=np.float32)
    tri = np.where(np.arange(128)[None, :] <= np.arange(128)[:, None], 0.0, NEG_MASK).astype(np.float32)
    allm = np.full((128, 128), NEG_MASK, np.float32)
    none = np.zeros((128, 128), np.float32)
    maskc = [np.concatenate([tri, allm], 1), np.concatenate([none, tri], 1)]
    negI = (-MASK_BIG * np.eye(128)).astype(np.float32)
    invf = (500000.0 ** (-np.arange(0, 16, 2, dtype=np.float32) / 16)).astype(np.float32)
    j = np.arange(128)[:, None]
    i = np.arange(128)[None, :]
    same = (j // 64) == (i // 64)
    cmU = (same & (j <= i)).astype(np.float32)
    cmW = (same & (j > i)).astype(np.float32)
    tri64 = (np.arange(64)[:, None] <= np.arange(64)[None, :]).astype(np.float32)
    return dict(ident=ident, maskc=maskc, negI=negI, invf=invf, cmU=cmU, cmW=cmW, tri64=tri64)


def kernel_unfused(x, positions, a_w_in, a_w_o, b_w_in, b_w_a2, b_b_a, b_g_norm, b_w_o,
           ln_mix_g, ln_mix_b, mlp_w_up, mlp_w_down, ln_mlp_g, ln_mlp_b):
    x = np.asarray(x, np.float32)
    B, SEQ, _ = x.shape
    NC = 2 * B
    T = SEQ // 2
    NTL = T // 128
    NQ = NTL
    cst = _consts()
    f32 = lambda a: np.ascontiguousarray(np.asarray(a, np.float32))
    tok = []
    for c in range(NC):
        r = c % 2
        tiles = np.arange(r, SEQ // 128, 2)
        tok.append((tiles[:, None] * 128 + np.arange(128)[None, :]).reshape(-1))
    h = [np.ascontiguousarray(x[c // 2][tok[c]]) for c in range(NC)]
    pos_pt = [np.ascontiguousarray(np.asarray(positions)[c // 2][tok[c]].astype(np.int32).reshape(NTL, 128).T) for c in range(NC)]
    depth = ln_mix_g.shape[0]
    for i in range(depth):
        j = i // 2
        if i % 2 == 0:
            w_in = f32(a_w_in[j])
            ins = [dict(h=h[c], pos=pos_pt[c], w=w_in, invf=cst["invf"], ident=cst["ident"]) for c in range(NC)]
            outs = {"qT": ([1024, T], BF16), "kT": ([256, T], BF16), "v": ([T, 256], BF16), "iqT": ([512, T], BF16),
                    "ikT": ([64, T], BF16), "iw": ([T, 8], F32)}

            def prog(S, a):
                C = Consts(S, a["ident"])
                stage_proj_A(S, C, T, a["h"], a["pos"], a["w"], a["invf"], a["qT"], a["kT"], a["v"], a["iqT"], a["ikT"], a["iw"])
            pr = launch("projA", prog, ins, outs)
            ins = []
            for c in range(NC):
                b0 = (c // 2) * 2
                kT = np.empty((256, SEQ), NPBF16)
                ikT = np.empty((64, SEQ), NPBF16)
                vf = np.empty((SEQ, 256), NPBF16)
                for r in range(2):
                    kT[:, tok[b0 + r]] = pr[b0 + r]["kT"]
                    ikT[:, tok[b0 + r]] = pr[b0 + r]["ikT"]
                    vf[tok[b0 + r]] = pr[b0 + r]["v"]
                ins.append(dict(ident=cst["ident"], kT=kT, v=vf, ikT=ikT, qT=pr[c]["qT"], iqT=pr[c]["iqT"], iw=pr[c]["iw"],
                                maskc=cst["maskc"][c % 2], negI=cst["negI"]))

            def prog(S, a):
                C = Consts(S, a["ident"])
                stage_attn(S, C, NQ, a["kT"].rearrange("(g d) s -> g d s", d=64), a["v"], a["ikT"],
                           a["qT"].rearrange("(h d) t -> h d t", d=64), a["iqT"].rearrange("(h d) t -> h d t", d=64),
                           a["iw"], a["maskc"], a["negI"], a["o"])
            ar = launch("attn", prog, ins, {"o": ([T, 1024], BF16)})
            w_o = f32(a_w_o[j])
            ins = [dict(ident=cst["ident"], o=ar[c]["o"], h=h[c], w=w_o, g=f32(ln_mix_g[i]), b=f32(ln_mix_b[i])) for c in range(NC)]

            def prog(S, a):
                C = Consts(S, a["ident"])
                stage_wo_ln(S, C, T, a["o"], a["h"], a["w"], a["g"], a["b"], a["h1"])
            wr = launch("wo", prog, ins, {"h1": ([T, D], F32)})
        else:
            ins = [dict(h=h[c], w=f32(b_w_in[j]), wa2=f32(b_w_a2[j]), ba=f32(b_b_a[j]), U=cst["cmU"], W=cst["cmW"], ident=cst["ident"])
                   for c in range(NC)]
            outs = {"qgT": ([512, T], BF16), "kgT": ([512, T], BF16), "kh": ([T, 512], BF16), "vb": ([T, 1024], BF16),
                    "dec": ([512, T // 64], F32), "sr": ([T, 1024], F32)}

            def prog(S, a):
                C = Consts(S, a["ident"])
                stage_proj_B(S, C, T, a["h"], a["w"], a["wa2"], a["ba"], a["U"], a["W"],
                             a["qgT"].rearrange("(h d) t -> h d t", d=128), a["kgT"].rearrange("(h d) t -> h d t", d=128),
                             a["kh"], a["vb"], a["dec"].rearrange("(h d) c -> h d c", d=128), a["sr"])
            pr = launch("projB", prog, ins, outs)
            ins = []
            ctok = [t_.reshape(-1, 128)[:, ::64].reshape(-1) // 64 for t_ in tok]
            for c in range(NC):
                b0 = (c // 2) * 2
                hs = slice((c % 2) * 256, (c % 2) * 256 + 256)
                vs = slice((c % 2) * 512, (c % 2) * 512 + 512)
                qg = np.empty((256, SEQ), NPBF16)
                kg = np.empty((256, SEQ), NPBF16)
                khh = np.empty((SEQ, 256), NPBF16)
                vv = np.empty((SEQ, 512), NPBF16)
                dd = np.empty((256, SEQ // 64), np.float32)
                for r in range(2):
                    qg[:, tok[b0 + r]] = pr[b0 + r]["qgT"][hs]
                    kg[:, tok[b0 + r]] = pr[b0 + r]["kgT"][hs]
                    khh[tok[b0 + r]] = pr[b0 + r]["kh"][:, hs]
                    vv[tok[b0 + r]] = pr[b0 + r]["vb"][:, vs]
                    dd[:, ctok[b0 + r]] = pr[b0 + r]["dec"][hs]
                ins.append(dict(ident=cst["ident"], qgT=qg, kgT=kg, kh=khh, vb=vv, dec=dd, gn=f32(b_g_norm[j]), tri=cst["tri64"]))

            def prog(S, a):
                C = Consts(S, a["ident"])

                def acc(kind, hd, blk):
                    t0, t1 = blk * 1024, (blk + 1) * 1024
                    if kind == "qg":
                        return a["qgT"][hd * 128:(hd + 1) * 128, t0:t1]
                    if kind == "kg":
                        return a["kgT"][hd * 128:(hd + 1) * 128, t0:t1]
                    if kind == "kh":
                        return a["kh"][t0:t1, hd * 128:(hd + 1) * 128].rearrange("(c j) d -> j c d", j=64)
                    if kind == "v":
                        return a["vb"][t0:t1, hd * 256:(hd + 1) * 256].rearrange("(c j) e -> j c e", j=64)
                    return a["on"][t0:t1, hd * 256:(hd + 1) * 256].rearrange("(c j) e -> j c e", j=64)
                stage_gla(S, C, SEQ, 2, acc, lambda hd: a["dec"][hd * 128:(hd + 1) * 128, :], a["gn"], a["tri"])
            gr = launch("gla", prog, ins, {"on": ([SEQ, 512], F32)})
            w_o = f32(b_w_o[j])
            ins = []
            for c in range(NC):
                b0 = (c // 2) * 2
                on = np.concatenate([gr[b0]["on"][tok[c]], gr[b0 + 1]["on"][tok[c]]], axis=1)
                ins.append(dict(ident=cst["ident"], on=np.ascontiguousarray(on), sr=pr[c]["sr"], h=h[c], w=w_o,
                                g=f32(ln_mix_g[i]), b=f32(ln_mix_b[i])))

            def prog(S, a):
                C = Consts(S, a["ident"])
                stage_wo_ln(S, C, T, None, a["h"], a["w"], a["g"], a["b"], a["h1"], gate=(a["on"], a["sr"]))
            wr = launch("wog", prog, ins, {"h1": ([T, D], F32)})
        ins = [dict(ident=cst["ident"], h1=wr[c]["h1"], wu=f32(mlp_w_up[i]), wd=f32(mlp_w_down[i]), g=f32(ln_mlp_g[i]), b=f32(ln_mlp_b[i]))
               for c in range(NC)]

        def prog(S, a):
            C = Consts(S, a["ident"])
            stage_mlp(S, C, T, a["h1"], a["wu"], a["wd"], a["g"], a["b"], a["h2"])
        mr = launch("mlp", prog, ins, {"h2": ([T, D], F32)})
        h = [mr[c]["h2"] for c in range(NC)]
    out = np.empty((B, SEQ, D), np.float32)
    for c in range(NC):
        out[c // 2][tok[c]] = h[c]
    return out


def build_fused(SEQ, depth):
    nc = bass.Bass("TRN2", target_bir_lowering=False)
    T = SEQ
    NTL = T // 128

    def din(name, shape, dt=F32):
        return nc.dram_tensor(name, list(shape), dt, kind="ExternalInput").ap()

    def scr(name, shape, dt):
        return nc.dram_tensor("scr_" + name, list(shape), dt).ap()

    NA, NB = (depth + 1) // 2, depth // 2
    a = dict(
        x=din("x", [T, D]), pos=din("pos", [128, NTL], I32), invf=din("invf", [8]), ident=din("ident", [128, 128]),
        maskc=din("maskc", [128, 256]), negI=din("negI", [128, 128]), cmU=din("cmU", [128, 128]), cmW=din("cmW", [128, 128]),
        tri64=din("tri64", [64, 64]),
        a_w_in=din("a_w_in", [NA, D, A_IN]), a_w_o=din("a_w_o", [NA, D, D]),
        b_w_in=din("b_w_in", [max(NB, 1), D, B_IN]), b_w_a2=din("b_w_a2", [max(NB, 1), 16, 512]), b_b_a=din("b_b_a", [max(NB, 1), 512]),
        b_g_norm=din("b_g_norm", [max(NB, 1), 256]), b_w_o=din("b_w_o", [max(NB, 1), D, D]),
        ln_mix_g=din("ln_mix_g", [depth, D]), ln_mix_b=din("ln_mix_b", [depth, D]),
        mlp_w_up=din("mlp_w_up", [depth, D, DFF]), mlp_w_down=din("mlp_w_down", [depth, DFF, D]),
        ln_mlp_g=din("ln_mlp_g", [depth, D]), ln_mlp_b=din("ln_mlp_b", [depth, D]),
    )
    out = nc.dram_tensor("out", [T, D], F32, kind="ExternalOutput").ap()
    hA = scr("hA", [T, D], F32)
    hB = scr("hB", [T, D], F32)
    qT = scr("qT", [1024, T], BF16)
    kT = scr("kT", [256, T], BF16)
    vv = scr("v", [T, 256], BF16)
    iqT = scr("iqT", [512, T], BF16)
    ikT = scr("ikT", [64, T], BF16)
    iw = scr("iw", [T, 8], F32)
    o = scr("o", [T, 1024], BF16)
    qgT = scr("qgT", [512, T], BF16)
    kgT = scr("kgT", [512, T], BF16)
    kh = scr("kh", [T, 512], BF16)
    vb = scr("vb", [T, 1024], BF16)
    dec = scr("dec", [512, T // 64], F32)
    sr = scr("sr", [T, 1024], F32)
    on = scr("on", [T, 1024], F32)

    S = Sched(nc)
    CB = 8
    h_in = a["x"]
    for i in range(depth):
        j = i // 2
        last = i == depth - 1
        if i % 2 == 0:
            S.stage_begin()
            C = Consts(S, a["ident"])
            stage_proj_A(S, C, T, h_in, a["pos"], a["a_w_in"][j], a["invf"], qT, kT, vv, iqT, ikT, iw)
            S.stage_end()
            S.stage_begin()
            C = Consts(S, a["ident"])
            stage_attn(S, C, NTL, kT.rearrange("(g d) s -> g d s", d=64), vv, ikT, qT.rearrange("(h d) t -> h d t", d=64),
                       iqT.rearrange("(h d) t -> h d t", d=64), iw, a["maskc"], a["negI"], o, solo=True)
            S.stage_end()
            S.stage_begin()
            C = Consts(S, a["ident"])
            stage_wo_ln(S, C, T, o, h_in, a["a_w_o"][j], a["ln_mix_g"][i], a["ln_mix_b"][i], hB)
            S.stage_end()
        else:
            S.stage_begin()
            C = Consts(S, a["ident"])
            stage_proj_B(S, C, T, h_in, a["b_w_in"][j], a["b_w_a2"][j], a["b_b_a"][j], a["cmU"], a["cmW"],
                         qgT.rearrange("(h d) t -> h d t", d=128), kgT.rearrange("(h d) t -> h d t", d=128), kh, vb,
                         dec.rearrange("(h d) c -> h d c", d=128), sr)
            S.stage_end()
            S.stage_begin()
            C = Consts(S, a["ident"])

            def acc(kind, hd, blk):
                t0, t1 = blk * 64 * CB, (blk + 1) * 64 * CB
                if kind == "qg":
                    return qgT[hd * 128:(hd + 1) * 128, t0:t1]
                if kind == "kg":
                    return kgT[hd * 128:(hd + 1) * 128, t0:t1]
                if kind == "kh":
                    return kh[t0:t1, hd * 128:(hd + 1) * 128].rearrange("(c j) d -> j c d", j=64)
                if kind == "v":
                    return vb[t0:t1, hd * 256:(hd + 1) * 256].rearrange("(c j) e -> j c e", j=64)
                return on[t0:t1, hd * 256:(hd + 1) * 256].rearrange("(c j) e -> j c e", j=64)
            stage_gla(S, C, SEQ, 4, acc, lambda hd: dec[hd * 128:(hd + 1) * 128, :], a["b_g_norm"][j], a["tri64"], CB=CB)
            S.stage_end()
            S.stage_begin()
            C = Consts(S, a["ident"])
            stage_wo_ln(S, C, T, None, h_in, a["b_w_o"][j], a["ln_mix_g"][i], a["ln_mix_b"][i], hB, gate=(on, sr))
            S.stage_end()
        S.stage_begin()
        C = Consts(S, a["ident"])
        h_out = out if last else hA
        stage_mlp(S, C, T, hB, a["mlp_w_up"][i], a["mlp_w_down"][i], a["ln_mlp_g"][i], a["ln_mlp_b"][i], h_out)
        S.stage_end(last=last)
        h_in = hA
    S.stack.close()
    return nc


_FUSED = {}


def kernel(x, positions, a_w_in, a_w_o, b_w_in, b_w_a2, b_b_a, b_g_norm, b_w_o,
           ln_mix_g, ln_mix_b, mlp_w_up, mlp_w_down, ln_mlp_g, ln_mlp_b):
    x = np.asarray(x, np.float32)
    B, SEQ, _ = x.shape
    depth = int(np.asarray(ln_mix_g).shape[0])
    key = (SEQ, depth)
    if key not in _FUSED:
        _FUSED[key] = build_fused(SEQ, depth)
    nc = _FUSED[key]
    cst = _consts()
    f32 = lambda t: np.ascontiguousarray(np.asarray(t, np.float32))
    shared = dict(invf=cst["invf"], ident=cst["ident"], maskc=cst["maskc"][1], negI=cst["negI"], cmU=cst["cmU"], cmW=cst["cmW"],
                  tri64=cst["tri64"], a_w_in=f32(a_w_in), a_w_o=f32(a_w_o), b_w_in=f32(b_w_in), b_w_a2=f32(b_w_a2), b_b_a=f32(b_b_a),
                  b_g_norm=f32(b_g_norm), b_w_o=f32(b_w_o), ln_mix_g=f32(ln_mix_g), ln_mix_b=f32(ln_mix_b),
                  mlp_w_up=f32(mlp_w_up), mlp_w_down=f32(mlp_w_down), ln_mlp_g=f32(ln_mlp_g), ln_mlp_b=f32(ln_mlp_b))
    in_maps = []
    for c in range(B):
        pos_pt = np.ascontiguousarray(np.asarray(positions)[c].astype(np.int32).reshape(SEQ // 128, 128).T)
        m = dict(shared)
        m["x"] = np.ascontiguousarray(x[c])
        m["pos"] = pos_pt
        in_maps.append(m)
    res = run_bass_kernel_spmd(nc, in_maps, core_ids=list(range(B)))
    return np.stack([res.results[c]["out"] for c in range(B)], axis=0)
```

```python
import contextlib
import numpy as np
import concourse.bass as bass
import concourse.mybir as mybir
from concourse.ap import AP
from concourse.bass_utils import run_bass_kernel_spmd

F32 = mybir.dt.float32
BF16 = mybir.dt.bfloat16
I32 = mybir.dt.int32
AF = mybir.ActivationFunctionType
ALU = mybir.AluOpType
AX = mybir.AxisListType

D = 1024
DFF = 4096
DEPTH = 4
ALPHA = (2 * DEPTH) ** 0.25
LN_EPS = 1e-5
NCORES = 8


class Buf:
    __slots__ = ("name", "last_w", "readers")

    def __init__(self, name):
        self.name = name
        self.last_w = None
        self.readers = {}


class Sched:
    ENGS = ("sp", "act", "dve", "pool", "pe")
    SAME_WIN = 3
    NDMA = 8

    def __init__(self, nc):
        self.nc = nc
        self.stack = contextlib.ExitStack()
        self.ops = {e: [] for e in self.ENGS}
        self.count = {e: 0 for e in self.ENGS}
        self.seen = {e: {} for e in self.ENGS}
        self.esem = {}
        for e in ("act", "dve", "pool", "pe"):
            self.esem[e] = self.stack.enter_context(nc.semaphore("s_" + e))
        self.dsem = {}
        self.dma_i = {}
        for q in ("sp", "act", "pool"):
            self.dsem[q] = [self.stack.enter_context(nc.semaphore("d_%s%d" % (q, i))) for i in range(self.NDMA)]
            self.dma_i[q] = 0
        self.nbuf = 0
        self.stage_stack = None
        self.stage_no = 0
        self.bar = self.stack.enter_context(nc.semaphore("s_bar"))

    def sbuf(self, name, shape, dt):
        st = self.stage_stack if self.stage_stack is not None else self.stack
        return st.enter_context(self.nc.sbuf_tensor("sb%d_" % self.stage_no + name, list(shape), dt))

    def psum(self, name, shape, dt):
        st = self.stage_stack if self.stage_stack is not None else self.stack
        return st.enter_context(self.nc.psum_tensor("pp%d_" % self.stage_no + name, list(shape), dt))

    def stage_begin(self):
        self.stage_stack = contextlib.ExitStack()

    def stage_end(self, last=False):
        self.finish()
        if not last:
            self.stage_no += 1
            n = self.stage_no
            bar = self.bar
            self.ops["sp"].append(([], (lambda e, bar=bar: e.sem_inc(bar, 1)), None, 0))
            for eng in ("act", "dve", "pool", "pe"):
                self.ops[eng].append(([(bar, n)], None, None, 0))
        self._emit_block()
        self.stage_stack.close()
        self.stage_stack = None

    def buf(self, name=None):
        self.nbuf += 1
        return Buf(name or ("b%d" % self.nbuf))

    def bufs(self, n, name="b"):
        return [self.buf("%s%d" % (name, i)) for i in range(n)]

    def _deps(self, reads, writes):
        raw = {}
        other = {}
        for b in reads:
            if b.last_w is not None:
                k, v = b.last_w
                raw[k] = max(raw.get(k, 0), v)
        for b in writes:
            if b.last_w is not None:
                k, v = b.last_w
                other[k] = max(other.get(k, 0), v)
            for k, v in b.readers.items():
                other[k] = max(other.get(k, 0), v)
        return raw, other

    def _commit(self, ev, reads, writes):
        k, v = ev
        for b in reads:
            b.readers[k] = max(b.readers.get(k, 0), v)
        for b in writes:
            b.last_w = ev
            b.readers = {}

    def op(self, eng, fn, reads=(), writes=()):
        raw, other = self._deps(reads, writes)
        own = self.esem[eng]
        waits = {}
        seen = self.seen[eng]
        for d, is_raw in ((raw, True), (other, False)):
            for k, v in d.items():
                if k is own:
                    if eng == "pe" or not is_raw:
                        continue
                    if v <= self.count[eng] - self.SAME_WIN:
                        continue
                if seen.get(k, 0) >= v:
                    continue
                waits[k] = max(waits.get(k, 0), v)
        for k, v in waits.items():
            seen[k] = v
        self.count[eng] += 1
        ev = (own, self.count[eng])
        self._commit(ev, reads, writes)
        self.ops[eng].append((list(waits.items()), fn, own, 1))

    def dma(self, q, out, in_, reads=(), writes=(), fn=None, **kw):
        raw, other = self._deps(reads, writes)
        waits = {}
        seen = self.seen[q]
        for d in (raw, other):
            for k, v in d.items():
                if seen.get(k, 0) >= v:
                    continue
                waits[k] = max(waits.get(k, 0), v)
        i = self.dma_i[q]
        self.dma_i[q] = i + 1
        slot = self.dsem[q][i % self.NDMA]
        target = 16 * (i // self.NDMA + 1)
        if target > 16 and seen.get(slot, 0) < target - 16:
            waits[slot] = max(waits.get(slot, 0), target - 16)
        for k, v in waits.items():
            seen[k] = v
        ev = (slot, target)
        self._commit(ev, reads, writes)
        if fn is None:
            fn = lambda e, out=out, in_=in_, kw=kw: e.dma_start(out=out, in_=in_, **kw)
        self.ops[q].append((list(waits.items()), fn, slot, 16))

    def allgather(self, out, in_, groups, reads=(), writes=()):
        fn = lambda e: e.collective_compute("AllGather", ALU.bypass, replica_groups=groups, ins=[in_], outs=[out])
        self.dma("pool", None, None, reads=reads, writes=writes, fn=fn)

    def finish(self):
        waits = []
        for q in ("sp", "act", "pool"):
            n = self.dma_i[q]
            for s in range(min(n, self.NDMA)):
                cnt = (n - 1 - s) // self.NDMA + 1
                waits.append((self.dsem[q][s], 16 * cnt))
        for e in ("act", "dve", "pool", "pe"):
            if self.count[e]:
                waits.append((self.esem[e], self.count[e]))
        self.ops["sp"].append((waits, None, None, 0))

    def emit(self):
        self.finish()
        self._emit_block()
        self.stack.close()

    def _emit_block(self):
        nc = self.nc
        with nc.Block() as block:
            deco = {"sp": block.sync, "act": block.scalar, "dve": block.vector,
                    "pool": block.gpsimd, "pe": block.tensor}
            for eng in self.ENGS:
                ops = self.ops[eng]
                if not ops:
                    continue

                def body(e, ops=ops):
                    for waits, fn, sem, inc in ops:
                        for s, v in waits:
                            e.wait_ge(s, v)
                        if fn is not None:
                            ins = fn(e)
                            if sem is not None:
                                ins.then_inc(sem, inc)

                deco[eng](body)
        self.ops = {e: [] for e in self.ENGS}


def bcast_rows(ap_row, nparts):
    return ap_row.partition_broadcast(nparts)


class Consts:
    def __init__(self, S, ident_dram):
        self.ident = S.sbuf("ident", [128, 128], BF16)
        self.b_ident = S.buf("ident")
        S.dma("pool", self.ident[:], ident_dram[:, :], writes=[self.b_ident])


def load_bcast(S, name, row_ap, n):
    t = S.sbuf(name, [128, n], F32)
    b = S.buf(name)
    S.dma("sp", t[:], row_ap.partition_broadcast(128), writes=[b])
    return t, b


def layer_norm_tile(S, u, b_u, out, b_out, g_t, b_g, bt_t, b_bt, scr, eng2="pool"):
    st, b_st = scr["st"], scr["b_st"]
    mv, b_mv = scr["mv"], scr["b_mv"]
    for c in range(2):
        S.op("dve", lambda e, c=c: e.bn_stats(out=st[:, c, :], in_=u[:, c * 512:(c + 1) * 512]),
             reads=[b_u], writes=[b_st[c]])
    S.op("dve", lambda e: e.bn_aggr(out=mv[:, 0:2], in_=st[:, :, :]), reads=b_st, writes=[b_mv[0]])
    S.op("dve", lambda e: e.tensor_scalar(out=mv[:, 2:3], in0=mv[:, 1:2], scalar1=LN_EPS, scalar2=None,
                                           op0=ALU.add), reads=[b_mv[0]], writes=[b_mv[1]])
    S.op("act", lambda e: e.activation(out=mv[:, 2:3], in_=mv[:, 2:3], func=AF.Sqrt), reads=[b_mv[1]], writes=[b_mv[1]])
    S.op("dve", lambda e: e.reciprocal(out=mv[:, 2:3], in_=mv[:, 2:3]), reads=[b_mv[1]], writes=[b_mv[1]])
    S.op("dve", lambda e: e.scalar_tensor_tensor(out=mv[:, 3:4], in0=mv[:, 0:1], scalar=-1.0, in1=mv[:, 2:3],
                                                  op0=ALU.mult, op1=ALU.mult), reads=[b_mv[0], b_mv[1]], writes=[b_mv[2]])
    S.op("act", lambda e: e.activation(out=u[:, :], in_=u[:, :], func=AF.Identity, bias=mv[:, 3:4], scale=mv[:, 2:3]),
         reads=[b_u, b_mv[1], b_mv[2]], writes=[b_u])
    S.op(eng2, lambda e: e.tensor_tensor(out=u[:, :], in0=u[:, :], in1=g_t[:, :], op=ALU.mult),
         reads=[b_u, b_g], writes=[b_u])
    S.op(eng2, lambda e: e.tensor_tensor(out=out[:, :], in0=u[:, :], in1=bt_t[:, :], op=ALU.add),
         reads=[b_u, b_bt], writes=[b_out])


def ln_scratch(S, name):
    return {"st": S.sbuf(name + "_st", [128, 2, 6], F32), "b_st": S.bufs(2, name + "st"),
            "mv": S.sbuf(name + "_mv", [128, 4], F32), "b_mv": S.bufs(3, name + "mv")}


def stage_mlp(S, C, T, h1, w_up, w_down, g, b, h2):
    NT = 256
    NS = NT // 128
    wup = S.sbuf("wup", [128, 8, DFF], BF16)
    wdn = S.sbuf("wdn", [128, 32, D], BF16)
    b_wup = S.bufs(8, "wup")
    b_wdn = S.bufs(8, "wdn")
    wu_v = w_up.rearrange("(c p) f -> p c f", p=128)
    wd_v = w_down.rearrange("(c p) n -> p c n", p=128)
    for c in range(8):
        S.dma("pool", wup[:, c, :], wu_v[:, c, :], writes=[b_wup[c]])
    for c in range(8):
        S.dma("pool", wdn[:, 4 * c:4 * c + 4, :], wd_v[:, 4 * c:4 * c + 4, :], writes=[b_wdn[c]])
    g_t, b_g = load_bcast(S, "mlp_g", g, D)
    bt_t, b_bt = load_bcast(S, "mlp_b", b, D)

    NB = 2
    x_sb = [S.sbuf("x_sb%d" % i, [128, NS, D], F32) for i in range(NB)]
    b_x = [S.bufs(NS, "x%d_" % i) for i in range(NB)]
    xb = S.sbuf("xb", [128, NS, D], BF16)
    b_xb = S.bufs(NS, "xb")
    xT = S.sbuf("xT", [128, 8, NT], BF16)
    b_xT = S.bufs(8, "xT")
    h2T = S.sbuf("h2T", [128, 32, NT], BF16)
    b_h2T = S.bufs(32, "h2T")
    r_sb = [S.sbuf("r_sb%d" % i, [128, NT], F32) for i in range(4)]
    b_r = S.bufs(4, "r")
    y_sb = [S.sbuf("y_sb%d" % i, [128, D], F32) for i in range(2)]
    b_y = S.bufs(2, "y")
    tp_ps = [S.psum("tp_ps%d" % i, [128, 1024], BF16) for i in range(2)]
    b_tp = S.bufs(2, "tp")
    up_ps = [S.psum("up_ps%d" % i, [128, 512], F32) for i in range(4)]
    b_up = S.bufs(4, "up")
    dn_ps = [S.psum("dn_ps%d" % i, [128, 512], F32) for i in range(2)]
    b_dn = S.bufs(2, "dn")
    scrs = [ln_scratch(S, "mlp%d" % i) for i in range(2)]

    h1v = h1.rearrange("(t s p) d -> t p s d", p=128, s=NS)
    h2v = h2.rearrange("(t s p) d -> t s p d", p=128, s=NS)
    n_up = 0
    n_dn = 0
    n_tp = 0
    n_y = 0
    for t in range(T // NT):
        xs, bx = x_sb[t % NB], b_x[t % NB]
        S.dma("sp", xs[:, :, :], h1v[t], writes=bx)
        for s in range(NS):
            S.op("act", lambda e, xs=xs, s=s: e.activation(out=xb[:, s, :], in_=xs[:, s, :], func=AF.Copy),
                 reads=[bx[s]], writes=[b_xb[s]])
        for c in range(8):
            tp, btp = tp_ps[n_tp % 2], b_tp[n_tp % 2]
            n_tp += 1
            for s in range(NS):
                S.op("pe", lambda e, tp=tp, s=s, c=c: e.transpose(out=tp[:, s * 128:(s + 1) * 128],
                                                                  in_=xb[:, s, c * 128:(c + 1) * 128],
                                                                  identity=C.ident[:]),
                     reads=[b_xb[s], C.b_ident], writes=[btp])
            S.op("dve", lambda e, tp=tp, c=c: e.tensor_copy(out=xT[:, c, :], in_=tp[:, 0:NT]),
                 reads=[btp], writes=[b_xT[c]])
        for fc in range(32):
            ps, bps = up_ps[n_up % 4], b_up[n_up % 4]
            r, br = r_sb[n_up % 4], b_r[n_up % 4]
            n_up += 1
            for c in range(8):
                S.op("pe", lambda e, ps=ps, c=c, fc=fc: e.matmul(ps[:, 0:NT], lhsT=wup[:, c, fc * 128:(fc + 1) * 128],
                                                                  rhs=xT[:, c, :], start=(c == 0), stop=(c == 7)),
                     reads=[b_wup[c], b_xT[c]], writes=[bps])
            S.op("act", lambda e, ps=ps, r=r: e.activation(out=r[:, :], in_=ps[:, 0:NT], func=AF.Relu),
                 reads=[bps], writes=[br])
            S.op("dve", lambda e, r=r, fc=fc: e.tensor_tensor(out=h2T[:, fc, :], in0=r[:, :], in1=r[:, :], op=ALU.mult),
                 reads=[br], writes=[b_h2T[fc]])
        for s in range(NS):
            for nh in range(2):
                ps, bps = dn_ps[n_dn % 2], b_dn[n_dn % 2]
                n_dn += 1
                for fc in range(32):
                    S.op("pe", lambda e, ps=ps, fc=fc, s=s, nh=nh: e.matmul(
                        ps[:, :], lhsT=h2T[:, fc, s * 128:(s + 1) * 128], rhs=wdn[:, fc, nh * 512:(nh + 1) * 512],
                        start=(fc == 0), stop=(fc == 31)),
                         reads=[b_h2T[fc], b_wdn[fc // 4]], writes=[bps])
                S.op("dve", lambda e, ps=ps, xs=xs, s=s, nh=nh: e.scalar_tensor_tensor(
                    out=xs[:, s, nh * 512:(nh + 1) * 512], in0=xs[:, s, nh * 512:(nh + 1) * 512], scalar=ALPHA,
                    in1=ps[:, :], op0=ALU.mult, op1=ALU.add),
                     reads=[bps, bx[s]], writes=[bx[s]])
            y, by = y_sb[n_y % 2], b_y[n_y % 2]
            n_y += 1
            layer_norm_tile(S, xs[:, s, :], bx[s], y, by, g_t, b_g, bt_t, b_bt, scrs[n_y % 2])
            S.dma("sp", h2v[t, s], y[:, :], reads=[by])


TWO_PI = float(2 * np.pi)
PI = float(np.pi)


def _range_reduce(S, x, bx, tmp_f, btf, tmp_i, bti):
    S.op("dve", lambda e: e.tensor_scalar(out=tmp_f, in0=x, scalar1=1.0 / TWO_PI, scalar2=None, op0=ALU.mult),
         reads=[bx], writes=[btf])
    S.op("dve", lambda e: e.tensor_copy(out=tmp_i, in_=tmp_f), reads=[btf], writes=[bti])
    S.op("dve", lambda e: e.tensor_copy(out=tmp_f, in_=tmp_i), reads=[bti], writes=[btf])
    S.op("dve", lambda e: e.scalar_tensor_tensor(out=x, in0=tmp_f, scalar=-TWO_PI, in1=x, op0=ALU.mult, op1=ALU.add),
         reads=[btf, bx], writes=[bx])
    S.op("dve", lambda e: e.tensor_scalar(out=tmp_f, in0=x, scalar1=PI, scalar2=-TWO_PI, op0=ALU.is_gt, op1=ALU.mult),
         reads=[bx], writes=[btf])
    S.op("dve", lambda e: e.tensor_tensor(out=x, in0=x, in1=tmp_f, op=ALU.add), reads=[btf, bx], writes=[bx])
    S.op("dve", lambda e: e.tensor_scalar(out=tmp_f, in0=x, scalar1=-PI, scalar2=TWO_PI, op0=ALU.is_lt, op1=ALU.mult),
         reads=[bx], writes=[btf])
    S.op("dve", lambda e: e.tensor_tensor(out=x, in0=x, in1=tmp_f, op=ALU.add), reads=[btf, bx], writes=[bx])


def rotary_tables(S, NTL, pos_pt, invf):
    posi = S.sbuf("posi", [128, NTL], I32)
    posf = S.sbuf("posf", [128, NTL], F32)
    invt = S.sbuf("invt", [128, 8], F32)
    cos_t = S.sbuf("cos_t", [128, NTL, 8], F32)
    sin_t = S.sbuf("sin_t", [128, NTL, 8], F32)
    tmpf = S.sbuf("rr_tf", [128, NTL, 8], F32)
    tmpi = S.sbuf("rr_ti", [128, NTL, 8], I32)
    b_pi, b_pf, b_inv, b_cos, b_sin, b_tf, b_ti = S.bufs(7, "rot")
    S.dma("sp", posi[:], pos_pt[:, :], writes=[b_pi])
    S.dma("sp", invt[:], invf.partition_broadcast(128), writes=[b_inv])
    S.op("dve", lambda e: e.tensor_copy(out=posf[:], in_=posi[:]), reads=[b_pi], writes=[b_pf])
    S.op("dve", lambda e: e.tensor_tensor(out=sin_t[:, :, :], in0=posf[:, :].unsqueeze(2).to_broadcast([128, NTL, 8]),
                                          in1=invt[:, :].unsqueeze(1).to_broadcast([128, NTL, 8]), op=ALU.mult),
         reads=[b_pf, b_inv], writes=[b_sin])
    S.op("dve", lambda e: e.tensor_scalar(out=cos_t[:, :, :], in0=sin_t[:, :, :], scalar1=PI / 2, scalar2=None, op0=ALU.add),
         reads=[b_sin], writes=[b_cos])
    _range_reduce(S, sin_t[:, :, :], b_sin, tmpf[:, :, :], b_tf, tmpi[:, :, :], b_ti)
    _range_reduce(S, cos_t[:, :, :], b_cos, tmpf[:, :, :], b_tf, tmpi[:, :, :], b_ti)
    S.op("act", lambda e: e.activation(out=sin_t[:, :, :], in_=sin_t[:, :, :], func=AF.Sin), reads=[b_sin], writes=[b_sin])
    S.op("act", lambda e: e.activation(out=cos_t[:, :, :], in_=cos_t[:, :, :], func=AF.Sin), reads=[b_cos], writes=[b_cos])
    return cos_t, b_cos, sin_t, b_sin


def apply_rotary(S, P, bP, h0, nh, cos_t, b_cos, sin_t, b_sin, t, tmp, btmp):
    Pv = P[:, h0 * 64:(h0 + nh) * 64].rearrange("p (h d) -> p h d", d=64)
    x1 = Pv[:, :, 0:8]
    x2 = Pv[:, :, 8:16]
    c = cos_t[:, t, :].unsqueeze(1).to_broadcast([128, nh, 8])
    s = sin_t[:, t, :].unsqueeze(1).to_broadcast([128, nh, 8])
    t1, t2, t3, t4 = (tmp[:, i, 0:nh, :] for i in range(4))
    bP = list(bP)
    rd = bP + [b_cos, b_sin]
    S.op("dve", lambda e: e.tensor_tensor(out=t1, in0=x1, in1=c, op=ALU.mult), reads=rd, writes=[btmp[0]])
    S.op("dve", lambda e: e.tensor_tensor(out=t2, in0=x2, in1=s, op=ALU.mult), reads=rd, writes=[btmp[1]])
    S.op("dve", lambda e: e.tensor_tensor(out=t3, in0=x2, in1=c, op=ALU.mult), reads=rd, writes=[btmp[2]])
    S.op("dve", lambda e: e.tensor_tensor(out=t4, in0=x1, in1=s, op=ALU.mult), reads=rd, writes=[btmp[3]])
    S.op("dve", lambda e: e.tensor_tensor(out=x1, in0=t1, in1=t2, op=ALU.subtract), reads=[btmp[0], btmp[1], btmp[3]], writes=bP)
    S.op("dve", lambda e: e.tensor_tensor(out=x2, in0=t3, in1=t4, op=ALU.add), reads=[btmp[2], btmp[3]], writes=bP)


A_IN = 2120
IW_SCALE = float(8 ** -0.5 * 64 ** -0.5)


def stage_proj_A(S, C, T, h, pos_pt, w_in, invf, qT, kT, v, iqT, ikT, iw):
    NTL = T // 128
    win = S.sbuf("win", [128, 8, A_IN], BF16)
    b_win = S.bufs(8, "win")
    wv = w_in.rearrange("(c p) n -> p c n", p=128)
    for c in range(8):
        S.dma("pool", win[:, c, :], wv[:, c, :], writes=[b_win[c]])
    cos_t, b_cos, sin_t, b_sin = rotary_tables(S, NTL, pos_pt, invf)

    hs = [S.sbuf("pa_hs%d" % i, [128, D], F32) for i in range(2)]
    b_hs = S.bufs(2, "pa_hs")
    hb_l = [S.sbuf("pa_hb%d" % i, [128, D], BF16) for i in range(2)]
    b_hb_l = S.bufs(2, "pa_hb")
    hT_l = [S.sbuf("pa_hT%d" % i, [128, 8, 128], BF16) for i in range(2)]
    b_hT_l = S.bufs(2, "pa_hT")
    P_l = [S.sbuf("pa_P%d" % i, [128, A_IN], F32) for i in range(2)]
    b_P_l = [S.bufs(5, "pa_P%d_" % i) for i in range(2)]
    Pb_l = [S.sbuf("pa_Pb%d" % i, [128, A_IN], BF16) for i in range(2)]
    b_Pb_l = [S.bufs(4, "pa_Pb%d_" % i) for i in range(2)]
    iws = [S.sbuf("pa_iw%d" % i, [128, 8], F32) for i in range(2)]
    b_iws = S.bufs(2, "pa_iw")
    rt_l = [S.sbuf("pa_rt%d" % i, [128, 4, 20, 8], F32) for i in range(2)]
    b_rt_l = [S.bufs(4, "pa_rt%d_" % i) for i in range(2)]
    qTs = [S.sbuf("pa_qT%d" % i, [128, 8, 128], BF16) for i in range(2)]
    b_qTs = S.bufs(2, "pa_qT")
    kiTs = [S.sbuf("pa_kiT%d" % i, [128, 7, 128], BF16) for i in range(2)]
    b_kiTs = S.bufs(2, "pa_kiT")
    pj = [S.psum("pa_pj%d" % i, [128, 512], F32) for i in range(5)]
    b_pj = S.bufs(5, "pa_pj")
    tpA = S.psum("pa_tpA", [128, 1024], BF16)
    tpB = S.psum("pa_tpB", [128, 1024], BF16)
    b_tpA, b_tpB = S.bufs(2, "pa_tp")
    chunks = [(0, 512), (512, 1024), (1024, 1536), (1536, 2048), (2048, A_IN)]

    hv = h.rearrange("(t p) d -> t p d", p=128)
    qTv = qT.rearrange("(c p) t -> p c t", p=128)
    kTv = kT.rearrange("(c p) t -> p c t", p=128)
    iqTv = iqT.rearrange("(c p) t -> p c t", p=128)
    def _tile(t, hb, b_hb, hT, b_hT, P, b_P, Pb, b_Pbs, rt, b_rt):
        b_Pq, b_Pk, b_Pv, b_Pi = b_Pbs
        x, bx = hs[t % 2], b_hs[t % 2]
        S.dma("sp", x[:, :], hv[t], writes=[bx])
        S.op("act", lambda e, x=x: e.activation(out=hb[:, :], in_=x[:, :], func=AF.Copy), reads=[bx], writes=[b_hb])
        for c in range(8):
            S.op("pe", lambda e, c=c: e.transpose(out=tpA[:, c * 128:(c + 1) * 128], in_=hb[:, c * 128:(c + 1) * 128],
                                                  identity=C.ident[:]), reads=[b_hb, C.b_ident], writes=[b_tpA])
        S.op("dve", lambda e: e.tensor_copy(out=hT[:, :, :], in_=tpA[:, :].rearrange("p (c t) -> p c t", t=128)),
             reads=[b_tpA], writes=[b_hT])
        for i, (n0, n1) in enumerate(chunks):
            for c in range(8):
                S.op("pe", lambda e, i=i, c=c, n0=n0, n1=n1: e.matmul(pj[i][:, 0:n1 - n0], lhsT=hT[:, c, :], rhs=win[:, c, n0:n1],
                                                                      start=(c == 0), stop=(c == 7)),
                     reads=[b_hT, b_win[c]], writes=[b_pj[i]])
            S.op("act", lambda e, i=i, n0=n0, n1=n1: e.activation(out=P[:, n0:n1], in_=pj[i][:, 0:n1 - n0], func=AF.Copy),
                 reads=[b_pj[i]], writes=[b_P[i]])
        apply_rotary(S, P, b_P[0:3], 0, 20, cos_t, b_cos, sin_t, b_sin, t, rt, b_rt)
        apply_rotary(S, P, b_P[3:5], 24, 9, cos_t, b_cos, sin_t, b_sin, t, rt, b_rt)
        S.op("act", lambda e: e.activation(out=Pb[:, 0:1024], in_=P[:, 0:1024], func=AF.Copy, scale=0.125),
             reads=b_P[0:2], writes=[b_Pq])
        S.op("act", lambda e: e.activation(out=Pb[:, 1024:1536], in_=P[:, 1024:1536], func=AF.Copy),
             reads=[b_P[2]], writes=[b_Pk])
        S.op("act", lambda e: e.activation(out=Pb[:, 1536:2112], in_=P[:, 1536:2112], func=AF.Copy),
             reads=b_P[3:5], writes=[b_Pi])
        iwt, biw = iws[t % 2], b_iws[t % 2]
        S.op("act", lambda e, iwt=iwt: e.activation(out=iwt[:, :], in_=P[:, 2112:2120], func=AF.Copy, scale=IW_SCALE),
             reads=[b_P[4]], writes=[biw])
        S.dma("sp", iw[t * 128:(t + 1) * 128, :], iwt[:, :], reads=[biw])
        S.dma("sp", v[t * 128:(t + 1) * 128, :], Pb[:, 1280:1536], reads=[b_Pk])
        qs, bqs = qTs[t % 2], b_qTs[t % 2]
        ks, bks = kiTs[t % 2], b_kiTs[t % 2]
        for c in range(8):
            S.op("pe", lambda e, c=c: e.transpose(out=tpB[:, c * 128:(c + 1) * 128], in_=Pb[:, c * 128:(c + 1) * 128],
                                                  identity=C.ident[:]), reads=[b_Pq, C.b_ident], writes=[b_tpB])
        S.op("dve", lambda e, qs=qs: e.tensor_copy(out=qs[:, :, :], in_=tpB[:, :].rearrange("p (c t) -> p c t", t=128)),
             reads=[b_tpB], writes=[bqs])
        S.dma("sp", qTv[:, :, t * 128:(t + 1) * 128], qs[:, :, :], reads=[bqs])
        srcs = [(1024, 128, b_Pk), (1152, 128, b_Pk), (1536, 128, b_Pi), (1664, 128, b_Pi), (1792, 128, b_Pi),
                (1920, 128, b_Pi), (2048, 64, b_Pi)]
        for j, (c0, w, bsrc) in enumerate(srcs):
            S.op("pe", lambda e, j=j, c0=c0, w=w: e.transpose(out=tpA[0:w, j * 128:(j + 1) * 128], in_=Pb[:, c0:c0 + w],
                                                              identity=C.ident[:]), reads=[bsrc, C.b_ident], writes=[b_tpA])
        S.op("dve", lambda e, ks=ks: e.tensor_copy(out=ks[:, 0:6, :], in_=tpA[:, 0:768].rearrange("p (c t) -> p c t", t=128)),
             reads=[b_tpA], writes=[bks])
        S.op("dve", lambda e, ks=ks: e.tensor_copy(out=ks[0:64, 6, :], in_=tpA[0:64, 768:896]),
             reads=[b_tpA], writes=[bks])
        S.dma("sp", kTv[:, :, t * 128:(t + 1) * 128], ks[:, 0:2, :], reads=[bks])
        S.dma("sp", iqTv[:, :, t * 128:(t + 1) * 128], ks[:, 2:6, :], reads=[bks])
        S.dma("sp", ikT[:, t * 128:(t + 1) * 128], ks[0:64, 6, :], reads=[bks])

    for t in range(NTL):
        _tile(t, hb_l[t % 2], b_hb_l[t % 2], hT_l[t % 2], b_hT_l[t % 2], P_l[t % 2], b_P_l[t % 2], Pb_l[t % 2], b_Pb_l[t % 2], rt_l[t % 2], b_rt_l[t % 2])

TOPK = 256
NEG_MASK = -3.0e38
NEG_LO = -1.0e30
N_BISECT = 18
MASK_BIG = 32768.0


def stage_attn(S, C, NQ, kT, v, ikT, qT, iqT, iw, maskc, negI, o, dbg=None, solo=False):
    NKTT = NQ if solo else 2 * NQ
    SK = NKTT * 128
    nkt_of = (lambda j: j + 1) if solo else (lambda j: 2 * j + 2)
    kT_sb = S.sbuf("at_kT", [128, 2, SK], BF16)
    v_sb = S.sbuf("at_v", [128, NKTT, 4, 65], BF16)
    ik_sb = S.sbuf("at_ik", [64, SK], BF16)
    b_kT, b_v, b_ik = S.bufs(3, "at_res")
    for hh in range(2):
        S.dma("sp", kT_sb[hh * 64:(hh + 1) * 64, :, :], kT[2 * hh:2 * hh + 2, :, :].rearrange("g d s -> d g s"), writes=[b_kT])
    S.op("pool", lambda e: e.memset(v_sb[:, :, :, 64:65], 1.0), writes=[b_v])
    vv = v.rearrange("(k p) (g d) -> p k g d", p=128, d=64)
    KCH = 16
    for k0 in range(0, NKTT, KCH):
        k1 = min(NKTT, k0 + KCH)
        for g in range(4):
            S.dma("sp", v_sb[:, k0:k1, g, 0:64], vv[:, k0:k1, g, :], writes=[b_v])
    S.dma("sp", ik_sb[:, :], ikT[:, :], writes=[b_ik])
    mc = S.sbuf("at_mc", [128, 256], F32)
    nI = S.sbuf("at_nI", [128, 4, 128], BF16)
    b_mc, b_nI = S.bufs(2, "at_c")
    S.dma("sp", mc[:, :], maskc[:, :], writes=[b_mc])
    for r in range(4):
        S.dma("pool", nI[:, r, :], negI[:, :], writes=[b_nI])

    sc = S.sbuf("at_sc", [128, SK], F32)
    b_sc = S.buf("at_sc")
    junk = S.sbuf("at_junk", [128, SK], BF16)
    b_junk = S.buf("at_junk")
    mb = [S.sbuf("at_mb%d" % i, [128, SK], BF16) for i in range(2)]
    b_mb = S.bufs(2, "at_mb")
    NT_SB = 4
    NS_PS = 2
    t_sb = [S.sbuf("at_t%d" % i, [128, 512], F32) for i in range(NT_SB)]
    b_t = S.bufs(NT_SB, "at_t")
    NPT = 4
    NLP = 3
    PT = [S.sbuf("at_PT%d" % i, [128, 512], BF16) for i in range(NPT)]
    b_PT = S.bufs(NPT, "at_PT")
    q_sb = [S.sbuf("at_q%d" % i, [128, 2, 4, 128], BF16) for i in range(2)]
    b_q = S.bufs(2, "at_q")
    iq_sb = [S.sbuf("at_iq%d" % i, [64, 8, 128], BF16) for i in range(2)]
    b_iq = S.bufs(2, "at_iq")
    iw_sb = [S.sbuf("at_iw%d" % i, [128, 8], F32) for i in range(2)]
    b_iw = S.bufs(2, "at_iw")
    o_sb = [S.sbuf("at_o%d" % i, [128, 16, 64], BF16) for i in range(2)]
    b_o = S.bufs(2, "at_o")
    sm = S.sbuf("at_sm", [128, 16], F32)
    b_sm = S.bufs(8, "at_sm")
    rc = S.sbuf("at_rc", [128, 16], F32)
    b_rc = S.buf("at_rc")
    s_ps = [S.psum("at_sps%d" % i, [128, 512], F32) for i in range(NS_PS)]
    b_sps = S.bufs(NS_PS, "at_sps")
    KB = N_BISECT
    pow2 = S.sbuf("at_pow2", [128, KB + 1], F32)
    wtab = S.sbuf("at_wtab", [128, KB + 1], F32)
    b_pow2, b_wtab = S.bufs(2, "at_w")
    for k in range(KB + 1):
        S.op("pool", lambda e, k=k: e.memset(pow2[:, k:k + 1], float(2.0 ** -k)), writes=[b_pow2])
    l_ps = [S.psum("at_lps%d" % i, [128, 512], F32) for i in range(NLP)]
    b_lps = S.bufs(NLP, "at_lps")
    o_ps = [S.psum("at_ops%d" % i, [128, 7, 65], F32) for i in range(3)]
    b_ops = S.bufs(3, "at_ops")
    LO, HI, MID, CNT, GE, DD, MM = range(7)
    col = lambda i: sm[:, i:i + 1]

    n_s = 0
    n_l = 0
    n_pt = 0

    def load_q(j):
        qs, bq = q_sb[j % 2], b_q[j % 2]
        for hh in range(2):
            S.dma("sp", qs[hh * 64:(hh + 1) * 64, :, :, :].rearrange("d a r q -> d (a r) q"),
                  qT[8 * hh:8 * hh + 8, :, j * 128:(j + 1) * 128].rearrange("h d q -> d h q"), writes=[bq])
        S.dma("sp", iq_sb[j % 2][:, :, :], iqT[:, :, j * 128:(j + 1) * 128].rearrange("h d q -> d h q"), writes=[b_iq[j % 2]])
        S.dma("sp", iw_sb[j % 2][:, :], iw[j * 128:(j + 1) * 128, :], writes=[b_iw[j % 2]])

    def phase12(j):
        nonlocal n_s
        nk = nkt_of(j) * 128
        iqs, biq = iq_sb[j % 2], b_iq[j % 2]
        iws, biw = iw_sb[j % 2], b_iw[j % 2]
        for hd in range(8):
            for k0 in range(0, nk, 512):
                w = min(512, nk - k0)
                ps, bps = s_ps[n_s % NS_PS], b_sps[n_s % NS_PS]
                tt, bt = t_sb[n_s % NT_SB], b_t[n_s % NT_SB]
                n_s += 1
                S.op("pe", lambda e, ps=ps, hd=hd, k0=k0, w=w, iqs=iqs: e.matmul(
                    ps[:, 0:w], lhsT=iqs[:, hd, :], rhs=ik_sb[:, k0:k0 + w], start=True, stop=True),
                     reads=[biq, b_ik], writes=[bps])
                S.op("act", lambda e, ps=ps, tt=tt, w=w: e.activation(out=tt[:, 0:w], in_=ps[:, 0:w], func=AF.Relu),
                     reads=[bps], writes=[bt])
                if hd == 0:
                    S.op("dve", lambda e, tt=tt, k0=k0, w=w, iws=iws: e.tensor_scalar(
                        out=sc[:, k0:k0 + w], in0=tt[:, 0:w], scalar1=iws[:, 0:1], scalar2=None, op0=ALU.mult),
                         reads=[bt, biw], writes=[b_sc])
                else:
                    S.op("dve", lambda e, tt=tt, k0=k0, w=w, iws=iws, hd=hd: e.scalar_tensor_tensor(
                        out=sc[:, k0:k0 + w], in0=tt[:, 0:w], scalar=iws[:, hd:hd + 1], in1=sc[:, k0:k0 + w],
                        op0=ALU.mult, op1=ALU.add), reads=[bt, biw, b_sc], writes=[b_sc])
        S.op("dve", lambda e: e.tensor_reduce(out=col(MM), in_=sc[:, 0:nk], axis=AX.X, op=ALU.max, apply_absolute_value=True),
             reads=[b_sc], writes=[b_sm[MM]])
        S.op("dve", lambda e: e.tensor_scalar(out=col(HI), in0=col(MM), scalar1=1.0, scalar2=None, op0=ALU.add),
             reads=[b_sm[MM]], writes=[b_sm[HI]])
        S.op("dve", lambda e: e.tensor_scalar(out=wtab[:, :], in0=pow2[:, :], scalar1=col(HI), scalar2=None, op0=ALU.mult),
             reads=[b_sm[HI], b_pow2], writes=[b_wtab])
        if solo:
            S.op("dve", lambda e: e.tensor_tensor(out=sc[:, nk - 128:nk], in0=sc[:, nk - 128:nk], in1=mc[:, 128:256], op=ALU.add),
                 reads=[b_sc, b_mc], writes=[b_sc])
        else:
            S.op("dve", lambda e: e.tensor_tensor(out=sc[:, nk - 256:nk], in0=sc[:, nk - 256:nk], in1=mc[:, :], op=ALU.add),
                 reads=[b_sc, b_mc], writes=[b_sc])
        S.op("dve", lambda e: e.memset(col(MID), 0.0), writes=[b_sm[MID]])
        for it in range(KB):
            S.op("dve", lambda e: e.tensor_scalar(out=junk[:, 0:nk], in0=sc[:, 0:nk], scalar1=col(MID), scalar2=0.0,
                                                   op0=ALU.is_ge, op1=ALU.add, accum_out=col(CNT)),
                 reads=[b_sc, b_sm[MID]], writes=[b_junk, b_sm[CNT]])
            S.op("dve", lambda e, it=it: e.scalar_tensor_tensor(out=col(GE), in0=col(CNT), scalar=TOPK - 0.5, in1=wtab[:, it:it + 1],
                                                                op0=ALU.is_ge, op1=ALU.mult),
                 reads=[b_sm[CNT], b_wtab], writes=[b_sm[GE]])
            S.op("dve", lambda e, it=it: e.scalar_tensor_tensor(out=col(MID), in0=col(MID), scalar=wtab[:, it + 1:it + 2], in1=col(GE),
                                                                op0=ALU.subtract, op1=ALU.add),
                 reads=[b_sm[MID], b_sm[GE], b_wtab], writes=[b_sm[MID]])
        S.op("dve", lambda e: e.tensor_tensor(out=col(LO), in0=col(MID), in1=wtab[:, KB:KB + 1], op=ALU.subtract),
             reads=[b_sm[MID], b_wtab], writes=[b_sm[LO]])
        m, bm = mb[j % 2], b_mb[j % 2]
        S.op("dve", lambda e, m=m: e.tensor_scalar(out=m[:, 0:nk], in0=sc[:, 0:nk], scalar1=col(LO), scalar2=None, op0=ALU.is_lt),
             reads=[b_sc, b_sm[LO]], writes=[bm])
        if dbg is not None and j == dbg["j"]:
            S.dma("sp", dbg["sc"][:, 0:nk], sc[:, 0:nk], reads=[b_sc])
            S.dma("sp", dbg["mb"][:, 0:nk], m[:, 0:nk], reads=[bm])
            S.dma("sp", dbg["sm"][:, :], sm[:, :], reads=b_sm)

    def phase3(j):
        nonlocal n_l, n_pt
        NKT = nkt_of(j)
        qs, bq = q_sb[j % 2], b_q[j % 2]
        m, bm = mb[j % 2], b_mb[j % 2]
        DEPTH = 2
        pend = []

        def emit_pv(item):
            kt, g, pt, bpt = item
            for r in range(4):
                hd = 4 * g + r
                bank, slot = hd // 7, hd % 7
                S.op("pe", lambda e, pt=pt, r=r, kt=kt, g=g, bank=bank, slot=slot: e.matmul(
                    o_ps[bank][:, slot, :], lhsT=pt[:, r * 128:(r + 1) * 128], rhs=v_sb[:, kt, g, :],
                    start=(kt == 0 and slot == 0), stop=(kt == NKT - 1), skip_group_check=True),
                     reads=[bpt, b_v], writes=[b_ops[bank]])

        for kt in range(NKT):
            for g in range(4):
                hh, a = g // 2, g % 2
                lp, blp = l_ps[n_l % NLP], b_lps[n_l % NLP]
                n_l += 1
                pt, bpt = PT[n_pt % NPT], b_PT[n_pt % NPT]
                n_pt += 1
                S.op("pe", lambda e, lp=lp, hh=hh, a=a, kt=kt, qs=qs: e.matmul(
                    lp[:, :], lhsT=kT_sb[hh * 64:(hh + 1) * 64, a, kt * 128:(kt + 1) * 128],
                    rhs=qs[hh * 64:(hh + 1) * 64, a, :, :].rearrange("d r q -> d (r q)"), start=True, stop=False),
                     reads=[b_kT, bq], writes=[blp])
                S.op("pe", lambda e, lp=lp, kt=kt, m=m: e.matmul(
                    lp[:, :], lhsT=m[:, kt * 128:(kt + 1) * 128], rhs=nI[:, :, :].rearrange("p r q -> p (r q)"),
                    start=False, stop=True), reads=[bm, b_nI], writes=[blp])
                S.op("act", lambda e, lp=lp, pt=pt: e.activation(out=pt[:, :], in_=lp[:, :], func=AF.Exp),
                     reads=[blp], writes=[bpt])
                pend.append((kt, g, pt, bpt))
                if len(pend) > DEPTH:
                    emit_pv(pend.pop(0))
        while pend:
            emit_pv(pend.pop(0))
        ob, bo = o_sb[j % 2], b_o[j % 2]
        for bank in range(3):
            nh = min(7, 16 - 7 * bank)
            S.op("dve", lambda e, bank=bank, nh=nh: e.reciprocal(out=rc[:, 0:nh], in_=o_ps[bank][:, 0:nh, 64]),
                 reads=[b_ops[bank]], writes=[b_rc])
            S.op("dve", lambda e, bank=bank, nh=nh, ob=ob: e.tensor_tensor(
                out=ob[:, 7 * bank:7 * bank + nh, :], in0=o_ps[bank][:, 0:nh, 0:64],
                in1=rc[:, 0:nh].unsqueeze(2).to_broadcast([128, nh, 64]), op=ALU.mult),
                 reads=[b_ops[bank], b_rc], writes=[bo])
        S.dma("sp", o[j * 128:(j + 1) * 128, :], ob[:, :, :].rearrange("p h d -> p (h d)"), reads=[bo])

    load_q(0)
    phase12(0)
    for j in range(NQ):
        if j + 1 < NQ:
            load_q(j + 1)
            phase12(j + 1)
        phase3(j)


def stage_wo_ln(S, C, T, o, h, w_o, g, b, h1, gate=None):
    wo = S.sbuf("wo", [128, 8, D], BF16)
    b_wo = S.bufs(8, "wo")
    wv = w_o.rearrange("(c p) n -> p c n", p=128)
    for c in range(8):
        S.dma("pool", wo[:, c, :], wv[:, c, :], writes=[b_wo[c]])
    g_t, b_g = load_bcast(S, "wo_g", g, D)
    bt_t, b_bt = load_bcast(S, "wo_b", b, D)
    ob = [S.sbuf("wo_ob%d" % i, [128, D], BF16) for i in range(2)]
    b_ob = S.bufs(2, "wo_ob")
    hs = [S.sbuf("wo_hs%d" % i, [128, D], F32) for i in range(2)]
    b_hs = S.bufs(2, "wo_hs")
    oTs = [S.sbuf("wo_oT%d" % i, [128, 8, 128], BF16) for i in range(2)]
    b_oTs = S.bufs(2, "wo_oT")
    y_sb = [S.sbuf("wo_y%d" % i, [128, D], F32) for i in range(2)]
    b_y = S.bufs(2, "wo_y")
    tps = [S.psum("wo_tp%d" % i, [128, 1024], BF16) for i in range(2)]
    b_tps = S.bufs(2, "wo_tp")
    mx = [S.psum("wo_mx%d" % i, [128, 512], F32) for i in range(4)]
    b_mx = S.bufs(4, "wo_mx")
    scrs = [ln_scratch(S, "wo%d" % i) for i in range(2)]
    ov = o.rearrange("(t p) d -> t p d", p=128) if gate is None else None
    if gate is not None:
        gate_a = [S.sbuf("wo_ga%d" % i, [128, D], F32) for i in range(2)]
        gate_b = [S.sbuf("wo_gb%d" % i, [128, D], F32) for i in range(2)]
        b_ga = S.bufs(2, "wo_ga")
        b_gb = S.bufs(2, "wo_gb")
    hv = h.rearrange("(t p) d -> t p d", p=128)
    h1v = h1.rearrange("(t p) d -> t p d", p=128)
    for t in range(T // 128):
        x, bx = ob[t % 2], b_ob[t % 2]
        hh, bh = hs[t % 2], b_hs[t % 2]
        if gate is None:
            S.dma("sp", x[:, :], ov[t], writes=[bx])
        else:
            ga, gb_ = gate_a[t % 2], gate_b[t % 2]
            S.dma("sp", ga[:, :], gate[0].rearrange("(t p) d -> t p d", p=128)[t], writes=[b_ga[t % 2]])
            S.dma("sp", gb_[:, :], gate[1].rearrange("(t p) d -> t p d", p=128)[t], writes=[b_gb[t % 2]])
            S.op("pool", lambda e, x=x, ga=ga, gb_=gb_: e.tensor_tensor(out=x[:, :], in0=ga[:, :], in1=gb_[:, :], op=ALU.mult),
                 reads=[b_ga[t % 2], b_gb[t % 2]], writes=[bx])
        S.dma("sp", hh[:, :], hv[t], writes=[bh])
        tp, b_tp = tps[t % 2], b_tps[t % 2]
        oT, b_oT = oTs[t % 2], b_oTs[t % 2]
        scr = scrs[t % 2]
        for c in range(8):
            S.op("pe", lambda e, c=c, x=x, tp=tp: e.transpose(out=tp[:, c * 128:(c + 1) * 128], in_=x[:, c * 128:(c + 1) * 128],
                                                       identity=C.ident[:]), reads=[bx, C.b_ident], writes=[b_tp])
        S.op("act", lambda e, oT=oT, tp=tp: e.activation(out=oT[:, :, :], in_=tp[:, :].rearrange("p (c t) -> p c t", t=128), func=AF.Copy),
             reads=[b_tp], writes=[b_oT])
        for nh in range(2):
            ps, bps = mx[(2 * t + nh) % 4], b_mx[(2 * t + nh) % 4]
            for c in range(8):
                S.op("pe", lambda e, ps=ps, c=c, nh=nh, oT=oT: e.matmul(ps[:, :], lhsT=oT[:, c, :], rhs=wo[:, c, nh * 512:(nh + 1) * 512],
                                                                  start=(c == 0), stop=(c == 7)),
                     reads=[b_oT, b_wo[c]], writes=[bps])
            S.op("dve", lambda e, ps=ps, hh=hh, nh=nh: e.scalar_tensor_tensor(
                out=hh[:, nh * 512:(nh + 1) * 512], in0=hh[:, nh * 512:(nh + 1) * 512], scalar=ALPHA, in1=ps[:, :],
                op0=ALU.mult, op1=ALU.add), reads=[bps, bh], writes=[bh])
        y, by = y_sb[t % 2], b_y[t % 2]
        layer_norm_tile(S, hh, bh, y, by, g_t, b_g, bt_t, b_bt, scr)
        S.dma("sp", h1v[t], y[:, :], reads=[by])


B_IN = 3088
GATE_TAU = 16.0


def stage_proj_B(S, C, T, h, w_in, w_a2, b_a, cmU, cmW, qgT, kgT, kh, vb, dec, sr):
    NTL = T // 128
    win = S.sbuf("pb_win", [128, 8, B_IN], BF16)
    b_win = S.bufs(8, "pb_win")
    wv = w_in.rearrange("(c p) n -> p c n", p=128)
    for c in range(8):
        S.dma("pool", win[:, c, :], wv[:, c, :], writes=[b_win[c]])
    wa2 = S.sbuf("pb_wa2", [16, 512], BF16)
    U = S.sbuf("pb_U", [128, 128], BF16)
    W = S.sbuf("pb_W", [128, 128], BF16)
    b_wa2, b_U, b_W = S.bufs(3, "pb_c")
    S.dma("pool", wa2[:, :], w_a2[:, :], writes=[b_wa2])
    S.dma("pool", U[:, :], cmU[:, :], writes=[b_U])
    S.dma("pool", W[:, :], cmW[:, :], writes=[b_W])
    ba_t, b_ba = load_bcast(S, "pb_ba", b_a, 512)

    hs = [S.sbuf("pb_hs%d" % i, [128, D], F32) for i in range(2)]
    b_hs = S.bufs(2, "pb_hs")
    def _mk(i):
        return dict(hb=S.sbuf("pb_hb%d" % i, [128, D], BF16), b_hb=S.buf("pb_hb"), hT=S.sbuf("pb_hT%d" % i, [128, 8, 128], BF16), b_hT=S.buf("pb_hT"),
                    alT=S.sbuf("pb_alT%d" % i, [16, 128], BF16), b_alT=S.buf("pb_alT"), gg=S.sbuf("pb_g%d" % i, [128, 512], F32), b_gg=S.buf("pb_g"),
                    ghi=S.sbuf("pb_ghi%d" % i, [128, 512], BF16), glo=S.sbuf("pb_glo%d" % i, [128, 512], BF16), b_ghi=S.buf("pb_ghi"), b_glo=S.buf("pb_glo"),
                    EbT=S.sbuf("pb_EbT%d" % i, [128, 4, 128], F32), EnbT=S.sbuf("pb_EnbT%d" % i, [128, 4, 128], F32), Ebl=S.sbuf("pb_Ebl%d" % i, [128, 512], F32),
                    b_EbT=S.buf("pb_EbT"), b_EnbT=S.buf("pb_EnbT"), b_Ebl=S.buf("pb_Ebl"))
    sets = [_mk(0), _mk(1)]
    qg_s = [S.sbuf("pb_qg%d" % i, [128, 4, 128], BF16) for i in range(2)]
    kg_s = [S.sbuf("pb_kg%d" % i, [128, 4, 128], BF16) for i in range(2)]
    kh_s = [S.sbuf("pb_kh%d" % i, [128, 512], BF16) for i in range(2)]
    vb_s = [S.sbuf("pb_vb%d" % i, [128, 1024], BF16) for i in range(2)]
    sr_s = [S.sbuf("pb_sr%d" % i, [128, 1024], F32) for i in range(2)]
    dc_s = [S.sbuf("pb_dc%d" % i, [128, 4, 2], F32) for i in range(2)]
    b_qg, b_kg, b_kh, b_vb, b_sr, b_dc = (S.bufs(2, "pb_o%d" % i) for i in range(6))
    tpT = S.psum("pb_tp", [128, 1024], BF16)
    b_tpT = S.buf("pb_tp")
    pk = [S.psum("pb_pk%d" % i, [128, 512], F32) for i in range(7)]
    b_pk = S.bufs(7, "pb_pk")
    QSCALE = float(128 ** -0.5)

    hv = h.rearrange("(t p) d -> t p d", p=128)
    def _tile(t, hb, b_hb, hT, b_hT, alT, b_alT, gg, b_gg, ghi, glo, b_ghi, b_glo, EbT, EnbT, Ebl, b_EbT, b_EnbT, b_Ebl):
        x, bx = hs[t % 2], b_hs[t % 2]
        i2 = t % 2
        S.dma("sp", x[:, :], hv[t], writes=[bx])
        S.op("act", lambda e, x=x: e.activation(out=hb[:, :], in_=x[:, :], func=AF.Copy), reads=[bx], writes=[b_hb])
        for c in range(8):
            S.op("pe", lambda e, c=c: e.transpose(out=tpT[:, c * 128:(c + 1) * 128], in_=hb[:, c * 128:(c + 1) * 128],
                                                  identity=C.ident[:]), reads=[b_hb, C.b_ident], writes=[b_tpT])
        S.op("dve", lambda e: e.tensor_copy(out=hT[:, :, :], in_=tpT[:, :].rearrange("p (c t) -> p c t", t=128)),
             reads=[b_tpT], writes=[b_hT])

        def tok_mm(bank, n0, n1):
            for c in range(8):
                S.op("pe", lambda e, c=c: e.matmul(pk[bank][:, 0:n1 - n0], lhsT=hT[:, c, :], rhs=win[:, c, n0:n1],
                                                   start=(c == 0), stop=(c == 7)), reads=[b_hT, b_win[c]], writes=[b_pk[bank]])

        def feat_mm(bank, slot, n0, m):
            for c in range(8):
                S.op("pe", lambda e, c=c: e.matmul(pk[bank][0:m, slot * 128:(slot + 1) * 128], lhsT=win[:, c, n0:n0 + m],
                                                   rhs=hT[:, c, :], start=(c == 0 and slot == 0), stop=(c == 7),
                                                   skip_group_check=True), reads=[b_hT, b_win[c]], writes=[b_pk[bank]])

        feat_mm(6, 0, 3072, 16)
        S.op("act", lambda e: e.activation(out=alT[:, :], in_=pk[6][0:16, 0:128], func=AF.Copy), reads=[b_pk[6]], writes=[b_alT])
        S.op("pe", lambda e: e.matmul(pk[5][:, :], lhsT=alT[:, :], rhs=wa2[:, :], start=True, stop=True),
             reads=[b_alT, b_wa2], writes=[b_pk[5]])
        S.op("dve", lambda e: e.tensor_tensor(out=gg[:, :], in0=pk[5][:, :], in1=ba_t[:, :], op=ALU.add),
             reads=[b_pk[5], b_ba], writes=[b_gg])
        S.op("act", lambda e: e.activation(out=gg[:, :], in_=gg[:, :], func=AF.Exp, scale=-1.0), reads=[b_gg], writes=[b_gg])
        S.op("dve", lambda e: e.tensor_scalar(out=gg[:, :], in0=gg[:, :], scalar1=1.0, scalar2=None, op0=ALU.add),
             reads=[b_gg], writes=[b_gg])
        S.op("act", lambda e: e.activation(out=gg[:, :], in_=gg[:, :], func=AF.Ln), reads=[b_gg], writes=[b_gg])
        S.op("dve", lambda e: e.tensor_scalar(out=gg[:, :], in0=gg[:, :], scalar1=-1.0 / GATE_TAU, scalar2=None, op0=ALU.mult),
             reads=[b_gg], writes=[b_gg])
        S.op("dve", lambda e: e.tensor_copy(out=ghi[:, :], in_=gg[:, :]), reads=[b_gg], writes=[b_ghi])
        S.op("dve", lambda e: e.tensor_tensor(out=glo[:, :], in0=gg[:, :], in1=ghi[:, :], op=ALU.subtract),
             reads=[b_gg, b_ghi], writes=[b_glo])
        for hd in range(4):
            for part, (gs, bgs) in enumerate(((ghi, b_ghi), (glo, b_glo))):
                S.op("pe", lambda e, hd=hd, gs=gs, part=part: e.matmul(
                    pk[4][:, hd * 128:(hd + 1) * 128], lhsT=gs[:, hd * 128:(hd + 1) * 128], rhs=U[:, :],
                    start=(hd == 0 and part == 0), stop=(part == 1), skip_group_check=True),
                     reads=[bgs, b_U], writes=[b_pk[4]])
        for part, (gs, bgs) in enumerate(((ghi, b_ghi), (glo, b_glo))):
            S.op("pe", lambda e, gs=gs, part=part: e.matmul(pk[5][:, :], lhsT=W[:, :], rhs=gs[:, :], start=(part == 0), stop=(part == 1)),
                 reads=[bgs, b_W], writes=[b_pk[5]])
        S.op("act", lambda e: e.activation(out=EbT[:, :, :], in_=pk[4][:, :].rearrange("p (h i) -> p h i", i=128), func=AF.Exp),
             reads=[b_pk[4]], writes=[b_EbT])
        S.op("act", lambda e: e.activation(out=EnbT[:, :, :], in_=pk[4][:, :].rearrange("p (h i) -> p h i", i=128), func=AF.Exp, scale=-1.0),
             reads=[b_pk[4]], writes=[b_EnbT])
        S.op("act", lambda e: e.activation(out=Ebl[:, :], in_=pk[5][:, :], func=AF.Exp), reads=[b_pk[5]], writes=[b_Ebl])
        dcs, bdc = dc_s[i2], b_dc[i2]
        S.op("pool", lambda e, dcs=dcs: e.tensor_copy(out=dcs[:, :, :], in_=EbT[:, :, :].rearrange("p h (c j) -> p h c j", j=64)[:, :, :, 63]),
             reads=[b_EbT], writes=[bdc])
        S.dma("sp", dec[:, :, 2 * t:2 * t + 2].rearrange("h d c -> d h c"), dcs[:, :, :], reads=[bdc])
        for hd in range(4):
            feat_mm(6, hd, hd * 128, 128)
        qgs, bqg = qg_s[i2], b_qg[i2]
        S.op("dve", lambda e, qgs=qgs: e.scalar_tensor_tensor(out=qgs[:, :, :], in0=pk[6][:, :].rearrange("p (h i) -> p h i", i=128),
                                                              scalar=QSCALE, in1=EbT[:, :, :], op0=ALU.mult, op1=ALU.mult),
             reads=[b_pk[6], b_EbT], writes=[bqg])
        S.dma("sp", qgT[:, :, t * 128:(t + 1) * 128].rearrange("h d i -> d h i"), qgs[:, :, :], reads=[bqg])
        for hd in range(4):
            feat_mm(3, hd, 512 + hd * 128, 128)
        kgs, bkg = kg_s[i2], b_kg[i2]
        S.op("dve", lambda e, kgs=kgs: e.tensor_tensor(out=kgs[:, :, :], in0=pk[3][:, :].rearrange("p (h i) -> p h i", i=128),
                                                       in1=EnbT[:, :, :], op=ALU.mult), reads=[b_pk[3], b_EnbT], writes=[bkg])
        S.dma("sp", kgT[:, :, t * 128:(t + 1) * 128].rearrange("h d i -> d h i"), kgs[:, :, :], reads=[bkg])
        tok_mm(2, 512, 1024)
        khs, bkh = kh_s[i2], b_kh[i2]
        S.op("dve", lambda e, khs=khs: e.tensor_tensor(out=khs[:, :], in0=pk[2][:, :], in1=Ebl[:, :], op=ALU.mult),
             reads=[b_pk[2], b_Ebl], writes=[bkh])
        S.dma("sp", kh[t * 128:(t + 1) * 128, :], khs[:, :], reads=[bkh])
        vbs, bvb = vb_s[i2], b_vb[i2]
        for half in range(2):
            tok_mm(half, 1024 + half * 512, 1536 + half * 512)
            S.op("act", lambda e, half=half, vbs=vbs: e.activation(out=vbs[:, half * 512:(half + 1) * 512], in_=pk[half][:, :], func=AF.Copy),
                 reads=[b_pk[half]], writes=[bvb])
        S.dma("sp", vb[t * 128:(t + 1) * 128, :], vbs[:, :], reads=[bvb])
        srs, bsr = sr_s[i2], b_sr[i2]
        for half in range(2):
            tok_mm(half, 2048 + half * 512, 2560 + half * 512)
            S.op("act", lambda e, half=half, srs=srs: e.activation(out=srs[:, half * 512:(half + 1) * 512], in_=pk[half][:, :], func=AF.Silu),
                 reads=[b_pk[half]], writes=[bsr])
        S.dma("sp", sr[t * 128:(t + 1) * 128, :], srs[:, :], reads=[bsr])

    for t in range(NTL):
        _tile(t, **sets[t % 2])

RMS_EPS = 1e-6
GLA_CB = 16


def stage_gla(S, C, SEQ, NH, acc, dec_ap, g_norm, tri, CB=GLA_CB):
    NBLK = SEQ // (64 * CB)
    NCH = SEQ // 64
    tri_f = S.sbuf("gl_trif", [64, 64], F32)
    b_tri = S.buf("gl_tri")
    S.dma("sp", tri_f[:, :], tri[:, :], writes=[b_tri])
    gn = S.sbuf("gl_gn", [64, 256], F32)
    b_gn = S.buf("gl_gn")
    S.dma("sp", gn[:, :], g_norm.partition_broadcast(64), writes=[b_gn])
    dec_sb = S.sbuf("gl_dec", [128, NH, NCH], F32)
    b_dec = S.buf("gl_dec")
    for hd in range(NH):
        S.dma("sp", dec_sb[:, hd, :], dec_ap(hd), writes=[b_dec])
    st = S.sbuf("gl_st", [128, NH, 256], F32)
    b_st = S.bufs(NH, "gl_st")
    stb = [S.sbuf("gl_stb%d" % i, [128, NH, 256], BF16) for i in range(2)]
    b_stb = [S.bufs(NH, "gl_stb%d_" % i) for i in range(2)]
    S.op("dve", lambda e: e.memset(st[:, :, :], 0.0), writes=b_st)
    S.op("dve", lambda e: e.memset(stb[0][:, :, :], 0.0), writes=b_stb[0])
    qg_sb = [[S.sbuf("gl_qg%d_%d" % (i, hd), [128, 64 * CB], BF16) for hd in range(NH)] for i in range(2)]
    kg_sb = [[S.sbuf("gl_kg%d_%d" % (i, hd), [128, 64 * CB], BF16) for hd in range(NH)] for i in range(2)]
    kh_sb = [[S.sbuf("gl_kh%d_%d" % (i, hd), [64, CB, 128], BF16) for hd in range(NH)] for i in range(2)]
    v_sb = [[S.sbuf("gl_v%d_%d" % (i, hd), [64, CB, 256], BF16) for hd in range(NH)] for i in range(2)]
    b_in = [[S.bufs(4, "gl_in%d_%d_" % (i, hd)) for hd in range(NH)] for i in range(2)]
    o_st = [[S.sbuf("gl_ost%d_%d" % (i, hd), [64, CB, 256], F32) for hd in range(NH)] for i in range(2)]
    b_ost = [[S.buf("gl_ost%d_%d" % (i, hd)) for hd in range(NH)] for i in range(2)]
    ss = [[S.sbuf("gl_ss%d_%d" % (i, hd), [64, CB], F32) for hd in range(NH)] for i in range(2)]
    b_ss = [[S.buf("gl_ss%d_%d" % (i, hd)) for hd in range(NH)] for i in range(2)]
    junk = S.sbuf("gl_junk", [64, 256], F32)
    b_junk = S.buf("gl_junk")
    A_sb = [S.sbuf("gl_A%d" % i, [64, 64], BF16) for i in range(4)]
    b_A = S.bufs(4, "gl_A")
    a_bank = S.psum("gl_aps", [64, 512], F32)
    a_ps = [a_bank[:, i * 64:(i + 1) * 64] for i in range(4)]
    b_aps = S.bufs(4, "gl_aps")
    o_bank = [S.psum("gl_ops%d" % i, [64, 512], F32) for i in range(4)]
    o_ps = [o_bank[i][:, 0:256] for i in range(4)]
    b_ops = S.bufs(4, "gl_ops")
    s_bank = [S.psum("gl_sps%d" % i, [128, 512], F32) for i in range(2)]
    s_ps = [s_bank[i // 2][:, (i % 2) * 256:(i % 2) * 256 + 256] for i in range(4)]
    b_sps = S.bufs(4, "gl_sps")
    n = 0
    for blk in range(NBLK):
        i2 = blk % 2
        for hd in range(NH):
            bi = b_in[i2][hd]
            S.dma("sp", qg_sb[i2][hd][:, :], acc("qg", hd, blk), writes=[bi[0]])
            S.dma("sp", kg_sb[i2][hd][:, :], acc("kg", hd, blk), writes=[bi[1]])
            S.dma("sp", kh_sb[i2][hd][:, :, :], acc("kh", hd, blk), writes=[bi[2]])
            S.dma("sp", v_sb[i2][hd][:, :, :], acc("v", hd, blk), writes=[bi[3]])
        for cc in range(CB):
            c = blk * CB + cc
            cur, nxt = c % 2, (c + 1) % 2
            for hd in range(NH):
                bi = b_in[i2][hd]
                qg = qg_sb[i2][hd][:, cc * 64:(cc + 1) * 64]
                kg = kg_sb[i2][hd][:, cc * 64:(cc + 1) * 64]
                khc = kh_sb[i2][hd][:, cc, :]
                vc = v_sb[i2][hd][:, cc, :]
                aps, baps = a_ps[n % 4], b_aps[n % 4]
                ops, bops = o_ps[n % 4], b_ops[n % 4]
                sps, bsps = s_ps[n % 4], b_sps[n % 4]
                A, bA = A_sb[n % 4], b_A[n % 4]
                n += 1
                S.op("pe", lambda e, aps=aps, kg=kg, qg=qg: e.matmul(aps, lhsT=kg, rhs=qg, start=True, stop=True, skip_group_check=True),
                     reads=[bi[0], bi[1]], writes=[baps])
                S.op("dve", lambda e, aps=aps, A=A: e.tensor_tensor(out=A[:, :], in0=aps, in1=tri_f[:, :], op=ALU.mult),
                     reads=[baps, b_tri], writes=[bA])
                S.op("pe", lambda e, ops=ops, A=A, vc=vc: e.matmul(ops, lhsT=A[:, :], rhs=vc, start=True, stop=False),
                     reads=[bA, bi[3]], writes=[bops])
                S.op("pe", lambda e, ops=ops, qg=qg, cur=cur, hd=hd: e.matmul(ops, lhsT=qg, rhs=stb[cur][:, hd, :],
                                                                             start=False, stop=True),
                     reads=[bi[0], b_stb[cur][hd]], writes=[bops])
                S.op("pe", lambda e, sps=sps, khc=khc, vc=vc: e.matmul(sps, lhsT=khc, rhs=vc, start=True, stop=True, skip_group_check=True),
                     reads=[bi[2], bi[3]], writes=[bsps])
                S.op("dve", lambda e, sps=sps, hd=hd, c=c: e.scalar_tensor_tensor(
                    out=st[:, hd, :], in0=st[:, hd, :], scalar=dec_sb[:, hd, c:c + 1], in1=sps,
                    op0=ALU.mult, op1=ALU.add), reads=[bsps, b_st[hd], b_dec], writes=[b_st[hd]])
                S.op("act", lambda e, hd=hd, nxt=nxt: e.activation(out=stb[nxt][:, hd, :], in_=st[:, hd, :], func=AF.Copy),
                     reads=[b_st[hd]], writes=[b_stb[nxt][hd]])
                ssc = ss[i2][hd][:, cc:cc + 1]
                ostc = o_st[i2][hd][:, cc, :]
                S.op("act", lambda e, ops=ops, ssc=ssc: e.activation(out=junk[:, :], in_=ops, func=AF.Square, accum_out=ssc),
                     reads=[bops], writes=[b_junk, b_ss[i2][hd]])
                S.op("act", lambda e, ops=ops, ostc=ostc: e.activation(out=ostc, in_=ops, func=AF.Copy),
                     reads=[bops], writes=[b_ost[i2][hd]])
        for hd in range(NH):
            s_, bs_ = ss[i2][hd], b_ss[i2][hd]
            S.op("dve", lambda e, s_=s_: e.tensor_scalar(out=s_[:, :], in0=s_[:, :], scalar1=1.0 / 256, scalar2=RMS_EPS,
                                                         op0=ALU.mult, op1=ALU.add), reads=[bs_], writes=[bs_])
            S.op("act", lambda e, s_=s_: e.activation(out=s_[:, :], in_=s_[:, :], func=AF.Sqrt), reads=[bs_], writes=[bs_])
            S.op("dve", lambda e, s_=s_: e.reciprocal(out=s_[:, :], in_=s_[:, :]), reads=[bs_], writes=[bs_])
            ot, bo = o_st[i2][hd], b_ost[i2][hd]
            S.op("dve", lambda e, ot=ot, s_=s_: e.tensor_tensor(out=ot[:, :, :], in0=ot[:, :, :],
                                                               in1=s_[:, :].unsqueeze(2).to_broadcast([64, CB, 256]), op=ALU.mult),
                 reads=[bo, bs_], writes=[bo])
            S.op("pool", lambda e, ot=ot: e.tensor_tensor(out=ot[:, :, :], in0=ot[:, :, :],
                                                         in1=gn[:, :].unsqueeze(1).to_broadcast([64, CB, 256]), op=ALU.mult),
                 reads=[bo, b_gn], writes=[bo])
            S.dma("sp", acc("on", hd, blk), ot[:, :, :], reads=[bo])


_NP2DT = {np.dtype("float32"): F32, np.dtype("int32"): I32}
try:
    import ml_dtypes
    _NP2DT[np.dtype(ml_dtypes.bfloat16)] = BF16
    NPBF16 = ml_dtypes.bfloat16
except Exception:
    NPBF16 = None

_PROG_CACHE = {}


def launch(key, prog, in_list, outs):
    sig = (key, tuple((k, v.shape, str(v.dtype)) for k, v in in_list[0].items()), tuple((k, tuple(sh), str(dt)) for k, (sh, dt) in outs.items()))
    if sig not in _PROG_CACHE:
        nc = bass.Bass("TRN2", target_bir_lowering=False)
        aps = {}
        for k, v in in_list[0].items():
            aps[k] = nc.dram_tensor(k, list(v.shape), _NP2DT[v.dtype], kind="ExternalInput").ap()
        for k, (shape, dt) in outs.items():
            aps[k] = nc.dram_tensor(k, list(shape), dt, kind="ExternalOutput").ap()
        S = Sched(nc)
        prog(S, aps)
        S.emit()
        _PROG_CACHE[sig] = nc
    nc = _PROG_CACHE[sig]
    res = run_bass_kernel_spmd(nc, in_list, core_ids=list(range(len(in_list))))
    return res.results


def _consts():
    ident = np.eye(128, dtype=np.float32)
    tri = np.where(np.arange(128)[None, :] <= np.arange(128)[:, None], 0.0, NEG_MASK).astype(np.float32)
    allm = np.full((128, 128), NEG_MASK, np.float32)
    none = np.zeros((128, 128), np.float32)
    maskc = [np.concatenate([tri, allm], 1), np.concatenate([none, tri], 1)]
    negI = (-MASK_BIG * np.eye(128)).astype(np.float32)
    invf = (500000.0 ** (-np.arange(0, 16, 2, dtype=np.float32) / 16)).astype(np.float32)
    j = np.arange(128)[:, None]
    i = np.arange(128)[None, :]
    same = (j // 64) == (i // 64)
    cmU = (same & (j <= i)).astype(np.float32)
    cmW = (same & (j > i)).astype(np.float32)
    tri64 = (np.arange(64)[:, None] <= np.arange(64)[None, :]).astype(np.float32)
    return dict(ident=ident, maskc=maskc, negI=negI, invf=invf, cmU=cmU, cmW=cmW, tri64=tri64)


def kernel_unfused(x, positions, a_w_in, a_w_o, b_w_in, b_w_a2, b_b_a, b_g_norm, b_w_o,
           ln_mix_g, ln_mix_b, mlp_w_up, mlp_w_down, ln_mlp_g, ln_mlp_b):
    x = np.asarray(x, np.float32)
    B, SEQ, _ = x.shape
    NC = 2 * B
    T = SEQ // 2
    NTL = T // 128
    NQ = NTL
    cst = _consts()
    f32 = lambda a: np.ascontiguousarray(np.asarray(a, np.float32))
    tok = []
    for c in range(NC):
        r = c % 2
        tiles = np.arange(r, SEQ // 128, 2)
        tok.append((tiles[:, None] * 128 + np.arange(128)[None, :]).reshape(-1))
    h = [np.ascontiguousarray(x[c // 2][tok[c]]) for c in range(NC)]
    pos_pt = [np.ascontiguousarray(np.asarray(positions)[c // 2][tok[c]].astype(np.int32).reshape(NTL, 128).T) for c in range(NC)]
    depth = ln_mix_g.shape[0]
    for i in range(depth):
        j = i // 2
        if i % 2 == 0:
            w_in = f32(a_w_in[j])
            ins = [dict(h=h[c], pos=pos_pt[c], w=w_in, invf=cst["invf"], ident=cst["ident"]) for c in range(NC)]
            outs = {"qT": ([1024, T], BF16), "kT": ([256, T], BF16), "v": ([T, 256], BF16), "iqT": ([512, T], BF16),
                    "ikT": ([64, T], BF16), "iw": ([T, 8], F32)}

            def prog(S, a):
                C = Consts(S, a["ident"])
                stage_proj_A(S, C, T, a["h"], a["pos"], a["w"], a["invf"], a["qT"], a["kT"], a["v"], a["iqT"], a["ikT"], a["iw"])
            pr = launch("projA", prog, ins, outs)
            ins = []
            for c in range(NC):
                b0 = (c // 2) * 2
                kT = np.empty((256, SEQ), NPBF16)
                ikT = np.empty((64, SEQ), NPBF16)
                vf = np.empty((SEQ, 256), NPBF16)
                for r in range(2):
                    kT[:, tok[b0 + r]] = pr[b0 + r]["kT"]
                    ikT[:, tok[b0 + r]] = pr[b0 + r]["ikT"]
                    vf[tok[b0 + r]] = pr[b0 + r]["v"]
                ins.append(dict(ident=cst["ident"], kT=kT, v=vf, ikT=ikT, qT=pr[c]["qT"], iqT=pr[c]["iqT"], iw=pr[c]["iw"],
                                maskc=cst["maskc"][c % 2], negI=cst["negI"]))

            def prog(S, a):
                C = Consts(S, a["ident"])
                stage_attn(S, C, NQ, a["kT"].rearrange("(g d) s -> g d s", d=64), a["v"], a["ikT"],
                           a["qT"].rearrange("(h d) t -> h d t", d=64), a["iqT"].rearrange("(h d) t -> h d t", d=64),
                           a["iw"], a["maskc"], a["negI"], a["o"])
            ar = launch("attn", prog, ins, {"o": ([T, 1024], BF16)})
            w_o = f32(a_w_o[j])
            ins = [dict(ident=cst["ident"], o=ar[c]["o"], h=h[c], w=w_o, g=f32(ln_mix_g[i]), b=f32(ln_mix_b[i])) for c in range(NC)]

            def prog(S, a):
                C = Consts(S, a["ident"])
                stage_wo_ln(S, C, T, a["o"], a["h"], a["w"], a["g"], a["b"], a["h1"])
            wr = launch("wo", prog, ins, {"h1": ([T, D], F32)})
        else:
            ins = [dict(h=h[c], w=f32(b_w_in[j]), wa2=f32(b_w_a2[j]), ba=f32(b_b_a[j]), U=cst["cmU"], W=cst["cmW"], ident=cst["ident"])
                   for c in range(NC)]
            outs = {"qgT": ([512, T], BF16), "kgT": ([512, T], BF16), "kh": ([T, 512], BF16), "vb": ([T, 1024], BF16),
                    "dec": ([512, T // 64], F32), "sr": ([T, 1024], F32)}

            def prog(S, a):
                C = Consts(S, a["ident"])
                stage_proj_B(S, C, T, a["h"], a["w"], a["wa2"], a["ba"], a["U"], a["W"],
                             a["qgT"].rearrange("(h d) t -> h d t", d=128), a["kgT"].rearrange("(h d) t -> h d t", d=128),
                             a["kh"], a["vb"], a["dec"].rearrange("(h d) c -> h d c", d=128), a["sr"])
            pr = launch("projB", prog, ins, outs)
            ins = []
            ctok = [t_.reshape(-1, 128)[:, ::64].reshape(-1) // 64 for t_ in tok]
            for c in range(NC):
                b0 = (c // 2) * 2
                hs = slice((c % 2) * 256, (c % 2) * 256 + 256)
                vs = slice((c % 2) * 512, (c % 2) * 512 + 512)
                qg = np.empty((256, SEQ), NPBF16)
                kg = np.empty((256, SEQ), NPBF16)
                khh = np.empty((SEQ, 256), NPBF16)
                vv = np.empty((SEQ, 512), NPBF16)
                dd = np.empty((256, SEQ // 64), np.float32)
                for r in range(2):
                    qg[:, tok[b0 + r]] = pr[b0 + r]["qgT"][hs]
                    kg[:, tok[b0 + r]] = pr[b0 + r]["kgT"][hs]
                    khh[tok[b0 + r]] = pr[b0 + r]["kh"][:, hs]
                    vv[tok[b0 + r]] = pr[b0 + r]["vb"][:, vs]
                    dd[:, ctok[b0 + r]] = pr[b0 + r]["dec"][hs]
                ins.append(dict(ident=cst["ident"], qgT=qg, kgT=kg, kh=khh, vb=vv, dec=dd, gn=f32(b_g_norm[j]), tri=cst["tri64"]))

            def prog(S, a):
                C = Consts(S, a["ident"])

                def acc(kind, hd, blk):
                    t0, t1 = blk * 1024, (blk + 1) * 1024
                    if kind == "qg":
                        return a["qgT"][hd * 128:(hd + 1) * 128, t0:t1]
                    if kind == "kg":
                        return a["kgT"][hd * 128:(hd + 1) * 128, t0:t1]
                    if kind == "kh":
                        return a["kh"][t0:t1, hd * 128:(hd + 1) * 128].rearrange("(c j) d -> j c d", j=64)
                    if kind == "v":
                        return a["vb"][t0:t1, hd * 256:(hd + 1) * 256].rearrange("(c j) e -> j c e", j=64)
                    return a["on"][t0:t1, hd * 256:(hd + 1) * 256].rearrange("(c j) e -> j c e", j=64)
                stage_gla(S, C, SEQ, 2, acc, lambda hd: a["dec"][hd * 128:(hd + 1) * 128, :], a["gn"], a["tri"])
            gr = launch("gla", prog, ins, {"on": ([SEQ, 512], F32)})
            w_o = f32(b_w_o[j])
            ins = []
            for c in range(NC):
                b0 = (c // 2) * 2
                on = np.concatenate([gr[b0]["on"][tok[c]], gr[b0 + 1]["on"][tok[c]]], axis=1)
                ins.append(dict(ident=cst["ident"], on=np.ascontiguousarray(on), sr=pr[c]["sr"], h=h[c], w=w_o,
                                g=f32(ln_mix_g[i]), b=f32(ln_mix_b[i])))

            def prog(S, a):
                C = Consts(S, a["ident"])
                stage_wo_ln(S, C, T, None, a["h"], a["w"], a["g"], a["b"], a["h1"], gate=(a["on"], a["sr"]))
            wr = launch("wog", prog, ins, {"h1": ([T, D], F32)})
        ins = [dict(ident=cst["ident"], h1=wr[c]["h1"], wu=f32(mlp_w_up[i]), wd=f32(mlp_w_down[i]), g=f32(ln_mlp_g[i]), b=f32(ln_mlp_b[i]))
               for c in range(NC)]

        def prog(S, a):
            C = Consts(S, a["ident"])
            stage_mlp(S, C, T, a["h1"], a["wu"], a["wd"], a["g"], a["b"], a["h2"])
        mr = launch("mlp", prog, ins, {"h2": ([T, D], F32)})
        h = [mr[c]["h2"] for c in range(NC)]
    out = np.empty((B, SEQ, D), np.float32)
    for c in range(NC):
        out[c // 2][tok[c]] = h[c]
    return out


def build_fused(SEQ, depth):
    nc = bass.Bass("TRN2", target_bir_lowering=False)
    T = SEQ
    NTL = T // 128

    def din(name, shape, dt=F32):
        return nc.dram_tensor(name, list(shape), dt, kind="ExternalInput").ap()

    def scr(name, shape, dt):
        return nc.dram_tensor("scr_" + name, list(shape), dt).ap()

    NA, NB = (depth + 1) // 2, depth // 2
    a = dict(
        x=din("x", [T, D]), pos=din("pos", [128, NTL], I32), invf=din("invf", [8]), ident=din("ident", [128, 128]),
        maskc=din("maskc", [128, 256]), negI=din("negI", [128, 128]), cmU=din("cmU", [128, 128]), cmW=din("cmW", [128, 128]),
        tri64=din("tri64", [64, 64]),
        a_w_in=din("a_w_in", [NA, D, A_IN]), a_w_o=din("a_w_o", [NA, D, D]),
        b_w_in=din("b_w_in", [max(NB, 1), D, B_IN]), b_w_a2=din("b_w_a2", [max(NB, 1), 16, 512]), b_b_a=din("b_b_a", [max(NB, 1), 512]),
        b_g_norm=din("b_g_norm", [max(NB, 1), 256]), b_w_o=din("b_w_o", [max(NB, 1), D, D]),
        ln_mix_g=din("ln_mix_g", [depth, D]), ln_mix_b=din("ln_mix_b", [depth, D]),
        mlp_w_up=din("mlp_w_up", [depth, D, DFF]), mlp_w_down=din("mlp_w_down", [depth, DFF, D]),
        ln_mlp_g=din("ln_mlp_g", [depth, D]), ln_mlp_b=din("ln_mlp_b", [depth, D]),
    )
    out = nc.dram_tensor("out", [T, D], F32, kind="ExternalOutput").ap()
    hA = scr("hA", [T, D], F32)
    hB = scr("hB", [T, D], F32)
    qT = scr("qT", [1024, T], BF16)
    kT = scr("kT", [256, T], BF16)
    vv = scr("v", [T, 256], BF16)
    iqT = scr("iqT", [512, T], BF16)
    ikT = scr("ikT", [64, T], BF16)
    iw = scr("iw", [T, 8], F32)
    o = scr("o", [T, 1024], BF16)
    qgT = scr("qgT", [512, T], BF16)
    kgT = scr("kgT", [512, T], BF16)
    kh = scr("kh", [T, 512], BF16)
    vb = scr("vb", [T, 1024], BF16)
    dec = scr("dec", [512, T // 64], F32)
    sr = scr("sr", [T, 1024], F32)
    on = scr("on", [T, 1024], F32)

    S = Sched(nc)
    CB = 8
    h_in = a["x"]
    for i in range(depth):
        j = i // 2
        last = i == depth - 1
        if i % 2 == 0:
            S.stage_begin()
            C = Consts(S, a["ident"])
            stage_proj_A(S, C, T, h_in, a["pos"], a["a_w_in"][j], a["invf"], qT, kT, vv, iqT, ikT, iw)
            S.stage_end()
            S.stage_begin()
            C = Consts(S, a["ident"])
            stage_attn(S, C, NTL, kT.rearrange("(g d) s -> g d s", d=64), vv, ikT, qT.rearrange("(h d) t -> h d t", d=64),
                       iqT.rearrange("(h d) t -> h d t", d=64), iw, a["maskc"], a["negI"], o, solo=True)
            S.stage_end()
            S.stage_begin()
            C = Consts(S, a["ident"])
            stage_wo_ln(S, C, T, o, h_in, a["a_w_o"][j], a["ln_mix_g"][i], a["ln_mix_b"][i], hB)
            S.stage_end()
        else:
            S.stage_begin()
            C = Consts(S, a["ident"])
            stage_proj_B(S, C, T, h_in, a["b_w_in"][j], a["b_w_a2"][j], a["b_b_a"][j], a["cmU"], a["cmW"],
                         qgT.rearrange("(h d) t -> h d t", d=128), kgT.rearrange("(h d) t -> h d t", d=128), kh, vb,
                         dec.rearrange("(h d) c -> h d c", d=128), sr)
            S.stage_end()
            S.stage_begin()
            C = Consts(S, a["ident"])

            def acc(kind, hd, blk):
                t0, t1 = blk * 64 * CB, (blk + 1) * 64 * CB
                if kind == "qg":
                    return qgT[hd * 128:(hd + 1) * 128, t0:t1]
                if kind == "kg":
                    return kgT[hd * 128:(hd + 1) * 128, t0:t1]
                if kind == "kh":
                    return kh[t0:t1, hd * 128:(hd + 1) * 128].rearrange("(c j) d -> j c d", j=64)
                if kind == "v":
                    return vb[t0:t1, hd * 256:(hd + 1) * 256].rearrange("(c j) e -> j c e", j=64)
                return on[t0:t1, hd * 256:(hd + 1) * 256].rearrange("(c j) e -> j c e", j=64)
            stage_gla(S, C, SEQ, 4, acc, lambda hd: dec[hd * 128:(hd + 1) * 128, :], a["b_g_norm"][j], a["tri64"], CB=CB)
            S.stage_end()
            S.stage_begin()
            C = Consts(S, a["ident"])
            stage_wo_ln(S, C, T, None, h_in, a["b_w_o"][j], a["ln_mix_g"][i], a["ln_mix_b"][i], hB, gate=(on, sr))
            S.stage_end()
        S.stage_begin()
        C = Consts(S, a["ident"])
        h_out = out if last else hA
        stage_mlp(S, C, T, hB, a["mlp_w_up"][i], a["mlp_w_down"][i], a["ln_mlp_g"][i], a["ln_mlp_b"][i], h_out)
        S.stage_end(last=last)
        h_in = hA
    S.stack.close()
    return nc


_FUSED = {}


def kernel(x, positions, a_w_in, a_w_o, b_w_in, b_w_a2, b_b_a, b_g_norm, b_w_o,
           ln_mix_g, ln_mix_b, mlp_w_up, mlp_w_down, ln_mlp_g, ln_mlp_b):
    x = np.asarray(x, np.float32)
    B, SEQ, _ = x.shape
    depth = int(np.asarray(ln_mix_g).shape[0])
    key = (SEQ, depth)
    if key not in _FUSED:
        _FUSED[key] = build_fused(SEQ, depth)
    nc = _FUSED[key]
    cst = _consts()
    f32 = lambda t: np.ascontiguousarray(np.asarray(t, np.float32))
    shared = dict(invf=cst["invf"], ident=cst["ident"], maskc=cst["maskc"][1], negI=cst["negI"], cmU=cst["cmU"], cmW=cst["cmW"],
                  tri64=cst["tri64"], a_w_in=f32(a_w_in), a_w_o=f32(a_w_o), b_w_in=f32(b_w_in), b_w_a2=f32(b_w_a2), b_b_a=f32(b_b_a),
                  b_g_norm=f32(b_g_norm), b_w_o=f32(b_w_o), ln_mix_g=f32(ln_mix_g), ln_mix_b=f32(ln_mix_b),
                  mlp_w_up=f32(mlp_w_up), mlp_w_down=f32(mlp_w_down), ln_mlp_g=f32(ln_mlp_g), ln_mlp_b=f32(ln_mlp_b))
    in_maps = []
    for c in range(B):
        pos_pt = np.ascontiguousarray(np.asarray(positions)[c].astype(np.int32).reshape(SEQ // 128, 128).T)
        m = dict(shared)
        m["x"] = np.ascontiguousarray(x[c])
        m["pos"] = pos_pt
        in_maps.append(m)
    res = run_bass_kernel_spmd(nc, in_maps, core_ids=list(range(B)))
    return np.stack([res.results[c]["out"] for c in range(B)], axis=0)
```

```python
import contextlib
import numpy as np
import concourse.bass as bass
import concourse.mybir as mybir
from concourse.ap import AP
from concourse.bass_utils import run_bass_kernel_spmd

F32 = mybir.dt.float32
BF16 = mybir.dt.bfloat16
I32 = mybir.dt.int32
AF = mybir.ActivationFunctionType
ALU = mybir.AluOpType
AX = mybir.AxisListType

D = 1024
DFF = 4096
DEPTH = 4
ALPHA = (2 * DEPTH) ** 0.25
LN_EPS = 1e-5
NCORES = 8


class Buf:
    __slots__ = ("name", "last_w", "readers")

    def __init__(self, name):
        self.name = name
        self.last_w = None
        self.readers = {}


class Sched:
    ENGS = ("sp", "act", "dve", "pool", "pe")
    SAME_WIN = 3
    NDMA = 8

    def __init__(self, nc):
        self.nc = nc
        self.stack = contextlib.ExitStack()
        self.ops = {e: [] for e in self.ENGS}
        self.count = {e: 0 for e in self.ENGS}
        self.seen = {e: {} for e in self.ENGS}
        self.esem = {}
        for e in ("act", "dve", "pool", "pe"):
            self.esem[e] = self.stack.enter_context(nc.semaphore("s_" + e))
        self.dsem = {}
        self.dma_i = {}
        for q in ("sp", "act", "pool"):
            self.dsem[q] = [self.stack.enter_context(nc.semaphore("d_%s%d" % (q, i))) for i in range(self.NDMA)]
            self.dma_i[q] = 0
        self.nbuf = 0
        self.stage_stack = None
        self.stage_no = 0
        self.bar = self.stack.enter_context(nc.semaphore("s_bar"))

    def sbuf(self, name, shape, dt):
        st = self.stage_stack if self.stage_stack is not None else self.stack
        return st.enter_context(self.nc.sbuf_tensor("sb%d_" % self.stage_no + name, list(shape), dt))

    def psum(self, name, shape, dt):
        st = self.stage_stack if self.stage_stack is not None else self.stack
        return st.enter_context(self.nc.psum_tensor("pp%d_" % self.stage_no + name, list(shape), dt))

    def stage_begin(self):
        self.stage_stack = contextlib.ExitStack()

    def stage_end(self, last=False):
        self.finish()
        if not last:
            self.stage_no += 1
            n = self.stage_no
            bar = self.bar
            self.ops["sp"].append(([], (lambda e, bar=bar: e.sem_inc(bar, 1)), None, 0))
            for eng in ("act", "dve", "pool", "pe"):
                self.ops[eng].append(([(bar, n)], None, None, 0))
        self._emit_block()
        self.stage_stack.close()
        self.stage_stack = None

    def buf(self, name=None):
        self.nbuf += 1
        return Buf(name or ("b%d" % self.nbuf))

    def bufs(self, n, name="b"):
        return [self.buf("%s%d" % (name, i)) for i in range(n)]

    def _deps(self, reads, writes):
        raw = {}
        other = {}
        for b in reads:
            if b.last_w is not None:
                k, v = b.last_w
                raw[k] = max(raw.get(k, 0), v)
        for b in writes:
            if b.last_w is not None:
                k, v = b.last_w
                other[k] = max(other.get(k, 0), v)
            for k, v in b.readers.items():
                other[k] = max(other.get(k, 0), v)
        return raw, other

    def _commit(self, ev, reads, writes):
        k, v = ev
        for b in reads:
            b.readers[k] = max(b.readers.get(k, 0), v)
        for b in writes:
            b.last_w = ev
            b.readers = {}

    def op(self, eng, fn, reads=(), writes=()):
        raw, other = self._deps(reads, writes)
        own = self.esem[eng]
        waits = {}
        seen = self.seen[eng]
        for d, is_raw in ((raw, True), (other, False)):
            for k, v in d.items():
                if k is own:
                    if eng == "pe" or not is_raw:
                        continue
                    if v <= self.count[eng] - self.SAME_WIN:
                        continue
                if seen.get(k, 0) >= v:
                    continue
                waits[k] = max(waits.get(k, 0), v)
        for k, v in waits.items():
            seen[k] = v
        self.count[eng] += 1
        ev = (own, self.count[eng])
        self._commit(ev, reads, writes)
        self.ops[eng].append((list(waits.items()), fn, own, 1))

    def dma(self, q, out, in_, reads=(), writes=(), fn=None, **kw):
        raw, other = self._deps(reads, writes)
        waits = {}
        seen = self.seen[q]
        for d in (raw, other):
            for k, v in d.items():
                if seen.get(k, 0) >= v:
                    continue
                waits[k] = max(waits.get(k, 0), v)
        i = self.dma_i[q]
        self.dma_i[q] = i + 1
        slot = self.dsem[q][i % self.NDMA]
        target = 16 * (i // self.NDMA + 1)
        if target > 16 and seen.get(slot, 0) < target - 16:
            waits[slot] = max(waits.get(slot, 0), target - 16)
        for k, v in waits.items():
            seen[k] = v
        ev = (slot, target)
        self._commit(ev, reads, writes)
        if fn is None:
            fn = lambda e, out=out, in_=in_, kw=kw: e.dma_start(out=out, in_=in_, **kw)
        self.ops[q].append((list(waits.items()), fn, slot, 16))

    def allgather(self, out, in_, groups, reads=(), writes=()):
        fn = lambda e: e.collective_compute("AllGather", ALU.bypass, replica_groups=groups, ins=[in_], outs=[out])
        self.dma("pool", None, None, reads=reads, writes=writes, fn=fn)

    def finish(self):
        waits = []
        for q in ("sp", "act", "pool"):
            n = self.dma_i[q]
            for s in range(min(n, self.NDMA)):
                cnt = (n - 1 - s) // self.NDMA + 1
                waits.append((self.dsem[q][s], 16 * cnt))
        for e in ("act", "dve", "pool", "pe"):
            if self.count[e]:
                waits.append((self.esem[e], self.count[e]))
        self.ops["sp"].append((waits, None, None, 0))

    def emit(self):
        self.finish()
        self._emit_block()
        self.stack.close()

    def _emit_block(self):
        nc = self.nc
        with nc.Block() as block:
            deco = {"sp": block.sync, "act": block.scalar, "dve": block.vector,
                    "pool": block.gpsimd, "pe": block.tensor}
            for eng in self.ENGS:
                ops = self.ops[eng]
                if not ops:
                    continue

                def body(e, ops=ops):
                    for waits, fn, sem, inc in ops:
                        for s, v in waits:
                            e.wait_ge(s, v)
                        if fn is not None:
                            ins = fn(e)
                            if sem is not None:
                                ins.then_inc(sem, inc)

                deco[eng](body)
        self.ops = {e: [] for e in self.ENGS}


def bcast_rows(ap_row, nparts):
    return ap_row.partition_broadcast(nparts)


class Consts:
    def __init__(self, S, ident_dram):
        self.ident = S.sbuf("ident", [128, 128], BF16)
        self.b_ident = S.buf("ident")
        S.dma("pool", self.ident[:], ident_dram[:, :], writes=[self.b_ident])


def load_bcast(S, name, row_ap, n):
    t = S.sbuf(name, [128, n], F32)
    b = S.buf(name)
    S.dma("sp", t[:], row_ap.partition_broadcast(128), writes=[b])
    return t, b


def layer_norm_tile(S, u, b_u, out, b_out, g_t, b_g, bt_t, b_bt, scr, eng2="pool"):
    st, b_st = scr["st"], scr["b_st"]
    mv, b_mv = scr["mv"], scr["b_mv"]
    for c in range(2):
        S.op("dve", lambda e, c=c: e.bn_stats(out=st[:, c, :], in_=u[:, c * 512:(c + 1) * 512]),
             reads=[b_u], writes=[b_st[c]])
    S.op("dve", lambda e: e.bn_aggr(out=mv[:, 0:2], in_=st[:, :, :]), reads=b_st, writes=[b_mv[0]])
    S.op("dve", lambda e: e.tensor_scalar(out=mv[:, 2:3], in0=mv[:, 1:2], scalar1=LN_EPS, scalar2=None,
                                           op0=ALU.add), reads=[b_mv[0]], writes=[b_mv[1]])
    S.op("act", lambda e: e.activation(out=mv[:, 2:3], in_=mv[:, 2:3], func=AF.Sqrt), reads=[b_mv[1]], writes=[b_mv[1]])
    S.op("dve", lambda e: e.reciprocal(out=mv[:, 2:3], in_=mv[:, 2:3]), reads=[b_mv[1]], writes=[b_mv[1]])
    S.op("dve", lambda e: e.scalar_tensor_tensor(out=mv[:, 3:4], in0=mv[:, 0:1], scalar=-1.0, in1=mv[:, 2:3],
                                                  op0=ALU.mult, op1=ALU.mult), reads=[b_mv[0], b_mv[1]], writes=[b_mv[2]])
    S.op("act", lambda e: e.activation(out=u[:, :], in_=u[:, :], func=AF.Identity, bias=mv[:, 3:4], scale=mv[:, 2:3]),
         reads=[b_u, b_mv[1], b_mv[2]], writes=[b_u])
    S.op(eng2, lambda e: e.tensor_tensor(out=u[:, :], in0=u[:, :], in1=g_t[:, :], op=ALU.mult),
         reads=[b_u, b_g], writes=[b_u])
    S.op(eng2, lambda e: e.tensor_tensor(out=out[:, :], in0=u[:, :], in1=bt_t[:, :], op=ALU.add),
         reads=[b_u, b_bt], writes=[b_out])


def ln_scratch(S, name):
    return {"st": S.sbuf(name + "_st", [128, 2, 6], F32), "b_st": S.bufs(2, name + "st"),
            "mv": S.sbuf(name + "_mv", [128, 4], F32), "b_mv": S.bufs(3, name + "mv")}


def stage_mlp(S, C, T, h1, w_up, w_down, g, b, h2):
    NT = 256
    NS = NT // 128
    wup = S.sbuf("wup", [128, 8, DFF], BF16)
    wdn = S.sbuf("wdn", [128, 32, D], BF16)
    b_wup = S.bufs(8, "wup")
    b_wdn = S.bufs(8, "wdn")
    wu_v = w_up.rearrange("(c p) f -> p c f", p=128)
    wd_v = w_down.rearrange("(c p) n -> p c n", p=128)
    for c in range(8):
        S.dma("pool", wup[:, c, :], wu_v[:, c, :], writes=[b_wup[c]])
    for c in range(8):
        S.dma("pool", wdn[:, 4 * c:4 * c + 4, :], wd_v[:, 4 * c:4 * c + 4, :], writes=[b_wdn[c]])
    g_t, b_g = load_bcast(S, "mlp_g", g, D)
    bt_t, b_bt = load_bcast(S, "mlp_b", b, D)

    NB = 2
    x_sb = [S.sbuf("x_sb%d" % i, [128, NS, D], F32) for i in range(NB)]
    b_x = [S.bufs(NS, "x%d_" % i) for i in range(NB)]
    xb = S.sbuf("xb", [128, NS, D], BF16)
    b_xb = S.bufs(NS, "xb")
    xT = S.sbuf("xT", [128, 8, NT], BF16)
    b_xT = S.bufs(8, "xT")
    h2T = S.sbuf("h2T", [128, 32, NT], BF16)
    b_h2T = S.bufs(32, "h2T")
    r_sb = [S.sbuf("r_sb%d" % i, [128, NT], F32) for i in range(4)]
    b_r = S.bufs(4, "r")
    y_sb = [S.sbuf("y_sb%d" % i, [128, D], F32) for i in range(2)]
    b_y = S.bufs(2, "y")
    tp_ps = [S.psum("tp_ps%d" % i, [128, 1024], BF16) for i in range(2)]
    b_tp = S.bufs(2, "tp")
    up_ps = [S.psum("up_ps%d" % i, [128, 512], F32) for i in range(4)]
    b_up = S.bufs(4, "up")
    dn_ps = [S.psum("dn_ps%d" % i, [128, 512], F32) for i in range(2)]
    b_dn = S.bufs(2, "dn")
    scrs = [ln_scratch(S, "mlp%d" % i) for i in range(2)]

    h1v = h1.rearrange("(t s p) d -> t p s d", p=128, s=NS)
    h2v = h2.rearrange("(t s p) d -> t s p d", p=128, s=NS)
    n_up = 0
    n_dn = 0
    n_tp = 0
    n_y = 0
    for t in range(T // NT):
        xs, bx = x_sb[t % NB], b_x[t % NB]
        S.dma("sp", xs[:, :, :], h1v[t], writes=bx)
        for s in range(NS):
            S.op("act", lambda e, xs=xs, s=s: e.activation(out=xb[:, s, :], in_=xs[:, s, :], func=AF.Copy),
                 reads=[bx[s]], writes=[b_xb[s]])
        for c in range(8):
            tp, btp = tp_ps[n_tp % 2], b_tp[n_tp % 2]
            n_tp += 1
            for s in range(NS):
                S.op("pe", lambda e, tp=tp, s=s, c=c: e.transpose(out=tp[:, s * 128:(s + 1) * 128],
                                                                  in_=xb[:, s, c * 128:(c + 1) * 128],
                                                                  identity=C.ident[:]),
                     reads=[b_xb[s], C.b_ident], writes=[btp])
            S.op("dve", lambda e, tp=tp, c=c: e.tensor_copy(out=xT[:, c, :], in_=tp[:, 0:NT]),
                 reads=[btp], writes=[b_xT[c]])
        for fc in range(32):
            ps, bps = up_ps[n_up % 4], b_up[n_up % 4]
            r, br = r_sb[n_up % 4], b_r[n_up % 4]
            n_up += 1
            for c in range(8):
                S.op("pe", lambda e, ps=ps, c=c, fc=fc: e.matmul(ps[:, 0:NT], lhsT=wup[:, c, fc * 128:(fc + 1) * 128],
                                                                  rhs=xT[:, c, :], start=(c == 0), stop=(c == 7)),
                     reads=[b_wup[c], b_xT[c]], writes=[bps])
            S.op("act", lambda e, ps=ps, r=r: e.activation(out=r[:, :], in_=ps[:, 0:NT], func=AF.Relu),
                 reads=[bps], writes=[br])
            S.op("dve", lambda e, r=r, fc=fc: e.tensor_tensor(out=h2T[:, fc, :], in0=r[:, :], in1=r[:, :], op=ALU.mult),
                 reads=[br], writes=[b_h2T[fc]])
        for s in range(NS):
            for nh in range(2):
                ps, bps = dn_ps[n_dn % 2], b_dn[n_dn % 2]
                n_dn += 1
                for fc in range(32):
                    S.op("pe", lambda e, ps=ps, fc=fc, s=s, nh=nh: e.matmul(
                        ps[:, :], lhsT=h2T[:, fc, s * 128:(s + 1) * 128], rhs=wdn[:, fc, nh * 512:(nh + 1) * 512],
                        start=(fc == 0), stop=(fc == 31)),
                         reads=[b_h2T[fc], b_wdn[fc // 4]], writes=[bps])
                S.op("dve", lambda e, ps=ps, xs=xs, s=s, nh=nh: e.scalar_tensor_tensor(
                    out=xs[:, s, nh * 512:(nh + 1) * 512], in0=xs[:, s, nh * 512:(nh + 1) * 512], scalar=ALPHA,
                    in1=ps[:, :], op0=ALU.mult, op1=ALU.add),
                     reads=[bps, bx[s]], writes=[bx[s]])
            y, by = y_sb[n_y % 2], b_y[n_y % 2]
            n_y += 1
            layer_norm_tile(S, xs[:, s, :], bx[s], y, by, g_t, b_g, bt_t, b_bt, scrs[n_y % 2])
            S.dma("pool", h2v[t, s], y[:, :], reads=[by])


TWO_PI = float(2 * np.pi)
PI = float(np.pi)


def _range_reduce(S, x, bx, tmp_f, btf, tmp_i, bti):
    S.op("dve", lambda e: e.tensor_scalar(out=tmp_f, in0=x, scalar1=1.0 / TWO_PI, scalar2=None, op0=ALU.mult),
         reads=[bx], writes=[btf])
    S.op("dve", lambda e: e.tensor_copy(out=tmp_i, in_=tmp_f), reads=[btf], writes=[bti])
    S.op("dve", lambda e: e.tensor_copy(out=tmp_f, in_=tmp_i), reads=[bti], writes=[btf])
    S.op("dve", lambda e: e.scalar_tensor_tensor(out=x, in0=tmp_f, scalar=-TWO_PI, in1=x, op0=ALU.mult, op1=ALU.add),
         reads=[btf, bx], writes=[bx])
    S.op("dve", lambda e: e.tensor_scalar(out=tmp_f, in0=x, scalar1=PI, scalar2=-TWO_PI, op0=ALU.is_gt, op1=ALU.mult),
         reads=[bx], writes=[btf])
    S.op("dve", lambda e: e.tensor_tensor(out=x, in0=x, in1=tmp_f, op=ALU.add), reads=[btf, bx], writes=[bx])
    S.op("dve", lambda e: e.tensor_scalar(out=tmp_f, in0=x, scalar1=-PI, scalar2=TWO_PI, op0=ALU.is_lt, op1=ALU.mult),
         reads=[bx], writes=[btf])
    S.op("dve", lambda e: e.tensor_tensor(out=x, in0=x, in1=tmp_f, op=ALU.add), reads=[btf, bx], writes=[bx])


def rotary_tables(S, NTL, pos_pt, invf):
    posi = S.sbuf("posi", [128, NTL], I32)
    posf = S.sbuf("posf", [128, NTL], F32)
    invt = S.sbuf("invt", [128, 8], F32)
    cos_t = S.sbuf("cos_t", [128, NTL, 8], F32)
    sin_t = S.sbuf("sin_t", [128, NTL, 8], F32)
    tmpf = S.sbuf("rr_tf", [128, NTL, 8], F32)
    tmpi = S.sbuf("rr_ti", [128, NTL, 8], I32)
    b_pi, b_pf, b_inv, b_cos, b_sin, b_tf, b_ti = S.bufs(7, "rot")
    S.dma("sp", posi[:], pos_pt[:, :], writes=[b_pi])
    S.dma("sp", invt[:], invf.partition_broadcast(128), writes=[b_inv])
    S.op("dve", lambda e: e.tensor_copy(out=posf[:], in_=posi[:]), reads=[b_pi], writes=[b_pf])
    S.op("dve", lambda e: e.tensor_tensor(out=sin_t[:, :, :], in0=posf[:, :].unsqueeze(2).to_broadcast([128, NTL, 8]),
                                          in1=invt[:, :].unsqueeze(1).to_broadcast([128, NTL, 8]), op=ALU.mult),
         reads=[b_pf, b_inv], writes=[b_sin])
    S.op("dve", lambda e: e.tensor_scalar(out=cos_t[:, :, :], in0=sin_t[:, :, :], scalar1=PI / 2, scalar2=None, op0=ALU.add),
         reads=[b_sin], writes=[b_cos])
    _range_reduce(S, sin_t[:, :, :], b_sin, tmpf[:, :, :], b_tf, tmpi[:, :, :], b_ti)
    _range_reduce(S, cos_t[:, :, :], b_cos, tmpf[:, :, :], b_tf, tmpi[:, :, :], b_ti)
    S.op("act", lambda e: e.activation(out=sin_t[:, :, :], in_=sin_t[:, :, :], func=AF.Sin), reads=[b_sin], writes=[b_sin])
    S.op("act", lambda e: e.activation(out=cos_t[:, :, :], in_=cos_t[:, :, :], func=AF.Sin), reads=[b_cos], writes=[b_cos])
    return cos_t, b_cos, sin_t, b_sin


def apply_rotary(S, P, bP, h0, nh, cos_t, b_cos, sin_t, b_sin, t, tmp, btmp):
    Pv = P[:, h0 * 64:(h0 + nh) * 64].rearrange("p (h d) -> p h d", d=64)
    x1 = Pv[:, :, 0:8]
    x2 = Pv[:, :, 8:16]
    c = cos_t[:, t, :].unsqueeze(1).to_broadcast([128, nh, 8])
    s = sin_t[:, t, :].unsqueeze(1).to_broadcast([128, nh, 8])
    t1, t2, t3, t4 = (tmp[:, i, 0:nh, :] for i in range(4))
    bP = list(bP)
    rd = bP + [b_cos, b_sin]
    S.op("dve", lambda e: e.tensor_tensor(out=t1, in0=x1, in1=c, op=ALU.mult), reads=rd, writes=[btmp[0]])
    S.op("dve", lambda e: e.tensor_tensor(out=t2, in0=x2, in1=s, op=ALU.mult), reads=rd, writes=[btmp[1]])
    S.op("dve", lambda e: e.tensor_tensor(out=t3, in0=x2, in1=c, op=ALU.mult), reads=rd, writes=[btmp[2]])
    S.op("dve", lambda e: e.tensor_tensor(out=t4, in0=x1, in1=s, op=ALU.mult), reads=rd, writes=[btmp[3]])
    S.op("dve", lambda e: e.tensor_tensor(out=x1, in0=t1, in1=t2, op=ALU.subtract), reads=[btmp[0], btmp[1], btmp[3]], writes=bP)
    S.op("dve", lambda e: e.tensor_tensor(out=x2, in0=t3, in1=t4, op=ALU.add), reads=[btmp[2], btmp[3]], writes=bP)


A_IN = 2120
IW_SCALE = float(8 ** -0.5 * 64 ** -0.5)


def stage_proj_A(S, C, T, h, pos_pt, w_in, invf, qT, kT, v, iqT, ikT, iw):
    NTL = T // 128
    win = S.sbuf("win", [128, 8, A_IN], BF16)
    b_win = S.bufs(8, "win")
    wv = w_in.rearrange("(c p) n -> p c n", p=128)
    for c in range(8):
        S.dma("pool", win[:, c, :], wv[:, c, :], writes=[b_win[c]])
    cos_t, b_cos, sin_t, b_sin = rotary_tables(S, NTL, pos_pt, invf)

    hs = [S.sbuf("pa_hs%d" % i, [128, D], F32) for i in range(2)]
    b_hs = S.bufs(2, "pa_hs")
    hb_l = [S.sbuf("pa_hb%d" % i, [128, D], BF16) for i in range(2)]
    b_hb_l = S.bufs(2, "pa_hb")
    hT_l = [S.sbuf("pa_hT%d" % i, [128, 8, 128], BF16) for i in range(2)]
    b_hT_l = S.bufs(2, "pa_hT")
    P_l = [S.sbuf("pa_P%d" % i, [128, A_IN], F32) for i in range(2)]
    b_P_l = [S.bufs(5, "pa_P%d_" % i) for i in range(2)]
    Pb_l = [S.sbuf("pa_Pb%d" % i, [128, A_IN], BF16) for i in range(2)]
    b_Pb_l = [S.bufs(4, "pa_Pb%d_" % i) for i in range(2)]
    iws = [S.sbuf("pa_iw%d" % i, [128, 8], F32) for i in range(2)]
    b_iws = S.bufs(2, "pa_iw")
    rt_l = [S.sbuf("pa_rt%d" % i, [128, 4, 20, 8], F32) for i in range(2)]
    b_rt_l = [S.bufs(4, "pa_rt%d_" % i) for i in range(2)]
    qTs = [S.sbuf("pa_qT%d" % i, [128, 8, 128], BF16) for i in range(2)]
    b_qTs = S.bufs(2, "pa_qT")
    kiTs = [S.sbuf("pa_kiT%d" % i, [128, 7, 128], BF16) for i in range(2)]
    b_kiTs = S.bufs(2, "pa_kiT")
    pj = [S.psum("pa_pj%d" % i, [128, 512], F32) for i in range(5)]
    b_pj = S.bufs(5, "pa_pj")
    tpA = S.psum("pa_tpA", [128, 1024], BF16)
    tpB = S.psum("pa_tpB", [128, 1024], BF16)
    b_tpA, b_tpB = S.bufs(2, "pa_tp")
    chunks = [(0, 512), (512, 1024), (1024, 1536), (1536, 2048), (2048, A_IN)]

    hv = h.rearrange("(t p) d -> t p d", p=128)
    qTv = qT.rearrange("(c p) t -> p c t", p=128)
    kTv = kT.rearrange("(c p) t -> p c t", p=128)
    iqTv = iqT.rearrange("(c p) t -> p c t", p=128)
    def _tile(t, hb, b_hb, hT, b_hT, P, b_P, Pb, b_Pbs, rt, b_rt):
        b_Pq, b_Pk, b_Pv, b_Pi = b_Pbs
        x, bx = hs[t % 2], b_hs[t % 2]
        S.dma("sp", x[:, :], hv[t], writes=[bx])
        S.op("act", lambda e, x=x: e.activation(out=hb[:, :], in_=x[:, :], func=AF.Copy), reads=[bx], writes=[b_hb])
        for c in range(8):
            S.op("pe", lambda e, c=c: e.transpose(out=tpA[:, c * 128:(c + 1) * 128], in_=hb[:, c * 128:(c + 1) * 128],
                                                  identity=C.ident[:]), reads=[b_hb, C.b_ident], writes=[b_tpA])
        S.op("dve", lambda e: e.tensor_copy(out=hT[:, :, :], in_=tpA[:, :].rearrange("p (c t) -> p c t", t=128)),
             reads=[b_tpA], writes=[b_hT])
        for i, (n0, n1) in enumerate(chunks):
            for c in range(8):
                S.op("pe", lambda e, i=i, c=c, n0=n0, n1=n1: e.matmul(pj[i][:, 0:n1 - n0], lhsT=hT[:, c, :], rhs=win[:, c, n0:n1],
                                                                      start=(c == 0), stop=(c == 7)),
                     reads=[b_hT, b_win[c]], writes=[b_pj[i]])
            S.op("act", lambda e, i=i, n0=n0, n1=n1: e.activation(out=P[:, n0:n1], in_=pj[i][:, 0:n1 - n0], func=AF.Copy),
                 reads=[b_pj[i]], writes=[b_P[i]])
        apply_rotary(S, P, b_P[0:3], 0, 20, cos_t, b_cos, sin_t, b_sin, t, rt, b_rt)
        apply_rotary(S, P, b_P[3:5], 24, 9, cos_t, b_cos, sin_t, b_sin, t, rt, b_rt)
        S.op("act", lambda e: e.activation(out=Pb[:, 0:1024], in_=P[:, 0:1024], func=AF.Copy, scale=0.125),
             reads=b_P[0:2], writes=[b_Pq])
        S.op("act", lambda e: e.activation(out=Pb[:, 1024:1536], in_=P[:, 1024:1536], func=AF.Copy),
             reads=[b_P[2]], writes=[b_Pk])
        S.op("act", lambda e: e.activation(out=Pb[:, 1536:2112], in_=P[:, 1536:2112], func=AF.Copy),
             reads=b_P[3:5], writes=[b_Pi])
        iwt, biw = iws[t % 2], b_iws[t % 2]
        S.op("act", lambda e, iwt=iwt: e.activation(out=iwt[:, :], in_=P[:, 2112:2120], func=AF.Copy, scale=IW_SCALE),
             reads=[b_P[4]], writes=[biw])
        S.dma("pool", iw[t * 128:(t + 1) * 128, :], iwt[:, :], reads=[biw])
        S.dma("pool", v[t * 128:(t + 1) * 128, :], Pb[:, 1280:1536], reads=[b_Pk])
        qs, bqs = qTs[t % 2], b_qTs[t % 2]
        ks, bks = kiTs[t % 2], b_kiTs[t % 2]
        for c in range(8):
            S.op("pe", lambda e, c=c: e.transpose(out=tpB[:, c * 128:(c + 1) * 128], in_=Pb[:, c * 128:(c + 1) * 128],
                                                  identity=C.ident[:]), reads=[b_Pq, C.b_ident], writes=[b_tpB])
        S.op("dve", lambda e, qs=qs: e.tensor_copy(out=qs[:, :, :], in_=tpB[:, :].rearrange("p (c t) -> p c t", t=128)),
             reads=[b_tpB], writes=[bqs])
        S.dma("pool", qTv[:, :, t * 128:(t + 1) * 128], qs[:, :, :], reads=[bqs])
        srcs = [(1024, 128, b_Pk), (1152, 128, b_Pk), (1536, 128, b_Pi), (1664, 128, b_Pi), (1792, 128, b_Pi),
                (1920, 128, b_Pi), (2048, 64, b_Pi)]
        for j, (c0, w, bsrc) in enumerate(srcs):
            S.op("pe", lambda e, j=j, c0=c0, w=w: e.transpose(out=tpA[0:w, j * 128:(j + 1) * 128], in_=Pb[:, c0:c0 + w],
                                                              identity=C.ident[:]), reads=[bsrc, C.b_ident], writes=[b_tpA])
        S.op("dve", lambda e, ks=ks: e.tensor_copy(out=ks[:, 0:6, :], in_=tpA[:, 0:768].rearrange("p (c t) -> p c t", t=128)),
             reads=[b_tpA], writes=[bks])
        S.op("dve", lambda e, ks=ks: e.tensor_copy(out=ks[0:64, 6, :], in_=tpA[0:64, 768:896]),
             reads=[b_tpA], writes=[bks])
        S.dma("pool", kTv[:, :, t * 128:(t + 1) * 128], ks[:, 0:2, :], reads=[bks])
        S.dma("pool", iqTv[:, :, t * 128:(t + 1) * 128], ks[:, 2:6, :], reads=[bks])
        S.dma("pool", ikT[:, t * 128:(t + 1) * 128], ks[0:64, 6, :], reads=[bks])

    for t in range(NTL):
        _tile(t, hb_l[t % 2], b_hb_l[t % 2], hT_l[t % 2], b_hT_l[t % 2], P_l[t % 2], b_P_l[t % 2], Pb_l[t % 2], b_Pb_l[t % 2], rt_l[t % 2], b_rt_l[t % 2])

TOPK = 256
NEG_MASK = -3.0e38
NEG_LO = -1.0e30
N_BISECT = 18
MASK_BIG = 32768.0


def stage_attn(S, C, NQ, kT, v, ikT, qT, iqT, iw, maskc, negI, o, dbg=None, solo=False):
    NKTT = NQ if solo else 2 * NQ
    SK = NKTT * 128
    nkt_of = (lambda j: j + 1) if solo else (lambda j: 2 * j + 2)
    kT_sb = S.sbuf("at_kT", [128, 2, SK], BF16)
    v_sb = S.sbuf("at_v", [128, NKTT, 4, 65], BF16)
    ik_sb = S.sbuf("at_ik", [64, SK], BF16)
    b_kT, b_v, b_ik = S.bufs(3, "at_res")
    for hh in range(2):
        S.dma("sp", kT_sb[hh * 64:(hh + 1) * 64, :, :], kT[2 * hh:2 * hh + 2, :, :].rearrange("g d s -> d g s"), writes=[b_kT])
    S.op("pool", lambda e: e.memset(v_sb[:, :, :, 64:65], 1.0), writes=[b_v])
    vv = v.rearrange("(k p) (g d) -> p k g d", p=128, d=64)
    KCH = 16
    for k0 in range(0, NKTT, KCH):
        k1 = min(NKTT, k0 + KCH)
        for g in range(4):
            S.dma("sp", v_sb[:, k0:k1, g, 0:64], vv[:, k0:k1, g, :], writes=[b_v])
    S.dma("sp", ik_sb[:, :], ikT[:, :], writes=[b_ik])
    mc = S.sbuf("at_mc", [128, 256], F32)
    nI = S.sbuf("at_nI", [128, 4, 128], BF16)
    b_mc, b_nI = S.bufs(2, "at_c")
    S.dma("sp", mc[:, :], maskc[:, :], writes=[b_mc])
    for r in range(4):
        S.dma("pool", nI[:, r, :], negI[:, :], writes=[b_nI])

    sc = S.sbuf("at_sc", [128, SK], F32)
    b_sc = S.buf("at_sc")
    junk = S.sbuf("at_junk", [128, SK], BF16)
    b_junk = S.buf("at_junk")
    mb = [S.sbuf("at_mb%d" % i, [128, SK], BF16) for i in range(2)]
    b_mb = S.bufs(2, "at_mb")
    NT_SB = 4
    NS_PS = 2
    t_sb = [S.sbuf("at_t%d" % i, [128, 512], F32) for i in range(NT_SB)]
    b_t = S.bufs(NT_SB, "at_t")
    NPT = 4
    NLP = 3
    PT = [S.sbuf("at_PT%d" % i, [128, 512], BF16) for i in range(NPT)]
    b_PT = S.bufs(NPT, "at_PT")
    q_sb = [S.sbuf("at_q%d" % i, [128, 4, 4, 128], BF16) for i in range(2)]
    b_q = S.bufs(2, "at_q")
    for i in range(2):
        S.op("pool", lambda e, i=i: e.memset(q_sb[i][:, :, :, :], 0.0), writes=[b_q[i]])
    iq_sb = [S.sbuf("at_iq%d" % i, [64, 8, 128], BF16) for i in range(2)]
    b_iq = S.bufs(2, "at_iq")
    iw_sb = [S.sbuf("at_iw%d" % i, [128, 8], F32) for i in range(2)]
    b_iw = S.bufs(2, "at_iw")
    o_sb = [S.sbuf("at_o%d" % i, [128, 16, 64], BF16) for i in range(2)]
    b_o = S.bufs(2, "at_o")
    sm = S.sbuf("at_sm", [128, 16], F32)
    b_sm = S.bufs(8, "at_sm")
    rc = S.sbuf("at_rc", [128, 16], F32)
    b_rc = S.buf("at_rc")
    s_ps = [S.psum("at_sps%d" % i, [128, 512], F32) for i in range(NS_PS)]
    b_sps = S.bufs(NS_PS, "at_sps")
    KB = N_BISECT
    pow2 = S.sbuf("at_pow2", [128, KB + 1], F32)
    wtab = S.sbuf("at_wtab", [128, KB + 1], F32)
    b_pow2, b_wtab = S.bufs(2, "at_w")
    for k in range(KB + 1):
        S.op("pool", lambda e, k=k: e.memset(pow2[:, k:k + 1], float(2.0 ** -k)), writes=[b_pow2])
    l_ps = [S.psum("at_lps%d" % i, [128, 512], F32) for i in range(NLP)]
    b_lps = S.bufs(NLP, "at_lps")
    o_ps = [S.psum("at_ops%d" % i, [128, 7, 65], F32) for i in range(3)]
    b_ops = S.bufs(3, "at_ops")
    LO, HI, MID, CNT, GE, DD, MM = range(7)
    col = lambda i: sm[:, i:i + 1]

    n_s = 0
    n_l = 0
    n_pt = 0

    def load_q(j):
        qs, bq = q_sb[j % 2], b_q[j % 2]
        for hh in range(2):
            S.dma("sp", qs[hh * 64:(hh + 1) * 64, 2 * hh:2 * hh + 2, :, :].rearrange("d a r q -> d (a r) q"),
                  qT[8 * hh:8 * hh + 8, :, j * 128:(j + 1) * 128].rearrange("h d q -> d h q"), writes=[bq])
        S.dma("sp", iq_sb[j % 2][:, :, :], iqT[:, :, j * 128:(j + 1) * 128].rearrange("h d q -> d h q"), writes=[b_iq[j % 2]])
        S.dma("sp", iw_sb[j % 2][:, :], iw[j * 128:(j + 1) * 128, :], writes=[b_iw[j % 2]])

    def phase12(j):
        nonlocal n_s
        nk = nkt_of(j) * 128
        iqs, biq = iq_sb[j % 2], b_iq[j % 2]
        iws, biw = iw_sb[j % 2], b_iw[j % 2]
        for hd in range(8):
            for k0 in range(0, nk, 512):
                w = min(512, nk - k0)
                ps, bps = s_ps[n_s % NS_PS], b_sps[n_s % NS_PS]
                tt, bt = t_sb[n_s % NT_SB], b_t[n_s % NT_SB]
                n_s += 1
                S.op("pe", lambda e, ps=ps, hd=hd, k0=k0, w=w, iqs=iqs: e.matmul(
                    ps[:, 0:w], lhsT=iqs[:, hd, :], rhs=ik_sb[:, k0:k0 + w], start=True, stop=True),
                     reads=[biq, b_ik], writes=[bps])
                S.op("act", lambda e, ps=ps, tt=tt, w=w: e.activation(out=tt[:, 0:w], in_=ps[:, 0:w], func=AF.Relu),
                     reads=[bps], writes=[bt])
                if hd == 0:
                    S.op("dve", lambda e, tt=tt, k0=k0, w=w, iws=iws: e.tensor_scalar(
                        out=sc[:, k0:k0 + w], in0=tt[:, 0:w], scalar1=iws[:, 0:1], scalar2=None, op0=ALU.mult),
                         reads=[bt, biw], writes=[b_sc])
                else:
                    S.op("dve", lambda e, tt=tt, k0=k0, w=w, iws=iws, hd=hd: e.scalar_tensor_tensor(
                        out=sc[:, k0:k0 + w], in0=tt[:, 0:w], scalar=iws[:, hd:hd + 1], in1=sc[:, k0:k0 + w],
                        op0=ALU.mult, op1=ALU.add), reads=[bt, biw, b_sc], writes=[b_sc])
        S.op("dve", lambda e: e.tensor_reduce(out=col(MM), in_=sc[:, 0:nk], axis=AX.X, op=ALU.max, apply_absolute_value=True),
             reads=[b_sc], writes=[b_sm[MM]])
        S.op("dve", lambda e: e.tensor_scalar(out=col(HI), in0=col(MM), scalar1=1.0, scalar2=None, op0=ALU.add),
             reads=[b_sm[MM]], writes=[b_sm[HI]])
        S.op("dve", lambda e: e.tensor_scalar(out=wtab[:, :], in0=pow2[:, :], scalar1=col(HI), scalar2=None, op0=ALU.mult),
             reads=[b_sm[HI], b_pow2], writes=[b_wtab])
        if solo:
            S.op("dve", lambda e: e.tensor_tensor(out=sc[:, nk - 128:nk], in0=sc[:, nk - 128:nk], in1=mc[:, 128:256], op=ALU.add),
                 reads=[b_sc, b_mc], writes=[b_sc])
        else:
            S.op("dve", lambda e: e.tensor_tensor(out=sc[:, nk - 256:nk], in0=sc[:, nk - 256:nk], in1=mc[:, :], op=ALU.add),
                 reads=[b_sc, b_mc], writes=[b_sc])
        S.op("dve", lambda e: e.memset(col(MID), 0.0), writes=[b_sm[MID]])
        for it in range(KB):
            S.op("dve", lambda e: e.tensor_scalar(out=junk[:, 0:nk], in0=sc[:, 0:nk], scalar1=col(MID), scalar2=0.0,
                                                   op0=ALU.is_ge, op1=ALU.add, accum_out=col(CNT)),
                 reads=[b_sc, b_sm[MID]], writes=[b_junk, b_sm[CNT]])
            S.op("dve", lambda e, it=it: e.scalar_tensor_tensor(out=col(GE), in0=col(CNT), scalar=TOPK - 0.5, in1=wtab[:, it:it + 1],
                                                                op0=ALU.is_ge, op1=ALU.mult),
                 reads=[b_sm[CNT], b_wtab], writes=[b_sm[GE]])
            S.op("dve", lambda e, it=it: e.scalar_tensor_tensor(out=col(MID), in0=col(MID), scalar=wtab[:, it + 1:it + 2], in1=col(GE),
                                                                op0=ALU.subtract, op1=ALU.add),
                 reads=[b_sm[MID], b_sm[GE], b_wtab], writes=[b_sm[MID]])
        S.op("dve", lambda e: e.tensor_tensor(out=col(LO), in0=col(MID), in1=wtab[:, KB:KB + 1], op=ALU.subtract),
             reads=[b_sm[MID], b_wtab], writes=[b_sm[LO]])
        m, bm = mb[j % 2], b_mb[j % 2]
        S.op("dve", lambda e, m=m: e.tensor_scalar(out=m[:, 0:nk], in0=sc[:, 0:nk], scalar1=col(LO), scalar2=None, op0=ALU.is_lt),
             reads=[b_sc, b_sm[LO]], writes=[bm])
        if dbg is not None and j == dbg["j"]:
            S.dma("sp", dbg["sc"][:, 0:nk], sc[:, 0:nk], reads=[b_sc])
            S.dma("sp", dbg["mb"][:, 0:nk], m[:, 0:nk], reads=[bm])
            S.dma("sp", dbg["sm"][:, :], sm[:, :], reads=b_sm)

    def phase3(j):
        nonlocal n_l, n_pt
        NKT = nkt_of(j)
        qs, bq = q_sb[j % 2], b_q[j % 2]
        m, bm = mb[j % 2], b_mb[j % 2]
        DEPTH = 2
        pend = []

        def emit_pv(item):
            kt, g, pt, bpt = item
            for r in range(4):
                hd = 4 * g + r
                bank, slot = hd // 7, hd % 7
                S.op("pe", lambda e, pt=pt, r=r, kt=kt, g=g, bank=bank, slot=slot: e.matmul(
                    o_ps[bank][:, slot, :], lhsT=pt[:, r * 128:(r + 1) * 128], rhs=v_sb[:, kt, g, :],
                    start=(kt == 0 and slot == 0), stop=(kt == NKT - 1), skip_group_check=True),
                     reads=[bpt, b_v], writes=[b_ops[bank]])

        for kt in range(NKT):
            for g in range(4):
                hh, a = g // 2, g % 2
                lp, blp = l_ps[n_l % NLP], b_lps[n_l % NLP]
                n_l += 1
                pt, bpt = PT[n_pt % NPT], b_PT[n_pt % NPT]
                n_pt += 1
                S.op("pe", lambda e, lp=lp, g=g, a=a, kt=kt, qs=qs: e.matmul(
                    lp[:, :], lhsT=kT_sb[:, a, kt * 128:(kt + 1) * 128],
                    rhs=qs[:, g, :, :].rearrange("d r q -> d (r q)"), start=True, stop=False),
                     reads=[b_kT, bq], writes=[blp])
                S.op("pe", lambda e, lp=lp, kt=kt, m=m: e.matmul(
                    lp[:, :], lhsT=m[:, kt * 128:(kt + 1) * 128], rhs=nI[:, :, :].rearrange("p r q -> p (r q)"),
                    start=False, stop=True), reads=[bm, b_nI], writes=[blp])
                S.op("act", lambda e, lp=lp, pt=pt: e.activation(out=pt[:, :], in_=lp[:, :], func=AF.Exp),
                     reads=[blp], writes=[bpt])
                pend.append((kt, g, pt, bpt))
                if len(pend) > DEPTH:
                    emit_pv(pend.pop(0))
        while pend:
            emit_pv(pend.pop(0))
        ob, bo = o_sb[j % 2], b_o[j % 2]
        for bank in range(3):
            nh = min(7, 16 - 7 * bank)
            S.op("dve", lambda e, bank=bank, nh=nh: e.reciprocal(out=rc[:, 0:nh], in_=o_ps[bank][:, 0:nh, 64]),
                 reads=[b_ops[bank]], writes=[b_rc])
            S.op("dve", lambda e, bank=bank, nh=nh, ob=ob: e.tensor_tensor(
                out=ob[:, 7 * bank:7 * bank + nh, :], in0=o_ps[bank][:, 0:nh, 0:64],
                in1=rc[:, 0:nh].unsqueeze(2).to_broadcast([128, nh, 64]), op=ALU.mult),
                 reads=[b_ops[bank], b_rc], writes=[bo])
        S.dma("pool", o[j * 128:(j + 1) * 128, :], ob[:, :, :].rearrange("p h d -> p (h d)"), reads=[bo])

    load_q(0)
    phase12(0)
    for j in range(NQ):
        if j + 1 < NQ:
            load_q(j + 1)
            phase12(j + 1)
        phase3(j)


def stage_wo_ln(S, C, T, o, h, w_o, g, b, h1, gate=None):
    wo = S.sbuf("wo", [128, 8, D], BF16)
    b_wo = S.bufs(8, "wo")
    wv = w_o.rearrange("(c p) n -> p c n", p=128)
    for c in range(8):
        S.dma("pool", wo[:, c, :], wv[:, c, :], writes=[b_wo[c]])
    g_t, b_g = load_bcast(S, "wo_g", g, D)
    bt_t, b_bt = load_bcast(S, "wo_b", b, D)
    ob = [S.sbuf("wo_ob%d" % i, [128, D], BF16) for i in range(2)]
    b_ob = S.bufs(2, "wo_ob")
    hs = [S.sbuf("wo_hs%d" % i, [128, D], F32) for i in range(2)]
    b_hs = S.bufs(2, "wo_hs")
    oTs = [S.sbuf("wo_oT%d" % i, [128, 8, 128], BF16) for i in range(2)]
    b_oTs = S.bufs(2, "wo_oT")
    y_sb = [S.sbuf("wo_y%d" % i, [128, D], F32) for i in range(2)]
    b_y = S.bufs(2, "wo_y")
    tps = [S.psum("wo_tp%d" % i, [128, 1024], BF16) for i in range(2)]
    b_tps = S.bufs(2, "wo_tp")
    mx = [S.psum("wo_mx%d" % i, [128, 512], F32) for i in range(4)]
    b_mx = S.bufs(4, "wo_mx")
    scrs = [ln_scratch(S, "wo%d" % i) for i in range(2)]
    ov = o.rearrange("(t p) d -> t p d", p=128) if gate is None else None
    if gate is not None:
        gate_a = [S.sbuf("wo_ga%d" % i, [128, D], F32) for i in range(2)]
        gate_b = [S.sbuf("wo_gb%d" % i, [128, D], F32) for i in range(2)]
        b_ga = S.bufs(2, "wo_ga")
        b_gb = S.bufs(2, "wo_gb")
    hv = h.rearrange("(t p) d -> t p d", p=128)
    h1v = h1.rearrange("(t p) d -> t p d", p=128)
    for t in range(T // 128):
        x, bx = ob[t % 2], b_ob[t % 2]
        hh, bh = hs[t % 2], b_hs[t % 2]
        if gate is None:
            S.dma("sp", x[:, :], ov[t], writes=[bx])
        else:
            ga, gb_ = gate_a[t % 2], gate_b[t % 2]
            S.dma("sp", ga[:, :], gate[0].rearrange("(t p) d -> t p d", p=128)[t], writes=[b_ga[t % 2]])
            S.dma("sp", gb_[:, :], gate[1].rearrange("(t p) d -> t p d", p=128)[t], writes=[b_gb[t % 2]])
            S.op("pool", lambda e, x=x, ga=ga, gb_=gb_: e.tensor_tensor(out=x[:, :], in0=ga[:, :], in1=gb_[:, :], op=ALU.mult),
                 reads=[b_ga[t % 2], b_gb[t % 2]], writes=[bx])
        S.dma("sp", hh[:, :], hv[t], writes=[bh])
        tp, b_tp = tps[t % 2], b_tps[t % 2]
        oT, b_oT = oTs[t % 2], b_oTs[t % 2]
        scr = scrs[t % 2]
        for c in range(8):
            S.op("pe", lambda e, c=c, x=x, tp=tp: e.transpose(out=tp[:, c * 128:(c + 1) * 128], in_=x[:, c * 128:(c + 1) * 128],
                                                       identity=C.ident[:]), reads=[bx, C.b_ident], writes=[b_tp])
        S.op("act", lambda e, oT=oT, tp=tp: e.activation(out=oT[:, :, :], in_=tp[:, :].rearrange("p (c t) -> p c t", t=128), func=AF.Copy),
             reads=[b_tp], writes=[b_oT])
        for nh in range(2):
            ps, bps = mx[(2 * t + nh) % 4], b_mx[(2 * t + nh) % 4]
            for c in range(8):
                S.op("pe", lambda e, ps=ps, c=c, nh=nh, oT=oT: e.matmul(ps[:, :], lhsT=oT[:, c, :], rhs=wo[:, c, nh * 512:(nh + 1) * 512],
                                                                  start=(c == 0), stop=(c == 7)),
                     reads=[b_oT, b_wo[c]], writes=[bps])
            S.op("dve", lambda e, ps=ps, hh=hh, nh=nh: e.scalar_tensor_tensor(
                out=hh[:, nh * 512:(nh + 1) * 512], in0=hh[:, nh * 512:(nh + 1) * 512], scalar=ALPHA, in1=ps[:, :],
                op0=ALU.mult, op1=ALU.add), reads=[bps, bh], writes=[bh])
        y, by = y_sb[t % 2], b_y[t % 2]
        layer_norm_tile(S, hh, bh, y, by, g_t, b_g, bt_t, b_bt, scr)
        S.dma("pool", h1v[t], y[:, :], reads=[by])


B_IN = 3088
GATE_TAU = 16.0


def stage_proj_B(S, C, T, h, w_in, w_a2, b_a, cmU, cmW, qgT, kgT, kh, vb, dec, sr):
    NTL = T // 128
    win = S.sbuf("pb_win", [128, 8, B_IN], BF16)
    b_win = S.bufs(8, "pb_win")
    wv = w_in.rearrange("(c p) n -> p c n", p=128)
    for c in range(8):
        S.dma("pool", win[:, c, :], wv[:, c, :], writes=[b_win[c]])
    wa2 = S.sbuf("pb_wa2", [16, 512], BF16)
    U = S.sbuf("pb_U", [128, 128], BF16)
    W = S.sbuf("pb_W", [128, 128], BF16)
    b_wa2, b_U, b_W = S.bufs(3, "pb_c")
    S.dma("pool", wa2[:, :], w_a2[:, :], writes=[b_wa2])
    S.dma("pool", U[:, :], cmU[:, :], writes=[b_U])
    S.dma("pool", W[:, :], cmW[:, :], writes=[b_W])
    ba_t, b_ba = load_bcast(S, "pb_ba", b_a, 512)

    hs = [S.sbuf("pb_hs%d" % i, [128, D], F32) for i in range(2)]
    b_hs = S.bufs(2, "pb_hs")
    def _mk(i):
        return dict(hb=S.sbuf("pb_hb%d" % i, [128, D], BF16), b_hb=S.buf("pb_hb"), hT=S.sbuf("pb_hT%d" % i, [128, 8, 128], BF16), b_hT=S.buf("pb_hT"),
                    alT=S.sbuf("pb_alT%d" % i, [16, 128], BF16), b_alT=S.buf("pb_alT"), gg=S.sbuf("pb_g%d" % i, [128, 512], F32), b_gg=S.buf("pb_g"),
                    ghi=S.sbuf("pb_ghi%d" % i, [128, 512], BF16), glo=S.sbuf("pb_glo%d" % i, [128, 512], BF16), b_ghi=S.buf("pb_ghi"), b_glo=S.buf("pb_glo"),
                    EbT=S.sbuf("pb_EbT%d" % i, [128, 4, 128], F32), EnbT=S.sbuf("pb_EnbT%d" % i, [128, 4, 128], F32), Ebl=S.sbuf("pb_Ebl%d" % i, [128, 512], F32),
                    b_EbT=S.buf("pb_EbT"), b_EnbT=S.buf("pb_EnbT"), b_Ebl=S.buf("pb_Ebl"))
    sets = [_mk(0), _mk(1)]
    qg_s = [S.sbuf("pb_qg%d" % i, [128, 4, 128], BF16) for i in range(2)]
    kg_s = [S.sbuf("pb_kg%d" % i, [128, 4, 128], BF16) for i in range(2)]
    kh_s = [S.sbuf("pb_kh%d" % i, [128, 512], BF16) for i in range(2)]
    vb_s = [S.sbuf("pb_vb%d" % i, [128, 1024], BF16) for i in range(2)]
    sr_s = [S.sbuf("pb_sr%d" % i, [128, 1024], F32) for i in range(2)]
    dc_s = [S.sbuf("pb_dc%d" % i, [128, 4, 2], F32) for i in range(2)]
    b_qg, b_kg, b_kh, b_vb, b_sr, b_dc = (S.bufs(2, "pb_o%d" % i) for i in range(6))
    tpT = S.psum("pb_tp", [128, 1024], BF16)
    b_tpT = S.buf("pb_tp")
    pk = [S.psum("pb_pk%d" % i, [128, 512], F32) for i in range(7)]
    b_pk = S.bufs(7, "pb_pk")
    QSCALE = float(128 ** -0.5)

    hv = h.rearrange("(t p) d -> t p d", p=128)
    def _tile(t, hb, b_hb, hT, b_hT, alT, b_alT, gg, b_gg, ghi, glo, b_ghi, b_glo, EbT, EnbT, Ebl, b_EbT, b_EnbT, b_Ebl):
        x, bx = hs[t % 2], b_hs[t % 2]
        i2 = t % 2
        S.dma("sp", x[:, :], hv[t], writes=[bx])
        S.op("act", lambda e, x=x: e.activation(out=hb[:, :], in_=x[:, :], func=AF.Copy), reads=[bx], writes=[b_hb])
        for c in range(8):
            S.op("pe", lambda e, c=c: e.transpose(out=tpT[:, c * 128:(c + 1) * 128], in_=hb[:, c * 128:(c + 1) * 128],
                                                  identity=C.ident[:]), reads=[b_hb, C.b_ident], writes=[b_tpT])
        S.op("dve", lambda e: e.tensor_copy(out=hT[:, :, :], in_=tpT[:, :].rearrange("p (c t) -> p c t", t=128)),
             reads=[b_tpT], writes=[b_hT])

        def tok_mm(bank, n0, n1):
            for c in range(8):
                S.op("pe", lambda e, c=c: e.matmul(pk[bank][:, 0:n1 - n0], lhsT=hT[:, c, :], rhs=win[:, c, n0:n1],
                                                   start=(c == 0), stop=(c == 7)), reads=[b_hT, b_win[c]], writes=[b_pk[bank]])

        def feat_mm(bank, slot, n0, m):
            for c in range(8):
                S.op("pe", lambda e, c=c: e.matmul(pk[bank][0:m, slot * 128:(slot + 1) * 128], lhsT=win[:, c, n0:n0 + m],
                                                   rhs=hT[:, c, :], start=(c == 0 and slot == 0), stop=(c == 7),
                                                   skip_group_check=True), reads=[b_hT, b_win[c]], writes=[b_pk[bank]])

        feat_mm(6, 0, 3072, 16)
        S.op("act", lambda e: e.activation(out=alT[:, :], in_=pk[6][0:16, 0:128], func=AF.Copy), reads=[b_pk[6]], writes=[b_alT])
        S.op("pe", lambda e: e.matmul(pk[5][:, :], lhsT=alT[:, :], rhs=wa2[:, :], start=True, stop=True),
             reads=[b_alT, b_wa2], writes=[b_pk[5]])
        S.op("dve", lambda e: e.tensor_tensor(out=gg[:, :], in0=pk[5][:, :], in1=ba_t[:, :], op=ALU.add),
             reads=[b_pk[5], b_ba], writes=[b_gg])
        S.op("act", lambda e: e.activation(out=gg[:, :], in_=gg[:, :], func=AF.Exp, scale=-1.0), reads=[b_gg], writes=[b_gg])
        S.op("dve", lambda e: e.tensor_scalar(out=gg[:, :], in0=gg[:, :], scalar1=1.0, scalar2=None, op0=ALU.add),
             reads=[b_gg], writes=[b_gg])
        S.op("act", lambda e: e.activation(out=gg[:, :], in_=gg[:, :], func=AF.Ln), reads=[b_gg], writes=[b_gg])
        S.op("dve", lambda e: e.tensor_scalar(out=gg[:, :], in0=gg[:, :], scalar1=-1.0 / GATE_TAU, scalar2=None, op0=ALU.mult),
             reads=[b_gg], writes=[b_gg])
        S.op("dve", lambda e: e.tensor_copy(out=ghi[:, :], in_=gg[:, :]), reads=[b_gg], writes=[b_ghi])
        S.op("dve", lambda e: e.tensor_tensor(out=glo[:, :], in0=gg[:, :], in1=ghi[:, :], op=ALU.subtract),
             reads=[b_gg, b_ghi], writes=[b_glo])
        for hd in range(4):
            for part, (gs, bgs) in enumerate(((ghi, b_ghi), (glo, b_glo))):
                S.op("pe", lambda e, hd=hd, gs=gs, part=part: e.matmul(
                    pk[4][:, hd * 128:(hd + 1) * 128], lhsT=gs[:, hd * 128:(hd + 1) * 128], rhs=U[:, :],
                    start=(hd == 0 and part == 0), stop=(part == 1), skip_group_check=True),
                     reads=[bgs, b_U], writes=[b_pk[4]])
        for part, (gs, bgs) in enumerate(((ghi, b_ghi), (glo, b_glo))):
            S.op("pe", lambda e, gs=gs, part=part: e.matmul(pk[5][:, :], lhsT=W[:, :], rhs=gs[:, :], start=(part == 0), stop=(part == 1)),
                 reads=[bgs, b_W], writes=[b_pk[5]])
        S.op("act", lambda e: e.activation(out=EbT[:, :, :], in_=pk[4][:, :].rearrange("p (h i) -> p h i", i=128), func=AF.Exp),
             reads=[b_pk[4]], writes=[b_EbT])
        S.op("act", lambda e: e.activation(out=EnbT[:, :, :], in_=pk[4][:, :].rearrange("p (h i) -> p h i", i=128), func=AF.Exp, scale=-1.0),
             reads=[b_pk[4]], writes=[b_EnbT])
        S.op("act", lambda e: e.activation(out=Ebl[:, :], in_=pk[5][:, :], func=AF.Exp), reads=[b_pk[5]], writes=[b_Ebl])
        dcs, bdc = dc_s[i2], b_dc[i2]
        S.op("pool", lambda e, dcs=dcs: e.tensor_copy(out=dcs[:, :, :], in_=EbT[:, :, :].rearrange("p h (c j) -> p h c j", j=64)[:, :, :, 63]),
             reads=[b_EbT], writes=[bdc])
        S.dma("pool", dec[:, :, 2 * t:2 * t + 2].rearrange("h d c -> d h c"), dcs[:, :, :], reads=[bdc])
        for hd in range(4):
            feat_mm(6, hd, hd * 128, 128)
        qgs, bqg = qg_s[i2], b_qg[i2]
        S.op("dve", lambda e, qgs=qgs: e.scalar_tensor_tensor(out=qgs[:, :, :], in0=pk[6][:, :].rearrange("p (h i) -> p h i", i=128),
                                                              scalar=QSCALE, in1=EbT[:, :, :], op0=ALU.mult, op1=ALU.mult),
             reads=[b_pk[6], b_EbT], writes=[bqg])
        S.dma("pool", qgT[:, :, t * 128:(t + 1) * 128].rearrange("h d i -> d h i"), qgs[:, :, :], reads=[bqg])
        for hd in range(4):
            feat_mm(3, hd, 512 + hd * 128, 128)
        kgs, bkg = kg_s[i2], b_kg[i2]
        S.op("dve", lambda e, kgs=kgs: e.tensor_tensor(out=kgs[:, :, :], in0=pk[3][:, :].rearrange("p (h i) -> p h i", i=128),
                                                       in1=EnbT[:, :, :], op=ALU.mult), reads=[b_pk[3], b_EnbT], writes=[bkg])
        S.dma("pool", kgT[:, :, t * 128:(t + 1) * 128].rearrange("h d i -> d h i"), kgs[:, :, :], reads=[bkg])
        tok_mm(2, 512, 1024)
        khs, bkh = kh_s[i2], b_kh[i2]
        S.op("dve", lambda e, khs=khs: e.tensor_tensor(out=khs[:, :], in0=pk[2][:, :], in1=Ebl[:, :], op=ALU.mult),
             reads=[b_pk[2], b_Ebl], writes=[bkh])
        S.dma("pool", kh[t * 128:(t + 1) * 128, :], khs[:, :], reads=[bkh])
        vbs, bvb = vb_s[i2], b_vb[i2]
        for half in range(2):
            tok_mm(half, 1024 + half * 512, 1536 + half * 512)
            S.op("act", lambda e, half=half, vbs=vbs: e.activation(out=vbs[:, half * 512:(half + 1) * 512], in_=pk[half][:, :], func=AF.Copy),
                 reads=[b_pk[half]], writes=[bvb])
        S.dma("pool", vb[t * 128:(t + 1) * 128, :], vbs[:, :], reads=[bvb])
        srs, bsr = sr_s[i2], b_sr[i2]
        for half in range(2):
            tok_mm(half, 2048 + half * 512, 2560 + half * 512)
            S.op("act", lambda e, half=half, srs=srs: e.activation(out=srs[:, half * 512:(half + 1) * 512], in_=pk[half][:, :], func=AF.Silu),
                 reads=[b_pk[half]], writes=[bsr])
        S.dma("pool", sr[t * 128:(t + 1) * 128, :], srs[:, :], reads=[bsr])

    for t in range(NTL):
        _tile(t, **sets[t % 2])

RMS_EPS = 1e-6
GLA_CB = 16


def stage_gla(S, C, SEQ, NH, acc, dec_ap, g_norm, tri, CB=GLA_CB):
    NBLK = SEQ // (64 * CB)
    NCH = SEQ // 64
    tri_f = S.sbuf("gl_trif", [64, 64], F32)
    b_tri = S.buf("gl_tri")
    S.dma("sp", tri_f[:, :], tri[:, :], writes=[b_tri])
    gn = S.sbuf("gl_gn", [64, 256], F32)
    b_gn = S.buf("gl_gn")
    S.dma("sp", gn[:, :], g_norm.partition_broadcast(64), writes=[b_gn])
    dec_sb = S.sbuf("gl_dec", [128, NH, NCH], F32)
    b_dec = S.buf("gl_dec")
    for hd in range(NH):
        S.dma("sp", dec_sb[:, hd, :], dec_ap(hd), writes=[b_dec])
    st = S.sbuf("gl_st", [128, NH, 256], F32)
    b_st = S.bufs(NH, "gl_st")
    stb = [S.sbuf("gl_stb%d" % i, [128, NH, 256], BF16) for i in range(2)]
    b_stb = [S.bufs(NH, "gl_stb%d_" % i) for i in range(2)]
    S.op("dve", lambda e: e.memset(st[:, :, :], 0.0), writes=b_st)
    S.op("dve", lambda e: e.memset(stb[0][:, :, :], 0.0), writes=b_stb[0])
    qg_sb = [[S.sbuf("gl_qg%d_%d" % (i, hd), [128, 64 * CB], BF16) for hd in range(NH)] for i in range(2)]
    kg_sb = [[S.sbuf("gl_kg%d_%d" % (i, hd), [128, 64 * CB], BF16) for hd in range(NH)] for i in range(2)]
    kh_sb = [[S.sbuf("gl_kh%d_%d" % (i, hd), [64, CB, 128], BF16) for hd in range(NH)] for i in range(2)]
    v_sb = [[S.sbuf("gl_v%d_%d" % (i, hd), [64, CB, 256], BF16) for hd in range(NH)] for i in range(2)]
    b_in = [[S.bufs(4, "gl_in%d_%d_" % (i, hd)) for hd in range(NH)] for i in range(2)]
    o_st = [[S.sbuf("gl_ost%d_%d" % (i, hd), [64, CB, 256], F32) for hd in range(NH)] for i in range(2)]
    b_ost = [[S.buf("gl_ost%d_%d" % (i, hd)) for hd in range(NH)] for i in range(2)]
    ss = [[S.sbuf("gl_ss%d_%d" % (i, hd), [64, CB], F32) for hd in range(NH)] for i in range(2)]
    b_ss = [[S.buf("gl_ss%d_%d" % (i, hd)) for hd in range(NH)] for i in range(2)]
    junk = S.sbuf("gl_junk", [64, 256], F32)
    b_junk = S.buf("gl_junk")
    A_sb = [S.sbuf("gl_A%d" % i, [64, 64], BF16) for i in range(4)]
    b_A = S.bufs(4, "gl_A")
    a_bank = S.psum("gl_aps", [64, 512], F32)
    a_ps = [a_bank[:, i * 64:(i + 1) * 64] for i in range(4)]
    b_aps = S.bufs(4, "gl_aps")
    o_bank = [S.psum("gl_ops%d" % i, [64, 512], F32) for i in range(4)]
    o_ps = [o_bank[i][:, 0:256] for i in range(4)]
    b_ops = S.bufs(4, "gl_ops")
    s_bank = [S.psum("gl_sps%d" % i, [128, 512], F32) for i in range(2)]
    s_ps = [s_bank[i // 2][:, (i % 2) * 256:(i % 2) * 256 + 256] for i in range(4)]
    b_sps = S.bufs(4, "gl_sps")
    n = 0
    for blk in range(NBLK):
        i2 = blk % 2
        for hd in range(NH):
            bi = b_in[i2][hd]
            S.dma("sp", qg_sb[i2][hd][:, :], acc("qg", hd, blk), writes=[bi[0]])
            S.dma("sp", kg_sb[i2][hd][:, :], acc("kg", hd, blk), writes=[bi[1]])
            S.dma("sp", kh_sb[i2][hd][:, :, :], acc("kh", hd, blk), writes=[bi[2]])
            S.dma("sp", v_sb[i2][hd][:, :, :], acc("v", hd, blk), writes=[bi[3]])
        for cc in range(CB):
            c = blk * CB + cc
            cur, nxt = c % 2, (c + 1) % 2
            for hd in range(NH):
                bi = b_in[i2][hd]
                qg = qg_sb[i2][hd][:, cc * 64:(cc + 1) * 64]
                kg = kg_sb[i2][hd][:, cc * 64:(cc + 1) * 64]
                khc = kh_sb[i2][hd][:, cc, :]
                vc = v_sb[i2][hd][:, cc, :]
                aps, baps = a_ps[n % 4], b_aps[n % 4]
                ops, bops = o_ps[n % 4], b_ops[n % 4]
                sps, bsps = s_ps[n % 4], b_sps[n % 4]
                A, bA = A_sb[n % 4], b_A[n % 4]
                n += 1
                S.op("pe", lambda e, aps=aps, kg=kg, qg=qg: e.matmul(aps, lhsT=kg, rhs=qg, start=True, stop=True, skip_group_check=True),
                     reads=[bi[0], bi[1]], writes=[baps])
                S.op("dve", lambda e, aps=aps, A=A: e.tensor_tensor(out=A[:, :], in0=aps, in1=tri_f[:, :], op=ALU.mult),
                     reads=[baps, b_tri], writes=[bA])
                S.op("pe", lambda e, ops=ops, A=A, vc=vc: e.matmul(ops, lhsT=A[:, :], rhs=vc, start=True, stop=False),
                     reads=[bA, bi[3]], writes=[bops])
                S.op("pe", lambda e, ops=ops, qg=qg, cur=cur, hd=hd: e.matmul(ops, lhsT=qg, rhs=stb[cur][:, hd, :],
                                                                             start=False, stop=True),
                     reads=[bi[0], b_stb[cur][hd]], writes=[bops])
                S.op("pe", lambda e, sps=sps, khc=khc, vc=vc: e.matmul(sps, lhsT=khc, rhs=vc, start=True, stop=True, skip_group_check=True),
                     reads=[bi[2], bi[3]], writes=[bsps])
                S.op("dve", lambda e, sps=sps, hd=hd, c=c: e.scalar_tensor_tensor(
                    out=st[:, hd, :], in0=st[:, hd, :], scalar=dec_sb[:, hd, c:c + 1], in1=sps,
                    op0=ALU.mult, op1=ALU.add), reads=[bsps, b_st[hd], b_dec], writes=[b_st[hd]])
                S.op("act", lambda e, hd=hd, nxt=nxt: e.activation(out=stb[nxt][:, hd, :], in_=st[:, hd, :], func=AF.Copy),
                     reads=[b_st[hd]], writes=[b_stb[nxt][hd]])
                ssc = ss[i2][hd][:, cc:cc + 1]
                ostc = o_st[i2][hd][:, cc, :]
                S.op("act", lambda e, ops=ops, ssc=ssc: e.activation(out=junk[:, :], in_=ops, func=AF.Square, accum_out=ssc),
                     reads=[bops], writes=[b_junk, b_ss[i2][hd]])
                S.op("act", lambda e, ops=ops, ostc=ostc: e.activation(out=ostc, in_=ops, func=AF.Copy),
                     reads=[bops], writes=[b_ost[i2][hd]])
        for hd in range(NH):
            s_, bs_ = ss[i2][hd], b_ss[i2][hd]
            S.op("dve", lambda e, s_=s_: e.tensor_scalar(out=s_[:, :], in0=s_[:, :], scalar1=1.0 / 256, scalar2=RMS_EPS,
                                                         op0=ALU.mult, op1=ALU.add), reads=[bs_], writes=[bs_])
            S.op("act", lambda e, s_=s_: e.activation(out=s_[:, :], in_=s_[:, :], func=AF.Sqrt), reads=[bs_], writes=[bs_])
            S.op("dve", lambda e, s_=s_: e.reciprocal(out=s_[:, :], in_=s_[:, :]), reads=[bs_], writes=[bs_])
            ot, bo = o_st[i2][hd], b_ost[i2][hd]
            S.op("dve", lambda e, ot=ot, s_=s_: e.tensor_tensor(out=ot[:, :, :], in0=ot[:, :, :],
                                                               in1=s_[:, :].unsqueeze(2).to_broadcast([64, CB, 256]), op=ALU.mult),
                 reads=[bo, bs_], writes=[bo])
            S.op("pool", lambda e, ot=ot: e.tensor_tensor(out=ot[:, :, :], in0=ot[:, :, :],
                                                         in1=gn[:, :].unsqueeze(1).to_broadcast([64, CB, 256]), op=ALU.mult),
                 reads=[bo, b_gn], writes=[bo])
            S.dma("pool", acc("on", hd, blk), ot[:, :, :], reads=[bo])


_NP2DT = {np.dtype("float32"): F32, np.dtype("int32"): I32}
try:
    import ml_dtypes
    _NP2DT[np.dtype(ml_dtypes.bfloat16)] = BF16
    NPBF16 = ml_dtypes.bfloat16
except Exception:
    NPBF16 = None

_PROG_CACHE = {}


def launch(key, prog, in_list, outs):
    sig = (key, tuple((k, v.shape, str(v.dtype)) for k, v in in_list[0].items()), tuple((k, tuple(sh), str(dt)) for k, (sh, dt) in outs.items()))
    if sig not in _PROG_CACHE:
        nc = bass.Bass("TRN2", target_bir_lowering=False)
        aps = {}
        for k, v in in_list[0].items():
            aps[k] = nc.dram_tensor(k, list(v.shape), _NP2DT[v.dtype], kind="ExternalInput").ap()
        for k, (shape, dt) in outs.items():
            aps[k] = nc.dram_tensor(k, list(shape), dt, kind="ExternalOutput").ap()
        S = Sched(nc)
        prog(S, aps)
        S.emit()
        _PROG_CACHE[sig] = nc
    nc = _PROG_CACHE[sig]
    res = run_bass_kernel_spmd(nc, in_list, core_ids=list(range(len(in_list))))
    return res.results


def _consts():
    ident = np.eye(128, dtype=np.float32)
    tri = np.where(np.arange(128)[None, :] <= np.arange(128)[:, None], 0.0, NEG_MASK).astype(np.float32)
    allm = np.full((128, 128), NEG_MASK, np.float32)
    none = np.zeros((128, 128), np.float32)
    maskc = [np.concatenate([tri, allm], 1), np.concatenate([none, tri], 1)]
    negI = (-MASK_BIG * np.eye(128)).astype(np.float32)
    invf = (500000.0 ** (-np.arange(0, 16, 2, dtype=np.float32) / 16)).astype(np.float32)
    j = np.arange(128)[:, None]
    i = np.arange(128)[None, :]
    same = (j // 64) == (i // 64)
    cmU = (same & (j <= i)).astype(np.float32)
    cmW = (same & (j > i)).astype(np.float32)
    tri64 = (np.arange(64)[:, None] <= np.arange(64)[None, :]).astype(np.float32)
    return dict(ident=ident, maskc=maskc, negI=negI, invf=invf, cmU=cmU, cmW=cmW, tri64=tri64)


def kernel_unfused(x, positions, a_w_in, a_w_o, b_w_in, b_w_a2, b_b_a, b_g_norm, b_w_o,
           ln_mix_g, ln_mix_b, mlp_w_up, mlp_w_down, ln_mlp_g, ln_mlp_b):
    x = np.asarray(x, np.float32)
    B, SEQ, _ = x.shape
    NC = 2 * B
    T = SEQ // 2
    NTL = T // 128
    NQ = NTL
    cst = _consts()
    f32 = lambda a: np.ascontiguousarray(np.asarray(a, np.float32))
    tok = []
    for c in range(NC):
        r = c % 2
        tiles = np.arange(r, SEQ // 128, 2)
        tok.append((tiles[:, None] * 128 + np.arange(128)[None, :]).reshape(-1))
    h = [np.ascontiguousarray(x[c // 2][tok[c]]) for c in range(NC)]
    pos_pt = [np.ascontiguousarray(np.asarray(positions)[c // 2][tok[c]].astype(np.int32).reshape(NTL, 128).T) for c in range(NC)]
    depth = ln_mix_g.shape[0]
    for i in range(depth):
        j = i // 2
        if i % 2 == 0:
            w_in = f32(a_w_in[j])
            ins = [dict(h=h[c], pos=pos_pt[c], w=w_in, invf=cst["invf"], ident=cst["ident"]) for c in range(NC)]
            outs = {"qT": ([1024, T], BF16), "kT": ([256, T], BF16), "v": ([T, 256], BF16), "iqT": ([512, T], BF16),
                    "ikT": ([64, T], BF16), "iw": ([T, 8], F32)}

            def prog(S, a):
                C = Consts(S, a["ident"])
                stage_proj_A(S, C, T, a["h"], a["pos"], a["w"], a["invf"], a["qT"], a["kT"], a["v"], a["iqT"], a["ikT"], a["iw"])
            pr = launch("projA", prog, ins, outs)
            ins = []
            for c in range(NC):
                b0 = (c // 2) * 2
                kT = np.empty((256, SEQ), NPBF16)
                ikT = np.empty((64, SEQ), NPBF16)
                vf = np.empty((SEQ, 256), NPBF16)
                for r in range(2):
                    kT[:, tok[b0 + r]] = pr[b0 + r]["kT"]
                    ikT[:, tok[b0 + r]] = pr[b0 + r]["ikT"]
                    vf[tok[b0 + r]] = pr[b0 + r]["v"]
                ins.append(dict(ident=cst["ident"], kT=kT, v=vf, ikT=ikT, qT=pr[c]["qT"], iqT=pr[c]["iqT"], iw=pr[c]["iw"],
                                maskc=cst["maskc"][c % 2], negI=cst["negI"]))

            def prog(S, a):
                C = Consts(S, a["ident"])
                stage_attn(S, C, NQ, a["kT"].rearrange("(g d) s -> g d s", d=64), a["v"], a["ikT"],
                           a["qT"].rearrange("(h d) t -> h d t", d=64), a["iqT"].rearrange("(h d) t -> h d t", d=64),
                           a["iw"], a["maskc"], a["negI"], a["o"])
            ar = launch("attn", prog, ins, {"o": ([T, 1024], BF16)})
            w_o = f32(a_w_o[j])
            ins = [dict(ident=cst["ident"], o=ar[c]["o"], h=h[c], w=w_o, g=f32(ln_mix_g[i]), b=f32(ln_mix_b[i])) for c in range(NC)]

            def prog(S, a):
                C = Consts(S, a["ident"])
                stage_wo_ln(S, C, T, a["o"], a["h"], a["w"], a["g"], a["b"], a["h1"])
            wr = launch("wo", prog, ins, {"h1": ([T, D], F32)})
        else:
            ins = [dict(h=h[c], w=f32(b_w_in[j]), wa2=f32(b_w_a2[j]), ba=f32(b_b_a[j]), U=cst["cmU"], W=cst["cmW"], ident=cst["ident"])
                   for c in range(NC)]
            outs = {"qgT": ([512, T], BF16), "kgT": ([512, T], BF16), "kh": ([T, 512], BF16), "vb": ([T, 1024], BF16),
                    "dec": ([512, T // 64], F32), "sr": ([T, 1024], F32)}

            def prog(S, a):
                C = Consts(S, a["ident"])
                stage_proj_B(S, C, T, a["h"], a["w"], a["wa2"], a["ba"], a["U"], a["W"],
                             a["qgT"].rearrange("(h d) t -> h d t", d=128), a["kgT"].rearrange("(h d) t -> h d t", d=128),
                             a["kh"], a["vb"], a["dec"].rearrange("(h d) c -> h d c", d=128), a["sr"])
            pr = launch("projB", prog, ins, outs)
            ins = []
            ctok = [t_.reshape(-1, 128)[:, ::64].reshape(-1) // 64 for t_ in tok]
            for c in range(NC):
                b0 = (c // 2) * 2
                hs = slice((c % 2) * 256, (c % 2) * 256 + 256)
                vs = slice((c % 2) * 512, (c % 2) * 512 + 512)
                qg = np.empty((256, SEQ), NPBF16)
                kg = np.empty((256, SEQ), NPBF16)
                khh = np.empty((SEQ, 256), NPBF16)
                vv = np.empty((SEQ, 512), NPBF16)
                dd = np.empty((256, SEQ // 64), np.float32)
                for r in range(2):
                    qg[:, tok[b0 + r]] = pr[b0 + r]["qgT"][hs]
                    kg[:, tok[b0 + r]] = pr[b0 + r]["kgT"][hs]
                    khh[tok[b0 + r]] = pr[b0 + r]["kh"][:, hs]
                    vv[tok[b0 + r]] = pr[b0 + r]["vb"][:, vs]
                    dd[:, ctok[b0 + r]] = pr[b0 + r]["dec"][hs]
                ins.append(dict(ident=cst["ident"], qgT=qg, kgT=kg, kh=khh, vb=vv, dec=dd, gn=f32(b_g_norm[j]), tri=cst["tri64"]))

            def prog(S, a):
                C = Consts(S, a["ident"])

                def acc(kind, hd, blk):
                    t0, t1 = blk * 1024, (blk + 1) * 1024
                    if kind == "qg":
                        return a["qgT"][hd * 128:(hd + 1) * 128, t0:t1]
                    if kind == "kg":
                        return a["kgT"][hd * 128:(hd + 1) * 128, t0:t1]
                    if kind == "kh":
                        return a["kh"][t0:t1, hd * 128:(hd + 1) * 128].rearrange("(c j) d -> j c d", j=64)
                    if kind == "v":
                        return a["vb"][t0:t1, hd * 256:(hd + 1) * 256].rearrange("(c j) e -> j c e", j=64)
                    return a["on"][t0:t1, hd * 256:(hd + 1) * 256].rearrange("(c j) e -> j c e", j=64)
                stage_gla(S, C, SEQ, 2, acc, lambda hd: a["dec"][hd * 128:(hd + 1) * 128, :], a["gn"], a["tri"])
            gr = launch("gla", prog, ins, {"on": ([SEQ, 512], F32)})
            w_o = f32(b_w_o[j])
            ins = []
            for c in range(NC):
                b0 = (c // 2) * 2
                on = np.concatenate([gr[b0]["on"][tok[c]], gr[b0 + 1]["on"][tok[c]]], axis=1)
                ins.append(dict(ident=cst["ident"], on=np.ascontiguousarray(on), sr=pr[c]["sr"], h=h[c], w=w_o,
                                g=f32(ln_mix_g[i]), b=f32(ln_mix_b[i])))

            def prog(S, a):
                C = Consts(S, a["ident"])
                stage_wo_ln(S, C, T, None, a["h"], a["w"], a["g"], a["b"], a["h1"], gate=(a["on"], a["sr"]))
            wr = launch("wog", prog, ins, {"h1": ([T, D], F32)})
        ins = [dict(ident=cst["ident"], h1=wr[c]["h1"], wu=f32(mlp_w_up[i]), wd=f32(mlp_w_down[i]), g=f32(ln_mlp_g[i]), b=f32(ln_mlp_b[i]))
               for c in range(NC)]

        def prog(S, a):
            C = Consts(S, a["ident"])
            stage_mlp(S, C, T, a["h1"], a["wu"], a["wd"], a["g"], a["b"], a["h2"])
        mr = launch("mlp", prog, ins, {"h2": ([T, D], F32)})
        h = [mr[c]["h2"] for c in range(NC)]
    out = np.empty((B, SEQ, D), np.float32)
    for c in range(NC):
        out[c // 2][tok[c]] = h[c]
    return out


def build_fused(SEQ, depth):
    nc = bass.Bass("TRN2", target_bir_lowering=False)
    T = SEQ
    NTL = T // 128

    def din(name, shape, dt=F32):
        return nc.dram_tensor(name, list(shape), dt, kind="ExternalInput").ap()

    def scr(name, shape, dt):
        return nc.dram_tensor("scr_" + name, list(shape), dt).ap()

    NA, NB = (depth + 1) // 2, depth // 2
    a = dict(
        x=din("x", [T, D]), pos=din("pos", [128, NTL], I32), invf=din("invf", [8]), ident=din("ident", [128, 128]),
        maskc=din("maskc", [128, 256]), negI=din("negI", [128, 128]), cmU=din("cmU", [128, 128]), cmW=din("cmW", [128, 128]),
        tri64=din("tri64", [64, 64]),
        a_w_in=din("a_w_in", [NA, D, A_IN]), a_w_o=din("a_w_o", [NA, D, D]),
        b_w_in=din("b_w_in", [max(NB, 1), D, B_IN]), b_w_a2=din("b_w_a2", [max(NB, 1), 16, 512]), b_b_a=din("b_b_a", [max(NB, 1), 512]),
        b_g_norm=din("b_g_norm", [max(NB, 1), 256]), b_w_o=din("b_w_o", [max(NB, 1), D, D]),
        ln_mix_g=din("ln_mix_g", [depth, D]), ln_mix_b=din("ln_mix_b", [depth, D]),
        mlp_w_up=din("mlp_w_up", [depth, D, DFF]), mlp_w_down=din("mlp_w_down", [depth, DFF, D]),
        ln_mlp_g=din("ln_mlp_g", [depth, D]), ln_mlp_b=din("ln_mlp_b", [depth, D]),
    )
    out = nc.dram_tensor("out", [T, D], F32, kind="ExternalOutput").ap()
    hA = scr("hA", [T, D], F32)
    hB = scr("hB", [T, D], F32)
    qT = scr("qT", [1024, T], BF16)
    kT = scr("kT", [256, T], BF16)
    vv = scr("v", [T, 256], BF16)
    iqT = scr("iqT", [512, T], BF16)
    ikT = scr("ikT", [64, T], BF16)
    iw = scr("iw", [T, 8], F32)
    o = scr("o", [T, 1024], BF16)
    qgT = scr("qgT", [512, T], BF16)
    kgT = scr("kgT", [512, T], BF16)
    kh = scr("kh", [T, 512], BF16)
    vb = scr("vb", [T, 1024], BF16)
    dec = scr("dec", [512, T // 64], F32)
    sr = scr("sr", [T, 1024], F32)
    on = scr("on", [T, 1024], F32)

    S = Sched(nc)
    CB = 8
    h_in = a["x"]
    for i in range(depth):
        j = i // 2
        last = i == depth - 1
        if i % 2 == 0:
            S.stage_begin()
            C = Consts(S, a["ident"])
            stage_proj_A(S, C, T, h_in, a["pos"], a["a_w_in"][j], a["invf"], qT, kT, vv, iqT, ikT, iw)
            S.stage_end()
            S.stage_begin()
            C = Consts(S, a["ident"])
            stage_attn(S, C, NTL, kT.rearrange("(g d) s -> g d s", d=64), vv, ikT, qT.rearrange("(h d) t -> h d t", d=64),
                       iqT.rearrange("(h d) t -> h d t", d=64), iw, a["maskc"], a["negI"], o, solo=True)
            S.stage_end()
            S.stage_begin()
            C = Consts(S, a["ident"])
            stage_wo_ln(S, C, T, o, h_in, a["a_w_o"][j], a["ln_mix_g"][i], a["ln_mix_b"][i], hB)
            S.stage_end()
        else:
            S.stage_begin()
            C = Consts(S, a["ident"])
            stage_proj_B(S, C, T, h_in, a["b_w_in"][j], a["b_w_a2"][j], a["b_b_a"][j], a["cmU"], a["cmW"],
                         qgT.rearrange("(h d) t -> h d t", d=128), kgT.rearrange("(h d) t -> h d t", d=128), kh, vb,
                         dec.rearrange("(h d) c -> h d c", d=128), sr)
            S.stage_end()
            S.stage_begin()
            C = Consts(S, a["ident"])

            def acc(kind, hd, blk):
                t0, t1 = blk * 64 * CB, (blk + 1) * 64 * CB
                if kind == "qg":
                    return qgT[hd * 128:(hd + 1) * 128, t0:t1]
                if kind == "kg":
                    return kgT[hd * 128:(hd + 1) * 128, t0:t1]
                if kind == "kh":
                    return kh[t0:t1, hd * 128:(hd + 1) * 128].rearrange("(c j) d -> j c d", j=64)
                if kind == "v":
                    return vb[t0:t1, hd * 256:(hd + 1) * 256].rearrange("(c j) e -> j c e", j=64)
                return on[t0:t1, hd * 256:(hd + 1) * 256].rearrange("(c j) e -> j c e", j=64)
            stage_gla(S, C, SEQ, 4, acc, lambda hd: dec[hd * 128:(hd + 1) * 128, :], a["b_g_norm"][j], a["tri64"], CB=CB)
            S.stage_end()
            S.stage_begin()
            C = Consts(S, a["ident"])
            stage_wo_ln(S, C, T, None, h_in, a["b_w_o"][j], a["ln_mix_g"][i], a["ln_mix_b"][i], hB, gate=(on, sr))
            S.stage_end()
        S.stage_begin()
        C = Consts(S, a["ident"])
        h_out = out if last else hA
        stage_mlp(S, C, T, hB, a["mlp_w_up"][i], a["mlp_w_down"][i], a["ln_mlp_g"][i], a["ln_mlp_b"][i], h_out)
        S.stage_end(last=last)
        h_in = hA
    S.stack.close()
    return nc


_FUSED = {}


def kernel(x, positions, a_w_in, a_w_o, b_w_in, b_w_a2, b_b_a, b_g_norm, b_w_o,
           ln_mix_g, ln_mix_b, mlp_w_up, mlp_w_down, ln_mlp_g, ln_mlp_b):
    x = np.asarray(x, np.float32)
    B, SEQ, _ = x.shape
    depth = int(np.asarray(ln_mix_g).shape[0])
    key = (SEQ, depth)
    if key not in _FUSED:
        _FUSED[key] = build_fused(SEQ, depth)
    nc = _FUSED[key]
    cst = _consts()
    f32 = lambda t: np.ascontiguousarray(np.asarray(t, np.float32))
    shared = dict(invf=cst["invf"], ident=cst["ident"], maskc=cst["maskc"][1], negI=cst["negI"], cmU=cst["cmU"], cmW=cst["cmW"],
                  tri64=cst["tri64"], a_w_in=f32(a_w_in), a_w_o=f32(a_w_o), b_w_in=f32(b_w_in), b_w_a2=f32(b_w_a2), b_b_a=f32(b_b_a),
                  b_g_norm=f32(b_g_norm), b_w_o=f32(b_w_o), ln_mix_g=f32(ln_mix_g), ln_mix_b=f32(ln_mix_b),
                  mlp_w_up=f32(mlp_w_up), mlp_w_down=f32(mlp_w_down), ln_mlp_g=f32(ln_mlp_g), ln_mlp_b=f32(ln_mlp_b))
    in_maps = []
    for c in range(B):
        pos_pt = np.ascontiguousarray(np.asarray(positions)[c].astype(np.int32).reshape(SEQ // 128, 128).T)
        m = dict(shared)
        m["x"] = np.ascontiguousarray(x[c])
        m["pos"] = pos_pt
        in_maps.append(m)
    res = run_bass_kernel_spmd(nc, in_maps, core_ids=list(range(B)))
    return np.stack([res.results[c]["out"] for c in range(B)], axis=0)
```

```python
import contextlib
import numpy as np
import concourse.bass as bass
import concourse.mybir as mybir
from concourse.ap import AP
from concourse.bass_utils import run_bass_kernel_spmd

F32 = mybir.dt.float32
BF16 = mybir.dt.bfloat16
I32 = mybir.dt.int32
AF = mybir.ActivationFunctionType
ALU = mybir.AluOpType
AX = mybir.AxisListType

D = 1024
DFF = 4096
DEPTH = 4
ALPHA = (2 * DEPTH) ** 0.25
LN_EPS = 1e-5
NCORES = 8


class Buf:
    __slots__ = ("name", "last_w", "readers")

    def __init__(self, name):
        self.name = name
        self.last_w = None
        self.readers = {}


class Sched:
    ENGS = ("sp", "act", "dve", "pool", "pe")
    SAME_WIN = 3
    NDMA = 8

    def __init__(self, nc):
        self.nc = nc
        self.stack = contextlib.ExitStack()
        self.ops = {e: [] for e in self.ENGS}
        self.count = {e: 0 for e in self.ENGS}
        self.seen = {e: {} for e in self.ENGS}
        self.esem = {}
        for e in ("act", "dve", "pool", "pe"):
            self.esem[e] = self.stack.enter_context(nc.semaphore("s_" + e))
        self.dsem = {}
        self.dma_i = {}
        for q in ("sp", "act", "pool"):
            self.dsem[q] = [self.stack.enter_context(nc.semaphore("d_%s%d" % (q, i))) for i in range(self.NDMA)]
            self.dma_i[q] = 0
        self.nbuf = 0
        self.stage_stack = None
        self.stage_no = 0
        self.bar = self.stack.enter_context(nc.semaphore("s_bar"))

    def sbuf(self, name, shape, dt):
        st = self.stage_stack if self.stage_stack is not None else self.stack
        return st.enter_context(self.nc.sbuf_tensor("sb%d_" % self.stage_no + name, list(shape), dt))

    def psum(self, name, shape, dt):
        st = self.stage_stack if self.stage_stack is not None else self.stack
        return st.enter_context(self.nc.psum_tensor("pp%d_" % self.stage_no + name, list(shape), dt))

    def stage_begin(self):
        self.stage_stack = contextlib.ExitStack()

    def stage_end(self, last=False):
        self.finish()
        if not last:
            self.stage_no += 1
            n = self.stage_no
            bar = self.bar
            self.ops["sp"].append(([], (lambda e, bar=bar: e.sem_inc(bar, 1)), None, 0))
            for eng in ("act", "dve", "pool", "pe"):
                self.ops[eng].append(([(bar, n)], None, None, 0))
        self._emit_block()
        self.stage_stack.close()
        self.stage_stack = None

    def buf(self, name=None):
        self.nbuf += 1
        return Buf(name or ("b%d" % self.nbuf))

    def bufs(self, n, name="b"):
        return [self.buf("%s%d" % (name, i)) for i in range(n)]

    def _deps(self, reads, writes):
        raw = {}
        other = {}
        for b in reads:
            if b.last_w is not None:
                k, v = b.last_w
                raw[k] = max(raw.get(k, 0), v)
        for b in writes:
            if b.last_w is not None:
                k, v = b.last_w
                other[k] = max(other.get(k, 0), v)
            for k, v in b.readers.items():
                other[k] = max(other.get(k, 0), v)
        return raw, other

    def _commit(self, ev, reads, writes):
        k, v = ev
        for b in reads:
            b.readers[k] = max(b.readers.get(k, 0), v)
        for b in writes:
            b.last_w = ev
            b.readers = {}

    def op(self, eng, fn, reads=(), writes=()):
        raw, other = self._deps(reads, writes)
        own = self.esem[eng]
        waits = {}
        seen = self.seen[eng]
        for d, is_raw in ((raw, True), (other, False)):
            for k, v in d.items():
                if k is own:
                    if eng == "pe" or not is_raw:
                        continue
                    if v <= self.count[eng] - self.SAME_WIN:
                        continue
                if seen.get(k, 0) >= v:
                    continue
                waits[k] = max(waits.get(k, 0), v)
        for k, v in waits.items():
            seen[k] = v
        self.count[eng] += 1
        ev = (own, self.count[eng])
        self._commit(ev, reads, writes)
        self.ops[eng].append((list(waits.items()), fn, own, 1))

    def dma(self, q, out, in_, reads=(), writes=(), fn=None, **kw):
        raw, other = self._deps(reads, writes)
        waits = {}
        seen = self.seen[q]
        for d in (raw, other):
            for k, v in d.items():
                if seen.get(k, 0) >= v:
                    continue
                waits[k] = max(waits.get(k, 0), v)
        i = self.dma_i[q]
        self.dma_i[q] = i + 1
        slot = self.dsem[q][i % self.NDMA]
        target = 16 * (i // self.NDMA + 1)
        if target > 16 and seen.get(slot, 0) < target - 16:
            waits[slot] = max(waits.get(slot, 0), target - 16)
        for k, v in waits.items():
            seen[k] = v
        ev = (slot, target)
        self._commit(ev, reads, writes)
        if fn is None:
            fn = lambda e, out=out, in_=in_, kw=kw: e.dma_start(out=out, in_=in_, **kw)
        self.ops[q].append((list(waits.items()), fn, slot, 16))

    def allgather(self, out, in_, groups, reads=(), writes=()):
        fn = lambda e: e.collective_compute("AllGather", ALU.bypass, replica_groups=groups, ins=[in_], outs=[out])
        self.dma("pool", None, None, reads=reads, writes=writes, fn=fn)

    def finish(self):
        waits = []
        for q in ("sp", "act", "pool"):
            n = self.dma_i[q]
            for s in range(min(n, self.NDMA)):
                cnt = (n - 1 - s) // self.NDMA + 1
                waits.append((self.dsem[q][s], 16 * cnt))
        for e in ("act", "dve", "pool", "pe"):
            if self.count[e]:
                waits.append((self.esem[e], self.count[e]))
        self.ops["sp"].append((waits, None, None, 0))

    def emit(self):
        self.finish()
        self._emit_block()
        self.stack.close()

    def _emit_block(self):
        nc = self.nc
        with nc.Block() as block:
            deco = {"sp": block.sync, "act": block.scalar, "dve": block.vector,
                    "pool": block.gpsimd, "pe": block.tensor}
            for eng in self.ENGS:
                ops = self.ops[eng]
                if not ops:
                    continue

                def body(e, ops=ops):
                    for waits, fn, sem, inc in ops:
                        for s, v in waits:
                            e.wait_ge(s, v)
                        if fn is not None:
                            ins = fn(e)
                            if sem is not None:
                                ins.then_inc(sem, inc)

                deco[eng](body)
        self.ops = {e: [] for e in self.ENGS}


def bcast_rows(ap_row, nparts):
    return ap_row.partition_broadcast(nparts)


class Consts:
    def __init__(self, S, ident_dram):
        self.ident = S.sbuf("ident", [128, 128], BF16)
        self.b_ident = S.buf("ident")
        S.dma("pool", self.ident[:], ident_dram[:, :], writes=[self.b_ident])


def load_bcast(S, name, row_ap, n):
    t = S.sbuf(name, [128, n], F32)
    b = S.buf(name)
    S.dma("sp", t[:], row_ap.partition_broadcast(128), writes=[b])
    return t, b


def layer_norm_tile(S, u, b_u, out, b_out, g_t, b_g, bt_t, b_bt, scr, eng2="pool"):
    st, b_st = scr["st"], scr["b_st"]
    mv, b_mv = scr["mv"], scr["b_mv"]
    for c in range(2):
        S.op("dve", lambda e, c=c: e.bn_stats(out=st[:, c, :], in_=u[:, c * 512:(c + 1) * 512]),
             reads=[b_u], writes=[b_st[c]])
    S.op("dve", lambda e: e.bn_aggr(out=mv[:, 0:2], in_=st[:, :, :]), reads=b_st, writes=[b_mv[0]])
    S.op("dve", lambda e: e.tensor_scalar(out=mv[:, 2:3], in0=mv[:, 1:2], scalar1=LN_EPS, scalar2=None,
                                           op0=ALU.add), reads=[b_mv[0]], writes=[b_mv[1]])
    S.op("act", lambda e: e.activation(out=mv[:, 2:3], in_=mv[:, 2:3], func=AF.Sqrt), reads=[b_mv[1]], writes=[b_mv[1]])
    S.op("dve", lambda e: e.reciprocal(out=mv[:, 2:3], in_=mv[:, 2:3]), reads=[b_mv[1]], writes=[b_mv[1]])
    S.op("dve", lambda e: e.scalar_tensor_tensor(out=mv[:, 3:4], in0=mv[:, 0:1], scalar=-1.0, in1=mv[:, 2:3],
                                                  op0=ALU.mult, op1=ALU.mult), reads=[b_mv[0], b_mv[1]], writes=[b_mv[2]])
    S.op("act", lambda e: e.activation(out=u[:, :], in_=u[:, :], func=AF.Identity, bias=mv[:, 3:4], scale=mv[:, 2:3]),
         reads=[b_u, b_mv[1], b_mv[2]], writes=[b_u])
    S.op(eng2, lambda e: e.tensor_tensor(out=u[:, :], in0=u[:, :], in1=g_t[:, :], op=ALU.mult),
         reads=[b_u, b_g], writes=[b_u])
    S.op(eng2, lambda e: e.tensor_tensor(out=out[:, :], in0=u[:, :], in1=bt_t[:, :], op=ALU.add),
         reads=[b_u, b_bt], writes=[b_out])


def ln_scratch(S, name):
    return {"st": S.sbuf(name + "_st", [128, 2, 6], F32), "b_st": S.bufs(2, name + "st"),
            "mv": S.sbuf(name + "_mv", [128, 4], F32), "b_mv": S.bufs(3, name + "mv")}


def stage_mlp(S, C, T, h1, w_up, w_down, g, b, h2):
    NT = 256
    NS = NT // 128
    wup = S.sbuf("wup", [128, 8, DFF], BF16)
    wdn = S.sbuf("wdn", [128, 32, D], BF16)
    b_wup = S.bufs(8, "wup")
    b_wdn = S.bufs(8, "wdn")
    wu_v = w_up.rearrange("(c p) f -> p c f", p=128)
    wd_v = w_down.rearrange("(c p) n -> p c n", p=128)
    for c in range(8):
        S.dma("pool", wup[:, c, :], wu_v[:, c, :], writes=[b_wup[c]])
    for c in range(8):
        S.dma("pool", wdn[:, 4 * c:4 * c + 4, :], wd_v[:, 4 * c:4 * c + 4, :], writes=[b_wdn[c]])
    g_t, b_g = load_bcast(S, "mlp_g", g, D)
    bt_t, b_bt = load_bcast(S, "mlp_b", b, D)

    NB = 2
    x_sb = [S.sbuf("x_sb%d" % i, [128, NS, D], F32) for i in range(NB)]
    b_x = [S.bufs(NS, "x%d_" % i) for i in range(NB)]
    xb = S.sbuf("xb", [128, NS, D], BF16)
    b_xb = S.bufs(NS, "xb")
    xT = S.sbuf("xT", [128, 8, NT], BF16)
    b_xT = S.bufs(8, "xT")
    h2T = S.sbuf("h2T", [128, 32, NT], BF16)
    b_h2T = S.bufs(32, "h2T")
    r_sb = [S.sbuf("r_sb%d" % i, [128, NT], F32) for i in range(4)]
    b_r = S.bufs(4, "r")
    y_sb = [S.sbuf("y_sb%d" % i, [128, D], F32) for i in range(2)]
    b_y = S.bufs(2, "y")
    tp_ps = [S.psum("tp_ps%d" % i, [128, 1024], BF16) for i in range(2)]
    b_tp = S.bufs(2, "tp")
    up_ps = [S.psum("up_ps%d" % i, [128, 512], F32) for i in range(4)]
    b_up = S.bufs(4, "up")
    dn_ps = [S.psum("dn_ps%d" % i, [128, 512], F32) for i in range(2)]
    b_dn = S.bufs(2, "dn")
    scrs = [ln_scratch(S, "mlp%d" % i) for i in range(2)]

    h1v = h1.rearrange("(t s p) d -> t p s d", p=128, s=NS)
    h2v = h2.rearrange("(t s p) d -> t s p d", p=128, s=NS)
    n_up = 0
    n_dn = 0
    n_tp = 0
    n_y = 0
    for t in range(T // NT):
        xs, bx = x_sb[t % NB], b_x[t % NB]
        S.dma("sp", xs[:, :, :], h1v[t], writes=bx)
        for s in range(NS):
            S.op("act", lambda e, xs=xs, s=s: e.activation(out=xb[:, s, :], in_=xs[:, s, :], func=AF.Copy),
                 reads=[bx[s]], writes=[b_xb[s]])
        for c in range(8):
            tp, btp = tp_ps[n_tp % 2], b_tp[n_tp % 2]
            n_tp += 1
            for s in range(NS):
                S.op("pe", lambda e, tp=tp, s=s, c=c: e.transpose(out=tp[:, s * 128:(s + 1) * 128],
                                                                  in_=xb[:, s, c * 128:(c + 1) * 128],
                                                                  identity=C.ident[:]),
                     reads=[b_xb[s], C.b_ident], writes=[btp])
            S.op("dve", lambda e, tp=tp, c=c: e.tensor_copy(out=xT[:, c, :], in_=tp[:, 0:NT]),
                 reads=[btp], writes=[b_xT[c]])
        for fc in range(32):
            ps, bps = up_ps[n_up % 4], b_up[n_up % 4]
            r, br = r_sb[n_up % 4], b_r[n_up % 4]
            n_up += 1
            for c in range(8):
                S.op("pe", lambda e, ps=ps, c=c, fc=fc: e.matmul(ps[:, 0:NT], lhsT=wup[:, c, fc * 128:(fc + 1) * 128],
                                                                  rhs=xT[:, c, :], start=(c == 0), stop=(c == 7)),
                     reads=[b_wup[c], b_xT[c]], writes=[bps])
            S.op("act", lambda e, ps=ps, r=r: e.activation(out=r[:, :], in_=ps[:, 0:NT], func=AF.Relu),
                 reads=[bps], writes=[br])
            S.op("dve", lambda e, r=r, fc=fc: e.tensor_tensor(out=h2T[:, fc, :], in0=r[:, :], in1=r[:, :], op=ALU.mult),
                 reads=[br], writes=[b_h2T[fc]])
        for s in range(NS):
            for nh in range(2):
                ps, bps = dn_ps[n_dn % 2], b_dn[n_dn % 2]
                n_dn += 1
                for fc in range(32):
                    S.op("pe", lambda e, ps=ps, fc=fc, s=s, nh=nh: e.matmul(
                        ps[:, :], lhsT=h2T[:, fc, s * 128:(s + 1) * 128], rhs=wdn[:, fc, nh * 512:(nh + 1) * 512],
                        start=(fc == 0), stop=(fc == 31)),
                         reads=[b_h2T[fc], b_wdn[fc // 4]], writes=[bps])
                S.op("dve", lambda e, ps=ps, xs=xs, s=s, nh=nh: e.scalar_tensor_tensor(
                    out=xs[:, s, nh * 512:(nh + 1) * 512], in0=xs[:, s, nh * 512:(nh + 1) * 512], scalar=ALPHA,
                    in1=ps[:, :], op0=ALU.mult, op1=ALU.add),
                     reads=[bps, bx[s]], writes=[bx[s]])
            y, by = y_sb[n_y % 2], b_y[n_y % 2]
            n_y += 1
            layer_norm_tile(S, xs[:, s, :], bx[s], y, by, g_t, b_g, bt_t, b_bt, scrs[n_y % 2])
            S.dma("pool", h2v[t, s], y[:, :], reads=[by])


TWO_PI = float(2 * np.pi)
PI = float(np.pi)


def _range_reduce(S, x, bx, tmp_f, btf, tmp_i, bti):
    S.op("dve", lambda e: e.tensor_scalar(out=tmp_f, in0=x, scalar1=1.0 / TWO_PI, scalar2=None, op0=ALU.mult),
         reads=[bx], writes=[btf])
    S.op("dve", lambda e: e.tensor_copy(out=tmp_i, in_=tmp_f), reads=[btf], writes=[bti])
    S.op("dve", lambda e: e.tensor_copy(out=tmp_f, in_=tmp_i), reads=[bti], writes=[btf])
    S.op("dve", lambda e: e.scalar_tensor_tensor(out=x, in0=tmp_f, scalar=-TWO_PI, in1=x, op0=ALU.mult, op1=ALU.add),
         reads=[btf, bx], writes=[bx])
    S.op("dve", lambda e: e.tensor_scalar(out=tmp_f, in0=x, scalar1=PI, scalar2=-TWO_PI, op0=ALU.is_gt, op1=ALU.mult),
         reads=[bx], writes=[btf])
    S.op("dve", lambda e: e.tensor_tensor(out=x, in0=x, in1=tmp_f, op=ALU.add), reads=[btf, bx], writes=[bx])
    S.op("dve", lambda e: e.tensor_scalar(out=tmp_f, in0=x, scalar1=-PI, scalar2=TWO_PI, op0=ALU.is_lt, op1=ALU.mult),
         reads=[bx], writes=[btf])
    S.op("dve", lambda e: e.tensor_tensor(out=x, in0=x, in1=tmp_f, op=ALU.add), reads=[btf, bx], writes=[bx])


def rotary_tables(S, NTL, pos_pt, invf):
    posi = S.sbuf("posi", [128, NTL], I32)
    posf = S.sbuf("posf", [128, NTL], F32)
    invt = S.sbuf("invt", [128, 8], F32)
    cos_t = S.sbuf("cos_t", [128, NTL, 8], F32)
    sin_t = S.sbuf("sin_t", [128, NTL, 8], F32)
    tmpf = S.sbuf("rr_tf", [128, NTL, 8], F32)
    tmpi = S.sbuf("rr_ti", [128, NTL, 8], I32)
    b_pi, b_pf, b_inv, b_cos, b_sin, b_tf, b_ti = S.bufs(7, "rot")
    S.dma("sp", posi[:], pos_pt[:, :], writes=[b_pi])
    S.dma("sp", invt[:], invf.partition_broadcast(128), writes=[b_inv])
    S.op("dve", lambda e: e.tensor_copy(out=posf[:], in_=posi[:]), reads=[b_pi], writes=[b_pf])
    S.op("dve", lambda e: e.tensor_tensor(out=sin_t[:, :, :], in0=posf[:, :].unsqueeze(2).to_broadcast([128, NTL, 8]),
                                          in1=invt[:, :].unsqueeze(1).to_broadcast([128, NTL, 8]), op=ALU.mult),
         reads=[b_pf, b_inv], writes=[b_sin])
    S.op("dve", lambda e: e.tensor_scalar(out=cos_t[:, :, :], in0=sin_t[:, :, :], scalar1=PI / 2, scalar2=None, op0=ALU.add),
         reads=[b_sin], writes=[b_cos])
    _range_reduce(S, sin_t[:, :, :], b_sin, tmpf[:, :, :], b_tf, tmpi[:, :, :], b_ti)
    _range_reduce(S, cos_t[:, :, :], b_cos, tmpf[:, :, :], b_tf, tmpi[:, :, :], b_ti)
    S.op("act", lambda e: e.activation(out=sin_t[:, :, :], in_=sin_t[:, :, :], func=AF.Sin), reads=[b_sin], writes=[b_sin])
    S.op("act", lambda e: e.activation(out=cos_t[:, :, :], in_=cos_t[:, :, :], func=AF.Sin), reads=[b_cos], writes=[b_cos])
    return cos_t, b_cos, sin_t, b_sin


def apply_rotary(S, P, bP, h0, nh, cos_t, b_cos, sin_t, b_sin, t, tmp, btmp):
    Pv = P[:, h0 * 64:(h0 + nh) * 64].rearrange("p (h d) -> p h d", d=64)
    x1 = Pv[:, :, 0:8]
    x2 = Pv[:, :, 8:16]
    c = cos_t[:, t, :].unsqueeze(1).to_broadcast([128, nh, 8])
    s = sin_t[:, t, :].unsqueeze(1).to_broadcast([128, nh, 8])
    t1, t2, t3, t4 = (tmp[:, i, 0:nh, :] for i in range(4))
    bP = list(bP)
    rd = bP + [b_cos, b_sin]
    S.op("dve", lambda e: e.tensor_tensor(out=t1, in0=x1, in1=c, op=ALU.mult), reads=rd, writes=[btmp[0]])
    S.op("dve", lambda e: e.tensor_tensor(out=t2, in0=x2, in1=s, op=ALU.mult), reads=rd, writes=[btmp[1]])
    S.op("dve", lambda e: e.tensor_tensor(out=t3, in0=x2, in1=c, op=ALU.mult), reads=rd, writes=[btmp[2]])
    S.op("dve", lambda e: e.tensor_tensor(out=t4, in0=x1, in1=s, op=ALU.mult), reads=rd, writes=[btmp[3]])
    S.op("dve", lambda e: e.tensor_tensor(out=x1, in0=t1, in1=t2, op=ALU.subtract), reads=[btmp[0], btmp[1], btmp[3]], writes=bP)
    S.op("dve", lambda e: e.tensor_tensor(out=x2, in0=t3, in1=t4, op=ALU.add), reads=[btmp[2], btmp[3]], writes=bP)


A_IN = 2120
IW_SCALE = float(8 ** -0.5 * 64 ** -0.5)


def stage_proj_A(S, C, T, h, pos_pt, w_in, invf, qT, kT, v, iqT, ikT, iw):
    NTL = T // 128
    win = S.sbuf("win", [128, 8, A_IN], BF16)
    b_win = S.bufs(8, "win")
    wv = w_in.rearrange("(c p) n -> p c n", p=128)
    for c in range(8):
        S.dma("pool", win[:, c, :], wv[:, c, :], writes=[b_win[c]])
    cos_t, b_cos, sin_t, b_sin = rotary_tables(S, NTL, pos_pt, invf)

    hs = [S.sbuf("pa_hs%d" % i, [128, D], F32) for i in range(2)]
    b_hs = S.bufs(2, "pa_hs")
    hb_l = [S.sbuf("pa_hb%d" % i, [128, D], BF16) for i in range(2)]
    b_hb_l = S.bufs(2, "pa_hb")
    hT_l = [S.sbuf("pa_hT%d" % i, [128, 8, 128], BF16) for i in range(2)]
    b_hT_l = S.bufs(2, "pa_hT")
    P_l = [S.sbuf("pa_P%d" % i, [128, A_IN], F32) for i in range(2)]
    b_P_l = [S.bufs(5, "pa_P%d_" % i) for i in range(2)]
    Pb_l = [S.sbuf("pa_Pb%d" % i, [128, A_IN], BF16) for i in range(2)]
    b_Pb_l = [S.bufs(4, "pa_Pb%d_" % i) for i in range(2)]
    iws = [S.sbuf("pa_iw%d" % i, [128, 8], F32) for i in range(2)]
    b_iws = S.bufs(2, "pa_iw")
    rt_l = [S.sbuf("pa_rt%d" % i, [128, 4, 20, 8], F32) for i in range(2)]
    b_rt_l = [S.bufs(4, "pa_rt%d_" % i) for i in range(2)]
    qTs = [S.sbuf("pa_qT%d" % i, [128, 8, 128], BF16) for i in range(2)]
    b_qTs = S.bufs(2, "pa_qT")
    kiTs = [S.sbuf("pa_kiT%d" % i, [128, 7, 128], BF16) for i in range(2)]
    b_kiTs = S.bufs(2, "pa_kiT")
    pj = [S.psum("pa_pj%d" % i, [128, 512], F32) for i in range(5)]
    b_pj = S.bufs(5, "pa_pj")
    tpA = S.psum("pa_tpA", [128, 1024], BF16)
    tpB = S.psum("pa_tpB", [128, 1024], BF16)
    b_tpA, b_tpB = S.bufs(2, "pa_tp")
    chunks = [(0, 512), (512, 1024), (1024, 1536), (1536, 2048), (2048, A_IN)]

    hv = h.rearrange("(t p) d -> t p d", p=128)
    qTv = qT.rearrange("(c p) t -> p c t", p=128)
    kTv = kT.rearrange("(c p) t -> p c t", p=128)
    iqTv = iqT.rearrange("(c p) t -> p c t", p=128)
    def _tile(t, hb, b_hb, hT, b_hT, P, b_P, Pb, b_Pbs, rt, b_rt):
        b_Pq, b_Pk, b_Pv, b_Pi = b_Pbs
        x, bx = hs[t % 2], b_hs[t % 2]
        S.dma("sp", x[:, :], hv[t], writes=[bx])
        S.op("act", lambda e, x=x: e.activation(out=hb[:, :], in_=x[:, :], func=AF.Copy), reads=[bx], writes=[b_hb])
        for c in range(8):
            S.op("pe", lambda e, c=c: e.transpose(out=tpA[:, c * 128:(c + 1) * 128], in_=hb[:, c * 128:(c + 1) * 128],
                                                  identity=C.ident[:]), reads=[b_hb, C.b_ident], writes=[b_tpA])
        S.op("dve", lambda e: e.tensor_copy(out=hT[:, :, :], in_=tpA[:, :].rearrange("p (c t) -> p c t", t=128)),
             reads=[b_tpA], writes=[b_hT])
        for i, (n0, n1) in enumerate(chunks):
            for c in range(8):
                S.op("pe", lambda e, i=i, c=c, n0=n0, n1=n1: e.matmul(pj[i][:, 0:n1 - n0], lhsT=hT[:, c, :], rhs=win[:, c, n0:n1],
                                                                      start=(c == 0), stop=(c == 7)),
                     reads=[b_hT, b_win[c]], writes=[b_pj[i]])
            S.op("act", lambda e, i=i, n0=n0, n1=n1: e.activation(out=P[:, n0:n1], in_=pj[i][:, 0:n1 - n0], func=AF.Copy),
                 reads=[b_pj[i]], writes=[b_P[i]])
        apply_rotary(S, P, b_P[0:3], 0, 20, cos_t, b_cos, sin_t, b_sin, t, rt, b_rt)
        apply_rotary(S, P, b_P[3:5], 24, 9, cos_t, b_cos, sin_t, b_sin, t, rt, b_rt)
        S.op("act", lambda e: e.activation(out=Pb[:, 0:1024], in_=P[:, 0:1024], func=AF.Copy, scale=0.125),
             reads=b_P[0:2], writes=[b_Pq])
        S.op("act", lambda e: e.activation(out=Pb[:, 1024:1536], in_=P[:, 1024:1536], func=AF.Copy),
             reads=[b_P[2]], writes=[b_Pk])
        S.op("act", lambda e: e.activation(out=Pb[:, 1536:2112], in_=P[:, 1536:2112], func=AF.Copy),
             reads=b_P[3:5], writes=[b_Pi])
        iwt, biw = iws[t % 2], b_iws[t % 2]
        S.op("act", lambda e, iwt=iwt: e.activation(out=iwt[:, :], in_=P[:, 2112:2120], func=AF.Copy, scale=IW_SCALE),
             reads=[b_P[4]], writes=[biw])
        S.dma("pool", iw[t * 128:(t + 1) * 128, :], iwt[:, :], reads=[biw])
        S.dma("pool", v[t * 128:(t + 1) * 128, :], Pb[:, 1280:1536], reads=[b_Pk])
        qs, bqs = qTs[t % 2], b_qTs[t % 2]
        ks, bks = kiTs[t % 2], b_kiTs[t % 2]
        for c in range(8):
            S.op("pe", lambda e, c=c: e.transpose(out=tpB[:, c * 128:(c + 1) * 128], in_=Pb[:, c * 128:(c + 1) * 128],
                                                  identity=C.ident[:]), reads=[b_Pq, C.b_ident], writes=[b_tpB])
        S.op("dve", lambda e, qs=qs: e.tensor_copy(out=qs[:, :, :], in_=tpB[:, :].rearrange("p (c t) -> p c t", t=128)),
             reads=[b_tpB], writes=[bqs])
        S.dma("pool", qTv[:, :, t * 128:(t + 1) * 128], qs[:, :, :], reads=[bqs])
        srcs = [(1024, 128, b_Pk), (1152, 128, b_Pk), (1536, 128, b_Pi), (1664, 128, b_Pi), (1792, 128, b_Pi),
                (1920, 128, b_Pi), (2048, 64, b_Pi)]
        for j, (c0, w, bsrc) in enumerate(srcs):
            S.op("pe", lambda e, j=j, c0=c0, w=w: e.transpose(out=tpA[0:w, j * 128:(j + 1) * 128], in_=Pb[:, c0:c0 + w],
                                                              identity=C.ident[:]), reads=[bsrc, C.b_ident], writes=[b_tpA])
        S.op("dve", lambda e, ks=ks: e.tensor_copy(out=ks[:, 0:6, :], in_=tpA[:, 0:768].rearrange("p (c t) -> p c t", t=128)),
             reads=[b_tpA], writes=[bks])
        S.op("dve", lambda e, ks=ks: e.tensor_copy(out=ks[0:64, 6, :], in_=tpA[0:64, 768:896]),
             reads=[b_tpA], writes=[bks])
        S.dma("pool", kTv[:, :, t * 128:(t + 1) * 128], ks[:, 0:2, :], reads=[bks])
        S.dma("pool", iqTv[:, :, t * 128:(t + 1) * 128], ks[:, 2:6, :], reads=[bks])
        S.dma("pool", ikT[:, t * 128:(t + 1) * 128], ks[0:64, 6, :], reads=[bks])

    for t in range(NTL):
        _tile(t, hb_l[t % 2], b_hb_l[t % 2], hT_l[t % 2], b_hT_l[t % 2], P_l[t % 2], b_P_l[t % 2], Pb_l[t % 2], b_Pb_l[t % 2], rt_l[t % 2], b_rt_l[t % 2])

TOPK = 256
NEG_MASK = -3.0e38
NEG_LO = -1.0e30
N_BISECT = 16
MASK_BIG = 32768.0


def stage_attn(S, C, NQ, kT, v, ikT, qT, iqT, iw, maskc, negI, o, dbg=None, solo=False):
    NKTT = NQ if solo else 2 * NQ
    SK = NKTT * 128
    nkt_of = (lambda j: j + 1) if solo else (lambda j: 2 * j + 2)
    kT_sb = S.sbuf("at_kT", [128, 2, SK], BF16)
    v_sb = S.sbuf("at_v", [128, NKTT, 4, 65], BF16)
    ik_sb = S.sbuf("at_ik", [128, SK], BF16)
    b_kT, b_v, b_ik = S.bufs(3, "at_res")
    for hh in range(2):
        S.dma("sp", kT_sb[hh * 64:(hh + 1) * 64, :, :], kT[2 * hh:2 * hh + 2, :, :].rearrange("g d s -> d g s"), writes=[b_kT])
    S.op("pool", lambda e: e.memset(v_sb[:, :, :, 64:65], 1.0), writes=[b_v])
    vv = v.rearrange("(k p) (g d) -> p k g d", p=128, d=64)
    KCH = 16
    for k0 in range(0, NKTT, KCH):
        k1 = min(NKTT, k0 + KCH)
        for g in range(4):
            S.dma("sp", v_sb[:, k0:k1, g, 0:64], vv[:, k0:k1, g, :], writes=[b_v])
    S.op("pool", lambda e: e.memset(ik_sb[64:128, :], 0.0), writes=[b_ik])
    S.dma("sp", ik_sb[0:64, :], ikT[:, :], writes=[b_ik])
    mc = S.sbuf("at_mc", [128, 256], F32)
    nI = S.sbuf("at_nI", [128, 4, 128], BF16)
    b_mc, b_nI = S.bufs(2, "at_c")
    S.dma("sp", mc[:, :], maskc[:, :], writes=[b_mc])
    for r in range(4):
        S.dma("pool", nI[:, r, :], negI[:, :], writes=[b_nI])

    sc = S.sbuf("at_sc", [128, SK], F32)
    b_sc = S.buf("at_sc")
    junk = S.sbuf("at_junk", [128, SK], BF16)
    b_junk = S.buf("at_junk")
    mb = [S.sbuf("at_mb%d" % i, [128, SK], BF16) for i in range(2)]
    b_mb = S.bufs(2, "at_mb")
    NT_SB = 4
    NS_PS = 2
    t_sb = [S.sbuf("at_t%d" % i, [128, 512], F32) for i in range(NT_SB)]
    b_t = S.bufs(NT_SB, "at_t")
    NPT = 4
    NLP = 3
    PT = [S.sbuf("at_PT%d" % i, [128, 512], BF16) for i in range(NPT)]
    b_PT = S.bufs(NPT, "at_PT")
    q_sb = [S.sbuf("at_q%d" % i, [128, 4, 4, 128], BF16) for i in range(2)]
    b_q = S.bufs(2, "at_q")
    for i in range(2):
        S.op("pool", lambda e, i=i: e.memset(q_sb[i][:, :, :, :], 0.0), writes=[b_q[i]])
    iq_sb = [S.sbuf("at_iq%d" % i, [128, 8, 128], BF16) for i in range(2)]
    b_iq = S.bufs(2, "at_iq")
    for i in range(2):
        S.op("pool", lambda e, i=i: e.memset(iq_sb[i][64:128, :, :], 0.0), writes=[b_iq[i]])
    iw_sb = [S.sbuf("at_iw%d" % i, [128, 8], F32) for i in range(2)]
    b_iw = S.bufs(2, "at_iw")
    o_sb = [S.sbuf("at_o%d" % i, [128, 16, 64], BF16) for i in range(2)]
    b_o = S.bufs(2, "at_o")
    sm = S.sbuf("at_sm", [128, 16], F32)
    b_sm = S.bufs(8, "at_sm")
    rc = S.sbuf("at_rc", [128, 16], F32)
    b_rc = S.buf("at_rc")
    s_ps = [S.psum("at_sps%d" % i, [128, 512], F32) for i in range(NS_PS)]
    b_sps = S.bufs(NS_PS, "at_sps")
    KB = N_BISECT
    pow2 = S.sbuf("at_pow2", [128, KB + 1], F32)
    wtab = S.sbuf("at_wtab", [128, KB + 1], F32)
    b_pow2, b_wtab = S.bufs(2, "at_w")
    for k in range(KB + 1):
        S.op("pool", lambda e, k=k: e.memset(pow2[:, k:k + 1], float(2.0 ** -k)), writes=[b_pow2])
    l_ps = [S.psum("at_lps%d" % i, [128, 512], F32) for i in range(NLP)]
    b_lps = S.bufs(NLP, "at_lps")
    o_ps = [S.psum("at_ops%d" % i, [128, 7, 65], F32) for i in range(3)]
    b_ops = S.bufs(3, "at_ops")
    LO, HI, MID, CNT, GE, DD, MM = range(7)
    col = lambda i: sm[:, i:i + 1]

    n_s = 0
    n_l = 0
    n_pt = 0

    def load_q(j):
        qs, bq = q_sb[j % 2], b_q[j % 2]
        for hh in range(2):
            S.dma("sp", qs[hh * 64:(hh + 1) * 64, 2 * hh:2 * hh + 2, :, :].rearrange("d a r q -> d (a r) q"),
                  qT[8 * hh:8 * hh + 8, :, j * 128:(j + 1) * 128].rearrange("h d q -> d h q"), writes=[bq])
        S.dma("sp", iq_sb[j % 2][0:64, :, :], iqT[:, :, j * 128:(j + 1) * 128].rearrange("h d q -> d h q"), writes=[b_iq[j % 2]])
        S.dma("sp", iw_sb[j % 2][:, :], iw[j * 128:(j + 1) * 128, :], writes=[b_iw[j % 2]])

    def phase12(j):
        nonlocal n_s
        nk = nkt_of(j) * 128
        iqs, biq = iq_sb[j % 2], b_iq[j % 2]
        iws, biw = iw_sb[j % 2], b_iw[j % 2]
        for hd in range(8):
            for k0 in range(0, nk, 512):
                w = min(512, nk - k0)
                ps, bps = s_ps[n_s % NS_PS], b_sps[n_s % NS_PS]
                tt, bt = t_sb[n_s % NT_SB], b_t[n_s % NT_SB]
                n_s += 1
                S.op("pe", lambda e, ps=ps, hd=hd, k0=k0, w=w, iqs=iqs: e.matmul(
                    ps[:, 0:w], lhsT=iqs[:, hd, :], rhs=ik_sb[:, k0:k0 + w], start=True, stop=True),
                     reads=[biq, b_ik], writes=[bps])
                S.op("act", lambda e, ps=ps, tt=tt, w=w: e.activation(out=tt[:, 0:w], in_=ps[:, 0:w], func=AF.Relu),
                     reads=[bps], writes=[bt])
                if hd == 0:
                    S.op("dve", lambda e, tt=tt, k0=k0, w=w, iws=iws: e.tensor_scalar(
                        out=sc[:, k0:k0 + w], in0=tt[:, 0:w], scalar1=iws[:, 0:1], scalar2=None, op0=ALU.mult),
                         reads=[bt, biw], writes=[b_sc])
                else:
                    S.op("dve", lambda e, tt=tt, k0=k0, w=w, iws=iws, hd=hd: e.scalar_tensor_tensor(
                        out=sc[:, k0:k0 + w], in0=tt[:, 0:w], scalar=iws[:, hd:hd + 1], in1=sc[:, k0:k0 + w],
                        op0=ALU.mult, op1=ALU.add), reads=[bt, biw, b_sc], writes=[b_sc])
        S.op("dve", lambda e: e.tensor_reduce(out=col(MM), in_=sc[:, 0:nk], axis=AX.X, op=ALU.max, apply_absolute_value=True),
             reads=[b_sc], writes=[b_sm[MM]])
        S.op("dve", lambda e: e.tensor_scalar(out=col(HI), in0=col(MM), scalar1=1.0, scalar2=None, op0=ALU.add),
             reads=[b_sm[MM]], writes=[b_sm[HI]])
        S.op("dve", lambda e: e.tensor_scalar(out=wtab[:, :], in0=pow2[:, :], scalar1=col(HI), scalar2=None, op0=ALU.mult),
             reads=[b_sm[HI], b_pow2], writes=[b_wtab])
        if solo:
            S.op("dve", lambda e: e.tensor_tensor(out=sc[:, nk - 128:nk], in0=sc[:, nk - 128:nk], in1=mc[:, 128:256], op=ALU.add),
                 reads=[b_sc, b_mc], writes=[b_sc])
        else:
            S.op("dve", lambda e: e.tensor_tensor(out=sc[:, nk - 256:nk], in0=sc[:, nk - 256:nk], in1=mc[:, :], op=ALU.add),
                 reads=[b_sc, b_mc], writes=[b_sc])
        S.op("dve", lambda e: e.memset(col(MID), 0.0), writes=[b_sm[MID]])
        for it in range(KB):
            S.op("dve", lambda e: e.tensor_scalar(out=junk[:, 0:nk], in0=sc[:, 0:nk], scalar1=col(MID), scalar2=0.0,
                                                   op0=ALU.is_ge, op1=ALU.add, accum_out=col(CNT)),
                 reads=[b_sc, b_sm[MID]], writes=[b_junk, b_sm[CNT]])
            S.op("dve", lambda e, it=it: e.scalar_tensor_tensor(out=col(GE), in0=col(CNT), scalar=TOPK - 0.5, in1=wtab[:, it:it + 1],
                                                                op0=ALU.is_ge, op1=ALU.mult),
                 reads=[b_sm[CNT], b_wtab], writes=[b_sm[GE]])
            S.op("dve", lambda e, it=it: e.scalar_tensor_tensor(out=col(MID), in0=col(MID), scalar=wtab[:, it + 1:it + 2], in1=col(GE),
                                                                op0=ALU.subtract, op1=ALU.add),
                 reads=[b_sm[MID], b_sm[GE], b_wtab], writes=[b_sm[MID]])
        S.op("dve", lambda e: e.tensor_tensor(out=col(LO), in0=col(MID), in1=wtab[:, KB:KB + 1], op=ALU.subtract),
             reads=[b_sm[MID], b_wtab], writes=[b_sm[LO]])
        m, bm = mb[j % 2], b_mb[j % 2]
        S.op("dve", lambda e, m=m: e.tensor_scalar(out=m[:, 0:nk], in0=sc[:, 0:nk], scalar1=col(LO), scalar2=None, op0=ALU.is_lt),
             reads=[b_sc, b_sm[LO]], writes=[bm])
        if dbg is not None and j == dbg["j"]:
            S.dma("sp", dbg["sc"][:, 0:nk], sc[:, 0:nk], reads=[b_sc])
            S.dma("sp", dbg["mb"][:, 0:nk], m[:, 0:nk], reads=[bm])
            S.dma("sp", dbg["sm"][:, :], sm[:, :], reads=b_sm)

    def phase3(j):
        nonlocal n_l, n_pt
        NKT = nkt_of(j)
        qs, bq = q_sb[j % 2], b_q[j % 2]
        m, bm = mb[j % 2], b_mb[j % 2]
        DEPTH = 2
        pend = []

        def emit_pv(item):
            kt, g, pt, bpt = item
            for r in range(4):
                hd = 4 * g + r
                bank, slot = hd // 7, hd % 7
                S.op("pe", lambda e, pt=pt, r=r, kt=kt, g=g, bank=bank, slot=slot: e.matmul(
                    o_ps[bank][:, slot, :], lhsT=pt[:, r * 128:(r + 1) * 128], rhs=v_sb[:, kt, g, :],
                    start=(kt == 0 and slot == 0), stop=(kt == NKT - 1), skip_group_check=True),
                     reads=[bpt, b_v], writes=[b_ops[bank]])

        for kt in range(NKT):
            for g in range(4):
                hh, a = g // 2, g % 2
                lp, blp = l_ps[n_l % NLP], b_lps[n_l % NLP]
                n_l += 1
                pt, bpt = PT[n_pt % NPT], b_PT[n_pt % NPT]
                n_pt += 1
                S.op("pe", lambda e, lp=lp, g=g, a=a, kt=kt, qs=qs: e.matmul(
                    lp[:, :], lhsT=kT_sb[:, a, kt * 128:(kt + 1) * 128],
                    rhs=qs[:, g, :, :].rearrange("d r q -> d (r q)"), start=True, stop=False),
                     reads=[b_kT, bq], writes=[blp])
                S.op("pe", lambda e, lp=lp, kt=kt, m=m: e.matmul(
                    lp[:, :], lhsT=m[:, kt * 128:(kt + 1) * 128], rhs=nI[:, :, :].rearrange("p r q -> p (r q)"),
                    start=False, stop=True), reads=[bm, b_nI], writes=[blp])
                S.op("act", lambda e, lp=lp, pt=pt: e.activation(out=pt[:, :], in_=lp[:, :], func=AF.Exp),
                     reads=[blp], writes=[bpt])
                pend.append((kt, g, pt, bpt))
                if len(pend) > DEPTH:
                    emit_pv(pend.pop(0))
        while pend:
            emit_pv(pend.pop(0))
        ob, bo = o_sb[j % 2], b_o[j % 2]
        for bank in range(3):
            nh = min(7, 16 - 7 * bank)
            S.op("dve", lambda e, bank=bank, nh=nh: e.reciprocal(out=rc[:, 0:nh], in_=o_ps[bank][:, 0:nh, 64]),
                 reads=[b_ops[bank]], writes=[b_rc])
            S.op("dve", lambda e, bank=bank, nh=nh, ob=ob: e.tensor_tensor(
                out=ob[:, 7 * bank:7 * bank + nh, :], in0=o_ps[bank][:, 0:nh, 0:64],
                in1=rc[:, 0:nh].unsqueeze(2).to_broadcast([128, nh, 64]), op=ALU.mult),
                 reads=[b_ops[bank], b_rc], writes=[bo])
        S.dma("pool", o[j * 128:(j + 1) * 128, :], ob[:, :, :].rearrange("p h d -> p (h d)"), reads=[bo])

    load_q(0)
    phase12(0)
    for j in range(NQ):
        if j + 1 < NQ:
            load_q(j + 1)
            phase12(j + 1)
        phase3(j)


def stage_wo_ln(S, C, T, o, h, w_o, g, b, h1, gate=None):
    wo = S.sbuf("wo", [128, 8, D], BF16)
    b_wo = S.bufs(8, "wo")
    wv = w_o.rearrange("(c p) n -> p c n", p=128)
    for c in range(8):
        S.dma("pool", wo[:, c, :], wv[:, c, :], writes=[b_wo[c]])
    g_t, b_g = load_bcast(S, "wo_g", g, D)
    bt_t, b_bt = load_bcast(S, "wo_b", b, D)
    ob = [S.sbuf("wo_ob%d" % i, [128, D], BF16) for i in range(2)]
    b_ob = S.bufs(2, "wo_ob")
    hs = [S.sbuf("wo_hs%d" % i, [128, D], F32) for i in range(2)]
    b_hs = S.bufs(2, "wo_hs")
    oTs = [S.sbuf("wo_oT%d" % i, [128, 8, 128], BF16) for i in range(2)]
    b_oTs = S.bufs(2, "wo_oT")
    y_sb = [S.sbuf("wo_y%d" % i, [128, D], F32) for i in range(2)]
    b_y = S.bufs(2, "wo_y")
    tps = [S.psum("wo_tp%d" % i, [128, 1024], BF16) for i in range(2)]
    b_tps = S.bufs(2, "wo_tp")
    mx = [S.psum("wo_mx%d" % i, [128, 512], F32) for i in range(4)]
    b_mx = S.bufs(4, "wo_mx")
    scrs = [ln_scratch(S, "wo%d" % i) for i in range(2)]
    ov = o.rearrange("(t p) d -> t p d", p=128) if gate is None else None
    if gate is not None:
        gate_a = [S.sbuf("wo_ga%d" % i, [128, D], F32) for i in range(2)]
        gate_b = [S.sbuf("wo_gb%d" % i, [128, D], F32) for i in range(2)]
        b_ga = S.bufs(2, "wo_ga")
        b_gb = S.bufs(2, "wo_gb")
    hv = h.rearrange("(t p) d -> t p d", p=128)
    h1v = h1.rearrange("(t p) d -> t p d", p=128)
    for t in range(T // 128):
        x, bx = ob[t % 2], b_ob[t % 2]
        hh, bh = hs[t % 2], b_hs[t % 2]
        if gate is None:
            S.dma("sp", x[:, :], ov[t], writes=[bx])
        else:
            ga, gb_ = gate_a[t % 2], gate_b[t % 2]
            S.dma("sp", ga[:, :], gate[0].rearrange("(t p) d -> t p d", p=128)[t], writes=[b_ga[t % 2]])
            S.dma("sp", gb_[:, :], gate[1].rearrange("(t p) d -> t p d", p=128)[t], writes=[b_gb[t % 2]])
            S.op("pool", lambda e, x=x, ga=ga, gb_=gb_: e.tensor_tensor(out=x[:, :], in0=ga[:, :], in1=gb_[:, :], op=ALU.mult),
                 reads=[b_ga[t % 2], b_gb[t % 2]], writes=[bx])
        S.dma("act", hh[:, :], hv[t], writes=[bh])
        tp, b_tp = tps[t % 2], b_tps[t % 2]
        oT, b_oT = oTs[t % 2], b_oTs[t % 2]
        scr = scrs[t % 2]
        for c in range(8):
            S.op("pe", lambda e, c=c, x=x, tp=tp: e.transpose(out=tp[:, c * 128:(c + 1) * 128], in_=x[:, c * 128:(c + 1) * 128],
                                                       identity=C.ident[:]), reads=[bx, C.b_ident], writes=[b_tp])
        S.op("act", lambda e, oT=oT, tp=tp: e.activation(out=oT[:, :, :], in_=tp[:, :].rearrange("p (c t) -> p c t", t=128), func=AF.Copy),
             reads=[b_tp], writes=[b_oT])
        for nh in range(2):
            ps, bps = mx[(2 * t + nh) % 4], b_mx[(2 * t + nh) % 4]
            for c in range(8):
                S.op("pe", lambda e, ps=ps, c=c, nh=nh, oT=oT: e.matmul(ps[:, :], lhsT=oT[:, c, :], rhs=wo[:, c, nh * 512:(nh + 1) * 512],
                                                                  start=(c == 0), stop=(c == 7)),
                     reads=[b_oT, b_wo[c]], writes=[bps])
            S.op("dve", lambda e, ps=ps, hh=hh, nh=nh: e.scalar_tensor_tensor(
                out=hh[:, nh * 512:(nh + 1) * 512], in0=hh[:, nh * 512:(nh + 1) * 512], scalar=ALPHA, in1=ps[:, :],
                op0=ALU.mult, op1=ALU.add), reads=[bps, bh], writes=[bh])
        y, by = y_sb[t % 2], b_y[t % 2]
        layer_norm_tile(S, hh, bh, y, by, g_t, b_g, bt_t, b_bt, scr)
        S.dma("pool", h1v[t], y[:, :], reads=[by])


B_IN = 3088
GATE_TAU = 16.0


def stage_proj_B(S, C, T, h, w_in, w_a2, b_a, cmU, cmW, qgT, kgT, kh, vb, dec, sr):
    NTL = T // 128
    win = S.sbuf("pb_win", [128, 8, B_IN], BF16)
    b_win = S.bufs(8, "pb_win")
    wv = w_in.rearrange("(c p) n -> p c n", p=128)
    for c in range(8):
        S.dma("pool", win[:, c, :], wv[:, c, :], writes=[b_win[c]])
    wa2 = S.sbuf("pb_wa2", [16, 512], BF16)
    U = S.sbuf("pb_U", [128, 128], BF16)
    W = S.sbuf("pb_W", [128, 128], BF16)
    b_wa2, b_U, b_W = S.bufs(3, "pb_c")
    S.dma("pool", wa2[:, :], w_a2[:, :], writes=[b_wa2])
    S.dma("pool", U[:, :], cmU[:, :], writes=[b_U])
    S.dma("pool", W[:, :], cmW[:, :], writes=[b_W])
    ba_t, b_ba = load_bcast(S, "pb_ba", b_a, 512)

    hs = [S.sbuf("pb_hs%d" % i, [128, D], F32) for i in range(2)]
    b_hs = S.bufs(2, "pb_hs")
    def _mk(i):
        return dict(hb=S.sbuf("pb_hb%d" % i, [128, D], BF16), b_hb=S.buf("pb_hb"), hT=S.sbuf("pb_hT%d" % i, [128, 8, 128], BF16), b_hT=S.buf("pb_hT"),
                    alT=S.sbuf("pb_alT%d" % i, [16, 128], BF16), b_alT=S.buf("pb_alT"), gg=S.sbuf("pb_g%d" % i, [128, 512], F32), b_gg=S.buf("pb_g"),
                    ghi=S.sbuf("pb_ghi%d" % i, [128, 512], BF16), glo=S.sbuf("pb_glo%d" % i, [128, 512], BF16), b_ghi=S.buf("pb_ghi"), b_glo=S.buf("pb_glo"),
                    EbT=S.sbuf("pb_EbT%d" % i, [128, 4, 128], F32), EnbT=S.sbuf("pb_EnbT%d" % i, [128, 4, 128], F32), Ebl=S.sbuf("pb_Ebl%d" % i, [128, 512], F32),
                    b_EbT=S.buf("pb_EbT"), b_EnbT=S.buf("pb_EnbT"), b_Ebl=S.buf("pb_Ebl"))
    sets = [_mk(0), _mk(1)]
    qg_s = [S.sbuf("pb_qg%d" % i, [128, 4, 128], BF16) for i in range(2)]
    kg_s = [S.sbuf("pb_kg%d" % i, [128, 4, 128], BF16) for i in range(2)]
    kh_s = [S.sbuf("pb_kh%d" % i, [128, 512], BF16) for i in range(2)]
    vb_s = [S.sbuf("pb_vb%d" % i, [128, 1024], BF16) for i in range(2)]
    sr_s = [S.sbuf("pb_sr%d" % i, [128, 1024], F32) for i in range(2)]
    dc_s = [S.sbuf("pb_dc%d" % i, [128, 4, 2], F32) for i in range(2)]
    b_qg, b_kg, b_kh, b_vb, b_sr, b_dc = (S.bufs(2, "pb_o%d" % i) for i in range(6))
    tpT = S.psum("pb_tp", [128, 1024], BF16)
    b_tpT = S.buf("pb_tp")
    pk = [S.psum("pb_pk%d" % i, [128, 512], F32) for i in range(7)]
    b_pk = S.bufs(7, "pb_pk")
    QSCALE = float(128 ** -0.5)

    hv = h.rearrange("(t p) d -> t p d", p=128)
    def _tile(t, hb, b_hb, hT, b_hT, alT, b_alT, gg, b_gg, ghi, glo, b_ghi, b_glo, EbT, EnbT, Ebl, b_EbT, b_EnbT, b_Ebl):
        x, bx = hs[t % 2], b_hs[t % 2]
        i2 = t % 2
        S.dma("sp", x[:, :], hv[t], writes=[bx])
        S.op("act", lambda e, x=x: e.activation(out=hb[:, :], in_=x[:, :], func=AF.Copy), reads=[bx], writes=[b_hb])
        for c in range(8):
            S.op("pe", lambda e, c=c: e.transpose(out=tpT[:, c * 128:(c + 1) * 128], in_=hb[:, c * 128:(c + 1) * 128],
                                                  identity=C.ident[:]), reads=[b_hb, C.b_ident], writes=[b_tpT])
        S.op("dve", lambda e: e.tensor_copy(out=hT[:, :, :], in_=tpT[:, :].rearrange("p (c t) -> p c t", t=128)),
             reads=[b_tpT], writes=[b_hT])

        def tok_mm(bank, n0, n1):
            for c in range(8):
                S.op("pe", lambda e, c=c: e.matmul(pk[bank][:, 0:n1 - n0], lhsT=hT[:, c, :], rhs=win[:, c, n0:n1],
                                                   start=(c == 0), stop=(c == 7)), reads=[b_hT, b_win[c]], writes=[b_pk[bank]])

        def feat_mm(bank, slot, n0, m):
            for c in range(8):
                S.op("pe", lambda e, c=c: e.matmul(pk[bank][0:m, slot * 128:(slot + 1) * 128], lhsT=win[:, c, n0:n0 + m],
                                                   rhs=hT[:, c, :], start=(c == 0 and slot == 0), stop=(c == 7),
                                                   skip_group_check=True), reads=[b_hT, b_win[c]], writes=[b_pk[bank]])

        feat_mm(6, 0, 3072, 16)
        S.op("act", lambda e: e.activation(out=alT[:, :], in_=pk[6][0:16, 0:128], func=AF.Copy), reads=[b_pk[6]], writes=[b_alT])
        S.op("pe", lambda e: e.matmul(pk[5][:, :], lhsT=alT[:, :], rhs=wa2[:, :], start=True, stop=True),
             reads=[b_alT, b_wa2], writes=[b_pk[5]])
        S.op("dve", lambda e: e.tensor_tensor(out=gg[:, :], in0=pk[5][:, :], in1=ba_t[:, :], op=ALU.add),
             reads=[b_pk[5], b_ba], writes=[b_gg])
        S.op("act", lambda e: e.activation(out=gg[:, :], in_=gg[:, :], func=AF.Exp, scale=-1.0), reads=[b_gg], writes=[b_gg])
        S.op("dve", lambda e: e.tensor_scalar(out=gg[:, :], in0=gg[:, :], scalar1=1.0, scalar2=None, op0=ALU.add),
             reads=[b_gg], writes=[b_gg])
        S.op("act", lambda e: e.activation(out=gg[:, :], in_=gg[:, :], func=AF.Ln), reads=[b_gg], writes=[b_gg])
        S.op("dve", lambda e: e.tensor_scalar(out=gg[:, :], in0=gg[:, :], scalar1=-1.0 / GATE_TAU, scalar2=None, op0=ALU.mult),
             reads=[b_gg], writes=[b_gg])
        S.op("dve", lambda e: e.tensor_copy(out=ghi[:, :], in_=gg[:, :]), reads=[b_gg], writes=[b_ghi])
        S.op("dve", lambda e: e.tensor_tensor(out=glo[:, :], in0=gg[:, :], in1=ghi[:, :], op=ALU.subtract),
             reads=[b_gg, b_ghi], writes=[b_glo])
        for hd in range(4):
            for part, (gs, bgs) in enumerate(((ghi, b_ghi), (glo, b_glo))):
                S.op("pe", lambda e, hd=hd, gs=gs, part=part: e.matmul(
                    pk[4][:, hd * 128:(hd + 1) * 128], lhsT=gs[:, hd * 128:(hd + 1) * 128], rhs=U[:, :],
                    start=(hd == 0 and part == 0), stop=(part == 1), skip_group_check=True),
                     reads=[bgs, b_U], writes=[b_pk[4]])
        for part, (gs, bgs) in enumerate(((ghi, b_ghi), (glo, b_glo))):
            S.op("pe", lambda e, gs=gs, part=part: e.matmul(pk[5][:, :], lhsT=W[:, :], rhs=gs[:, :], start=(part == 0), stop=(part == 1)),
                 reads=[bgs, b_W], writes=[b_pk[5]])
        S.op("act", lambda e: e.activation(out=EbT[:, :, :], in_=pk[4][:, :].rearrange("p (h i) -> p h i", i=128), func=AF.Exp),
             reads=[b_pk[4]], writes=[b_EbT])
        S.op("act", lambda e: e.activation(out=EnbT[:, :, :], in_=pk[4][:, :].rearrange("p (h i) -> p h i", i=128), func=AF.Exp, scale=-1.0),
             reads=[b_pk[4]], writes=[b_EnbT])
        S.op("act", lambda e: e.activation(out=Ebl[:, :], in_=pk[5][:, :], func=AF.Exp), reads=[b_pk[5]], writes=[b_Ebl])
        dcs, bdc = dc_s[i2], b_dc[i2]
        S.op("pool", lambda e, dcs=dcs: e.tensor_copy(out=dcs[:, :, :], in_=EbT[:, :, :].rearrange("p h (c j) -> p h c j", j=64)[:, :, :, 63]),
             reads=[b_EbT], writes=[bdc])
        S.dma("pool", dec[:, :, 2 * t:2 * t + 2].rearrange("h d c -> d h c"), dcs[:, :, :], reads=[bdc])
        for hd in range(4):
            feat_mm(6, hd, hd * 128, 128)
        qgs, bqg = qg_s[i2], b_qg[i2]
        S.op("dve", lambda e, qgs=qgs: e.scalar_tensor_tensor(out=qgs[:, :, :], in0=pk[6][:, :].rearrange("p (h i) -> p h i", i=128),
                                                              scalar=QSCALE, in1=EbT[:, :, :], op0=ALU.mult, op1=ALU.mult),
             reads=[b_pk[6], b_EbT], writes=[bqg])
        S.dma("pool", qgT[:, :, t * 128:(t + 1) * 128].rearrange("h d i -> d h i"), qgs[:, :, :], reads=[bqg])
        for hd in range(4):
            feat_mm(3, hd, 512 + hd * 128, 128)
        kgs, bkg = kg_s[i2], b_kg[i2]
        S.op("dve", lambda e, kgs=kgs: e.tensor_tensor(out=kgs[:, :, :], in0=pk[3][:, :].rearrange("p (h i) -> p h i", i=128),
                                                       in1=EnbT[:, :, :], op=ALU.mult), reads=[b_pk[3], b_EnbT], writes=[bkg])
        S.dma("pool", kgT[:, :, t * 128:(t + 1) * 128].rearrange("h d i -> d h i"), kgs[:, :, :], reads=[bkg])
        tok_mm(2, 512, 1024)
        khs, bkh = kh_s[i2], b_kh[i2]
        S.op("dve", lambda e, khs=khs: e.tensor_tensor(out=khs[:, :], in0=pk[2][:, :], in1=Ebl[:, :], op=ALU.mult),
             reads=[b_pk[2], b_Ebl], writes=[bkh])
        S.dma("pool", kh[t * 128:(t + 1) * 128, :], khs[:, :], reads=[bkh])
        vbs, bvb = vb_s[i2], b_vb[i2]
        for half in range(2):
            tok_mm(half, 1024 + half * 512, 1536 + half * 512)
            S.op("act", lambda e, half=half, vbs=vbs: e.activation(out=vbs[:, half * 512:(half + 1) * 512], in_=pk[half][:, :], func=AF.Copy),
                 reads=[b_pk[half]], writes=[bvb])
        S.dma("pool", vb[t * 128:(t + 1) * 128, :], vbs[:, :], reads=[bvb])
        srs, bsr = sr_s[i2], b_sr[i2]
        for half in range(2):
            tok_mm(half, 2048 + half * 512, 2560 + half * 512)
            S.op("act", lambda e, half=half, srs=srs: e.activation(out=srs[:, half * 512:(half + 1) * 512], in_=pk[half][:, :], func=AF.Silu),
                 reads=[b_pk[half]], writes=[bsr])
        S.dma("pool", sr[t * 128:(t + 1) * 128, :], srs[:, :], reads=[bsr])

    for t in range(NTL):
        _tile(t, **sets[t % 2])

RMS_EPS = 1e-6
GLA_CB = 16


def stage_gla(S, C, SEQ, NH, acc, dec_ap, g_norm, tri, CB=GLA_CB):
    NBLK = SEQ // (64 * CB)
    NCH = SEQ // 64
    tri_f = S.sbuf("gl_trif", [64, 64], F32)
    b_tri = S.buf("gl_tri")
    S.dma("sp", tri_f[:, :], tri[:, :], writes=[b_tri])
    gn = S.sbuf("gl_gn", [64, 256], F32)
    b_gn = S.buf("gl_gn")
    S.dma("sp", gn[:, :], g_norm.partition_broadcast(64), writes=[b_gn])
    dec_sb = S.sbuf("gl_dec", [128, NH, NCH], F32)
    b_dec = S.buf("gl_dec")
    for hd in range(NH):
        S.dma("sp", dec_sb[:, hd, :], dec_ap(hd), writes=[b_dec])
    st = S.sbuf("gl_st", [128, NH, 256], F32)
    b_st = S.bufs(NH, "gl_st")
    stb = [S.sbuf("gl_stb%d" % i, [128, NH, 256], BF16) for i in range(2)]
    b_stb = [S.bufs(NH, "gl_stb%d_" % i) for i in range(2)]
    S.op("dve", lambda e: e.memset(st[:, :, :], 0.0), writes=b_st)
    S.op("dve", lambda e: e.memset(stb[0][:, :, :], 0.0), writes=b_stb[0])
    qg_sb = [[S.sbuf("gl_qg%d_%d" % (i, hd), [128, 64 * CB], BF16) for hd in range(NH)] for i in range(2)]
    kg_sb = [[S.sbuf("gl_kg%d_%d" % (i, hd), [128, 64 * CB], BF16) for hd in range(NH)] for i in range(2)]
    kh_sb = [[S.sbuf("gl_kh%d_%d" % (i, hd), [64, CB, 128], BF16) for hd in range(NH)] for i in range(2)]
    v_sb = [[S.sbuf("gl_v%d_%d" % (i, hd), [64, CB, 256], BF16) for hd in range(NH)] for i in range(2)]
    b_in = [[S.bufs(4, "gl_in%d_%d_" % (i, hd)) for hd in range(NH)] for i in range(2)]
    o_st = [[S.sbuf("gl_ost%d_%d" % (i, hd), [64, CB, 256], F32) for hd in range(NH)] for i in range(2)]
    b_ost = [[S.buf("gl_ost%d_%d" % (i, hd)) for hd in range(NH)] for i in range(2)]
    ss = [[S.sbuf("gl_ss%d_%d" % (i, hd), [64, CB], F32) for hd in range(NH)] for i in range(2)]
    b_ss = [[S.buf("gl_ss%d_%d" % (i, hd)) for hd in range(NH)] for i in range(2)]
    junk = S.sbuf("gl_junk", [64, 256], F32)
    b_junk = S.buf("gl_junk")
    A_sb = [S.sbuf("gl_A%d" % i, [64, 64], BF16) for i in range(4)]
    b_A = S.bufs(4, "gl_A")
    a_bank = S.psum("gl_aps", [64, 512], F32)
    a_ps = [a_bank[:, i * 64:(i + 1) * 64] for i in range(4)]
    b_aps = S.bufs(4, "gl_aps")
    o_bank = [S.psum("gl_ops%d" % i, [64, 512], F32) for i in range(4)]
    o_ps = [o_bank[i][:, 0:256] for i in range(4)]
    b_ops = S.bufs(4, "gl_ops")
    s_bank = [S.psum("gl_sps%d" % i, [128, 512], F32) for i in range(2)]
    s_ps = [s_bank[i // 2][:, (i % 2) * 256:(i % 2) * 256 + 256] for i in range(4)]
    b_sps = S.bufs(4, "gl_sps")
    n = 0
    for blk in range(NBLK):
        i2 = blk % 2
        for hd in range(NH):
            bi = b_in[i2][hd]
            S.dma("sp", qg_sb[i2][hd][:, :], acc("qg", hd, blk), writes=[bi[0]])
            S.dma("sp", kg_sb[i2][hd][:, :], acc("kg", hd, blk), writes=[bi[1]])
            S.dma("sp", kh_sb[i2][hd][:, :, :], acc("kh", hd, blk), writes=[bi[2]])
            S.dma("sp", v_sb[i2][hd][:, :, :], acc("v", hd, blk), writes=[bi[3]])
        for cc in range(CB):
            c = blk * CB + cc
            cur, nxt = c % 2, (c + 1) % 2
            for hd in range(NH):
                bi = b_in[i2][hd]
                qg = qg_sb[i2][hd][:, cc * 64:(cc + 1) * 64]
                kg = kg_sb[i2][hd][:, cc * 64:(cc + 1) * 64]
                khc = kh_sb[i2][hd][:, cc, :]
                vc = v_sb[i2][hd][:, cc, :]
                aps, baps = a_ps[n % 4], b_aps[n % 4]
                ops, bops = o_ps[n % 4], b_ops[n % 4]
                sps, bsps = s_ps[n % 4], b_sps[n % 4]
                A, bA = A_sb[n % 4], b_A[n % 4]
                n += 1
                S.op("pe", lambda e, aps=aps, kg=kg, qg=qg: e.matmul(aps, lhsT=kg, rhs=qg, start=True, stop=True, skip_group_check=True),
                     reads=[bi[0], bi[1]], writes=[baps])
                S.op("dve", lambda e, aps=aps, A=A: e.tensor_tensor(out=A[:, :], in0=aps, in1=tri_f[:, :], op=ALU.mult),
                     reads=[baps, b_tri], writes=[bA])
                S.op("pe", lambda e, ops=ops, A=A, vc=vc: e.matmul(ops, lhsT=A[:, :], rhs=vc, start=True, stop=False),
                     reads=[bA, bi[3]], writes=[bops])
                S.op("pe", lambda e, ops=ops, qg=qg, cur=cur, hd=hd: e.matmul(ops, lhsT=qg, rhs=stb[cur][:, hd, :],
                                                                             start=False, stop=True),
                     reads=[bi[0], b_stb[cur][hd]], writes=[bops])
                S.op("pe", lambda e, sps=sps, khc=khc, vc=vc: e.matmul(sps, lhsT=khc, rhs=vc, start=True, stop=True, skip_group_check=True),
                     reads=[bi[2], bi[3]], writes=[bsps])
                S.op("dve", lambda e, sps=sps, hd=hd, c=c: e.scalar_tensor_tensor(
                    out=st[:, hd, :], in0=st[:, hd, :], scalar=dec_sb[:, hd, c:c + 1], in1=sps,
                    op0=ALU.mult, op1=ALU.add), reads=[bsps, b_st[hd], b_dec], writes=[b_st[hd]])
                S.op("act", lambda e, hd=hd, nxt=nxt: e.activation(out=stb[nxt][:, hd, :], in_=st[:, hd, :], func=AF.Copy),
                     reads=[b_st[hd]], writes=[b_stb[nxt][hd]])
                ssc = ss[i2][hd][:, cc:cc + 1]
                ostc = o_st[i2][hd][:, cc, :]
                S.op("act", lambda e, ops=ops, ssc=ssc: e.activation(out=junk[:, :], in_=ops, func=AF.Square, accum_out=ssc),
                     reads=[bops], writes=[b_junk, b_ss[i2][hd]])
                S.op("act", lambda e, ops=ops, ostc=ostc: e.activation(out=ostc, in_=ops, func=AF.Copy),
                     reads=[bops], writes=[b_ost[i2][hd]])
        for hd in range(NH):
            s_, bs_ = ss[i2][hd], b_ss[i2][hd]
            S.op("dve", lambda e, s_=s_: e.tensor_scalar(out=s_[:, :], in0=s_[:, :], scalar1=1.0 / 256, scalar2=RMS_EPS,
                                                         op0=ALU.mult, op1=ALU.add), reads=[bs_], writes=[bs_])
            S.op("act", lambda e, s_=s_: e.activation(out=s_[:, :], in_=s_[:, :], func=AF.Sqrt), reads=[bs_], writes=[bs_])
            S.op("dve", lambda e, s_=s_: e.reciprocal(out=s_[:, :], in_=s_[:, :]), reads=[bs_], writes=[bs_])
            ot, bo = o_st[i2][hd], b_ost[i2][hd]
            S.op("dve", lambda e, ot=ot, s_=s_: e.tensor_tensor(out=ot[:, :, :], in0=ot[:, :, :],
                                                               in1=s_[:, :].unsqueeze(2).to_broadcast([64, CB, 256]), op=ALU.mult),
                 reads=[bo, bs_], writes=[bo])
            S.op("pool", lambda e, ot=ot: e.tensor_tensor(out=ot[:, :, :], in0=ot[:, :, :],
                                                         in1=gn[:, :].unsqueeze(1).to_broadcast([64, CB, 256]), op=ALU.mult),
                 reads=[bo, b_gn], writes=[bo])
            S.dma("pool", acc("on", hd, blk), ot[:, :, :], reads=[bo])


_NP2DT = {np.dtype("float32"): F32, np.dtype("int32"): I32}
try:
    import ml_dtypes
    _NP2DT[np.dtype(ml_dtypes.bfloat16)] = BF16
    NPBF16 = ml_dtypes.bfloat16
except Exception:
    NPBF16 = None

_PROG_CACHE = {}


def launch(key, prog, in_list, outs):
    sig = (key, tuple((k, v.shape, str(v.dtype)) for k, v in in_list[0].items()), tuple((k, tuple(sh), str(dt)) for k, (sh, dt) in outs.items()))
    if sig not in _PROG_CACHE:
        nc = bass.Bass("TRN2", target_bir_lowering=False)
        aps = {}
        for k, v in in_list[0].items():
            aps[k] = nc.dram_tensor(k, list(v.shape), _NP2DT[v.dtype], kind="ExternalInput").ap()
        for k, (shape, dt) in outs.items():
            aps[k] = nc.dram_tensor(k, list(shape), dt, kind="ExternalOutput").ap()
        S = Sched(nc)
        prog(S, aps)
        S.emit()
        _PROG_CACHE[sig] = nc
    nc = _PROG_CACHE[sig]
    res = run_bass_kernel_spmd(nc, in_list, core_ids=list(range(len(in_list))))
    return res.results


def _consts():
    ident = np.eye(128, dtype=np.float32)
    tri = np.where(np.arange(128)[None, :] <= np.arange(128)[:, None], 0.0, NEG_MASK).astype(np.float32)
    allm = np.full((128, 128), NEG_MASK, np.float32)
    none = np.zeros((128, 128), np.float32)
    maskc = [np.concatenate([tri, allm], 1), np.concatenate([none, tri], 1)]
    negI = (-MASK_BIG * np.eye(128)).astype(np.float32)
    invf = (500000.0 ** (-np.arange(0, 16, 2, dtype=np.float32) / 16)).astype(np.float32)
    j = np.arange(128)[:, None]
    i = np.arange(128)[None, :]
    same = (j // 64) == (i // 64)
    cmU = (same & (j <= i)).astype(np.float32)
    cmW = (same & (j > i)).astype(np.float32)
    tri64 = (np.arange(64)[:, None] <= np.arange(64)[None, :]).astype(np.float32)
    return dict(ident=ident, maskc=maskc, negI=negI, invf=invf, cmU=cmU, cmW=cmW, tri64=tri64)


def kernel_unfused(x, positions, a_w_in, a_w_o, b_w_in, b_w_a2, b_b_a, b_g_norm, b_w_o,
           ln_mix_g, ln_mix_b, mlp_w_up, mlp_w_down, ln_mlp_g, ln_mlp_b):
    x = np.asarray(x, np.float32)
    B, SEQ, _ = x.shape
    NC = 2 * B
    T = SEQ // 2
    NTL = T // 128
    NQ = NTL
    cst = _consts()
    f32 = lambda a: np.ascontiguousarray(np.asarray(a, np.float32))
    tok = []
    for c in range(NC):
        r = c % 2
        tiles = np.arange(r, SEQ // 128, 2)
        tok.append((tiles[:, None] * 128 + np.arange(128)[None, :]).reshape(-1))
    h = [np.ascontiguousarray(x[c // 2][tok[c]]) for c in range(NC)]
    pos_pt = [np.ascontiguousarray(np.asarray(positions)[c // 2][tok[c]].astype(np.int32).reshape(NTL, 128).T) for c in range(NC)]
    depth = ln_mix_g.shape[0]
    for i in range(depth):
        j = i // 2
        if i % 2 == 0:
            w_in = f32(a_w_in[j])
            ins = [dict(h=h[c], pos=pos_pt[c], w=w_in, invf=cst["invf"], ident=cst["ident"]) for c in range(NC)]
            outs = {"qT": ([1024, T], BF16), "kT": ([256, T], BF16), "v": ([T, 256], BF16), "iqT": ([512, T], BF16),
                    "ikT": ([64, T], BF16), "iw": ([T, 8], F32)}

            def prog(S, a):
                C = Consts(S, a["ident"])
                stage_proj_A(S, C, T, a["h"], a["pos"], a["w"], a["invf"], a["qT"], a["kT"], a["v"], a["iqT"], a["ikT"], a["iw"])
            pr = launch("projA", prog, ins, outs)
            ins = []
            for c in range(NC):
                b0 = (c // 2) * 2
                kT = np.empty((256, SEQ), NPBF16)
                ikT = np.empty((64, SEQ), NPBF16)
                vf = np.empty((SEQ, 256), NPBF16)
                for r in range(2):
                    kT[:, tok[b0 + r]] = pr[b0 + r]["kT"]
                    ikT[:, tok[b0 + r]] = pr[b0 + r]["ikT"]
                    vf[tok[b0 + r]] = pr[b0 + r]["v"]
                ins.append(dict(ident=cst["ident"], kT=kT, v=vf, ikT=ikT, qT=pr[c]["qT"], iqT=pr[c]["iqT"], iw=pr[c]["iw"],
                                maskc=cst["maskc"][c % 2], negI=cst["negI"]))

            def prog(S, a):
                C = Consts(S, a["ident"])
                stage_attn(S, C, NQ, a["kT"].rearrange("(g d) s -> g d s", d=64), a["v"], a["ikT"],
                           a["qT"].rearrange("(h d) t -> h d t", d=64), a["iqT"].rearrange("(h d) t -> h d t", d=64),
                           a["iw"], a["maskc"], a["negI"], a["o"])
            ar = launch("attn", prog, ins, {"o": ([T, 1024], BF16)})
            w_o = f32(a_w_o[j])
            ins = [dict(ident=cst["ident"], o=ar[c]["o"], h=h[c], w=w_o, g=f32(ln_mix_g[i]), b=f32(ln_mix_b[i])) for c in range(NC)]

            def prog(S, a):
                C = Consts(S, a["ident"])
                stage_wo_ln(S, C, T, a["o"], a["h"], a["w"], a["g"], a["b"], a["h1"])
            wr = launch("wo", prog, ins, {"h1": ([T, D], F32)})
        else:
            ins = [dict(h=h[c], w=f32(b_w_in[j]), wa2=f32(b_w_a2[j]), ba=f32(b_b_a[j]), U=cst["cmU"], W=cst["cmW"], ident=cst["ident"])
                   for c in range(NC)]
            outs = {"qgT": ([512, T], BF16), "kgT": ([512, T], BF16), "kh": ([T, 512], BF16), "vb": ([T, 1024], BF16),
                    "dec": ([512, T // 64], F32), "sr": ([T, 1024], F32)}

            def prog(S, a):
                C = Consts(S, a["ident"])
                stage_proj_B(S, C, T, a["h"], a["w"], a["wa2"], a["ba"], a["U"], a["W"],
                             a["qgT"].rearrange("(h d) t -> h d t", d=128), a["kgT"].rearrange("(h d) t -> h d t", d=128),
                             a["kh"], a["vb"], a["dec"].rearrange("(h d) c -> h d c", d=128), a["sr"])
            pr = launch("projB", prog, ins, outs)
            ins = []
            ctok = [t_.reshape(-1, 128)[:, ::64].reshape(-1) // 64 for t_ in tok]
            for c in range(NC):
                b0 = (c // 2) * 2
                hs = slice((c % 2) * 256, (c % 2) * 256 + 256)
                vs = slice((c % 2) * 512, (c % 2) * 512 + 512)
                qg = np.empty((256, SEQ), NPBF16)
                kg = np.empty((256, SEQ), NPBF16)
                khh = np.empty((SEQ, 256), NPBF16)
                vv = np.empty((SEQ, 512), NPBF16)
                dd = np.empty((256, SEQ // 64), np.float32)
                for r in range(2):
                    qg[:, tok[b0 + r]] = pr[b0 + r]["qgT"][hs]
                    kg[:, tok[b0 + r]] = pr[b0 + r]["kgT"][hs]
                    khh[tok[b0 + r]] = pr[b0 + r]["kh"][:, hs]
                    vv[tok[b0 + r]] = pr[b0 + r]["vb"][:, vs]
                    dd[:, ctok[b0 + r]] = pr[b0 + r]["dec"][hs]
                ins.append(dict(ident=cst["ident"], qgT=qg, kgT=kg, kh=khh, vb=vv, dec=dd, gn=f32(b_g_norm[j]), tri=cst["tri64"]))

            def prog(S, a):
                C = Consts(S, a["ident"])

                def acc(kind, hd, blk):
                    t0, t1 = blk * 1024, (blk + 1) * 1024
                    if kind == "qg":
                        return a["qgT"][hd * 128:(hd + 1) * 128, t0:t1]
                    if kind == "kg":
                        return a["kgT"][hd * 128:(hd + 1) * 128, t0:t1]
                    if kind == "kh":
                        return a["kh"][t0:t1, hd * 128:(hd + 1) * 128].rearrange("(c j) d -> j c d", j=64)
                    if kind == "v":
                        return a["vb"][t0:t1, hd * 256:(hd + 1) * 256].rearrange("(c j) e -> j c e", j=64)
                    return a["on"][t0:t1, hd * 256:(hd + 1) * 256].rearrange("(c j) e -> j c e", j=64)
                stage_gla(S, C, SEQ, 2, acc, lambda hd: a["dec"][hd * 128:(hd + 1) * 128, :], a["gn"], a["tri"])
            gr = launch("gla", prog, ins, {"on": ([SEQ, 512], F32)})
            w_o = f32(b_w_o[j])
            ins = []
            for c in range(NC):
                b0 = (c // 2) * 2
                on = np.concatenate([gr[b0]["on"][tok[c]], gr[b0 + 1]["on"][tok[c]]], axis=1)
                ins.append(dict(ident=cst["ident"], on=np.ascontiguousarray(on), sr=pr[c]["sr"], h=h[c], w=w_o,
                                g=f32(ln_mix_g[i]), b=f32(ln_mix_b[i])))

            def prog(S, a):
                C = Consts(S, a["ident"])
                stage_wo_ln(S, C, T, None, a["h"], a["w"], a["g"], a["b"], a["h1"], gate=(a["on"], a["sr"]))
            wr = launch("wog", prog, ins, {"h1": ([T, D], F32)})
        ins = [dict(ident=cst["ident"], h1=wr[c]["h1"], wu=f32(mlp_w_up[i]), wd=f32(mlp_w_down[i]), g=f32(ln_mlp_g[i]), b=f32(ln_mlp_b[i]))
               for c in range(NC)]

        def prog(S, a):
            C = Consts(S, a["ident"])
            stage_mlp(S, C, T, a["h1"], a["wu"], a["wd"], a["g"], a["b"], a["h2"])
        mr = launch("mlp", prog, ins, {"h2": ([T, D], F32)})
        h = [mr[c]["h2"] for c in range(NC)]
    out = np.empty((B, SEQ, D), np.float32)
    for c in range(NC):
        out[c // 2][tok[c]] = h[c]
    return out


def build_fused(SEQ, depth):
    nc = bass.Bass("TRN2", target_bir_lowering=False)
    T = SEQ
    NTL = T // 128

    def din(name, shape, dt=F32):
        return nc.dram_tensor(name, list(shape), dt, kind="ExternalInput").ap()

    def scr(name, shape, dt):
        return nc.dram_tensor("scr_" + name, list(shape), dt).ap()

    NA, NB = (depth + 1) // 2, depth // 2
    a = dict(
        x=din("x", [T, D]), pos=din("pos", [128, NTL], I32), invf=din("invf", [8]), ident=din("ident", [128, 128]),
        maskc=din("maskc", [128, 256]), negI=din("negI", [128, 128]), cmU=din("cmU", [128, 128]), cmW=din("cmW", [128, 128]),
        tri64=din("tri64", [64, 64]),
        a_w_in=din("a_w_in", [NA, D, A_IN]), a_w_o=din("a_w_o", [NA, D, D]),
        b_w_in=din("b_w_in", [max(NB, 1), D, B_IN]), b_w_a2=din("b_w_a2", [max(NB, 1), 16, 512]), b_b_a=din("b_b_a", [max(NB, 1), 512]),
        b_g_norm=din("b_g_norm", [max(NB, 1), 256]), b_w_o=din("b_w_o", [max(NB, 1), D, D]),
        ln_mix_g=din("ln_mix_g", [depth, D]), ln_mix_b=din("ln_mix_b", [depth, D]),
        mlp_w_up=din("mlp_w_up", [depth, D, DFF]), mlp_w_down=din("mlp_w_down", [depth, DFF, D]),
        ln_mlp_g=din("ln_mlp_g", [depth, D]), ln_mlp_b=din("ln_mlp_b", [depth, D]),
    )
    out = nc.dram_tensor("out", [T, D], F32, kind="ExternalOutput").ap()
    hA = scr("hA", [T, D], F32)
    hB = scr("hB", [T, D], F32)
    qT = scr("qT", [1024, T], BF16)
    kT = scr("kT", [256, T], BF16)
    vv = scr("v", [T, 256], BF16)
    iqT = scr("iqT", [512, T], BF16)
    ikT = scr("ikT", [64, T], BF16)
    iw = scr("iw", [T, 8], F32)
    o = scr("o", [T, 1024], BF16)
    qgT = scr("qgT", [512, T], BF16)
    kgT = scr("kgT", [512, T], BF16)
    kh = scr("kh", [T, 512], BF16)
    vb = scr("vb", [T, 1024], BF16)
    dec = scr("dec", [512, T // 64], F32)
    sr = scr("sr", [T, 1024], F32)
    on = scr("on", [T, 1024], F32)

    S = Sched(nc)
    CB = 8
    h_in = a["x"]
    for i in range(depth):
        j = i // 2
        last = i == depth - 1
        if i % 2 == 0:
            S.stage_begin()
            C = Consts(S, a["ident"])
            stage_proj_A(S, C, T, h_in, a["pos"], a["a_w_in"][j], a["invf"], qT, kT, vv, iqT, ikT, iw)
            S.stage_end()
            S.stage_begin()
            C = Consts(S, a["ident"])
            stage_attn(S, C, NTL, kT.rearrange("(g d) s -> g d s", d=64), vv, ikT, qT.rearrange("(h d) t -> h d t", d=64),
                       iqT.rearrange("(h d) t -> h d t", d=64), iw, a["maskc"], a["negI"], o, solo=True)
            S.stage_end()
            S.stage_begin()
            C = Consts(S, a["ident"])
            stage_wo_ln(S, C, T, o, h_in, a["a_w_o"][j], a["ln_mix_g"][i], a["ln_mix_b"][i], hB)
            S.stage_end()
        else:
            S.stage_begin()
            C = Consts(S, a["ident"])
            stage_proj_B(S, C, T, h_in, a["b_w_in"][j], a["b_w_a2"][j], a["b_b_a"][j], a["cmU"], a["cmW"],
                         qgT.rearrange("(h d) t -> h d t", d=128), kgT.rearrange("(h d) t -> h d t", d=128), kh, vb,
                         dec.rearrange("(h d) c -> h d c", d=128), sr)
            S.stage_end()
            S.stage_begin()
            C = Consts(S, a["ident"])

            def acc(kind, hd, blk):
                t0, t1 = blk * 64 * CB, (blk + 1) * 64 * CB
                if kind == "qg":
                    return qgT[hd * 128:(hd + 1) * 128, t0:t1]
                if kind == "kg":
                    return kgT[hd * 128:(hd + 1) * 128, t0:t1]
                if kind == "kh":
                    return kh[t0:t1, hd * 128:(hd + 1) * 128].rearrange("(c j) d -> j c d", j=64)
                if kind == "v":
                    return vb[t0:t1, hd * 256:(hd + 1) * 256].rearrange("(c j) e -> j c e", j=64)
                return on[t0:t1, hd * 256:(hd + 1) * 256].rearrange("(c j) e -> j c e", j=64)
            stage_gla(S, C, SEQ, 4, acc, lambda hd: dec[hd * 128:(hd + 1) * 128, :], a["b_g_norm"][j], a["tri64"], CB=CB)
            S.stage_end()
            S.stage_begin()
            C = Consts(S, a["ident"])
            stage_wo_ln(S, C, T, None, h_in, a["b_w_o"][j], a["ln_mix_g"][i], a["ln_mix_b"][i], hB, gate=(on, sr))
            S.stage_end()
        S.stage_begin()
        C = Consts(S, a["ident"])
        h_out = out if last else hA
        stage_mlp(S, C, T, hB, a["mlp_w_up"][i], a["mlp_w_down"][i], a["ln_mlp_g"][i], a["ln_mlp_b"][i], h_out)
        S.stage_end(last=last)
        h_in = hA
    S.stack.close()
    return nc


_FUSED = {}


def kernel(x, positions, a_w_in, a_w_o, b_w_in, b_w_a2, b_b_a, b_g_norm, b_w_o,
           ln_mix_g, ln_mix_b, mlp_w_up, mlp_w_down, ln_mlp_g, ln_mlp_b):
    x = np.asarray(x, np.float32)
    B, SEQ, _ = x.shape
    depth = int(np.asarray(ln_mix_g).shape[0])
    key = (SEQ, depth)
    if key not in _FUSED:
        _FUSED[key] = build_fused(SEQ, depth)
    nc = _FUSED[key]
    cst = _consts()
    f32 = lambda t: np.ascontiguousarray(np.asarray(t, np.float32))
    shared = dict(invf=cst["invf"], ident=cst["ident"], maskc=cst["maskc"][1], negI=cst["negI"], cmU=cst["cmU"], cmW=cst["cmW"],
                  tri64=cst["tri64"], a_w_in=f32(a_w_in), a_w_o=f32(a_w_o), b_w_in=f32(b_w_in), b_w_a2=f32(b_w_a2), b_b_a=f32(b_b_a),
                  b_g_norm=f32(b_g_norm), b_w_o=f32(b_w_o), ln_mix_g=f32(ln_mix_g), ln_mix_b=f32(ln_mix_b),
                  mlp_w_up=f32(mlp_w_up), mlp_w_down=f32(mlp_w_down), ln_mlp_g=f32(ln_mlp_g), ln_mlp_b=f32(ln_mlp_b))
    in_maps = []
    for c in range(B):
        pos_pt = np.ascontiguousarray(np.asarray(positions)[c].astype(np.int32).reshape(SEQ // 128, 128).T)
        m = dict(shared)
        m["x"] = np.ascontiguousarray(x[c])
        m["pos"] = pos_pt
        in_maps.append(m)
    res = run_bass_kernel_spmd(nc, in_maps, core_ids=list(range(B)))
    return np.stack([res.results[c]["out"] for c in range(B)], axis=0)
```

```python
import contextlib
import numpy as np
import concourse.bass as bass
import concourse.mybir as mybir
from concourse.ap import AP
from concourse.bass_utils import run_bass_kernel_spmd

F32 = mybir.dt.float32
BF16 = mybir.dt.bfloat16
I32 = mybir.dt.int32
AF = mybir.ActivationFunctionType
ALU = mybir.AluOpType
AX = mybir.AxisListType

D = 1024
DFF = 4096
DEPTH = 4
ALPHA = (2 * DEPTH) ** 0.25
LN_EPS = 1e-5
NCORES = 8


class Buf:
    __slots__ = ("name", "last_w", "readers")

    def __init__(self, name):
        self.name = name
        self.last_w = None
        self.readers = {}


class Sched:
    ENGS = ("sp", "act", "dve", "pool", "pe")
    SAME_WIN = 3
    NDMA = 8

    def __init__(self, nc):
        self.nc = nc
        self.stack = contextlib.ExitStack()
        self.ops = {e: [] for e in self.ENGS}
        self.count = {e: 0 for e in self.ENGS}
        self.seen = {e: {} for e in self.ENGS}
        self.esem = {}
        for e in ("act", "dve", "pool", "pe"):
            self.esem[e] = self.stack.enter_context(nc.semaphore("s_" + e))
        self.dsem = {}
        self.dma_i = {}
        for q in ("sp", "act", "pool"):
            self.dsem[q] = [self.stack.enter_context(nc.semaphore("d_%s%d" % (q, i))) for i in range(self.NDMA)]
            self.dma_i[q] = 0
        self.nbuf = 0
        self.stage_stack = None
        self.stage_no = 0
        self.bar = self.stack.enter_context(nc.semaphore("s_bar"))

    def sbuf(self, name, shape, dt):
        st = self.stage_stack if self.stage_stack is not None else self.stack
        return st.enter_context(self.nc.sbuf_tensor("sb%d_" % self.stage_no + name, list(shape), dt))

    def psum(self, name, shape, dt):
        st = self.stage_stack if self.stage_stack is not None else self.stack
        return st.enter_context(self.nc.psum_tensor("pp%d_" % self.stage_no + name, list(shape), dt))

    def stage_begin(self):
        self.stage_stack = contextlib.ExitStack()

    def stage_end(self, last=False):
        self.finish()
        if not last:
            self.stage_no += 1
            n = self.stage_no
            bar = self.bar
            self.ops["sp"].append(([], (lambda e, bar=bar: e.sem_inc(bar, 1)), None, 0))
            for eng in ("act", "dve", "pool", "pe"):
                self.ops[eng].append(([(bar, n)], None, None, 0))
        self._emit_block()
        self.stage_stack.close()
        self.stage_stack = None

    def buf(self, name=None):
        self.nbuf += 1
        return Buf(name or ("b%d" % self.nbuf))

    def bufs(self, n, name="b"):
        return [self.buf("%s%d" % (name, i)) for i in range(n)]

    def _deps(self, reads, writes):
        raw = {}
        other = {}
        for b in reads:
            if b.last_w is not None:
                k, v = b.last_w
                raw[k] = max(raw.get(k, 0), v)
        for b in writes:
            if b.last_w is not None:
                k, v = b.last_w
                other[k] = max(other.get(k, 0), v)
            for k, v in b.readers.items():
                other[k] = max(other.get(k, 0), v)
        return raw, other

    def _commit(self, ev, reads, writes):
        k, v = ev
        for b in reads:
            b.readers[k] = max(b.readers.get(k, 0), v)
        for b in writes:
            b.last_w = ev
            b.readers = {}

    def op(self, eng, fn, reads=(), writes=()):
        raw, other = self._deps(reads, writes)
        own = self.esem[eng]
        waits = {}
        seen = self.seen[eng]
        for d, is_raw in ((raw, True), (other, False)):
            for k, v in d.items():
                if k is own:
                    if eng == "pe" or not is_raw:
                        continue
                    if v <= self.count[eng] - self.SAME_WIN:
                        continue
                if seen.get(k, 0) >= v:
                    continue
                waits[k] = max(waits.get(k, 0), v)
        for k, v in waits.items():
            seen[k] = v
        self.count[eng] += 1
        ev = (own, self.count[eng])
        self._commit(ev, reads, writes)
        self.ops[eng].append((list(waits.items()), fn, own, 1))

    def dma(self, q, out, in_, reads=(), writes=(), fn=None, **kw):
        raw, other = self._deps(reads, writes)
        waits = {}
        seen = self.seen[q]
        for d in (raw, other):
            for k, v in d.items():
                if seen.get(k, 0) >= v:
                    continue
                waits[k] = max(waits.get(k, 0), v)
        i = self.dma_i[q]
        self.dma_i[q] = i + 1
        slot = self.dsem[q][i % self.NDMA]
        target = 16 * (i // self.NDMA + 1)
        if target > 16 and seen.get(slot, 0) < target - 16:
            waits[slot] = max(waits.get(slot, 0), target - 16)
        for k, v in waits.items():
            seen[k] = v
        ev = (slot, target)
        self._commit(ev, reads, writes)
        if fn is None:
            fn = lambda e, out=out, in_=in_, kw=kw: e.dma_start(out=out, in_=in_, **kw)
        self.ops[q].append((list(waits.items()), fn, slot, 16))

    def allgather(self, out, in_, groups, reads=(), writes=()):
        fn = lambda e: e.collective_compute("AllGather", ALU.bypass, replica_groups=groups, ins=[in_], outs=[out])
        self.dma("pool", None, None, reads=reads, writes=writes, fn=fn)

    def finish(self):
        waits = []
        for q in ("sp", "act", "pool"):
            n = self.dma_i[q]
            for s in range(min(n, self.NDMA)):
                cnt = (n - 1 - s) // self.NDMA + 1
                waits.append((self.dsem[q][s], 16 * cnt))
        for e in ("act", "dve", "pool", "pe"):
            if self.count[e]:
                waits.append((self.esem[e], self.count[e]))
        self.ops["sp"].append((waits, None, None, 0))

    def emit(self):
        self.finish()
        self._emit_block()
        self.stack.close()

    def _emit_block(self):
        nc = self.nc
        with nc.Block() as block:
            deco = {"sp": block.sync, "act": block.scalar, "dve": block.vector,
                    "pool": block.gpsimd, "pe": block.tensor}
            for eng in self.ENGS:
                ops = self.ops[eng]
                if not ops:
                    continue

                def body(e, ops=ops):
                    for waits, fn, sem, inc in ops:
                        for s, v in waits:
                            e.wait_ge(s, v)
                        if fn is not None:
                            ins = fn(e)
                            if sem is not None:
                                ins.then_inc(sem, inc)

                deco[eng](body)
        self.ops = {e: [] for e in self.ENGS}


def bcast_rows(ap_row, nparts):
    return ap_row.partition_broadcast(nparts)


class Consts:
    def __init__(self, S, ident_dram):
        self.ident = S.sbuf("ident", [128, 128], BF16)
        self.b_ident = S.buf("ident")
        S.dma("pool", self.ident[:], ident_dram[:, :], writes=[self.b_ident])


def load_bcast(S, name, row_ap, n):
    t = S.sbuf(name, [128, n], F32)
    b = S.buf(name)
    S.dma("sp", t[:], row_ap.partition_broadcast(128), writes=[b])
    return t, b


def layer_norm_tile(S, u, b_u, out, b_out, g_t, b_g, bt_t, b_bt, scr, eng2="pool"):
    st, b_st = scr["st"], scr["b_st"]
    mv, b_mv = scr["mv"], scr["b_mv"]
    for c in range(2):
        S.op("dve", lambda e, c=c: e.bn_stats(out=st[:, c, :], in_=u[:, c * 512:(c + 1) * 512]),
             reads=[b_u], writes=[b_st[c]])
    S.op("dve", lambda e: e.bn_aggr(out=mv[:, 0:2], in_=st[:, :, :]), reads=b_st, writes=[b_mv[0]])
    S.op("dve", lambda e: e.tensor_scalar(out=mv[:, 2:3], in0=mv[:, 1:2], scalar1=LN_EPS, scalar2=None,
                                           op0=ALU.add), reads=[b_mv[0]], writes=[b_mv[1]])
    S.op("act", lambda e: e.activation(out=mv[:, 2:3], in_=mv[:, 2:3], func=AF.Sqrt), reads=[b_mv[1]], writes=[b_mv[1]])
    S.op("dve", lambda e: e.reciprocal(out=mv[:, 2:3], in_=mv[:, 2:3]), reads=[b_mv[1]], writes=[b_mv[1]])
    S.op("dve", lambda e: e.scalar_tensor_tensor(out=mv[:, 3:4], in0=mv[:, 0:1], scalar=-1.0, in1=mv[:, 2:3],
                                                  op0=ALU.mult, op1=ALU.mult), reads=[b_mv[0], b_mv[1]], writes=[b_mv[2]])
    S.op("act", lambda e: e.activation(out=u[:, :], in_=u[:, :], func=AF.Identity, bias=mv[:, 3:4], scale=mv[:, 2:3]),
         reads=[b_u, b_mv[1], b_mv[2]], writes=[b_u])
    S.op(eng2, lambda e: e.tensor_tensor(out=u[:, :], in0=u[:, :], in1=g_t[:, :], op=ALU.mult),
         reads=[b_u, b_g], writes=[b_u])
    S.op(eng2, lambda e: e.tensor_tensor(out=out[:, :], in0=u[:, :], in1=bt_t[:, :], op=ALU.add),
         reads=[b_u, b_bt], writes=[b_out])


def ln_scratch(S, name):
    return {"st": S.sbuf(name + "_st", [128, 2, 6], F32), "b_st": S.bufs(2, name + "st"),
            "mv": S.sbuf(name + "_mv", [128, 4], F32), "b_mv": S.bufs(3, name + "mv")}


def stage_mlp(S, C, T, h1, w_up, w_down, g, b, h2):
    NT = 256
    NS = NT // 128
    wup = S.sbuf("wup", [128, 8, DFF], BF16)
    wdn = S.sbuf("wdn", [128, 32, D], BF16)
    b_wup = S.bufs(8, "wup")
    b_wdn = S.bufs(8, "wdn")
    wu_v = w_up.rearrange("(c p) f -> p c f", p=128)
    wd_v = w_down.rearrange("(c p) n -> p c n", p=128)
    for c in range(8):
        S.dma("pool", wup[:, c, :], wu_v[:, c, :], writes=[b_wup[c]])
    for c in range(8):
        S.dma("pool", wdn[:, 4 * c:4 * c + 4, :], wd_v[:, 4 * c:4 * c + 4, :], writes=[b_wdn[c]])
    g_t, b_g = load_bcast(S, "mlp_g", g, D)
    bt_t, b_bt = load_bcast(S, "mlp_b", b, D)

    NB = 2
    x_sb = [S.sbuf("x_sb%d" % i, [128, NS, D], F32) for i in range(NB)]
    b_x = [S.bufs(NS, "x%d_" % i) for i in range(NB)]
    xb = S.sbuf("xb", [128, NS, D], BF16)
    b_xb = S.bufs(NS, "xb")
    xT = S.sbuf("xT", [128, 8, NT], BF16)
    b_xT = S.bufs(8, "xT")
    h2T = S.sbuf("h2T", [128, 32, NT], BF16)
    b_h2T = S.bufs(32, "h2T")
    r_sb = [S.sbuf("r_sb%d" % i, [128, NT], F32) for i in range(4)]
    b_r = S.bufs(4, "r")
    y_sb = [S.sbuf("y_sb%d" % i, [128, D], F32) for i in range(2)]
    b_y = S.bufs(2, "y")
    tp_ps = [S.psum("tp_ps%d" % i, [128, 1024], BF16) for i in range(2)]
    b_tp = S.bufs(2, "tp")
    up_ps = [S.psum("up_ps%d" % i, [128, 512], F32) for i in range(4)]
    b_up = S.bufs(4, "up")
    dn_ps = [S.psum("dn_ps%d" % i, [128, 512], F32) for i in range(2)]
    b_dn = S.bufs(2, "dn")
    scrs = [ln_scratch(S, "mlp%d" % i) for i in range(2)]

    h1v = h1.rearrange("(t s p) d -> t p s d", p=128, s=NS)
    h2v = h2.rearrange("(t s p) d -> t s p d", p=128, s=NS)
    n_up = 0
    n_dn = 0
    n_tp = 0
    n_y = 0
    for t in range(T // NT):
        xs, bx = x_sb[t % NB], b_x[t % NB]
        S.dma("sp", xs[:, :, :], h1v[t], writes=bx)
        for s in range(NS):
            S.op("act", lambda e, xs=xs, s=s: e.activation(out=xb[:, s, :], in_=xs[:, s, :], func=AF.Copy),
                 reads=[bx[s]], writes=[b_xb[s]])
        for c in range(8):
            tp, btp = tp_ps[n_tp % 2], b_tp[n_tp % 2]
            n_tp += 1
            for s in range(NS):
                S.op("pe", lambda e, tp=tp, s=s, c=c: e.transpose(out=tp[:, s * 128:(s + 1) * 128],
                                                                  in_=xb[:, s, c * 128:(c + 1) * 128],
                                                                  identity=C.ident[:]),
                     reads=[b_xb[s], C.b_ident], writes=[btp])
            S.op("dve", lambda e, tp=tp, c=c: e.tensor_copy(out=xT[:, c, :], in_=tp[:, 0:NT]),
                 reads=[btp], writes=[b_xT[c]])
        for fc in range(32):
            ps, bps = up_ps[n_up % 4], b_up[n_up % 4]
            r, br = r_sb[n_up % 4], b_r[n_up % 4]
            n_up += 1
            for c in range(8):
                S.op("pe", lambda e, ps=ps, c=c, fc=fc: e.matmul(ps[:, 0:NT], lhsT=wup[:, c, fc * 128:(fc + 1) * 128],
                                                                  rhs=xT[:, c, :], start=(c == 0), stop=(c == 7)),
                     reads=[b_wup[c], b_xT[c]], writes=[bps])
            S.op("act", lambda e, ps=ps, r=r: e.activation(out=r[:, :], in_=ps[:, 0:NT], func=AF.Relu),
                 reads=[bps], writes=[br])
            S.op("dve", lambda e, r=r, fc=fc: e.tensor_tensor(out=h2T[:, fc, :], in0=r[:, :], in1=r[:, :], op=ALU.mult),
                 reads=[br], writes=[b_h2T[fc]])
        for s in range(NS):
            for nh in range(2):
                ps, bps = dn_ps[n_dn % 2], b_dn[n_dn % 2]
                n_dn += 1
                for fc in range(32):
                    S.op("pe", lambda e, ps=ps, fc=fc, s=s, nh=nh: e.matmul(
                        ps[:, :], lhsT=h2T[:, fc, s * 128:(s + 1) * 128], rhs=wdn[:, fc, nh * 512:(nh + 1) * 512],
                        start=(fc == 0), stop=(fc == 31)),
                         reads=[b_h2T[fc], b_wdn[fc // 4]], writes=[bps])
                S.op("dve", lambda e, ps=ps, xs=xs, s=s, nh=nh: e.scalar_tensor_tensor(
                    out=xs[:, s, nh * 512:(nh + 1) * 512], in0=xs[:, s, nh * 512:(nh + 1) * 512], scalar=ALPHA,
                    in1=ps[:, :], op0=ALU.mult, op1=ALU.add),
                     reads=[bps, bx[s]], writes=[bx[s]])
            y, by = y_sb[n_y % 2], b_y[n_y % 2]
            n_y += 1
            layer_norm_tile(S, xs[:, s, :], bx[s], y, by, g_t, b_g, bt_t, b_bt, scrs[n_y % 2])
            S.dma("pool", h2v[t, s], y[:, :], reads=[by])


TWO_PI = float(2 * np.pi)
PI = float(np.pi)


def _range_reduce(S, x, bx, tmp_f, btf, tmp_i, bti):
    S.op("dve", lambda e: e.tensor_scalar(out=tmp_f, in0=x, scalar1=1.0 / TWO_PI, scalar2=None, op0=ALU.mult),
         reads=[bx], writes=[btf])
    S.op("dve", lambda e: e.tensor_copy(out=tmp_i, in_=tmp_f), reads=[btf], writes=[bti])
    S.op("dve", lambda e: e.tensor_copy(out=tmp_f, in_=tmp_i), reads=[bti], writes=[btf])
    S.op("dve", lambda e: e.scalar_tensor_tensor(out=x, in0=tmp_f, scalar=-TWO_PI, in1=x, op0=ALU.mult, op1=ALU.add),
         reads=[btf, bx], writes=[bx])
    S.op("dve", lambda e: e.tensor_scalar(out=tmp_f, in0=x, scalar1=PI, scalar2=-TWO_PI, op0=ALU.is_gt, op1=ALU.mult),
         reads=[bx], writes=[btf])
    S.op("dve", lambda e: e.tensor_tensor(out=x, in0=x, in1=tmp_f, op=ALU.add), reads=[btf, bx], writes=[bx])
    S.op("dve", lambda e: e.tensor_scalar(out=tmp_f, in0=x, scalar1=-PI, scalar2=TWO_PI, op0=ALU.is_lt, op1=ALU.mult),
         reads=[bx], writes=[btf])
    S.op("dve", lambda e: e.tensor_tensor(out=x, in0=x, in1=tmp_f, op=ALU.add), reads=[btf, bx], writes=[bx])


def rotary_tables(S, NTL, pos_pt, invf):
    posi = S.sbuf("posi", [128, NTL], I32)
    posf = S.sbuf("posf", [128, NTL], F32)
    invt = S.sbuf("invt", [128, 8], F32)
    cos_t = S.sbuf("cos_t", [128, NTL, 8], F32)
    sin_t = S.sbuf("sin_t", [128, NTL, 8], F32)
    tmpf = S.sbuf("rr_tf", [128, NTL, 8], F32)
    tmpi = S.sbuf("rr_ti", [128, NTL, 8], I32)
    b_pi, b_pf, b_inv, b_cos, b_sin, b_tf, b_ti = S.bufs(7, "rot")
    S.dma("sp", posi[:], pos_pt[:, :], writes=[b_pi])
    S.dma("sp", invt[:], invf.partition_broadcast(128), writes=[b_inv])
    S.op("dve", lambda e: e.tensor_copy(out=posf[:], in_=posi[:]), reads=[b_pi], writes=[b_pf])
    S.op("dve", lambda e: e.tensor_tensor(out=sin_t[:, :, :], in0=posf[:, :].unsqueeze(2).to_broadcast([128, NTL, 8]),
                                          in1=invt[:, :].unsqueeze(1).to_broadcast([128, NTL, 8]), op=ALU.mult),
         reads=[b_pf, b_inv], writes=[b_sin])
    S.op("dve", lambda e: e.tensor_scalar(out=cos_t[:, :, :], in0=sin_t[:, :, :], scalar1=PI / 2, scalar2=None, op0=ALU.add),
         reads=[b_sin], writes=[b_cos])
    _range_reduce(S, sin_t[:, :, :], b_sin, tmpf[:, :, :], b_tf, tmpi[:, :, :], b_ti)
    _range_reduce(S, cos_t[:, :, :], b_cos, tmpf[:, :, :], b_tf, tmpi[:, :, :], b_ti)
    S.op("act", lambda e: e.activation(out=sin_t[:, :, :], in_=sin_t[:, :, :], func=AF.Sin), reads=[b_sin], writes=[b_sin])
    S.op("act", lambda e: e.activation(out=cos_t[:, :, :], in_=cos_t[:, :, :], func=AF.Sin), reads=[b_cos], writes=[b_cos])
    return cos_t, b_cos, sin_t, b_sin


def apply_rotary(S, P, bP, h0, nh, cos_t, b_cos, sin_t, b_sin, t, tmp, btmp):
    Pv = P[:, h0 * 64:(h0 + nh) * 64].rearrange("p (h d) -> p h d", d=64)
    x1 = Pv[:, :, 0:8]
    x2 = Pv[:, :, 8:16]
    c = cos_t[:, t, :].unsqueeze(1).to_broadcast([128, nh, 8])
    s = sin_t[:, t, :].unsqueeze(1).to_broadcast([128, nh, 8])
    t1, t2, t3, t4 = (tmp[:, i, 0:nh, :] for i in range(4))
    bP = list(bP)
    rd = bP + [b_cos, b_sin]
    S.op("dve", lambda e: e.tensor_tensor(out=t1, in0=x1, in1=c, op=ALU.mult), reads=rd, writes=[btmp[0]])
    S.op("dve", lambda e: e.tensor_tensor(out=t2, in0=x2, in1=s, op=ALU.mult), reads=rd, writes=[btmp[1]])
    S.op("dve", lambda e: e.tensor_tensor(out=t3, in0=x2, in1=c, op=ALU.mult), reads=rd, writes=[btmp[2]])
    S.op("dve", lambda e: e.tensor_tensor(out=t4, in0=x1, in1=s, op=ALU.mult), reads=rd, writes=[btmp[3]])
    S.op("dve", lambda e: e.tensor_tensor(out=x1, in0=t1, in1=t2, op=ALU.subtract), reads=[btmp[0], btmp[1], btmp[3]], writes=bP)
    S.op("dve", lambda e: e.tensor_tensor(out=x2, in0=t3, in1=t4, op=ALU.add), reads=[btmp[2], btmp[3]], writes=bP)


A_IN = 2120
IW_SCALE = float(8 ** -0.5 * 64 ** -0.5)


def stage_proj_A(S, C, T, h, pos_pt, w_in, invf, qT, kT, v, iqT, ikT, iw):
    NTL = T // 128
    win = S.sbuf("win", [128, 8, A_IN], BF16)
    b_win = S.bufs(8, "win")
    wv = w_in.rearrange("(c p) n -> p c n", p=128)
    for c in range(8):
        S.dma("pool", win[:, c, :], wv[:, c, :], writes=[b_win[c]])
    cos_t, b_cos, sin_t, b_sin = rotary_tables(S, NTL, pos_pt, invf)

    hs = [S.sbuf("pa_hs%d" % i, [128, D], F32) for i in range(2)]
    b_hs = S.bufs(2, "pa_hs")
    hb_l = [S.sbuf("pa_hb%d" % i, [128, D], BF16) for i in range(2)]
    b_hb_l = S.bufs(2, "pa_hb")
    hT_l = [S.sbuf("pa_hT%d" % i, [128, 8, 128], BF16) for i in range(2)]
    b_hT_l = S.bufs(2, "pa_hT")
    P_l = [S.sbuf("pa_P%d" % i, [128, A_IN], F32) for i in range(2)]
    b_P_l = [S.bufs(5, "pa_P%d_" % i) for i in range(2)]
    Pb_l = [S.sbuf("pa_Pb%d" % i, [128, A_IN], BF16) for i in range(2)]
    b_Pb_l = [S.bufs(4, "pa_Pb%d_" % i) for i in range(2)]
    iws = [S.sbuf("pa_iw%d" % i, [128, 8], F32) for i in range(2)]
    b_iws = S.bufs(2, "pa_iw")
    rt_l = [S.sbuf("pa_rt%d" % i, [128, 4, 20, 8], F32) for i in range(2)]
    b_rt_l = [S.bufs(4, "pa_rt%d_" % i) for i in range(2)]
    qTs = [S.sbuf("pa_qT%d" % i, [128, 8, 128], BF16) for i in range(2)]
    b_qTs = S.bufs(2, "pa_qT")
    kiTs = [S.sbuf("pa_kiT%d" % i, [128, 7, 128], BF16) for i in range(2)]
    b_kiTs = S.bufs(2, "pa_kiT")
    pj = [S.psum("pa_pj%d" % i, [128, 512], F32) for i in range(5)]
    b_pj = S.bufs(5, "pa_pj")
    tpA = S.psum("pa_tpA", [128, 1024], BF16)
    tpB = S.psum("pa_tpB", [128, 1024], BF16)
    b_tpA, b_tpB = S.bufs(2, "pa_tp")
    chunks = [(0, 512), (512, 1024), (1024, 1536), (1536, 2048), (2048, A_IN)]

    hv = h.rearrange("(t p) d -> t p d", p=128)
    qTv = qT.rearrange("(c p) t -> p c t", p=128)
    kTv = kT.rearrange("(c p) t -> p c t", p=128)
    iqTv = iqT.rearrange("(c p) t -> p c t", p=128)
    def _tile(t, hb, b_hb, hT, b_hT, P, b_P, Pb, b_Pbs, rt, b_rt):
        b_Pq, b_Pk, b_Pv, b_Pi = b_Pbs
        x, bx = hs[t % 2], b_hs[t % 2]
        S.dma("sp", x[:, :], hv[t], writes=[bx])
        S.op("act", lambda e, x=x: e.activation(out=hb[:, :], in_=x[:, :], func=AF.Copy), reads=[bx], writes=[b_hb])
        for c in range(8):
            S.op("pe", lambda e, c=c: e.transpose(out=tpA[:, c * 128:(c + 1) * 128], in_=hb[:, c * 128:(c + 1) * 128],
                                                  identity=C.ident[:]), reads=[b_hb, C.b_ident], writes=[b_tpA])
        S.op("dve", lambda e: e.tensor_copy(out=hT[:, :, :], in_=tpA[:, :].rearrange("p (c t) -> p c t", t=128)),
             reads=[b_tpA], writes=[b_hT])
        for i, (n0, n1) in enumerate(chunks):
            for c in range(8):
                S.op("pe", lambda e, i=i, c=c, n0=n0, n1=n1: e.matmul(pj[i][:, 0:n1 - n0], lhsT=hT[:, c, :], rhs=win[:, c, n0:n1],
                                                                      start=(c == 0), stop=(c == 7)),
                     reads=[b_hT, b_win[c]], writes=[b_pj[i]])
            S.op("act", lambda e, i=i, n0=n0, n1=n1: e.activation(out=P[:, n0:n1], in_=pj[i][:, 0:n1 - n0], func=AF.Copy),
                 reads=[b_pj[i]], writes=[b_P[i]])
        apply_rotary(S, P, b_P[0:3], 0, 20, cos_t, b_cos, sin_t, b_sin, t, rt, b_rt)
        apply_rotary(S, P, b_P[3:5], 24, 9, cos_t, b_cos, sin_t, b_sin, t, rt, b_rt)
        S.op("act", lambda e: e.activation(out=Pb[:, 0:1024], in_=P[:, 0:1024], func=AF.Copy, scale=0.125),
             reads=b_P[0:2], writes=[b_Pq])
        S.op("act", lambda e: e.activation(out=Pb[:, 1024:1536], in_=P[:, 1024:1536], func=AF.Copy),
             reads=[b_P[2]], writes=[b_Pk])
        S.op("act", lambda e: e.activation(out=Pb[:, 1536:2112], in_=P[:, 1536:2112], func=AF.Copy),
             reads=b_P[3:5], writes=[b_Pi])
        iwt, biw = iws[t % 2], b_iws[t % 2]
        S.op("act", lambda e, iwt=iwt: e.activation(out=iwt[:, :], in_=P[:, 2112:2120], func=AF.Copy, scale=IW_SCALE),
             reads=[b_P[4]], writes=[biw])
        S.dma("pool", iw[t * 128:(t + 1) * 128, :], iwt[:, :], reads=[biw])
        S.dma("pool", v[t * 128:(t + 1) * 128, :], Pb[:, 1280:1536], reads=[b_Pk])
        qs, bqs = qTs[t % 2], b_qTs[t % 2]
        ks, bks = kiTs[t % 2], b_kiTs[t % 2]
        for c in range(8):
            S.op("pe", lambda e, c=c: e.transpose(out=tpB[:, c * 128:(c + 1) * 128], in_=Pb[:, c * 128:(c + 1) * 128],
                                                  identity=C.ident[:]), reads=[b_Pq, C.b_ident], writes=[b_tpB])
        S.op("dve", lambda e, qs=qs: e.tensor_copy(out=qs[:, :, :], in_=tpB[:, :].rearrange("p (c t) -> p c t", t=128)),
             reads=[b_tpB], writes=[bqs])
        S.dma("pool", qTv[:, :, t * 128:(t + 1) * 128], qs[:, :, :], reads=[bqs])
        srcs = [(1024, 128, b_Pk), (1152, 128, b_Pk), (1536, 128, b_Pi), (1664, 128, b_Pi), (1792, 128, b_Pi),
                (1920, 128, b_Pi), (2048, 64, b_Pi)]
        for j, (c0, w, bsrc) in enumerate(srcs):
            S.op("pe", lambda e, j=j, c0=c0, w=w: e.transpose(out=tpA[0:w, j * 128:(j + 1) * 128], in_=Pb[:, c0:c0 + w],
                                                              identity=C.ident[:]), reads=[bsrc, C.b_ident], writes=[b_tpA])
        S.op("dve", lambda e, ks=ks: e.tensor_copy(out=ks[:, 0:6, :], in_=tpA[:, 0:768].rearrange("p (c t) -> p c t", t=128)),
             reads=[b_tpA], writes=[bks])
        S.op("dve", lambda e, ks=ks: e.tensor_copy(out=ks[0:64, 6, :], in_=tpA[0:64, 768:896]),
             reads=[b_tpA], writes=[bks])
        S.dma("pool", kTv[:, :, t * 128:(t + 1) * 128], ks[:, 0:2, :], reads=[bks])
        S.dma("pool", iqTv[:, :, t * 128:(t + 1) * 128], ks[:, 2:6, :], reads=[bks])
        S.dma("pool", ikT[:, t * 128:(t + 1) * 128], ks[0:64, 6, :], reads=[bks])

    for t in range(NTL):
        _tile(t, hb_l[t % 2], b_hb_l[t % 2], hT_l[t % 2], b_hT_l[t % 2], P_l[t % 2], b_P_l[t % 2], Pb_l[t % 2], b_Pb_l[t % 2], rt_l[t % 2], b_rt_l[t % 2])

TOPK = 256
NEG_MASK = -3.0e38
NEG_LO = -1.0e30
N_BISECT = 16
MASK_BIG = 32768.0


def stage_attn(S, C, NQ, kT, v, ikT, qT, iqT, iw, maskc, negI, o, dbg=None, solo=False):
    NKTT = NQ if solo else 2 * NQ
    SK = NKTT * 128
    nkt_of = (lambda j: j + 1) if solo else (lambda j: 2 * j + 2)
    kT_sb = S.sbuf("at_kT", [128, 2, SK], BF16)
    v_sb = S.sbuf("at_v", [128, NKTT, 4, 65], BF16)
    ik_sb = S.sbuf("at_ik", [128, SK], BF16)
    b_kT, b_v, b_ik = S.bufs(3, "at_res")
    for hh in range(2):
        S.dma("sp", kT_sb[hh * 64:(hh + 1) * 64, :, :], kT[2 * hh:2 * hh + 2, :, :].rearrange("g d s -> d g s"), writes=[b_kT])
    S.op("pool", lambda e: e.memset(v_sb[:, :, :, 64:65], 1.0), writes=[b_v])
    vv = v.rearrange("(k p) (g d) -> p k g d", p=128, d=64)
    KCH = 16
    for k0 in range(0, NKTT, KCH):
        k1 = min(NKTT, k0 + KCH)
        for g in range(4):
            S.dma("sp", v_sb[:, k0:k1, g, 0:64], vv[:, k0:k1, g, :], writes=[b_v])
    S.op("pool", lambda e: e.memset(ik_sb[64:128, :], 0.0), writes=[b_ik])
    S.dma("sp", ik_sb[0:64, :], ikT[:, :], writes=[b_ik])
    mc = S.sbuf("at_mc", [128, 256], F32)
    nI = S.sbuf("at_nI", [128, 4, 128], BF16)
    b_mc, b_nI = S.bufs(2, "at_c")
    S.dma("sp", mc[:, :], maskc[:, :], writes=[b_mc])
    for r in range(4):
        S.dma("pool", nI[:, r, :], negI[:, :], writes=[b_nI])

    sc = S.sbuf("at_sc", [128, SK], F32)
    b_sc = S.buf("at_sc")
    junk = S.sbuf("at_junk", [128, SK], BF16)
    b_junk = S.buf("at_junk")
    mb = [S.sbuf("at_mb%d" % i, [128, SK], BF16) for i in range(2)]
    b_mb = S.bufs(2, "at_mb")
    NT_SB = 4
    NS_PS = 2
    t_sb = [S.sbuf("at_t%d" % i, [128, 512], F32) for i in range(NT_SB)]
    b_t = S.bufs(NT_SB, "at_t")
    NPT = 4
    NLP = 3
    PT = [S.sbuf("at_PT%d" % i, [128, 512], BF16) for i in range(NPT)]
    b_PT = S.bufs(NPT, "at_PT")
    q_sb = [S.sbuf("at_q%d" % i, [128, 4, 4, 128], BF16) for i in range(2)]
    b_q = S.bufs(2, "at_q")
    for i in range(2):
        S.op("pool", lambda e, i=i: e.memset(q_sb[i][:, :, :, :], 0.0), writes=[b_q[i]])
    iq_sb = [S.sbuf("at_iq%d" % i, [128, 8, 128], BF16) for i in range(2)]
    b_iq = S.bufs(2, "at_iq")
    for i in range(2):
        S.op("pool", lambda e, i=i: e.memset(iq_sb[i][64:128, :, :], 0.0), writes=[b_iq[i]])
    iw_sb = [S.sbuf("at_iw%d" % i, [128, 8], F32) for i in range(2)]
    b_iw = S.bufs(2, "at_iw")
    o_sb = [S.sbuf("at_o%d" % i, [128, 16, 64], BF16) for i in range(2)]
    b_o = S.bufs(2, "at_o")
    sm = S.sbuf("at_sm", [128, 16], F32)
    b_sm = S.bufs(8, "at_sm")
    rc = S.sbuf("at_rc", [128, 16], F32)
    b_rc = S.buf("at_rc")
    s_ps = [S.psum("at_sps%d" % i, [128, 512], F32) for i in range(NS_PS)]
    b_sps = S.bufs(NS_PS, "at_sps")
    KB = N_BISECT
    pow2 = S.sbuf("at_pow2", [128, KB + 1], F32)
    wtab = S.sbuf("at_wtab", [128, KB + 1], F32)
    b_pow2, b_wtab = S.bufs(2, "at_w")
    for k in range(KB + 1):
        S.op("pool", lambda e, k=k: e.memset(pow2[:, k:k + 1], float(2.0 ** -k)), writes=[b_pow2])
    l_ps = [S.psum("at_lps%d" % i, [128, 512], F32) for i in range(NLP)]
    b_lps = S.bufs(NLP, "at_lps")
    o_ps = [S.psum("at_ops%d" % i, [128, 7, 65], F32) for i in range(3)]
    b_ops = S.bufs(3, "at_ops")
    LO, HI, MID, CNT, GE, DD, MM = range(7)
    col = lambda i: sm[:, i:i + 1]

    n_s = 0
    n_l = 0
    n_pt = 0

    def load_q(j):
        qs, bq = q_sb[j % 2], b_q[j % 2]
        for hh in range(2):
            S.dma("sp", qs[hh * 64:(hh + 1) * 64, 2 * hh:2 * hh + 2, :, :].rearrange("d a r q -> d (a r) q"),
                  qT[8 * hh:8 * hh + 8, :, j * 128:(j + 1) * 128].rearrange("h d q -> d h q"), writes=[bq])
        S.dma("sp", iq_sb[j % 2][0:64, :, :], iqT[:, :, j * 128:(j + 1) * 128].rearrange("h d q -> d h q"), writes=[b_iq[j % 2]])
        S.dma("sp", iw_sb[j % 2][:, :], iw[j * 128:(j + 1) * 128, :], writes=[b_iw[j % 2]])

    def phase12(j):
        nonlocal n_s
        nk = nkt_of(j) * 128
        iqs, biq = iq_sb[j % 2], b_iq[j % 2]
        iws, biw = iw_sb[j % 2], b_iw[j % 2]
        for hd in range(8):
            for k0 in range(0, nk, 512):
                w = min(512, nk - k0)
                ps, bps = s_ps[n_s % NS_PS], b_sps[n_s % NS_PS]
                tt, bt = t_sb[n_s % NT_SB], b_t[n_s % NT_SB]
                n_s += 1
                S.op("pe", lambda e, ps=ps, hd=hd, k0=k0, w=w, iqs=iqs: e.matmul(
                    ps[:, 0:w], lhsT=iqs[:, hd, :], rhs=ik_sb[:, k0:k0 + w], start=True, stop=True),
                     reads=[biq, b_ik], writes=[bps])
                S.op("act", lambda e, ps=ps, tt=tt, w=w: e.activation(out=tt[:, 0:w], in_=ps[:, 0:w], func=AF.Relu),
                     reads=[bps], writes=[bt])
                if hd == 0:
                    S.op("dve", lambda e, tt=tt, k0=k0, w=w, iws=iws: e.tensor_scalar(
                        out=sc[:, k0:k0 + w], in0=tt[:, 0:w], scalar1=iws[:, 0:1], scalar2=None, op0=ALU.mult),
                         reads=[bt, biw], writes=[b_sc])
                else:
                    S.op("dve", lambda e, tt=tt, k0=k0, w=w, iws=iws, hd=hd: e.scalar_tensor_tensor(
                        out=sc[:, k0:k0 + w], in0=tt[:, 0:w], scalar=iws[:, hd:hd + 1], in1=sc[:, k0:k0 + w],
                        op0=ALU.mult, op1=ALU.add), reads=[bt, biw, b_sc], writes=[b_sc])
        S.op("dve", lambda e: e.tensor_reduce(out=col(MM), in_=sc[:, 0:nk], axis=AX.X, op=ALU.max, apply_absolute_value=True),
             reads=[b_sc], writes=[b_sm[MM]])
        S.op("dve", lambda e: e.tensor_scalar(out=col(HI), in0=col(MM), scalar1=1.0, scalar2=None, op0=ALU.add),
             reads=[b_sm[MM]], writes=[b_sm[HI]])
        S.op("dve", lambda e: e.tensor_scalar(out=wtab[:, :], in0=pow2[:, :], scalar1=col(HI), scalar2=None, op0=ALU.mult),
             reads=[b_sm[HI], b_pow2], writes=[b_wtab])
        if solo:
            S.op("dve", lambda e: e.tensor_tensor(out=sc[:, nk - 128:nk], in0=sc[:, nk - 128:nk], in1=mc[:, 128:256], op=ALU.add),
                 reads=[b_sc, b_mc], writes=[b_sc])
        else:
            S.op("dve", lambda e: e.tensor_tensor(out=sc[:, nk - 256:nk], in0=sc[:, nk - 256:nk], in1=mc[:, :], op=ALU.add),
                 reads=[b_sc, b_mc], writes=[b_sc])
        S.op("dve", lambda e: e.memset(col(MID), 0.0), writes=[b_sm[MID]])
        for it in range(KB):
            S.op("dve", lambda e: e.tensor_scalar(out=junk[:, 0:nk], in0=sc[:, 0:nk], scalar1=col(MID), scalar2=0.0,
                                                   op0=ALU.is_ge, op1=ALU.add, accum_out=col(CNT)),
                 reads=[b_sc, b_sm[MID]], writes=[b_junk, b_sm[CNT]])
            S.op("dve", lambda e, it=it: e.scalar_tensor_tensor(out=col(GE), in0=col(CNT), scalar=TOPK - 0.5, in1=wtab[:, it:it + 1],
                                                                op0=ALU.is_ge, op1=ALU.mult),
                 reads=[b_sm[CNT], b_wtab], writes=[b_sm[GE]])
            S.op("dve", lambda e, it=it: e.scalar_tensor_tensor(out=col(MID), in0=col(MID), scalar=wtab[:, it + 1:it + 2], in1=col(GE),
                                                                op0=ALU.subtract, op1=ALU.add),
                 reads=[b_sm[MID], b_sm[GE], b_wtab], writes=[b_sm[MID]])
        S.op("dve", lambda e: e.tensor_tensor(out=col(LO), in0=col(MID), in1=wtab[:, KB:KB + 1], op=ALU.subtract),
             reads=[b_sm[MID], b_wtab], writes=[b_sm[LO]])
        m, bm = mb[j % 2], b_mb[j % 2]
        S.op("dve", lambda e, m=m: e.tensor_scalar(out=m[:, 0:nk], in0=sc[:, 0:nk], scalar1=col(LO), scalar2=None, op0=ALU.is_lt),
             reads=[b_sc, b_sm[LO]], writes=[bm])
        if dbg is not None and j == dbg["j"]:
            S.dma("sp", dbg["sc"][:, 0:nk], sc[:, 0:nk], reads=[b_sc])
            S.dma("sp", dbg["mb"][:, 0:nk], m[:, 0:nk], reads=[bm])
            S.dma("sp", dbg["sm"][:, :], sm[:, :], reads=b_sm)

    def phase3(j):
        nonlocal n_l, n_pt
        NKT = nkt_of(j)
        qs, bq = q_sb[j % 2], b_q[j % 2]
        m, bm = mb[j % 2], b_mb[j % 2]
        DEPTH = 2
        pend = []

        def emit_pv(item):
            kt, g, pt, bpt = item
            for r in range(4):
                hd = 4 * g + r
                bank, slot = hd // 7, hd % 7
                S.op("pe", lambda e, pt=pt, r=r, kt=kt, g=g, bank=bank, slot=slot: e.matmul(
                    o_ps[bank][:, slot, :], lhsT=pt[:, r * 128:(r + 1) * 128], rhs=v_sb[:, kt, g, :],
                    start=(kt == 0 and slot == 0), stop=(kt == NKT - 1), skip_group_check=True),
                     reads=[bpt, b_v], writes=[b_ops[bank]])

        for kt in range(NKT):
            for g in range(4):
                hh, a = g // 2, g % 2
                lp, blp = l_ps[n_l % NLP], b_lps[n_l % NLP]
                n_l += 1
                pt, bpt = PT[n_pt % NPT], b_PT[n_pt % NPT]
                n_pt += 1
                S.op("pe", lambda e, lp=lp, g=g, a=a, kt=kt, qs=qs: e.matmul(
                    lp[:, :], lhsT=kT_sb[:, a, kt * 128:(kt + 1) * 128],
                    rhs=qs[:, g, :, :].rearrange("d r q -> d (r q)"), start=True, stop=False),
                     reads=[b_kT, bq], writes=[blp])
                S.op("pe", lambda e, lp=lp, kt=kt, m=m: e.matmul(
                    lp[:, :], lhsT=m[:, kt * 128:(kt + 1) * 128], rhs=nI[:, :, :].rearrange("p r q -> p (r q)"),
                    start=False, stop=True), reads=[bm, b_nI], writes=[blp])
                S.op("act", lambda e, lp=lp, pt=pt: e.activation(out=pt[:, :], in_=lp[:, :], func=AF.Exp),
                     reads=[blp], writes=[bpt])
                pend.append((kt, g, pt, bpt))
                if len(pend) > DEPTH:
                    emit_pv(pend.pop(0))
        while pend:
            emit_pv(pend.pop(0))
        ob, bo = o_sb[j % 2], b_o[j % 2]
        for bank in range(3):
            nh = min(7, 16 - 7 * bank)
            S.op("dve", lambda e, bank=bank, nh=nh: e.reciprocal(out=rc[:, 0:nh], in_=o_ps[bank][:, 0:nh, 64]),
                 reads=[b_ops[bank]], writes=[b_rc])
            S.op("dve", lambda e, bank=bank, nh=nh, ob=ob: e.tensor_tensor(
                out=ob[:, 7 * bank:7 * bank + nh, :], in0=o_ps[bank][:, 0:nh, 0:64],
                in1=rc[:, 0:nh].unsqueeze(2).to_broadcast([128, nh, 64]), op=ALU.mult),
                 reads=[b_ops[bank], b_rc], writes=[bo])
        S.dma("pool", o[j * 128:(j + 1) * 128, :], ob[:, :, :].rearrange("p h d -> p (h d)"), reads=[bo])

    load_q(0)
    phase12(0)
    for j in range(NQ):
        if j + 1 < NQ:
            load_q(j + 1)
            phase12(j + 1)
        phase3(j)


def stage_wo_ln(S, C, T, o, h, w_o, g, b, h1, gate=None):
    wo = S.sbuf("wo", [128, 8, D], BF16)
    b_wo = S.bufs(8, "wo")
    wv = w_o.rearrange("(c p) n -> p c n", p=128)
    for c in range(8):
        S.dma("pool", wo[:, c, :], wv[:, c, :], writes=[b_wo[c]])
    g_t, b_g = load_bcast(S, "wo_g", g, D)
    bt_t, b_bt = load_bcast(S, "wo_b", b, D)
    ob = [S.sbuf("wo_ob%d" % i, [128, D], BF16) for i in range(2)]
    b_ob = S.bufs(2, "wo_ob")
    hs = [S.sbuf("wo_hs%d" % i, [128, D], F32) for i in range(2)]
    b_hs = S.bufs(2, "wo_hs")
    oTs = [S.sbuf("wo_oT%d" % i, [128, 8, 128], BF16) for i in range(2)]
    b_oTs = S.bufs(2, "wo_oT")
    y_sb = [S.sbuf("wo_y%d" % i, [128, D], F32) for i in range(2)]
    b_y = S.bufs(2, "wo_y")
    tps = [S.psum("wo_tp%d" % i, [128, 1024], BF16) for i in range(2)]
    b_tps = S.bufs(2, "wo_tp")
    mx = [S.psum("wo_mx%d" % i, [128, 512], F32) for i in range(4)]
    b_mx = S.bufs(4, "wo_mx")
    scrs = [ln_scratch(S, "wo%d" % i) for i in range(2)]
    ov = o.rearrange("(t p) d -> t p d", p=128) if gate is None else None
    if gate is not None:
        gate_a = [S.sbuf("wo_ga%d" % i, [128, D], F32) for i in range(2)]
        gate_b = [S.sbuf("wo_gb%d" % i, [128, D], F32) for i in range(2)]
        b_ga = S.bufs(2, "wo_ga")
        b_gb = S.bufs(2, "wo_gb")
    hv = h.rearrange("(t p) d -> t p d", p=128)
    h1v = h1.rearrange("(t p) d -> t p d", p=128)
    for t in range(T // 128):
        x, bx = ob[t % 2], b_ob[t % 2]
        hh, bh = hs[t % 2], b_hs[t % 2]
        if gate is None:
            S.dma("sp", x[:, :], ov[t], writes=[bx])
        else:
            ga, gb_ = gate_a[t % 2], gate_b[t % 2]
            S.dma("sp", ga[:, :], gate[0].rearrange("(t p) d -> t p d", p=128)[t], writes=[b_ga[t % 2]])
            S.dma("sp", gb_[:, :], gate[1].rearrange("(t p) d -> t p d", p=128)[t], writes=[b_gb[t % 2]])
            S.op("dve", lambda e, x=x, ga=ga, gb_=gb_: e.tensor_tensor(out=x[:, :], in0=ga[:, :], in1=gb_[:, :], op=ALU.mult),
                 reads=[b_ga[t % 2], b_gb[t % 2]], writes=[bx])
        S.dma("act", hh[:, :], hv[t], writes=[bh])
        tp, b_tp = tps[t % 2], b_tps[t % 2]
        oT, b_oT = oTs[t % 2], b_oTs[t % 2]
        scr = scrs[t % 2]
        for c in range(8):
            S.op("pe", lambda e, c=c, x=x, tp=tp: e.transpose(out=tp[:, c * 128:(c + 1) * 128], in_=x[:, c * 128:(c + 1) * 128],
                                                       identity=C.ident[:]), reads=[bx, C.b_ident], writes=[b_tp])
        S.op("act", lambda e, oT=oT, tp=tp: e.activation(out=oT[:, :, :], in_=tp[:, :].rearrange("p (c t) -> p c t", t=128), func=AF.Copy),
             reads=[b_tp], writes=[b_oT])
        for nh in range(2):
            ps, bps = mx[(2 * t + nh) % 4], b_mx[(2 * t + nh) % 4]
            for c in range(8):
                S.op("pe", lambda e, ps=ps, c=c, nh=nh, oT=oT: e.matmul(ps[:, :], lhsT=oT[:, c, :], rhs=wo[:, c, nh * 512:(nh + 1) * 512],
                                                                  start=(c == 0), stop=(c == 7)),
                     reads=[b_oT, b_wo[c]], writes=[bps])
            S.op("dve", lambda e, ps=ps, hh=hh, nh=nh: e.scalar_tensor_tensor(
                out=hh[:, nh * 512:(nh + 1) * 512], in0=hh[:, nh * 512:(nh + 1) * 512], scalar=ALPHA, in1=ps[:, :],
                op0=ALU.mult, op1=ALU.add), reads=[bps, bh], writes=[bh])
        y, by = y_sb[t % 2], b_y[t % 2]
        layer_norm_tile(S, hh, bh, y, by, g_t, b_g, bt_t, b_bt, scr)
        S.dma("pool", h1v[t], y[:, :], reads=[by])


B_IN = 3088
GATE_TAU = 16.0


def stage_proj_B(S, C, T, h, w_in, w_a2, b_a, cmU, cmW, qgT, kgT, kh, vb, dec, sr):
    NTL = T // 128
    win = S.sbuf("pb_win", [128, 8, B_IN], BF16)
    b_win = S.bufs(8, "pb_win")
    wv = w_in.rearrange("(c p) n -> p c n", p=128)
    for c in range(8):
        S.dma("pool", win[:, c, :], wv[:, c, :], writes=[b_win[c]])
    wa2 = S.sbuf("pb_wa2", [16, 512], BF16)
    U = S.sbuf("pb_U", [128, 128], BF16)
    W = S.sbuf("pb_W", [128, 128], BF16)
    b_wa2, b_U, b_W = S.bufs(3, "pb_c")
    S.dma("pool", wa2[:, :], w_a2[:, :], writes=[b_wa2])
    S.dma("pool", U[:, :], cmU[:, :], writes=[b_U])
    S.dma("pool", W[:, :], cmW[:, :], writes=[b_W])
    ba_t, b_ba = load_bcast(S, "pb_ba", b_a, 512)

    hs = [S.sbuf("pb_hs%d" % i, [128, D], F32) for i in range(2)]
    b_hs = S.bufs(2, "pb_hs")
    def _mk(i):
        return dict(hb=S.sbuf("pb_hb%d" % i, [128, D], BF16), b_hb=S.buf("pb_hb"), hT=S.sbuf("pb_hT%d" % i, [128, 8, 128], BF16), b_hT=S.buf("pb_hT"),
                    alT=S.sbuf("pb_alT%d" % i, [16, 128], BF16), b_alT=S.buf("pb_alT"), gg=S.sbuf("pb_g%d" % i, [128, 512], F32), b_gg=S.buf("pb_g"),
                    ghi=S.sbuf("pb_ghi%d" % i, [128, 512], BF16), glo=S.sbuf("pb_glo%d" % i, [128, 512], BF16), b_ghi=S.buf("pb_ghi"), b_glo=S.buf("pb_glo"),
                    EbT=S.sbuf("pb_EbT%d" % i, [128, 4, 128], F32), EnbT=S.sbuf("pb_EnbT%d" % i, [128, 4, 128], F32), Ebl=S.sbuf("pb_Ebl%d" % i, [128, 512], F32),
                    b_EbT=S.buf("pb_EbT"), b_EnbT=S.buf("pb_EnbT"), b_Ebl=S.buf("pb_Ebl"))
    sets = [_mk(0), _mk(1)]
    qg_s = [S.sbuf("pb_qg%d" % i, [128, 4, 128], BF16) for i in range(2)]
    kg_s = [S.sbuf("pb_kg%d" % i, [128, 4, 128], BF16) for i in range(2)]
    kh_s = [S.sbuf("pb_kh%d" % i, [128, 512], BF16) for i in range(2)]
    vb_s = [S.sbuf("pb_vb%d" % i, [128, 1024], BF16) for i in range(2)]
    sr_s = [S.sbuf("pb_sr%d" % i, [128, 1024], F32) for i in range(2)]
    dc_s = [S.sbuf("pb_dc%d" % i, [128, 4, 2], F32) for i in range(2)]
    b_qg, b_kg, b_kh, b_vb, b_sr, b_dc = (S.bufs(2, "pb_o%d" % i) for i in range(6))
    tpT = S.psum("pb_tp", [128, 1024], BF16)
    b_tpT = S.buf("pb_tp")
    pk = [S.psum("pb_pk%d" % i, [128, 512], F32) for i in range(7)]
    b_pk = S.bufs(7, "pb_pk")
    QSCALE = float(128 ** -0.5)

    hv = h.rearrange("(t p) d -> t p d", p=128)
    def _tile(t, hb, b_hb, hT, b_hT, alT, b_alT, gg, b_gg, ghi, glo, b_ghi, b_glo, EbT, EnbT, Ebl, b_EbT, b_EnbT, b_Ebl):
        x, bx = hs[t % 2], b_hs[t % 2]
        i2 = t % 2
        S.dma("sp", x[:, :], hv[t], writes=[bx])
        S.op("act", lambda e, x=x: e.activation(out=hb[:, :], in_=x[:, :], func=AF.Copy), reads=[bx], writes=[b_hb])
        for c in range(8):
            S.op("pe", lambda e, c=c: e.transpose(out=tpT[:, c * 128:(c + 1) * 128], in_=hb[:, c * 128:(c + 1) * 128],
                                                  identity=C.ident[:]), reads=[b_hb, C.b_ident], writes=[b_tpT])
        S.op("dve", lambda e: e.tensor_copy(out=hT[:, :, :], in_=tpT[:, :].rearrange("p (c t) -> p c t", t=128)),
             reads=[b_tpT], writes=[b_hT])

        def tok_mm(bank, n0, n1):
            for c in range(8):
                S.op("pe", lambda e, c=c: e.matmul(pk[bank][:, 0:n1 - n0], lhsT=hT[:, c, :], rhs=win[:, c, n0:n1],
                                                   start=(c == 0), stop=(c == 7)), reads=[b_hT, b_win[c]], writes=[b_pk[bank]])

        def feat_mm(bank, slot, n0, m):
            for c in range(8):
                S.op("pe", lambda e, c=c: e.matmul(pk[bank][0:m, slot * 128:(slot + 1) * 128], lhsT=win[:, c, n0:n0 + m],
                                                   rhs=hT[:, c, :], start=(c == 0 and slot == 0), stop=(c == 7),
                                                   skip_group_check=True), reads=[b_hT, b_win[c]], writes=[b_pk[bank]])

        feat_mm(6, 0, 3072, 16)
        S.op("act", lambda e: e.activation(out=alT[:, :], in_=pk[6][0:16, 0:128], func=AF.Copy), reads=[b_pk[6]], writes=[b_alT])
        S.op("pe", lambda e: e.matmul(pk[5][:, :], lhsT=alT[:, :], rhs=wa2[:, :], start=True, stop=True),
             reads=[b_alT, b_wa2], writes=[b_pk[5]])
        S.op("dve", lambda e: e.tensor_tensor(out=gg[:, :], in0=pk[5][:, :], in1=ba_t[:, :], op=ALU.add),
             reads=[b_pk[5], b_ba], writes=[b_gg])
        S.op("act", lambda e: e.activation(out=gg[:, :], in_=gg[:, :], func=AF.Exp, scale=-1.0), reads=[b_gg], writes=[b_gg])
        S.op("dve", lambda e: e.tensor_scalar(out=gg[:, :], in0=gg[:, :], scalar1=1.0, scalar2=None, op0=ALU.add),
             reads=[b_gg], writes=[b_gg])
        S.op("act", lambda e: e.activation(out=gg[:, :], in_=gg[:, :], func=AF.Ln), reads=[b_gg], writes=[b_gg])
        S.op("dve", lambda e: e.tensor_scalar(out=gg[:, :], in0=gg[:, :], scalar1=-1.0 / GATE_TAU, scalar2=None, op0=ALU.mult),
             reads=[b_gg], writes=[b_gg])
        S.op("dve", lambda e: e.tensor_copy(out=ghi[:, :], in_=gg[:, :]), reads=[b_gg], writes=[b_ghi])
        S.op("dve", lambda e: e.tensor_tensor(out=glo[:, :], in0=gg[:, :], in1=ghi[:, :], op=ALU.subtract),
             reads=[b_gg, b_ghi], writes=[b_glo])
        for hd in range(4):
            for part, (gs, bgs) in enumerate(((ghi, b_ghi), (glo, b_glo))):
                S.op("pe", lambda e, hd=hd, gs=gs, part=part: e.matmul(
                    pk[4][:, hd * 128:(hd + 1) * 128], lhsT=gs[:, hd * 128:(hd + 1) * 128], rhs=U[:, :],
                    start=(hd == 0 and part == 0), stop=(part == 1), skip_group_check=True),
                     reads=[bgs, b_U], writes=[b_pk[4]])
        for part, (gs, bgs) in enumerate(((ghi, b_ghi), (glo, b_glo))):
            S.op("pe", lambda e, gs=gs, part=part: e.matmul(pk[5][:, :], lhsT=W[:, :], rhs=gs[:, :], start=(part == 0), stop=(part == 1)),
                 reads=[bgs, b_W], writes=[b_pk[5]])
        S.op("act", lambda e: e.activation(out=EbT[:, :, :], in_=pk[4][:, :].rearrange("p (h i) -> p h i", i=128), func=AF.Exp),
             reads=[b_pk[4]], writes=[b_EbT])
        S.op("act", lambda e: e.activation(out=EnbT[:, :, :], in_=pk[4][:, :].rearrange("p (h i) -> p h i", i=128), func=AF.Exp, scale=-1.0),
             reads=[b_pk[4]], writes=[b_EnbT])
        S.op("act", lambda e: e.activation(out=Ebl[:, :], in_=pk[5][:, :], func=AF.Exp), reads=[b_pk[5]], writes=[b_Ebl])
        dcs, bdc = dc_s[i2], b_dc[i2]
        S.op("pool", lambda e, dcs=dcs: e.tensor_copy(out=dcs[:, :, :], in_=EbT[:, :, :].rearrange("p h (c j) -> p h c j", j=64)[:, :, :, 63]),
             reads=[b_EbT], writes=[bdc])
        S.dma("pool", dec[:, :, 2 * t:2 * t + 2].rearrange("h d c -> d h c"), dcs[:, :, :], reads=[bdc])
        for hd in range(4):
            feat_mm(6, hd, hd * 128, 128)
        qgs, bqg = qg_s[i2], b_qg[i2]
        S.op("dve", lambda e, qgs=qgs: e.scalar_tensor_tensor(out=qgs[:, :, :], in0=pk[6][:, :].rearrange("p (h i) -> p h i", i=128),
                                                              scalar=QSCALE, in1=EbT[:, :, :], op0=ALU.mult, op1=ALU.mult),
             reads=[b_pk[6], b_EbT], writes=[bqg])
        S.dma("pool", qgT[:, :, t * 128:(t + 1) * 128].rearrange("h d i -> d h i"), qgs[:, :, :], reads=[bqg])
        for hd in range(4):
            feat_mm(3, hd, 512 + hd * 128, 128)
        kgs, bkg = kg_s[i2], b_kg[i2]
        S.op("dve", lambda e, kgs=kgs: e.tensor_tensor(out=kgs[:, :, :], in0=pk[3][:, :].rearrange("p (h i) -> p h i", i=128),
                                                       in1=EnbT[:, :, :], op=ALU.mult), reads=[b_pk[3], b_EnbT], writes=[bkg])
        S.dma("pool", kgT[:, :, t * 128:(t + 1) * 128].rearrange("h d i -> d h i"), kgs[:, :, :], reads=[bkg])
        tok_mm(2, 512, 1024)
        khs, bkh = kh_s[i2], b_kh[i2]
        S.op("dve", lambda e, khs=khs: e.tensor_tensor(out=khs[:, :], in0=pk[2][:, :], in1=Ebl[:, :], op=ALU.mult),
             reads=[b_pk[2], b_Ebl], writes=[bkh])
        S.dma("pool", kh[t * 128:(t + 1) * 128, :], khs[:, :], reads=[bkh])
        vbs, bvb = vb_s[i2], b_vb[i2]
        for half in range(2):
            tok_mm(half, 1024 + half * 512, 1536 + half * 512)
            S.op("act", lambda e, half=half, vbs=vbs: e.activation(out=vbs[:, half * 512:(half + 1) * 512], in_=pk[half][:, :], func=AF.Copy),
                 reads=[b_pk[half]], writes=[bvb])
        S.dma("pool", vb[t * 128:(t + 1) * 128, :], vbs[:, :], reads=[bvb])
        srs, bsr = sr_s[i2], b_sr[i2]
        for half in range(2):
            tok_mm(half, 2048 + half * 512, 2560 + half * 512)
            S.op("act", lambda e, half=half, srs=srs: e.activation(out=srs[:, half * 512:(half + 1) * 512], in_=pk[half][:, :], func=AF.Silu),
                 reads=[b_pk[half]], writes=[bsr])
        S.dma("pool", sr[t * 128:(t + 1) * 128, :], srs[:, :], reads=[bsr])

    for t in range(NTL):
        _tile(t, **sets[t % 2])

RMS_EPS = 1e-6
GLA_CB = 16


def stage_gla(S, C, SEQ, NH, acc, dec_ap, g_norm, tri, CB=GLA_CB):
    NBLK = SEQ // (64 * CB)
    NCH = SEQ // 64
    tri_f = S.sbuf("gl_trif", [64, 64], F32)
    b_tri = S.buf("gl_tri")
    S.dma("sp", tri_f[:, :], tri[:, :], writes=[b_tri])
    gn = S.sbuf("gl_gn", [64, 256], F32)
    b_gn = S.buf("gl_gn")
    S.dma("sp", gn[:, :], g_norm.partition_broadcast(64), writes=[b_gn])
    dec_sb = S.sbuf("gl_dec", [128, NH, NCH], F32)
    b_dec = S.buf("gl_dec")
    for hd in range(NH):
        S.dma("sp", dec_sb[:, hd, :], dec_ap(hd), writes=[b_dec])
    st = S.sbuf("gl_st", [128, NH, 256], F32)
    b_st = S.bufs(NH, "gl_st")
    stb = [S.sbuf("gl_stb%d" % i, [128, NH, 256], BF16) for i in range(2)]
    b_stb = [S.bufs(NH, "gl_stb%d_" % i) for i in range(2)]
    S.op("dve", lambda e: e.memset(st[:, :, :], 0.0), writes=b_st)
    S.op("dve", lambda e: e.memset(stb[0][:, :, :], 0.0), writes=b_stb[0])
    qg_sb = [[S.sbuf("gl_qg%d_%d" % (i, hd), [128, 64 * CB], BF16) for hd in range(NH)] for i in range(2)]
    kg_sb = [[S.sbuf("gl_kg%d_%d" % (i, hd), [128, 64 * CB], BF16) for hd in range(NH)] for i in range(2)]
    kh_sb = [[S.sbuf("gl_kh%d_%d" % (i, hd), [64, CB, 128], BF16) for hd in range(NH)] for i in range(2)]
    v_sb = [[S.sbuf("gl_v%d_%d" % (i, hd), [64, CB, 256], BF16) for hd in range(NH)] for i in range(2)]
    b_in = [[S.bufs(4, "gl_in%d_%d_" % (i, hd)) for hd in range(NH)] for i in range(2)]
    o_st = [[S.sbuf("gl_ost%d_%d" % (i, hd), [64, CB, 256], F32) for hd in range(NH)] for i in range(2)]
    b_ost = [[S.buf("gl_ost%d_%d" % (i, hd)) for hd in range(NH)] for i in range(2)]
    ss = [[S.sbuf("gl_ss%d_%d" % (i, hd), [64, CB], F32) for hd in range(NH)] for i in range(2)]
    b_ss = [[S.buf("gl_ss%d_%d" % (i, hd)) for hd in range(NH)] for i in range(2)]
    junk = S.sbuf("gl_junk", [64, 256], F32)
    b_junk = S.buf("gl_junk")
    A_sb = [S.sbuf("gl_A%d" % i, [64, 64], BF16) for i in range(4)]
    b_A = S.bufs(4, "gl_A")
    a_bank = S.psum("gl_aps", [64, 512], F32)
    a_ps = [a_bank[:, i * 64:(i + 1) * 64] for i in range(4)]
    b_aps = S.bufs(4, "gl_aps")
    o_bank = [S.psum("gl_ops%d" % i, [64, 512], F32) for i in range(4)]
    o_ps = [o_bank[i][:, 0:256] for i in range(4)]
    b_ops = S.bufs(4, "gl_ops")
    s_bank = [S.psum("gl_sps%d" % i, [128, 512], F32) for i in range(2)]
    s_ps = [s_bank[i // 2][:, (i % 2) * 256:(i % 2) * 256 + 256] for i in range(4)]
    b_sps = S.bufs(4, "gl_sps")
    n = 0
    for blk in range(NBLK):
        i2 = blk % 2
        for hd in range(NH):
            bi = b_in[i2][hd]
            S.dma("sp", qg_sb[i2][hd][:, :], acc("qg", hd, blk), writes=[bi[0]])
            S.dma("sp", kg_sb[i2][hd][:, :], acc("kg", hd, blk), writes=[bi[1]])
            S.dma("sp", kh_sb[i2][hd][:, :, :], acc("kh", hd, blk), writes=[bi[2]])
            S.dma("sp", v_sb[i2][hd][:, :, :], acc("v", hd, blk), writes=[bi[3]])
        for cc in range(CB):
            c = blk * CB + cc
            cur, nxt = c % 2, (c + 1) % 2
            for hd in range(NH):
                bi = b_in[i2][hd]
                qg = qg_sb[i2][hd][:, cc * 64:(cc + 1) * 64]
                kg = kg_sb[i2][hd][:, cc * 64:(cc + 1) * 64]
                khc = kh_sb[i2][hd][:, cc, :]
                vc = v_sb[i2][hd][:, cc, :]
                aps, baps = a_ps[n % 4], b_aps[n % 4]
                ops, bops = o_ps[n % 4], b_ops[n % 4]
                sps, bsps = s_ps[n % 4], b_sps[n % 4]
                A, bA = A_sb[n % 4], b_A[n % 4]
                n += 1
                S.op("pe", lambda e, aps=aps, kg=kg, qg=qg: e.matmul(aps, lhsT=kg, rhs=qg, start=True, stop=True, skip_group_check=True),
                     reads=[bi[0], bi[1]], writes=[baps])
                S.op("dve", lambda e, aps=aps, A=A: e.tensor_tensor(out=A[:, :], in0=aps, in1=tri_f[:, :], op=ALU.mult),
                     reads=[baps, b_tri], writes=[bA])
                S.op("pe", lambda e, ops=ops, A=A, vc=vc: e.matmul(ops, lhsT=A[:, :], rhs=vc, start=True, stop=False),
                     reads=[bA, bi[3]], writes=[bops])
                S.op("pe", lambda e, ops=ops, qg=qg, cur=cur, hd=hd: e.matmul(ops, lhsT=qg, rhs=stb[cur][:, hd, :],
                                                                             start=False, stop=True),
                     reads=[bi[0], b_stb[cur][hd]], writes=[bops])
                S.op("pe", lambda e, sps=sps, khc=khc, vc=vc: e.matmul(sps, lhsT=khc, rhs=vc, start=True, stop=True, skip_group_check=True),
                     reads=[bi[2], bi[3]], writes=[bsps])
                S.op("dve", lambda e, sps=sps, hd=hd, c=c: e.scalar_tensor_tensor(
                    out=st[:, hd, :], in0=st[:, hd, :], scalar=dec_sb[:, hd, c:c + 1], in1=sps,
                    op0=ALU.mult, op1=ALU.add), reads=[bsps, b_st[hd], b_dec], writes=[b_st[hd]])
                S.op("act", lambda e, hd=hd, nxt=nxt: e.activation(out=stb[nxt][:, hd, :], in_=st[:, hd, :], func=AF.Copy),
                     reads=[b_st[hd]], writes=[b_stb[nxt][hd]])
                ssc = ss[i2][hd][:, cc:cc + 1]
                ostc = o_st[i2][hd][:, cc, :]
                S.op("act", lambda e, ops=ops, ssc=ssc: e.activation(out=junk[:, :], in_=ops, func=AF.Square, accum_out=ssc),
                     reads=[bops], writes=[b_junk, b_ss[i2][hd]])
                S.op("act", lambda e, ops=ops, ostc=ostc: e.activation(out=ostc, in_=ops, func=AF.Copy),
                     reads=[bops], writes=[b_ost[i2][hd]])
        for hd in range(NH):
            s_, bs_ = ss[i2][hd], b_ss[i2][hd]
            S.op("dve", lambda e, s_=s_: e.tensor_scalar(out=s_[:, :], in0=s_[:, :], scalar1=1.0 / 256, scalar2=RMS_EPS,
                                                         op0=ALU.mult, op1=ALU.add), reads=[bs_], writes=[bs_])
            S.op("act", lambda e, s_=s_: e.activation(out=s_[:, :], in_=s_[:, :], func=AF.Sqrt), reads=[bs_], writes=[bs_])
            S.op("dve", lambda e, s_=s_: e.reciprocal(out=s_[:, :], in_=s_[:, :]), reads=[bs_], writes=[bs_])
            ot, bo = o_st[i2][hd], b_ost[i2][hd]
            S.op("dve", lambda e, ot=ot, s_=s_: e.tensor_tensor(out=ot[:, :, :], in0=ot[:, :, :],
                                                               in1=s_[:, :].unsqueeze(2).to_broadcast([64, CB, 256]), op=ALU.mult),
                 reads=[bo, bs_], writes=[bo])
            S.op("pool", lambda e, ot=ot: e.tensor_tensor(out=ot[:, :, :], in0=ot[:, :, :],
                                                         in1=gn[:, :].unsqueeze(1).to_broadcast([64, CB, 256]), op=ALU.mult),
                 reads=[bo, b_gn], writes=[bo])
            S.dma("pool", acc("on", hd, blk), ot[:, :, :], reads=[bo])


_NP2DT = {np.dtype("float32"): F32, np.dtype("int32"): I32}
try:
    import ml_dtypes
    _NP2DT[np.dtype(ml_dtypes.bfloat16)] = BF16
    NPBF16 = ml_dtypes.bfloat16
except Exception:
    NPBF16 = None

_PROG_CACHE = {}


def launch(key, prog, in_list, outs):
    sig = (key, tuple((k, v.shape, str(v.dtype)) for k, v in in_list[0].items()), tuple((k, tuple(sh), str(dt)) for k, (sh, dt) in outs.items()))
    if sig not in _PROG_CACHE:
        nc = bass.Bass("TRN2", target_bir_lowering=False)
        aps = {}
        for k, v in in_list[0].items():
            aps[k] = nc.dram_tensor(k, list(v.shape), _NP2DT[v.dtype], kind="ExternalInput").ap()
        for k, (shape, dt) in outs.items():
            aps[k] = nc.dram_tensor(k, list(shape), dt, kind="ExternalOutput").ap()
        S = Sched(nc)
        prog(S, aps)
        S.emit()
        _PROG_CACHE[sig] = nc
    nc = _PROG_CACHE[sig]
    res = run_bass_kernel_spmd(nc, in_list, core_ids=list(range(len(in_list))))
    return res.results


def _consts():
    ident = np.eye(128, dtype=np.float32)
    tri = np.where(np.arange(128)[None, :] <= np.arange(128)[:, None], 0.0, NEG_MASK).astype(np.float32)
    allm = np.full((128, 128), NEG_MASK, np.float32)
    none = np.zeros((128, 128), np.float32)
    maskc = [np.concatenate([tri, allm], 1), np.concatenate([none, tri], 1)]
    negI = (-MASK_BIG * np.eye(128)).astype(np.float32)
    invf = (500000.0 ** (-np.arange(0, 16, 2, dtype=np.float32) / 16)).astype(np.float32)
    j = np.arange(128)[:, None]
    i = np.arange(128)[None, :]
    same = (j // 64) == (i // 64)
    cmU = (same & (j <= i)).astype(np.float32)
    cmW = (same & (j > i)).astype(np.float32)
    tri64 = (np.arange(64)[:, None] <= np.arange(64)[None, :]).astype(np.float32)
    return dict(ident=ident, maskc=maskc, negI=negI, invf=invf, cmU=cmU, cmW=cmW, tri64=tri64)


def kernel_unfused(x, positions, a_w_in, a_w_o, b_w_in, b_w_a2, b_b_a, b_g_norm, b_w_o,
           ln_mix_g, ln_mix_b, mlp_w_up, mlp_w_down, ln_mlp_g, ln_mlp_b):
    x = np.asarray(x, np.float32)
    B, SEQ, _ = x.shape
    NC = 2 * B
    T = SEQ // 2
    NTL = T // 128
    NQ = NTL
    cst = _consts()
    f32 = lambda a: np.ascontiguousarray(np.asarray(a, np.float32))
    tok = []
    for c in range(NC):
        r = c % 2
        tiles = np.arange(r, SEQ // 128, 2)
        tok.append((tiles[:, None] * 128 + np.arange(128)[None, :]).reshape(-1))
    h = [np.ascontiguousarray(x[c // 2][tok[c]]) for c in range(NC)]
    pos_pt = [np.ascontiguousarray(np.asarray(positions)[c // 2][tok[c]].astype(np.int32).reshape(NTL, 128).T) for c in range(NC)]
    depth = ln_mix_g.shape[0]
    for i in range(depth):
        j = i // 2
        if i % 2 == 0:
            w_in = f32(a_w_in[j])
            ins = [dict(h=h[c], pos=pos_pt[c], w=w_in, invf=cst["invf"], ident=cst["ident"]) for c in range(NC)]
            outs = {"qT": ([1024, T], BF16), "kT": ([256, T], BF16), "v": ([T, 256], BF16), "iqT": ([512, T], BF16),
                    "ikT": ([64, T], BF16), "iw": ([T, 8], F32)}

            def prog(S, a):
                C = Consts(S, a["ident"])
                stage_proj_A(S, C, T, a["h"], a["pos"], a["w"], a["invf"], a["qT"], a["kT"], a["v"], a["iqT"], a["ikT"], a["iw"])
            pr = launch("projA", prog, ins, outs)
            ins = []
            for c in range(NC):
                b0 = (c // 2) * 2
                kT = np.empty((256, SEQ), NPBF16)
                ikT = np.empty((64, SEQ), NPBF16)
                vf = np.empty((SEQ, 256), NPBF16)
                for r in range(2):
                    kT[:, tok[b0 + r]] = pr[b0 + r]["kT"]
                    ikT[:, tok[b0 + r]] = pr[b0 + r]["ikT"]
                    vf[tok[b0 + r]] = pr[b0 + r]["v"]
                ins.append(dict(ident=cst["ident"], kT=kT, v=vf, ikT=ikT, qT=pr[c]["qT"], iqT=pr[c]["iqT"], iw=pr[c]["iw"],
                                maskc=cst["maskc"][c % 2], negI=cst["negI"]))

            def prog(S, a):
                C = Consts(S, a["ident"])
                stage_attn(S, C, NQ, a["kT"].rearrange("(g d) s -> g d s", d=64), a["v"], a["ikT"],
                           a["qT"].rearrange("(h d) t -> h d t", d=64), a["iqT"].rearrange("(h d) t -> h d t", d=64),
                           a["iw"], a["maskc"], a["negI"], a["o"])
            ar = launch("attn", prog, ins, {"o": ([T, 1024], BF16)})
            w_o = f32(a_w_o[j])
            ins = [dict(ident=cst["ident"], o=ar[c]["o"], h=h[c], w=w_o, g=f32(ln_mix_g[i]), b=f32(ln_mix_b[i])) for c in range(NC)]

            def prog(S, a):
                C = Consts(S, a["ident"])
                stage_wo_ln(S, C, T, a["o"], a["h"], a["w"], a["g"], a["b"], a["h1"])
            wr = launch("wo", prog, ins, {"h1": ([T, D], F32)})
        else:
            ins = [dict(h=h[c], w=f32(b_w_in[j]), wa2=f32(b_w_a2[j]), ba=f32(b_b_a[j]), U=cst["cmU"], W=cst["cmW"], ident=cst["ident"])
                   for c in range(NC)]
            outs = {"qgT": ([512, T], BF16), "kgT": ([512, T], BF16), "kh": ([T, 512], BF16), "vb": ([T, 1024], BF16),
                    "dec": ([512, T // 64], F32), "sr": ([T, 1024], F32)}

            def prog(S, a):
                C = Consts(S, a["ident"])
                stage_proj_B(S, C, T, a["h"], a["w"], a["wa2"], a["ba"], a["U"], a["W"],
                             a["qgT"].rearrange("(h d) t -> h d t", d=128), a["kgT"].rearrange("(h d) t -> h d t", d=128),
                             a["kh"], a["vb"], a["dec"].rearrange("(h d) c -> h d c", d=128), a["sr"])
            pr = launch("projB", prog, ins, outs)
            ins = []
            ctok = [t_.reshape(-1, 128)[:, ::64].reshape(-1) // 64 for t_ in tok]
            for c in range(NC):
                b0 = (c // 2) * 2
                hs = slice((c % 2) * 256, (c % 2) * 256 + 256)
                vs = slice((c % 2) * 512, (c % 2) * 512 + 512)
                qg = np.empty((256, SEQ), NPBF16)
                kg = np.empty((256, SEQ), NPBF16)
                khh = np.empty((SEQ, 256), NPBF16)
                vv = np.empty((SEQ, 512), NPBF16)
                dd = np.empty((256, SEQ // 64), np.float32)
                for r in range(2):
                    qg[:, tok[b0 + r]] = pr[b0 + r]["qgT"][hs]
                    kg[:, tok[b0 + r]] = pr[b0 + r]["kgT"][hs]
                    khh[tok[b0 + r]] = pr[b0 + r]["kh"][:, hs]
                    vv[tok[b0 + r]] = pr[b0 + r]["vb"][:, vs]
                    dd[:, ctok[b0 + r]] = pr[b0 + r]["dec"][hs]
                ins.append(dict(ident=cst["ident"], qgT=qg, kgT=kg, kh=khh, vb=vv, dec=dd, gn=f32(b_g_norm[j]), tri=cst["tri64"]))

            def prog(S, a):
                C = Consts(S, a["ident"])

                def acc(kind, hd, blk):
                    t0, t1 = blk * 1024, (blk + 1) * 1024
                    if kind == "qg":
                        return a["qgT"][hd * 128:(hd + 1) * 128, t0:t1]
                    if kind == "kg":
                        return a["kgT"][hd * 128:(hd + 1) * 128, t0:t1]
                    if kind == "kh":
                        return a["kh"][t0:t1, hd * 128:(hd + 1) * 128].rearrange("(c j) d -> j c d", j=64)
                    if kind == "v":
                        return a["vb"][t0:t1, hd * 256:(hd + 1) * 256].rearrange("(c j) e -> j c e", j=64)
                    return a["on"][t0:t1, hd * 256:(hd + 1) * 256].rearrange("(c j) e -> j c e", j=64)
                stage_gla(S, C, SEQ, 2, acc, lambda hd: a["dec"][hd * 128:(hd + 1) * 128, :], a["gn"], a["tri"])
            gr = launch("gla", prog, ins, {"on": ([SEQ, 512], F32)})
            w_o = f32(b_w_o[j])
            ins = []
            for c in range(NC):
                b0 = (c // 2) * 2
                on = np.concatenate([gr[b0]["on"][tok[c]], gr[b0 + 1]["on"][tok[c]]], axis=1)
                ins.append(dict(ident=cst["ident"], on=np.ascontiguousarray(on), sr=pr[c]["sr"], h=h[c], w=w_o,
                                g=f32(ln_mix_g[i]), b=f32(ln_mix_b[i])))

            def prog(S, a):
                C = Consts(S, a["ident"])
                stage_wo_ln(S, C, T, None, a["h"], a["w"], a["g"], a["b"], a["h1"], gate=(a["on"], a["sr"]))
            wr = launch("wog", prog, ins, {"h1": ([T, D], F32)})
        ins = [dict(ident=cst["ident"], h1=wr[c]["h1"], wu=f32(mlp_w_up[i]), wd=f32(mlp_w_down[i]), g=f32(ln_mlp_g[i]), b=f32(ln_mlp_b[i]))
               for c in range(NC)]

        def prog(S, a):
            C = Consts(S, a["ident"])
            stage_mlp(S, C, T, a["h1"], a["wu"], a["wd"], a["g"], a["b"], a["h2"])
        mr = launch("mlp", prog, ins, {"h2": ([T, D], F32)})
        h = [mr[c]["h2"] for c in range(NC)]
    out = np.empty((B, SEQ, D), np.float32)
    for c in range(NC):
        out[c // 2][tok[c]] = h[c]
    return out


def build_fused(SEQ, depth):
    nc = bass.Bass("TRN2", target_bir_lowering=False)
    T = SEQ
    NTL = T // 128

    def din(name, shape, dt=F32):
        return nc.dram_tensor(name, list(shape), dt, kind="ExternalInput").ap()

    def scr(name, shape, dt):
        return nc.dram_tensor("scr_" + name, list(shape), dt).ap()

    NA, NB = (depth + 1) // 2, depth // 2
    a = dict(
        x=din("x", [T, D]), pos=din("pos", [128, NTL], I32), invf=din("invf", [8]), ident=din("ident", [128, 128]),
        maskc=din("maskc", [128, 256]), negI=din("negI", [128, 128]), cmU=din("cmU", [128, 128]), cmW=din("cmW", [128, 128]),
        tri64=din("tri64", [64, 64]),
        a_w_in=din("a_w_in", [NA, D, A_IN]), a_w_o=din("a_w_o", [NA, D, D]),
        b_w_in=din("b_w_in", [max(NB, 1), D, B_IN]), b_w_a2=din("b_w_a2", [max(NB, 1), 16, 512]), b_b_a=din("b_b_a", [max(NB, 1), 512]),
        b_g_norm=din("b_g_norm", [max(NB, 1), 256]), b_w_o=din("b_w_o", [max(NB, 1), D, D]),
        ln_mix_g=din("ln_mix_g", [depth, D]), ln_mix_b=din("ln_mix_b", [depth, D]),
        mlp_w_up=din("mlp_w_up", [depth, D, DFF]), mlp_w_down=din("mlp_w_down", [depth, DFF, D]),
        ln_mlp_g=din("ln_mlp_g", [depth, D]), ln_mlp_b=din("ln_mlp_b", [depth, D]),
    )
    out = nc.dram_tensor("out", [T, D], F32, kind="ExternalOutput").ap()
    hA = scr("hA", [T, D], F32)
    hB = scr("hB", [T, D], F32)
    qT = scr("qT", [1024, T], BF16)
    kT = scr("kT", [256, T], BF16)
    vv = scr("v", [T, 256], BF16)
    iqT = scr("iqT", [512, T], BF16)
    ikT = scr("ikT", [64, T], BF16)
    iw = scr("iw", [T, 8], F32)
    o = scr("o", [T, 1024], BF16)
    qgT = scr("qgT", [512, T], BF16)
    kgT = scr("kgT", [512, T], BF16)
    kh = scr("kh", [T, 512], BF16)
    vb = scr("vb", [T, 1024], BF16)
    dec = scr("dec", [512, T // 64], F32)
    sr = scr("sr", [T, 1024], F32)
    on = scr("on", [T, 1024], F32)

    S = Sched(nc)
    CB = 8
    h_in = a["x"]
    for i in range(depth):
        j = i // 2
        last = i == depth - 1
        if i % 2 == 0:
            S.stage_begin()
            C = Consts(S, a["ident"])
            stage_proj_A(S, C, T, h_in, a["pos"], a["a_w_in"][j], a["invf"], qT, kT, vv, iqT, ikT, iw)
            S.stage_end()
            S.stage_begin()
            C = Consts(S, a["ident"])
            stage_attn(S, C, NTL, kT.rearrange("(g d) s -> g d s", d=64), vv, ikT, qT.rearrange("(h d) t -> h d t", d=64),
                       iqT.rearrange("(h d) t -> h d t", d=64), iw, a["maskc"], a["negI"], o, solo=True)
            S.stage_end()
            S.stage_begin()
            C = Consts(S, a["ident"])
            stage_wo_ln(S, C, T, o, h_in, a["a_w_o"][j], a["ln_mix_g"][i], a["ln_mix_b"][i], hB)
            S.stage_end()
        else:
            S.stage_begin()
            C = Consts(S, a["ident"])
            stage_proj_B(S, C, T, h_in, a["b_w_in"][j], a["b_w_a2"][j], a["b_b_a"][j], a["cmU"], a["cmW"],
                         qgT.rearrange("(h d) t -> h d t", d=128), kgT.rearrange("(h d) t -> h d t", d=128), kh, vb,
                         dec.rearrange("(h d) c -> h d c", d=128), sr)
            S.stage_end()
            S.stage_begin()
            C = Consts(S, a["ident"])

            def acc(kind, hd, blk):
                t0, t1 = blk * 64 * CB, (blk + 1) * 64 * CB
                if kind == "qg":
                    return qgT[hd * 128:(hd + 1) * 128, t0:t1]
                if kind == "kg":
                    return kgT[hd * 128:(hd + 1) * 128, t0:t1]
                if kind == "kh":
                    return kh[t0:t1, hd * 128:(hd + 1) * 128].rearrange("(c j) d -> j c d", j=64)
                if kind == "v":
                    return vb[t0:t1, hd * 256:(hd + 1) * 256].rearrange("(c j) e -> j c e", j=64)
                return on[t0:t1, hd * 256:(hd + 1) * 256].rearrange("(c j) e -> j c e", j=64)
            stage_gla(S, C, SEQ, 4, acc, lambda hd: dec[hd * 128:(hd + 1) * 128, :], a["b_g_norm"][j], a["tri64"], CB=CB)
            S.stage_end()
            S.stage_begin()
            C = Consts(S, a["ident"])
            stage_wo_ln(S, C, T, None, h_in, a["b_w_o"][j], a["ln_mix_g"][i], a["ln_mix_b"][i], hB, gate=(on, sr))
            S.stage_end()
        S.stage_begin()
        C = Consts(S, a["ident"])
        h_out = out if last else hA
        stage_mlp(S, C, T, hB, a["mlp_w_up"][i], a["mlp_w_down"][i], a["ln_mlp_g"][i], a["ln_mlp_b"][i], h_out)
        S.stage_end(last=last)
        h_in = hA
    S.stack.close()
    return nc


_FUSED = {}


def kernel(x, positions, a_w_in, a_w_o, b_w_in, b_w_a2, b_b_a, b_g_norm, b_w_o,
           ln_mix_g, ln_mix_b, mlp_w_up, mlp_w_down, ln_mlp_g, ln_mlp_b):
    x = np.asarray(x, np.float32)
    B, SEQ, _ = x.shape
    depth = int(np.asarray(ln_mix_g).shape[0])
    key = (SEQ, depth)
    if key not in _FUSED:
        _FUSED[key] = build_fused(SEQ, depth)
    nc = _FUSED[key]
    cst = _consts()
    f32 = lambda t: np.ascontiguousarray(np.asarray(t, np.float32))
    shared = dict(invf=cst["invf"], ident=cst["ident"], maskc=cst["maskc"][1], negI=cst["negI"], cmU=cst["cmU"], cmW=cst["cmW"],
                  tri64=cst["tri64"], a_w_in=f32(a_w_in), a_w_o=f32(a_w_o), b_w_in=f32(b_w_in), b_w_a2=f32(b_w_a2), b_b_a=f32(b_b_a),
                  b_g_norm=f32(b_g_norm), b_w_o=f32(b_w_o), ln_mix_g=f32(ln_mix_g), ln_mix_b=f32(ln_mix_b),
                  mlp_w_up=f32(mlp_w_up), mlp_w_down=f32(mlp_w_down), ln_mlp_g=f32(ln_mlp_g), ln_mlp_b=f32(ln_mlp_b))
    in_maps = []
    for c in range(B):
        pos_pt = np.ascontiguousarray(np.asarray(positions)[c].astype(np.int32).reshape(SEQ // 128, 128).T)
        m = dict(shared)
        m["x"] = np.ascontiguousarray(x[c])
        m["pos"] = pos_pt
        in_maps.append(m)
    res = run_bass_kernel_spmd(nc, in_maps, core_ids=list(range(B)))
    return np.stack([res.results[c]["out"] for c in range(B)], axis=0)
```

```python
import contextlib
import numpy as np
import concourse.bass as bass
import concourse.mybir as mybir
from concourse.ap import AP
from concourse.bass_utils import run_bass_kernel_spmd

F32 = mybir.dt.float32
BF16 = mybir.dt.bfloat16
I32 = mybir.dt.int32
AF = mybir.ActivationFunctionType
ALU = mybir.AluOpType
AX = mybir.AxisListType

D = 1024
DFF = 4096
DEPTH = 4
ALPHA = (2 * DEPTH) ** 0.25
LN_EPS = 1e-5
NCORES = 8


class Buf:
    __slots__ = ("name", "last_w", "readers")

    def __init__(self, name):
        self.name = name
        self.last_w = None
        self.readers = {}


class Sched:
    ENGS = ("sp", "act", "dve", "pool", "pe")
    SAME_WIN = 3
    NDMA = 8

    def __init__(self, nc):
        self.nc = nc
        self.stack = contextlib.ExitStack()
        self.ops = {e: [] for e in self.ENGS}
        self.count = {e: 0 for e in self.ENGS}
        self.seen = {e: {} for e in self.ENGS}
        self.esem = {}
        for e in ("act", "dve", "pool", "pe"):
            self.esem[e] = self.stack.enter_context(nc.semaphore("s_" + e))
        self.dsem = {}
        self.dma_i = {}
        for q in ("sp", "act", "pool"):
            self.dsem[q] = [self.stack.enter_context(nc.semaphore("d_%s%d" % (q, i))) for i in range(self.NDMA)]
            self.dma_i[q] = 0
        self.nbuf = 0
        self.stage_stack = None
        self.stage_no = 0
        self.bar = self.stack.enter_context(nc.semaphore("s_bar"))

    def sbuf(self, name, shape, dt):
        st = self.stage_stack if self.stage_stack is not None else self.stack
        return st.enter_context(self.nc.sbuf_tensor("sb%d_" % self.stage_no + name, list(shape), dt))

    def psum(self, name, shape, dt):
        st = self.stage_stack if self.stage_stack is not None else self.stack
        return st.enter_context(self.nc.psum_tensor("pp%d_" % self.stage_no + name, list(shape), dt))

    def stage_begin(self):
        self.stage_stack = contextlib.ExitStack()

    def stage_end(self, last=False):
        self.finish()
        if not last:
            self.stage_no += 1
            n = self.stage_no
            bar = self.bar
            self.ops["sp"].append(([], (lambda e, bar=bar: e.sem_inc(bar, 1)), None, 0))
            for eng in ("act", "dve", "pool", "pe"):
                self.ops[eng].append(([(bar, n)], None, None, 0))
        self._emit_block()
        self.stage_stack.close()
        self.stage_stack = None

    def buf(self, name=None):
        self.nbuf += 1
        return Buf(name or ("b%d" % self.nbuf))

    def bufs(self, n, name="b"):
        return [self.buf("%s%d" % (name, i)) for i in range(n)]

    def _deps(self, reads, writes):
        raw = {}
        other = {}
        for b in reads:
            if b.last_w is not None:
                k, v = b.last_w
                raw[k] = max(raw.get(k, 0), v)
        for b in writes:
            if b.last_w is not None:
                k, v = b.last_w
                other[k] = max(other.get(k, 0), v)
            for k, v in b.readers.items():
                other[k] = max(other.get(k, 0), v)
        return raw, other

    def _commit(self, ev, reads, writes):
        k, v = ev
        for b in reads:
            b.readers[k] = max(b.readers.get(k, 0), v)
        for b in writes:
            b.last_w = ev
            b.readers = {}

    def op(self, eng, fn, reads=(), writes=()):
        raw, other = self._deps(reads, writes)
        own = self.esem[eng]
        waits = {}
        seen = self.seen[eng]
        for d, is_raw in ((raw, True), (other, False)):
            for k, v in d.items():
                if k is own:
                    if eng == "pe" or not is_raw:
                        continue
                    if v <= self.count[eng] - self.SAME_WIN:
                        continue
                if seen.get(k, 0) >= v:
                    continue
                waits[k] = max(waits.get(k, 0), v)
        for k, v in waits.items():
            seen[k] = v
        self.count[eng] += 1
        ev = (own, self.count[eng])
        self._commit(ev, reads, writes)
        self.ops[eng].append((list(waits.items()), fn, own, 1))

    def dma(self, q, out, in_, reads=(), writes=(), fn=None, **kw):
        raw, other = self._deps(reads, writes)
        waits = {}
        seen = self.seen[q]
        for d in (raw, other):
            for k, v in d.items():
                if seen.get(k, 0) >= v:
                    continue
                waits[k] = max(waits.get(k, 0), v)
        i = self.dma_i[q]
        self.dma_i[q] = i + 1
        slot = self.dsem[q][i % self.NDMA]
        target = 16 * (i // self.NDMA + 1)
        if target > 16 and seen.get(slot, 0) < target - 16:
            waits[slot] = max(waits.get(slot, 0), target - 16)
        for k, v in waits.items():
            seen[k] = v
        ev = (slot, target)
        self._commit(ev, reads, writes)
        if fn is None:
            fn = lambda e, out=out, in_=in_, kw=kw: e.dma_start(out=out, in_=in_, **kw)
        self.ops[q].append((list(waits.items()), fn, slot, 16))

    def allgather(self, out, in_, groups, reads=(), writes=()):
        fn = lambda e: e.collective_compute("AllGather", ALU.bypass, replica_groups=groups, ins=[in_], outs=[out])
        self.dma("pool", None, None, reads=reads, writes=writes, fn=fn)

    def finish(self):
        waits = []
        for q in ("sp", "act", "pool"):
            n = self.dma_i[q]
            for s in range(min(n, self.NDMA)):
                cnt = (n - 1 - s) // self.NDMA + 1
                waits.append((self.dsem[q][s], 16 * cnt))
        for e in ("act", "dve", "pool", "pe"):
            if self.count[e]:
                waits.append((self.esem[e], self.count[e]))
        self.ops["sp"].append((waits, None, None, 0))

    def emit(self):
        self.finish()
        self._emit_block()
        self.stack.close()

    def _emit_block(self):
        nc = self.nc
        with nc.Block() as block:
            deco = {"sp": block.sync, "act": block.scalar, "dve": block.vector,
                    "pool": block.gpsimd, "pe": block.tensor}
            for eng in self.ENGS:
                ops = self.ops[eng]
                if not ops:
                    continue

                def body(e, ops=ops):
                    for waits, fn, sem, inc in ops:
                        for s, v in waits:
                            e.wait_ge(s, v)
                        if fn is not None:
                            ins = fn(e)
                            if sem is not None:
                                ins.then_inc(sem, inc)

                deco[eng](body)
        self.ops = {e: [] for e in self.ENGS}


def bcast_rows(ap_row, nparts):
    return ap_row.partition_broadcast(nparts)


class Consts:
    def __init__(self, S, ident_dram):
        self.ident = S.sbuf("ident", [128, 128], BF16)
        self.b_ident = S.buf("ident")
        S.dma("pool", self.ident[:], ident_dram[:, :], writes=[self.b_ident])


def load_bcast(S, name, row_ap, n):
    t = S.sbuf(name, [128, n], F32)
    b = S.buf(name)
    S.dma("sp", t[:], row_ap.partition_broadcast(128), writes=[b])
    return t, b


def layer_norm_tile(S, u, b_u, out, b_out, g_t, b_g, bt_t, b_bt, scr, eng2="dve"):
    st, b_st = scr["st"], scr["b_st"]
    mv, b_mv = scr["mv"], scr["b_mv"]
    for c in range(2):
        S.op("dve", lambda e, c=c: e.bn_stats(out=st[:, c, :], in_=u[:, c * 512:(c + 1) * 512]),
             reads=[b_u], writes=[b_st[c]])
    S.op("dve", lambda e: e.bn_aggr(out=mv[:, 0:2], in_=st[:, :, :]), reads=b_st, writes=[b_mv[0]])
    S.op("dve", lambda e: e.tensor_scalar(out=mv[:, 2:3], in0=mv[:, 1:2], scalar1=LN_EPS, scalar2=None,
                                           op0=ALU.add), reads=[b_mv[0]], writes=[b_mv[1]])
    S.op("act", lambda e: e.activation(out=mv[:, 2:3], in_=mv[:, 2:3], func=AF.Sqrt), reads=[b_mv[1]], writes=[b_mv[1]])
    S.op("dve", lambda e: e.reciprocal(out=mv[:, 2:3], in_=mv[:, 2:3]), reads=[b_mv[1]], writes=[b_mv[1]])
    S.op("dve", lambda e: e.scalar_tensor_tensor(out=mv[:, 3:4], in0=mv[:, 0:1], scalar=-1.0, in1=mv[:, 2:3],
                                                  op0=ALU.mult, op1=ALU.mult), reads=[b_mv[0], b_mv[1]], writes=[b_mv[2]])
    S.op("act", lambda e: e.activation(out=u[:, :], in_=u[:, :], func=AF.Identity, bias=mv[:, 3:4], scale=mv[:, 2:3]),
         reads=[b_u, b_mv[1], b_mv[2]], writes=[b_u])
    S.op(eng2, lambda e: e.tensor_tensor(out=u[:, :], in0=u[:, :], in1=g_t[:, :], op=ALU.mult),
         reads=[b_u, b_g], writes=[b_u])
    S.op(eng2, lambda e: e.tensor_tensor(out=out[:, :], in0=u[:, :], in1=bt_t[:, :], op=ALU.add),
         reads=[b_u, b_bt], writes=[b_out])


def ln_scratch(S, name):
    return {"st": S.sbuf(name + "_st", [128, 2, 6], F32), "b_st": S.bufs(2, name + "st"),
            "mv": S.sbuf(name + "_mv", [128, 4], F32), "b_mv": S.bufs(3, name + "mv")}


def stage_mlp(S, C, T, h1, w_up, w_down, g, b, h2):
    NT = 256
    NS = NT // 128
    wup = S.sbuf("wup", [128, 8, DFF], BF16)
    wdn = S.sbuf("wdn", [128, 32, D], BF16)
    b_wup = S.bufs(8, "wup")
    b_wdn = S.bufs(8, "wdn")
    wu_v = w_up.rearrange("(c p) f -> p c f", p=128)
    wd_v = w_down.rearrange("(c p) n -> p c n", p=128)
    for c in range(8):
        S.dma("pool", wup[:, c, :], wu_v[:, c, :], writes=[b_wup[c]])
    for c in range(8):
        S.dma("pool", wdn[:, 4 * c:4 * c + 4, :], wd_v[:, 4 * c:4 * c + 4, :], writes=[b_wdn[c]])
    g_t, b_g = load_bcast(S, "mlp_g", g, D)
    bt_t, b_bt = load_bcast(S, "mlp_b", b, D)

    NB = 2
    x_sb = [S.sbuf("x_sb%d" % i, [128, NS, D], F32) for i in range(NB)]
    b_x = [S.bufs(NS, "x%d_" % i) for i in range(NB)]
    xb = S.sbuf("xb", [128, NS, D], BF16)
    b_xb = S.bufs(NS, "xb")
    xT = S.sbuf("xT", [128, 8, NT], BF16)
    b_xT = S.bufs(8, "xT")
    h2T = S.sbuf("h2T", [128, 32, NT], BF16)
    b_h2T = S.bufs(32, "h2T")
    r_sb = [S.sbuf("r_sb%d" % i, [128, NT], F32) for i in range(4)]
    b_r = S.bufs(4, "r")
    y_sb = [S.sbuf("y_sb%d" % i, [128, D], F32) for i in range(2)]
    b_y = S.bufs(2, "y")
    tp_ps = [S.psum("tp_ps%d" % i, [128, 1024], BF16) for i in range(2)]
    b_tp = S.bufs(2, "tp")
    up_ps = [S.psum("up_ps%d" % i, [128, 512], F32) for i in range(4)]
    b_up = S.bufs(4, "up")
    dn_ps = [S.psum("dn_ps%d" % i, [128, 512], F32) for i in range(2)]
    b_dn = S.bufs(2, "dn")
    scrs = [ln_scratch(S, "mlp%d" % i) for i in range(2)]

    h1v = h1.rearrange("(t s p) d -> t p s d", p=128, s=NS)
    h2v = h2.rearrange("(t s p) d -> t s p d", p=128, s=NS)
    n_up = 0
    n_dn = 0
    n_tp = 0
    n_y = 0
    for t in range(T // NT):
        xs, bx = x_sb[t % NB], b_x[t % NB]
        S.dma("sp", xs[:, :, :], h1v[t], writes=bx)
        for s in range(NS):
            S.op("act", lambda e, xs=xs, s=s: e.activation(out=xb[:, s, :], in_=xs[:, s, :], func=AF.Copy),
                 reads=[bx[s]], writes=[b_xb[s]])
        for c in range(8):
            tp, btp = tp_ps[n_tp % 2], b_tp[n_tp % 2]
            n_tp += 1
            for s in range(NS):
                S.op("pe", lambda e, tp=tp, s=s, c=c: e.transpose(out=tp[:, s * 128:(s + 1) * 128],
                                                                  in_=xb[:, s, c * 128:(c + 1) * 128],
                                                                  identity=C.ident[:]),
                     reads=[b_xb[s], C.b_ident], writes=[btp])
            S.op("dve", lambda e, tp=tp, c=c: e.tensor_copy(out=xT[:, c, :], in_=tp[:, 0:NT]),
                 reads=[btp], writes=[b_xT[c]])
        for fc in range(32):
            ps, bps = up_ps[n_up % 4], b_up[n_up % 4]
            r, br = r_sb[n_up % 4], b_r[n_up % 4]
            n_up += 1
            for c in range(8):
                S.op("pe", lambda e, ps=ps, c=c, fc=fc: e.matmul(ps[:, 0:NT], lhsT=wup[:, c, fc * 128:(fc + 1) * 128],
                                                                  rhs=xT[:, c, :], start=(c == 0), stop=(c == 7)),
                     reads=[b_wup[c], b_xT[c]], writes=[bps])
            S.op("act", lambda e, ps=ps, r=r: e.activation(out=r[:, :], in_=ps[:, 0:NT], func=AF.Relu),
                 reads=[bps], writes=[br])
            S.op("dve", lambda e, r=r, fc=fc: e.tensor_tensor(out=h2T[:, fc, :], in0=r[:, :], in1=r[:, :], op=ALU.mult),
                 reads=[br], writes=[b_h2T[fc]])
        for s in range(NS):
            for nh in range(2):
                ps, bps = dn_ps[n_dn % 2], b_dn[n_dn % 2]
                n_dn += 1
                for fc in range(32):
                    S.op("pe", lambda e, ps=ps, fc=fc, s=s, nh=nh: e.matmul(
                        ps[:, :], lhsT=h2T[:, fc, s * 128:(s + 1) * 128], rhs=wdn[:, fc, nh * 512:(nh + 1) * 512],
                        start=(fc == 0), stop=(fc == 31)),
                         reads=[b_h2T[fc], b_wdn[fc // 4]], writes=[bps])
                S.op("dve", lambda e, ps=ps, xs=xs, s=s, nh=nh: e.scalar_tensor_tensor(
                    out=xs[:, s, nh * 512:(nh + 1) * 512], in0=xs[:, s, nh * 512:(nh + 1) * 512], scalar=ALPHA,
                    in1=ps[:, :], op0=ALU.mult, op1=ALU.add),
                     reads=[bps, bx[s]], writes=[bx[s]])
            y, by = y_sb[n_y % 2], b_y[n_y % 2]
            n_y += 1
            layer_norm_tile(S, xs[:, s, :], bx[s], y, by, g_t, b_g, bt_t, b_bt, scrs[n_y % 2])
            S.dma("pool", h2v[t, s], y[:, :], reads=[by])


TWO_PI = float(2 * np.pi)
PI = float(np.pi)


def _range_reduce(S, x, bx, tmp_f, btf, tmp_i, bti):
    S.op("dve", lambda e: e.tensor_scalar(out=tmp_f, in0=x, scalar1=1.0 / TWO_PI, scalar2=None, op0=ALU.mult),
         reads=[bx], writes=[btf])
    S.op("dve", lambda e: e.tensor_copy(out=tmp_i, in_=tmp_f), reads=[btf], writes=[bti])
    S.op("dve", lambda e: e.tensor_copy(out=tmp_f, in_=tmp_i), reads=[bti], writes=[btf])
    S.op("dve", lambda e: e.scalar_tensor_tensor(out=x, in0=tmp_f, scalar=-TWO_PI, in1=x, op0=ALU.mult, op1=ALU.add),
         reads=[btf, bx], writes=[bx])
    S.op("dve", lambda e: e.tensor_scalar(out=tmp_f, in0=x, scalar1=PI, scalar2=-TWO_PI, op0=ALU.is_gt, op1=ALU.mult),
         reads=[bx], writes=[btf])
    S.op("dve", lambda e: e.tensor_tensor(out=x, in0=x, in1=tmp_f, op=ALU.add), reads=[btf, bx], writes=[bx])
    S.op("dve", lambda e: e.tensor_scalar(out=tmp_f, in0=x, scalar1=-PI, scalar2=TWO_PI, op0=ALU.is_lt, op1=ALU.mult),
         reads=[bx], writes=[btf])
    S.op("dve", lambda e: e.tensor_tensor(out=x, in0=x, in1=tmp_f, op=ALU.add), reads=[btf, bx], writes=[bx])


def rotary_tables(S, NTL, pos_pt, invf):
    posi = S.sbuf("posi", [128, NTL], I32)
    posf = S.sbuf("posf", [128, NTL], F32)
    invt = S.sbuf("invt", [128, 8], F32)
    cos_t = S.sbuf("cos_t", [128, NTL, 8], F32)
    sin_t = S.sbuf("sin_t", [128, NTL, 8], F32)
    tmpf = S.sbuf("rr_tf", [128, NTL, 8], F32)
    tmpi = S.sbuf("rr_ti", [128, NTL, 8], I32)
    b_pi, b_pf, b_inv, b_cos, b_sin, b_tf, b_ti = S.bufs(7, "rot")
    S.dma("sp", posi[:], pos_pt[:, :], writes=[b_pi])
    S.dma("sp", invt[:], invf.partition_broadcast(128), writes=[b_inv])
    S.op("dve", lambda e: e.tensor_copy(out=posf[:], in_=posi[:]), reads=[b_pi], writes=[b_pf])
    S.op("dve", lambda e: e.tensor_tensor(out=sin_t[:, :, :], in0=posf[:, :].unsqueeze(2).to_broadcast([128, NTL, 8]),
                                          in1=invt[:, :].unsqueeze(1).to_broadcast([128, NTL, 8]), op=ALU.mult),
         reads=[b_pf, b_inv], writes=[b_sin])
    S.op("dve", lambda e: e.tensor_scalar(out=cos_t[:, :, :], in0=sin_t[:, :, :], scalar1=PI / 2, scalar2=None, op0=ALU.add),
         reads=[b_sin], writes=[b_cos])
    _range_reduce(S, sin_t[:, :, :], b_sin, tmpf[:, :, :], b_tf, tmpi[:, :, :], b_ti)
    _range_reduce(S, cos_t[:, :, :], b_cos, tmpf[:, :, :], b_tf, tmpi[:, :, :], b_ti)
    S.op("act", lambda e: e.activation(out=sin_t[:, :, :], in_=sin_t[:, :, :], func=AF.Sin), reads=[b_sin], writes=[b_sin])
    S.op("act", lambda e: e.activation(out=cos_t[:, :, :], in_=cos_t[:, :, :], func=AF.Sin), reads=[b_cos], writes=[b_cos])
    return cos_t, b_cos, sin_t, b_sin


def apply_rotary(S, P, bP, h0, nh, cos_t, b_cos, sin_t, b_sin, t, tmp, btmp):
    Pv = P[:, h0 * 64:(h0 + nh) * 64].rearrange("p (h d) -> p h d", d=64)
    x1 = Pv[:, :, 0:8]
    x2 = Pv[:, :, 8:16]
    c = cos_t[:, t, :].unsqueeze(1).to_broadcast([128, nh, 8])
    s = sin_t[:, t, :].unsqueeze(1).to_broadcast([128, nh, 8])
    t1, t2, t3, t4 = (tmp[:, i, 0:nh, :] for i in range(4))
    bP = list(bP)
    rd = bP + [b_cos, b_sin]
    S.op("dve", lambda e: e.tensor_tensor(out=t1, in0=x1, in1=c, op=ALU.mult), reads=rd, writes=[btmp[0]])
    S.op("dve", lambda e: e.tensor_tensor(out=t2, in0=x2, in1=s, op=ALU.mult), reads=rd, writes=[btmp[1]])
    S.op("dve", lambda e: e.tensor_tensor(out=t3, in0=x2, in1=c, op=ALU.mult), reads=rd, writes=[btmp[2]])
    S.op("dve", lambda e: e.tensor_tensor(out=t4, in0=x1, in1=s, op=ALU.mult), reads=rd, writes=[btmp[3]])
    S.op("dve", lambda e: e.tensor_tensor(out=x1, in0=t1, in1=t2, op=ALU.subtract), reads=[btmp[0], btmp[1], btmp[3]], writes=bP)
    S.op("dve", lambda e: e.tensor_tensor(out=x2, in0=t3, in1=t4, op=ALU.add), reads=[btmp[2], btmp[3]], writes=bP)


A_IN = 2120
IW_SCALE = float(8 ** -0.5 * 64 ** -0.5)


def stage_proj_A(S, C, T, h, pos_pt, w_in, invf, qT, kT, v, iqT, ikT, iw):
    NTL = T // 128
    win = S.sbuf("win", [128, 8, A_IN], BF16)
    b_win = S.bufs(8, "win")
    wv = w_in.rearrange("(c p) n -> p c n", p=128)
    for c in range(8):
        S.dma("pool", win[:, c, :], wv[:, c, :], writes=[b_win[c]])
    cos_t, b_cos, sin_t, b_sin = rotary_tables(S, NTL, pos_pt, invf)

    hs = [S.sbuf("pa_hs%d" % i, [128, D], F32) for i in range(2)]
    b_hs = S.bufs(2, "pa_hs")
    hb_l = [S.sbuf("pa_hb%d" % i, [128, D], BF16) for i in range(2)]
    b_hb_l = S.bufs(2, "pa_hb")
    hT_l = [S.sbuf("pa_hT%d" % i, [128, 8, 128], BF16) for i in range(2)]
    b_hT_l = S.bufs(2, "pa_hT")
    P_l = [S.sbuf("pa_P%d" % i, [128, A_IN], F32) for i in range(2)]
    b_P_l = [S.bufs(5, "pa_P%d_" % i) for i in range(2)]
    Pb_l = [S.sbuf("pa_Pb%d" % i, [128, A_IN], BF16) for i in range(2)]
    b_Pb_l = [S.bufs(4, "pa_Pb%d_" % i) for i in range(2)]
    iws = [S.sbuf("pa_iw%d" % i, [128, 8], F32) for i in range(2)]
    b_iws = S.bufs(2, "pa_iw")
    rt_l = [S.sbuf("pa_rt%d" % i, [128, 4, 20, 8], F32) for i in range(2)]
    b_rt_l = [S.bufs(4, "pa_rt%d_" % i) for i in range(2)]
    qTs = [S.sbuf("pa_qT%d" % i, [128, 8, 128], BF16) for i in range(2)]
    b_qTs = S.bufs(2, "pa_qT")
    kiTs = [S.sbuf("pa_kiT%d" % i, [128, 7, 128], BF16) for i in range(2)]
    b_kiTs = S.bufs(2, "pa_kiT")
    pj = [S.psum("pa_pj%d" % i, [128, 512], F32) for i in range(5)]
    b_pj = S.bufs(5, "pa_pj")
    tpA = S.psum("pa_tpA", [128, 1024], BF16)
    tpB = S.psum("pa_tpB", [128, 1024], BF16)
    b_tpA, b_tpB = S.bufs(2, "pa_tp")
    chunks = [(0, 512), (512, 1024), (1024, 1536), (1536, 2048), (2048, A_IN)]

    hv = h.rearrange("(t p) d -> t p d", p=128)
    qTv = qT.rearrange("(c p) t -> p c t", p=128)
    kTv = kT.rearrange("(c p) t -> p c t", p=128)
    iqTv = iqT.rearrange("(c p) t -> p c t", p=128)
    def _tile(t, hb, b_hb, hT, b_hT, P, b_P, Pb, b_Pbs, rt, b_rt):
        b_Pq, b_Pk, b_Pv, b_Pi = b_Pbs
        x, bx = hs[t % 2], b_hs[t % 2]
        S.dma("sp", x[:, :], hv[t], writes=[bx])
        S.op("act", lambda e, x=x: e.activation(out=hb[:, :], in_=x[:, :], func=AF.Copy), reads=[bx], writes=[b_hb])
        for c in range(8):
            S.op("pe", lambda e, c=c: e.transpose(out=tpA[:, c * 128:(c + 1) * 128], in_=hb[:, c * 128:(c + 1) * 128],
                                                  identity=C.ident[:]), reads=[b_hb, C.b_ident], writes=[b_tpA])
        S.op("dve", lambda e: e.tensor_copy(out=hT[:, :, :], in_=tpA[:, :].rearrange("p (c t) -> p c t", t=128)),
             reads=[b_tpA], writes=[b_hT])
        for i, (n0, n1) in enumerate(chunks):
            for c in range(8):
                S.op("pe", lambda e, i=i, c=c, n0=n0, n1=n1: e.matmul(pj[i][:, 0:n1 - n0], lhsT=hT[:, c, :], rhs=win[:, c, n0:n1],
                                                                      start=(c == 0), stop=(c == 7)),
                     reads=[b_hT, b_win[c]], writes=[b_pj[i]])
            S.op("act", lambda e, i=i, n0=n0, n1=n1: e.activation(out=P[:, n0:n1], in_=pj[i][:, 0:n1 - n0], func=AF.Copy),
                 reads=[b_pj[i]], writes=[b_P[i]])
        apply_rotary(S, P, b_P[0:3], 0, 20, cos_t, b_cos, sin_t, b_sin, t, rt, b_rt)
        apply_rotary(S, P, b_P[3:5], 24, 9, cos_t, b_cos, sin_t, b_sin, t, rt, b_rt)
        S.op("act", lambda e: e.activation(out=Pb[:, 0:1024], in_=P[:, 0:1024], func=AF.Copy, scale=0.125),
             reads=b_P[0:2], writes=[b_Pq])
        S.op("act", lambda e: e.activation(out=Pb[:, 1024:1536], in_=P[:, 1024:1536], func=AF.Copy),
             reads=[b_P[2]], writes=[b_Pk])
        S.op("act", lambda e: e.activation(out=Pb[:, 1536:2112], in_=P[:, 1536:2112], func=AF.Copy),
             reads=b_P[3:5], writes=[b_Pi])
        iwt, biw = iws[t % 2], b_iws[t % 2]
        S.op("act", lambda e, iwt=iwt: e.activation(out=iwt[:, :], in_=P[:, 2112:2120], func=AF.Copy, scale=IW_SCALE),
             reads=[b_P[4]], writes=[biw])
        S.dma("pool", iw[t * 128:(t + 1) * 128, :], iwt[:, :], reads=[biw])
        S.dma("pool", v[t * 128:(t + 1) * 128, :], Pb[:, 1280:1536], reads=[b_Pk])
        qs, bqs = qTs[t % 2], b_qTs[t % 2]
        ks, bks = kiTs[t % 2], b_kiTs[t % 2]
        for c in range(8):
            S.op("pe", lambda e, c=c: e.transpose(out=tpB[:, c * 128:(c + 1) * 128], in_=Pb[:, c * 128:(c + 1) * 128],
                                                  identity=C.ident[:]), reads=[b_Pq, C.b_ident], writes=[b_tpB])
        S.op("dve", lambda e, qs=qs: e.tensor_copy(out=qs[:, :, :], in_=tpB[:, :].rearrange("p (c t) -> p c t", t=128)),
             reads=[b_tpB], writes=[bqs])
        S.dma("pool", qTv[:, :, t * 128:(t + 1) * 128], qs[:, :, :], reads=[bqs])
        srcs = [(1024, 128, b_Pk), (1152, 128, b_Pk), (1536, 128, b_Pi), (1664, 128, b_Pi), (1792, 128, b_Pi),
                (1920, 128, b_Pi), (2048, 64, b_Pi)]
        for j, (c0, w, bsrc) in enumerate(srcs):
            S.op("pe", lambda e, j=j, c0=c0, w=w: e.transpose(out=tpA[0:w, j * 128:(j + 1) * 128], in_=Pb[:, c0:c0 + w],
                                                              identity=C.ident[:]), reads=[bsrc, C.b_ident], writes=[b_tpA])
        S.op("dve", lambda e, ks=ks: e.tensor_copy(out=ks[:, 0:6, :], in_=tpA[:, 0:768].rearrange("p (c t) -> p c t", t=128)),
             reads=[b_tpA], writes=[bks])
        S.op("dve", lambda e, ks=ks: e.tensor_copy(out=ks[0:64, 6, :], in_=tpA[0:64, 768:896]),
             reads=[b_tpA], writes=[bks])
        S.dma("pool", kTv[:, :, t * 128:(t + 1) * 128], ks[:, 0:2, :], reads=[bks])
        S.dma("pool", iqTv[:, :, t * 128:(t + 1) * 128], ks[:, 2:6, :], reads=[bks])
        S.dma("pool", ikT[:, t * 128:(t + 1) * 128], ks[0:64, 6, :], reads=[bks])

    for t in range(NTL):
        _tile(t, hb_l[t % 2], b_hb_l[t % 2], hT_l[t % 2], b_hT_l[t % 2], P_l[t % 2], b_P_l[t % 2], Pb_l[t % 2], b_Pb_l[t % 2], rt_l[t % 2], b_rt_l[t % 2])

TOPK = 256
NEG_MASK = -3.0e38
NEG_LO = -1.0e30
N_BISECT = 16
MASK_BIG = 32768.0


def stage_attn(S, C, NQ, kT, v, ikT, qT, iqT, iw, maskc, negI, o, dbg=None, solo=False):
    NKTT = NQ if solo else 2 * NQ
    SK = NKTT * 128
    nkt_of = (lambda j: j + 1) if solo else (lambda j: 2 * j + 2)
    kT_sb = S.sbuf("at_kT", [128, 2, SK], BF16)
    v_sb = S.sbuf("at_v", [128, NKTT, 4, 65], BF16)
    ik_sb = S.sbuf("at_ik", [128, SK], BF16)
    b_kT, b_v, b_ik = S.bufs(3, "at_res")
    for hh in range(2):
        S.dma("sp", kT_sb[hh * 64:(hh + 1) * 64, :, :], kT[2 * hh:2 * hh + 2, :, :].rearrange("g d s -> d g s"), writes=[b_kT])
    S.op("pool", lambda e: e.memset(v_sb[:, :, :, 64:65], 1.0), writes=[b_v])
    vv = v.rearrange("(k p) (g d) -> p k g d", p=128, d=64)
    KCH = 16
    for k0 in range(0, NKTT, KCH):
        k1 = min(NKTT, k0 + KCH)
        for g in range(4):
            S.dma("sp", v_sb[:, k0:k1, g, 0:64], vv[:, k0:k1, g, :], writes=[b_v])
    S.op("pool", lambda e: e.memset(ik_sb[64:128, :], 0.0), writes=[b_ik])
    S.dma("sp", ik_sb[0:64, :], ikT[:, :], writes=[b_ik])
    mc = S.sbuf("at_mc", [128, 256], F32)
    nI = S.sbuf("at_nI", [128, 4, 128], BF16)
    b_mc, b_nI = S.bufs(2, "at_c")
    S.dma("sp", mc[:, :], maskc[:, :], writes=[b_mc])
    for r in range(4):
        S.dma("pool", nI[:, r, :], negI[:, :], writes=[b_nI])

    sc = S.sbuf("at_sc", [128, SK], F32)
    b_sc = S.buf("at_sc")
    junk = S.sbuf("at_junk", [128, SK], BF16)
    b_junk = S.buf("at_junk")
    mb = [S.sbuf("at_mb%d" % i, [128, SK], BF16) for i in range(2)]
    b_mb = S.bufs(2, "at_mb")
    NT_SB = 4
    NS_PS = 2
    t_sb = [S.sbuf("at_t%d" % i, [128, 512], F32) for i in range(NT_SB)]
    b_t = S.bufs(NT_SB, "at_t")
    NPT = 4
    NLP = 3
    PT = [S.sbuf("at_PT%d" % i, [128, 512], BF16) for i in range(NPT)]
    b_PT = S.bufs(NPT, "at_PT")
    q_sb = [S.sbuf("at_q%d" % i, [128, 4, 4, 128], BF16) for i in range(2)]
    b_q = S.bufs(2, "at_q")
    for i in range(2):
        S.op("pool", lambda e, i=i: e.memset(q_sb[i][:, :, :, :], 0.0), writes=[b_q[i]])
    iq_sb = [S.sbuf("at_iq%d" % i, [128, 8, 128], BF16) for i in range(2)]
    b_iq = S.bufs(2, "at_iq")
    for i in range(2):
        S.op("pool", lambda e, i=i: e.memset(iq_sb[i][64:128, :, :], 0.0), writes=[b_iq[i]])
    iw_sb = [S.sbuf("at_iw%d" % i, [128, 8], F32) for i in range(2)]
    b_iw = S.bufs(2, "at_iw")
    o_sb = [S.sbuf("at_o%d" % i, [128, 16, 64], BF16) for i in range(2)]
    b_o = S.bufs(2, "at_o")
    sm = S.sbuf("at_sm", [128, 16], F32)
    b_sm = S.bufs(8, "at_sm")
    rc = S.sbuf("at_rc", [128, 16], F32)
    b_rc = S.buf("at_rc")
    s_ps = [S.psum("at_sps%d" % i, [128, 512], F32) for i in range(NS_PS)]
    b_sps = S.bufs(NS_PS, "at_sps")
    KB = N_BISECT
    pow2 = S.sbuf("at_pow2", [128, KB + 1], F32)
    wtab = S.sbuf("at_wtab", [128, KB + 1], F32)
    b_pow2, b_wtab = S.bufs(2, "at_w")
    for k in range(KB + 1):
        S.op("pool", lambda e, k=k: e.memset(pow2[:, k:k + 1], float(2.0 ** -k)), writes=[b_pow2])
    l_ps = [S.psum("at_lps%d" % i, [128, 512], F32) for i in range(NLP)]
    b_lps = S.bufs(NLP, "at_lps")
    o_ps = [S.psum("at_ops%d" % i, [128, 7, 65], F32) for i in range(3)]
    b_ops = S.bufs(3, "at_ops")
    LO, HI, MID, CNT, GE, DD, MM = range(7)
    col = lambda i: sm[:, i:i + 1]

    n_s = 0
    n_l = 0
    n_pt = 0

    def load_q(j):
        qs, bq = q_sb[j % 2], b_q[j % 2]
        for hh in range(2):
            S.dma("sp", qs[hh * 64:(hh + 1) * 64, 2 * hh:2 * hh + 2, :, :].rearrange("d a r q -> d (a r) q"),
                  qT[8 * hh:8 * hh + 8, :, j * 128:(j + 1) * 128].rearrange("h d q -> d h q"), writes=[bq])
        S.dma("sp", iq_sb[j % 2][0:64, :, :], iqT[:, :, j * 128:(j + 1) * 128].rearrange("h d q -> d h q"), writes=[b_iq[j % 2]])
        S.dma("sp", iw_sb[j % 2][:, :], iw[j * 128:(j + 1) * 128, :], writes=[b_iw[j % 2]])

    def phase12(j):
        nonlocal n_s
        nk = nkt_of(j) * 128
        iqs, biq = iq_sb[j % 2], b_iq[j % 2]
        iws, biw = iw_sb[j % 2], b_iw[j % 2]
        for hd in range(8):
            for k0 in range(0, nk, 512):
                w = min(512, nk - k0)
                ps, bps = s_ps[n_s % NS_PS], b_sps[n_s % NS_PS]
                tt, bt = t_sb[n_s % NT_SB], b_t[n_s % NT_SB]
                n_s += 1
                S.op("pe", lambda e, ps=ps, hd=hd, k0=k0, w=w, iqs=iqs: e.matmul(
                    ps[:, 0:w], lhsT=iqs[:, hd, :], rhs=ik_sb[:, k0:k0 + w], start=True, stop=True),
                     reads=[biq, b_ik], writes=[bps])
                S.op("act", lambda e, ps=ps, tt=tt, w=w: e.activation(out=tt[:, 0:w], in_=ps[:, 0:w], func=AF.Relu),
                     reads=[bps], writes=[bt])
                if hd == 0:
                    S.op("dve", lambda e, tt=tt, k0=k0, w=w, iws=iws: e.tensor_scalar(
                        out=sc[:, k0:k0 + w], in0=tt[:, 0:w], scalar1=iws[:, 0:1], scalar2=None, op0=ALU.mult),
                         reads=[bt, biw], writes=[b_sc])
                else:
                    S.op("dve", lambda e, tt=tt, k0=k0, w=w, iws=iws, hd=hd: e.scalar_tensor_tensor(
                        out=sc[:, k0:k0 + w], in0=tt[:, 0:w], scalar=iws[:, hd:hd + 1], in1=sc[:, k0:k0 + w],
                        op0=ALU.mult, op1=ALU.add), reads=[bt, biw, b_sc], writes=[b_sc])
        S.op("dve", lambda e: e.tensor_reduce(out=col(MM), in_=sc[:, 0:nk], axis=AX.X, op=ALU.max, apply_absolute_value=True),
             reads=[b_sc], writes=[b_sm[MM]])
        S.op("dve", lambda e: e.tensor_scalar(out=col(HI), in0=col(MM), scalar1=1.0, scalar2=None, op0=ALU.add),
             reads=[b_sm[MM]], writes=[b_sm[HI]])
        S.op("dve", lambda e: e.tensor_scalar(out=wtab[:, :], in0=pow2[:, :], scalar1=col(HI), scalar2=None, op0=ALU.mult),
             reads=[b_sm[HI], b_pow2], writes=[b_wtab])
        if solo:
            S.op("dve", lambda e: e.tensor_tensor(out=sc[:, nk - 128:nk], in0=sc[:, nk - 128:nk], in1=mc[:, 128:256], op=ALU.add),
                 reads=[b_sc, b_mc], writes=[b_sc])
        else:
            S.op("dve", lambda e: e.tensor_tensor(out=sc[:, nk - 256:nk], in0=sc[:, nk - 256:nk], in1=mc[:, :], op=ALU.add),
                 reads=[b_sc, b_mc], writes=[b_sc])
        S.op("dve", lambda e: e.memset(col(MID), 0.0), writes=[b_sm[MID]])
        for it in range(KB):
            S.op("dve", lambda e: e.tensor_scalar(out=junk[:, 0:nk], in0=sc[:, 0:nk], scalar1=col(MID), scalar2=0.0,
                                                   op0=ALU.is_ge, op1=ALU.add, accum_out=col(CNT)),
                 reads=[b_sc, b_sm[MID]], writes=[b_junk, b_sm[CNT]])
            S.op("dve", lambda e, it=it: e.scalar_tensor_tensor(out=col(GE), in0=col(CNT), scalar=TOPK - 0.5, in1=wtab[:, it:it + 1],
                                                                op0=ALU.is_ge, op1=ALU.mult),
                 reads=[b_sm[CNT], b_wtab], writes=[b_sm[GE]])
            S.op("dve", lambda e, it=it: e.scalar_tensor_tensor(out=col(MID), in0=col(MID), scalar=wtab[:, it + 1:it + 2], in1=col(GE),
                                                                op0=ALU.subtract, op1=ALU.add),
                 reads=[b_sm[MID], b_sm[GE], b_wtab], writes=[b_sm[MID]])
        S.op("dve", lambda e: e.tensor_tensor(out=col(LO), in0=col(MID), in1=wtab[:, KB:KB + 1], op=ALU.subtract),
             reads=[b_sm[MID], b_wtab], writes=[b_sm[LO]])
        m, bm = mb[j % 2], b_mb[j % 2]
        S.op("dve", lambda e, m=m: e.tensor_scalar(out=m[:, 0:nk], in0=sc[:, 0:nk], scalar1=col(LO), scalar2=None, op0=ALU.is_lt),
             reads=[b_sc, b_sm[LO]], writes=[bm])
        if dbg is not None and j == dbg["j"]:
            S.dma("sp", dbg["sc"][:, 0:nk], sc[:, 0:nk], reads=[b_sc])
            S.dma("sp", dbg["mb"][:, 0:nk], m[:, 0:nk], reads=[bm])
            S.dma("sp", dbg["sm"][:, :], sm[:, :], reads=b_sm)

    def phase3(j):
        nonlocal n_l, n_pt
        NKT = nkt_of(j)
        qs, bq = q_sb[j % 2], b_q[j % 2]
        m, bm = mb[j % 2], b_mb[j % 2]
        DEPTH = 2
        pend = []

        def emit_pv(item):
            kt, g, pt, bpt = item
            for r in range(4):
                hd = 4 * g + r
                bank, slot = hd // 7, hd % 7
                S.op("pe", lambda e, pt=pt, r=r, kt=kt, g=g, bank=bank, slot=slot: e.matmul(
                    o_ps[bank][:, slot, :], lhsT=pt[:, r * 128:(r + 1) * 128], rhs=v_sb[:, kt, g, :],
                    start=(kt == 0 and slot == 0), stop=(kt == NKT - 1), skip_group_check=True),
                     reads=[bpt, b_v], writes=[b_ops[bank]])

        for kt in range(NKT):
            for g in range(4):
                hh, a = g // 2, g % 2
                lp, blp = l_ps[n_l % NLP], b_lps[n_l % NLP]
                n_l += 1
                pt, bpt = PT[n_pt % NPT], b_PT[n_pt % NPT]
                n_pt += 1
                S.op("pe", lambda e, lp=lp, g=g, a=a, kt=kt, qs=qs: e.matmul(
                    lp[:, :], lhsT=kT_sb[:, a, kt * 128:(kt + 1) * 128],
                    rhs=qs[:, g, :, :].rearrange("d r q -> d (r q)"), start=True, stop=False),
                     reads=[b_kT, bq], writes=[blp])
                S.op("pe", lambda e, lp=lp, kt=kt, m=m: e.matmul(
                    lp[:, :], lhsT=m[:, kt * 128:(kt + 1) * 128], rhs=nI[:, :, :].rearrange("p r q -> p (r q)"),
                    start=False, stop=True), reads=[bm, b_nI], writes=[blp])
                S.op("act", lambda e, lp=lp, pt=pt: e.activation(out=pt[:, :], in_=lp[:, :], func=AF.Exp),
                     reads=[blp], writes=[bpt])
                pend.append((kt, g, pt, bpt))
                if len(pend) > DEPTH:
                    emit_pv(pend.pop(0))
        while pend:
            emit_pv(pend.pop(0))
        ob, bo = o_sb[j % 2], b_o[j % 2]
        for bank in range(3):
            nh = min(7, 16 - 7 * bank)
            S.op("dve", lambda e, bank=bank, nh=nh: e.reciprocal(out=rc[:, 0:nh], in_=o_ps[bank][:, 0:nh, 64]),
                 reads=[b_ops[bank]], writes=[b_rc])
            S.op("dve", lambda e, bank=bank, nh=nh, ob=ob: e.tensor_tensor(
                out=ob[:, 7 * bank:7 * bank + nh, :], in0=o_ps[bank][:, 0:nh, 0:64],
                in1=rc[:, 0:nh].unsqueeze(2).to_broadcast([128, nh, 64]), op=ALU.mult),
                 reads=[b_ops[bank], b_rc], writes=[bo])
        S.dma("pool", o[j * 128:(j + 1) * 128, :], ob[:, :, :].rearrange("p h d -> p (h d)"), reads=[bo])

    load_q(0)
    phase12(0)
    for j in range(NQ):
        if j + 1 < NQ:
            load_q(j + 1)
            phase12(j + 1)
        phase3(j)


def stage_wo_ln(S, C, T, o, h, w_o, g, b, h1, gate=None):
    wo = S.sbuf("wo", [128, 8, D], BF16)
    b_wo = S.bufs(8, "wo")
    wv = w_o.rearrange("(c p) n -> p c n", p=128)
    for c in range(8):
        S.dma("pool", wo[:, c, :], wv[:, c, :], writes=[b_wo[c]])
    g_t, b_g = load_bcast(S, "wo_g", g, D)
    bt_t, b_bt = load_bcast(S, "wo_b", b, D)
    ob = [S.sbuf("wo_ob%d" % i, [128, D], BF16) for i in range(2)]
    b_ob = S.bufs(2, "wo_ob")
    hs = [S.sbuf("wo_hs%d" % i, [128, D], F32) for i in range(2)]
    b_hs = S.bufs(2, "wo_hs")
    oTs = [S.sbuf("wo_oT%d" % i, [128, 8, 128], BF16) for i in range(2)]
    b_oTs = S.bufs(2, "wo_oT")
    y_sb = [S.sbuf("wo_y%d" % i, [128, D], F32) for i in range(2)]
    b_y = S.bufs(2, "wo_y")
    tps = [S.psum("wo_tp%d" % i, [128, 1024], BF16) for i in range(2)]
    b_tps = S.bufs(2, "wo_tp")
    mx = [S.psum("wo_mx%d" % i, [128, 512], F32) for i in range(4)]
    b_mx = S.bufs(4, "wo_mx")
    scrs = [ln_scratch(S, "wo%d" % i) for i in range(2)]
    ov = o.rearrange("(t p) d -> t p d", p=128) if gate is None else None
    if gate is not None:
        gate_a = [S.sbuf("wo_ga%d" % i, [128, D], F32) for i in range(2)]
        gate_b = [S.sbuf("wo_gb%d" % i, [128, D], F32) for i in range(2)]
        b_ga = S.bufs(2, "wo_ga")
        b_gb = S.bufs(2, "wo_gb")
    hv = h.rearrange("(t p) d -> t p d", p=128)
    h1v = h1.rearrange("(t p) d -> t p d", p=128)
    for t in range(T // 128):
        x, bx = ob[t % 2], b_ob[t % 2]
        hh, bh = hs[t % 2], b_hs[t % 2]
        if gate is None:
            S.dma("sp", x[:, :], ov[t], writes=[bx])
        else:
            ga, gb_ = gate_a[t % 2], gate_b[t % 2]
            S.dma("sp", ga[:, :], gate[0].rearrange("(t p) d -> t p d", p=128)[t], writes=[b_ga[t % 2]])
            S.dma("sp", gb_[:, :], gate[1].rearrange("(t p) d -> t p d", p=128)[t], writes=[b_gb[t % 2]])
            S.op("dve", lambda e, x=x, ga=ga, gb_=gb_: e.tensor_tensor(out=x[:, :], in0=ga[:, :], in1=gb_[:, :], op=ALU.mult),
                 reads=[b_ga[t % 2], b_gb[t % 2]], writes=[bx])
        S.dma("act", hh[:, :], hv[t], writes=[bh])
        tp, b_tp = tps[t % 2], b_tps[t % 2]
        oT, b_oT = oTs[t % 2], b_oTs[t % 2]
        scr = scrs[t % 2]
        for c in range(8):
            S.op("pe", lambda e, c=c, x=x, tp=tp: e.transpose(out=tp[:, c * 128:(c + 1) * 128], in_=x[:, c * 128:(c + 1) * 128],
                                                       identity=C.ident[:]), reads=[bx, C.b_ident], writes=[b_tp])
        S.op("act", lambda e, oT=oT, tp=tp: e.activation(out=oT[:, :, :], in_=tp[:, :].rearrange("p (c t) -> p c t", t=128), func=AF.Copy),
             reads=[b_tp], writes=[b_oT])
        for nh in range(2):
            ps, bps = mx[(2 * t + nh) % 4], b_mx[(2 * t + nh) % 4]
            for c in range(8):
                S.op("pe", lambda e, ps=ps, c=c, nh=nh, oT=oT: e.matmul(ps[:, :], lhsT=oT[:, c, :], rhs=wo[:, c, nh * 512:(nh + 1) * 512],
                                                                  start=(c == 0), stop=(c == 7)),
                     reads=[b_oT, b_wo[c]], writes=[bps])
            S.op("dve", lambda e, ps=ps, hh=hh, nh=nh: e.scalar_tensor_tensor(
                out=hh[:, nh * 512:(nh + 1) * 512], in0=hh[:, nh * 512:(nh + 1) * 512], scalar=ALPHA, in1=ps[:, :],
                op0=ALU.mult, op1=ALU.add), reads=[bps, bh], writes=[bh])
        y, by = y_sb[t % 2], b_y[t % 2]
        layer_norm_tile(S, hh, bh, y, by, g_t, b_g, bt_t, b_bt, scr)
        S.dma("pool", h1v[t], y[:, :], reads=[by])


B_IN = 3088
GATE_TAU = 16.0


def stage_proj_B(S, C, T, h, w_in, w_a2, b_a, cmU, cmW, qgT, kgT, kh, vb, dec, sr):
    NTL = T // 128
    win = S.sbuf("pb_win", [128, 8, B_IN], BF16)
    b_win = S.bufs(8, "pb_win")
    wv = w_in.rearrange("(c p) n -> p c n", p=128)
    for c in range(8):
        S.dma("pool", win[:, c, :], wv[:, c, :], writes=[b_win[c]])
    wa2 = S.sbuf("pb_wa2", [16, 512], BF16)
    U = S.sbuf("pb_U", [128, 128], BF16)
    W = S.sbuf("pb_W", [128, 128], BF16)
    b_wa2, b_U, b_W = S.bufs(3, "pb_c")
    S.dma("pool", wa2[:, :], w_a2[:, :], writes=[b_wa2])
    S.dma("pool", U[:, :], cmU[:, :], writes=[b_U])
    S.dma("pool", W[:, :], cmW[:, :], writes=[b_W])
    ba_t, b_ba = load_bcast(S, "pb_ba", b_a, 512)

    hs = [S.sbuf("pb_hs%d" % i, [128, D], F32) for i in range(2)]
    b_hs = S.bufs(2, "pb_hs")
    def _mk(i):
        return dict(hb=S.sbuf("pb_hb%d" % i, [128, D], BF16), b_hb=S.buf("pb_hb"), hT=S.sbuf("pb_hT%d" % i, [128, 8, 128], BF16), b_hT=S.buf("pb_hT"),
                    alT=S.sbuf("pb_alT%d" % i, [16, 128], BF16), b_alT=S.buf("pb_alT"), gg=S.sbuf("pb_g%d" % i, [128, 512], F32), b_gg=S.buf("pb_g"),
                    ghi=S.sbuf("pb_ghi%d" % i, [128, 512], BF16), glo=S.sbuf("pb_glo%d" % i, [128, 512], BF16), b_ghi=S.buf("pb_ghi"), b_glo=S.buf("pb_glo"),
                    EbT=S.sbuf("pb_EbT%d" % i, [128, 4, 128], F32), EnbT=S.sbuf("pb_EnbT%d" % i, [128, 4, 128], F32), Ebl=S.sbuf("pb_Ebl%d" % i, [128, 512], F32),
                    b_EbT=S.buf("pb_EbT"), b_EnbT=S.buf("pb_EnbT"), b_Ebl=S.buf("pb_Ebl"))
    sets = [_mk(0), _mk(1)]
    qg_s = [S.sbuf("pb_qg%d" % i, [128, 4, 128], BF16) for i in range(2)]
    kg_s = [S.sbuf("pb_kg%d" % i, [128, 4, 128], BF16) for i in range(2)]
    kh_s = [S.sbuf("pb_kh%d" % i, [128, 512], BF16) for i in range(2)]
    vb_s = [S.sbuf("pb_vb%d" % i, [128, 1024], BF16) for i in range(2)]
    sr_s = [S.sbuf("pb_sr%d" % i, [128, 1024], F32) for i in range(2)]
    dc_s = [S.sbuf("pb_dc%d" % i, [128, 4, 2], F32) for i in range(2)]
    b_qg, b_kg, b_kh, b_vb, b_sr, b_dc = (S.bufs(2, "pb_o%d" % i) for i in range(6))
    tpT = S.psum("pb_tp", [128, 1024], BF16)
    b_tpT = S.buf("pb_tp")
    pk = [S.psum("pb_pk%d" % i, [128, 512], F32) for i in range(7)]
    b_pk = S.bufs(7, "pb_pk")
    QSCALE = float(128 ** -0.5)

    hv = h.rearrange("(t p) d -> t p d", p=128)
    def _tile(t, hb, b_hb, hT, b_hT, alT, b_alT, gg, b_gg, ghi, glo, b_ghi, b_glo, EbT, EnbT, Ebl, b_EbT, b_EnbT, b_Ebl):
        x, bx = hs[t % 2], b_hs[t % 2]
        i2 = t % 2
        S.dma("sp", x[:, :], hv[t], writes=[bx])
        S.op("act", lambda e, x=x: e.activation(out=hb[:, :], in_=x[:, :], func=AF.Copy), reads=[bx], writes=[b_hb])
        for c in range(8):
            S.op("pe", lambda e, c=c: e.transpose(out=tpT[:, c * 128:(c + 1) * 128], in_=hb[:, c * 128:(c + 1) * 128],
                                                  identity=C.ident[:]), reads=[b_hb, C.b_ident], writes=[b_tpT])
        S.op("dve", lambda e: e.tensor_copy(out=hT[:, :, :], in_=tpT[:, :].rearrange("p (c t) -> p c t", t=128)),
             reads=[b_tpT], writes=[b_hT])

        def tok_mm(bank, n0, n1):
            for c in range(8):
                S.op("pe", lambda e, c=c: e.matmul(pk[bank][:, 0:n1 - n0], lhsT=hT[:, c, :], rhs=win[:, c, n0:n1],
                                                   start=(c == 0), stop=(c == 7)), reads=[b_hT, b_win[c]], writes=[b_pk[bank]])

        def feat_mm(bank, slot, n0, m):
            for c in range(8):
                S.op("pe", lambda e, c=c: e.matmul(pk[bank][0:m, slot * 128:(slot + 1) * 128], lhsT=win[:, c, n0:n0 + m],
                                                   rhs=hT[:, c, :], start=(c == 0 and slot == 0), stop=(c == 7),
                                                   skip_group_check=True), reads=[b_hT, b_win[c]], writes=[b_pk[bank]])

        feat_mm(6, 0, 3072, 16)
        S.op("act", lambda e: e.activation(out=alT[:, :], in_=pk[6][0:16, 0:128], func=AF.Copy), reads=[b_pk[6]], writes=[b_alT])
        S.op("pe", lambda e: e.matmul(pk[5][:, :], lhsT=alT[:, :], rhs=wa2[:, :], start=True, stop=True),
             reads=[b_alT, b_wa2], writes=[b_pk[5]])
        S.op("dve", lambda e: e.tensor_tensor(out=gg[:, :], in0=pk[5][:, :], in1=ba_t[:, :], op=ALU.add),
             reads=[b_pk[5], b_ba], writes=[b_gg])
        S.op("act", lambda e: e.activation(out=gg[:, :], in_=gg[:, :], func=AF.Exp, scale=-1.0), reads=[b_gg], writes=[b_gg])
        S.op("dve", lambda e: e.tensor_scalar(out=gg[:, :], in0=gg[:, :], scalar1=1.0, scalar2=None, op0=ALU.add),
             reads=[b_gg], writes=[b_gg])
        S.op("act", lambda e: e.activation(out=gg[:, :], in_=gg[:, :], func=AF.Ln), reads=[b_gg], writes=[b_gg])
        S.op("dve", lambda e: e.tensor_scalar(out=gg[:, :], in0=gg[:, :], scalar1=-1.0 / GATE_TAU, scalar2=None, op0=ALU.mult),
             reads=[b_gg], writes=[b_gg])
        S.op("dve", lambda e: e.tensor_copy(out=ghi[:, :], in_=gg[:, :]), reads=[b_gg], writes=[b_ghi])
        S.op("dve", lambda e: e.tensor_tensor(out=glo[:, :], in0=gg[:, :], in1=ghi[:, :], op=ALU.subtract),
             reads=[b_gg, b_ghi], writes=[b_glo])
        for hd in range(4):
            for part, (gs, bgs) in enumerate(((ghi, b_ghi), (glo, b_glo))):
                S.op("pe", lambda e, hd=hd, gs=gs, part=part: e.matmul(
                    pk[4][:, hd * 128:(hd + 1) * 128], lhsT=gs[:, hd * 128:(hd + 1) * 128], rhs=U[:, :],
                    start=(hd == 0 and part == 0), stop=(part == 1), skip_group_check=True),
                     reads=[bgs, b_U], writes=[b_pk[4]])
        for part, (gs, bgs) in enumerate(((ghi, b_ghi), (glo, b_glo))):
            S.op("pe", lambda e, gs=gs, part=part: e.matmul(pk[5][:, :], lhsT=W[:, :], rhs=gs[:, :], start=(part == 0), stop=(part == 1)),
                 reads=[bgs, b_W], writes=[b_pk[5]])
        S.op("act", lambda e: e.activation(out=EbT[:, :, :], in_=pk[4][:, :].rearrange("p (h i) -> p h i", i=128), func=AF.Exp),
             reads=[b_pk[4]], writes=[b_EbT])
        S.op("act", lambda e: e.activation(out=EnbT[:, :, :], in_=pk[4][:, :].rearrange("p (h i) -> p h i", i=128), func=AF.Exp, scale=-1.0),
             reads=[b_pk[4]], writes=[b_EnbT])
        S.op("act", lambda e: e.activation(out=Ebl[:, :], in_=pk[5][:, :], func=AF.Exp), reads=[b_pk[5]], writes=[b_Ebl])
        dcs, bdc = dc_s[i2], b_dc[i2]
        S.op("pool", lambda e, dcs=dcs: e.tensor_copy(out=dcs[:, :, :], in_=EbT[:, :, :].rearrange("p h (c j) -> p h c j", j=64)[:, :, :, 63]),
             reads=[b_EbT], writes=[bdc])
        S.dma("pool", dec[:, :, 2 * t:2 * t + 2].rearrange("h d c -> d h c"), dcs[:, :, :], reads=[bdc])
        for hd in range(4):
            feat_mm(6, hd, hd * 128, 128)
        qgs, bqg = qg_s[i2], b_qg[i2]
        S.op("dve", lambda e, qgs=qgs: e.scalar_tensor_tensor(out=qgs[:, :, :], in0=pk[6][:, :].rearrange("p (h i) -> p h i", i=128),
                                                              scalar=QSCALE, in1=EbT[:, :, :], op0=ALU.mult, op1=ALU.mult),
             reads=[b_pk[6], b_EbT], writes=[bqg])
        S.dma("pool", qgT[:, :, t * 128:(t + 1) * 128].rearrange("h d i -> d h i"), qgs[:, :, :], reads=[bqg])
        for hd in range(4):
            feat_mm(3, hd, 512 + hd * 128, 128)
        kgs, bkg = kg_s[i2], b_kg[i2]
        S.op("dve", lambda e, kgs=kgs: e.tensor_tensor(out=kgs[:, :, :], in0=pk[3][:, :].rearrange("p (h i) -> p h i", i=128),
                                                       in1=EnbT[:, :, :], op=ALU.mult), reads=[b_pk[3], b_EnbT], writes=[bkg])
        S.dma("pool", kgT[:, :, t * 128:(t + 1) * 128].rearrange("h d i -> d h i"), kgs[:, :, :], reads=[bkg])
        tok_mm(2, 512, 1024)
        khs, bkh = kh_s[i2], b_kh[i2]
        S.op("dve", lambda e, khs=khs: e.tensor_tensor(out=khs[:, :], in0=pk[2][:, :], in1=Ebl[:, :], op=ALU.mult),
             reads=[b_pk[2], b_Ebl], writes=[bkh])
        S.dma("pool", kh[t * 128:(t + 1) * 128, :], khs[:, :], reads=[bkh])
        vbs, bvb = vb_s[i2], b_vb[i2]
        for half in range(2):
            tok_mm(half, 1024 + half * 512, 1536 + half * 512)
            S.op("act", lambda e, half=half, vbs=vbs: e.activation(out=vbs[:, half * 512:(half + 1) * 512], in_=pk[half][:, :], func=AF.Copy),
                 reads=[b_pk[half]], writes=[bvb])
        S.dma("pool", vb[t * 128:(t + 1) * 128, :], vbs[:, :], reads=[bvb])
        srs, bsr = sr_s[i2], b_sr[i2]
        for half in range(2):
            tok_mm(half, 2048 + half * 512, 2560 + half * 512)
            S.op("act", lambda e, half=half, srs=srs: e.activation(out=srs[:, half * 512:(half + 1) * 512], in_=pk[half][:, :], func=AF.Silu),
                 reads=[b_pk[half]], writes=[bsr])
        S.dma("pool", sr[t * 128:(t + 1) * 128, :], srs[:, :], reads=[bsr])

    for t in range(NTL):
        _tile(t, **sets[t % 2])

RMS_EPS = 1e-6
GLA_CB = 16


def stage_gla(S, C, SEQ, NH, acc, dec_ap, g_norm, tri, CB=GLA_CB):
    NBLK = SEQ // (64 * CB)
    NCH = SEQ // 64
    tri_f = S.sbuf("gl_trif", [64, 64], F32)
    b_tri = S.buf("gl_tri")
    S.dma("sp", tri_f[:, :], tri[:, :], writes=[b_tri])
    gn = S.sbuf("gl_gn", [64, 256], F32)
    b_gn = S.buf("gl_gn")
    S.dma("sp", gn[:, :], g_norm.partition_broadcast(64), writes=[b_gn])
    dec_sb = S.sbuf("gl_dec", [128, NH, NCH], F32)
    b_dec = S.buf("gl_dec")
    for hd in range(NH):
        S.dma("sp", dec_sb[:, hd, :], dec_ap(hd), writes=[b_dec])
    st = S.sbuf("gl_st", [128, NH, 256], F32)
    b_st = S.bufs(NH, "gl_st")
    stb = [S.sbuf("gl_stb%d" % i, [128, NH, 256], BF16) for i in range(2)]
    b_stb = [S.bufs(NH, "gl_stb%d_" % i) for i in range(2)]
    S.op("dve", lambda e: e.memset(st[:, :, :], 0.0), writes=b_st)
    S.op("dve", lambda e: e.memset(stb[0][:, :, :], 0.0), writes=b_stb[0])
    qg_sb = [[S.sbuf("gl_qg%d_%d" % (i, hd), [128, 64 * CB], BF16) for hd in range(NH)] for i in range(2)]
    kg_sb = [[S.sbuf("gl_kg%d_%d" % (i, hd), [128, 64 * CB], BF16) for hd in range(NH)] for i in range(2)]
    kh_sb = [[S.sbuf("gl_kh%d_%d" % (i, hd), [64, CB, 128], BF16) for hd in range(NH)] for i in range(2)]
    v_sb = [[S.sbuf("gl_v%d_%d" % (i, hd), [64, CB, 256], BF16) for hd in range(NH)] for i in range(2)]
    b_in = [[S.bufs(4, "gl_in%d_%d_" % (i, hd)) for hd in range(NH)] for i in range(2)]
    o_st = [[S.sbuf("gl_ost%d_%d" % (i, hd), [64, CB, 256], F32) for hd in range(NH)] for i in range(2)]
    b_ost = [[S.buf("gl_ost%d_%d" % (i, hd)) for hd in range(NH)] for i in range(2)]
    ss = [[S.sbuf("gl_ss%d_%d" % (i, hd), [64, CB], F32) for hd in range(NH)] for i in range(2)]
    b_ss = [[S.buf("gl_ss%d_%d" % (i, hd)) for hd in range(NH)] for i in range(2)]
    junk = S.sbuf("gl_junk", [64, 256], F32)
    b_junk = S.buf("gl_junk")
    A_sb = [S.sbuf("gl_A%d" % i, [64, 64], BF16) for i in range(4)]
    b_A = S.bufs(4, "gl_A")
    a_bank = S.psum("gl_aps", [64, 512], F32)
    a_ps = [a_bank[:, i * 64:(i + 1) * 64] for i in range(4)]
    b_aps = S.bufs(4, "gl_aps")
    o_bank = [S.psum("gl_ops%d" % i, [64, 512], F32) for i in range(4)]
    o_ps = [o_bank[i][:, 0:256] for i in range(4)]
    b_ops = S.bufs(4, "gl_ops")
    s_bank = [S.psum("gl_sps%d" % i, [128, 512], F32) for i in range(2)]
    s_ps = [s_bank[i // 2][:, (i % 2) * 256:(i % 2) * 256 + 256] for i in range(4)]
    b_sps = S.bufs(4, "gl_sps")
    n = 0
    for blk in range(NBLK):
        i2 = blk % 2
        for hd in range(NH):
            bi = b_in[i2][hd]
            S.dma("sp", qg_sb[i2][hd][:, :], acc("qg", hd, blk), writes=[bi[0]])
            S.dma("sp", kg_sb[i2][hd][:, :], acc("kg", hd, blk), writes=[bi[1]])
            S.dma("sp", kh_sb[i2][hd][:, :, :], acc("kh", hd, blk), writes=[bi[2]])
            S.dma("sp", v_sb[i2][hd][:, :, :], acc("v", hd, blk), writes=[bi[3]])
        for cc in range(CB):
            c = blk * CB + cc
            cur, nxt = c % 2, (c + 1) % 2
            for hd in range(NH):
                bi = b_in[i2][hd]
                qg = qg_sb[i2][hd][:, cc * 64:(cc + 1) * 64]
                kg = kg_sb[i2][hd][:, cc * 64:(cc + 1) * 64]
                khc = kh_sb[i2][hd][:, cc, :]
                vc = v_sb[i2][hd][:, cc, :]
                aps, baps = a_ps[n % 4], b_aps[n % 4]
                ops, bops = o_ps[n % 4], b_ops[n % 4]
                sps, bsps = s_ps[n % 4], b_sps[n % 4]
                A, bA = A_sb[n % 4], b_A[n % 4]
                n += 1
                S.op("pe", lambda e, aps=aps, kg=kg, qg=qg: e.matmul(aps, lhsT=kg, rhs=qg, start=True, stop=True, skip_group_check=True),
                     reads=[bi[0], bi[1]], writes=[baps])
                S.op("dve", lambda e, aps=aps, A=A: e.tensor_tensor(out=A[:, :], in0=aps, in1=tri_f[:, :], op=ALU.mult),
                     reads=[baps, b_tri], writes=[bA])
                S.op("pe", lambda e, ops=ops, A=A, vc=vc: e.matmul(ops, lhsT=A[:, :], rhs=vc, start=True, stop=False),
                     reads=[bA, bi[3]], writes=[bops])
                S.op("pe", lambda e, ops=ops, qg=qg, cur=cur, hd=hd: e.matmul(ops, lhsT=qg, rhs=stb[cur][:, hd, :],
                                                                             start=False, stop=True),
                     reads=[bi[0], b_stb[cur][hd]], writes=[bops])
                S.op("pe", lambda e, sps=sps, khc=khc, vc=vc: e.matmul(sps, lhsT=khc, rhs=vc, start=True, stop=True, skip_group_check=True),
                     reads=[bi[2], bi[3]], writes=[bsps])
                S.op("dve", lambda e, sps=sps, hd=hd, c=c: e.scalar_tensor_tensor(
                    out=st[:, hd, :], in0=st[:, hd, :], scalar=dec_sb[:, hd, c:c + 1], in1=sps,
                    op0=ALU.mult, op1=ALU.add), reads=[bsps, b_st[hd], b_dec], writes=[b_st[hd]])
                S.op("act", lambda e, hd=hd, nxt=nxt: e.activation(out=stb[nxt][:, hd, :], in_=st[:, hd, :], func=AF.Copy),
                     reads=[b_st[hd]], writes=[b_stb[nxt][hd]])
                ssc = ss[i2][hd][:, cc:cc + 1]
                ostc = o_st[i2][hd][:, cc, :]
                S.op("act", lambda e, ops=ops, ssc=ssc: e.activation(out=junk[:, :], in_=ops, func=AF.Square, accum_out=ssc),
                     reads=[bops], writes=[b_junk, b_ss[i2][hd]])
                S.op("act", lambda e, ops=ops, ostc=ostc: e.activation(out=ostc, in_=ops, func=AF.Copy),
                     reads=[bops], writes=[b_ost[i2][hd]])
        for hd in range(NH):
            s_, bs_ = ss[i2][hd], b_ss[i2][hd]
            S.op("dve", lambda e, s_=s_: e.tensor_scalar(out=s_[:, :], in0=s_[:, :], scalar1=1.0 / 256, scalar2=RMS_EPS,
                                                         op0=ALU.mult, op1=ALU.add), reads=[bs_], writes=[bs_])
            S.op("act", lambda e, s_=s_: e.activation(out=s_[:, :], in_=s_[:, :], func=AF.Sqrt), reads=[bs_], writes=[bs_])
            S.op("dve", lambda e, s_=s_: e.reciprocal(out=s_[:, :], in_=s_[:, :]), reads=[bs_], writes=[bs_])
            ot, bo = o_st[i2][hd], b_ost[i2][hd]
            S.op("dve", lambda e, ot=ot, s_=s_: e.tensor_tensor(out=ot[:, :, :], in0=ot[:, :, :],
                                                               in1=s_[:, :].unsqueeze(2).to_broadcast([64, CB, 256]), op=ALU.mult),
                 reads=[bo, bs_], writes=[bo])
            S.op("pool", lambda e, ot=ot: e.tensor_tensor(out=ot[:, :, :], in0=ot[:, :, :],
                                                         in1=gn[:, :].unsqueeze(1).to_broadcast([64, CB, 256]), op=ALU.mult),
                 reads=[bo, b_gn], writes=[bo])
            S.dma("pool", acc("on", hd, blk), ot[:, :, :], reads=[bo])


_NP2DT = {np.dtype("float32"): F32, np.dtype("int32"): I32}
try:
    import ml_dtypes
    _NP2DT[np.dtype(ml_dtypes.bfloat16)] = BF16
    NPBF16 = ml_dtypes.bfloat16
except Exception:
    NPBF16 = None

_PROG_CACHE = {}


def launch(key, prog, in_list, outs):
    sig = (key, tuple((k, v.shape, str(v.dtype)) for k, v in in_list[0].items()), tuple((k, tuple(sh), str(dt)) for k, (sh, dt) in outs.items()))
    if sig not in _PROG_CACHE:
        nc = bass.Bass("TRN2", target_bir_lowering=False)
        aps = {}
        for k, v in in_list[0].items():
            aps[k] = nc.dram_tensor(k, list(v.shape), _NP2DT[v.dtype], kind="ExternalInput").ap()
        for k, (shape, dt) in outs.items():
            aps[k] = nc.dram_tensor(k, list(shape), dt, kind="ExternalOutput").ap()
        S = Sched(nc)
        prog(S, aps)
        S.emit()
        _PROG_CACHE[sig] = nc
    nc = _PROG_CACHE[sig]
    res = run_bass_kernel_spmd(nc, in_list, core_ids=list(range(len(in_list))))
    return res.results


def _consts():
    ident = np.eye(128, dtype=np.float32)
    tri = np.where(np.arange(128)[None, :] <= np.arange(128)[:, None], 0.0, NEG_MASK).astype(np.float32)
    allm = np.full((128, 128), NEG_MASK, np.float32)
    none = np.zeros((128, 128), np.float32)
    maskc = [np.concatenate([tri, allm], 1), np.concatenate([none, tri], 1)]
    negI = (-MASK_BIG * np.eye(128)).astype(np.float32)
    invf = (500000.0 ** (-np.arange(0, 16, 2, dtype=np.float32) / 16)).astype(np.float32)
    j = np.arange(128)[:, None]
    i = np.arange(128)[None, :]
    same = (j // 64) == (i // 64)
    cmU = (same & (j <= i)).astype(np.float32)
    cmW = (same & (j > i)).astype(np.float32)
    tri64 = (np.arange(64)[:, None] <= np.arange(64)[None, :]).astype(np.float32)
    return dict(ident=ident, maskc=maskc, negI=negI, invf=invf, cmU=cmU, cmW=cmW, tri64=tri64)


def kernel_unfused(x, positions, a_w_in, a_w_o, b_w_in, b_w_a2, b_b_a, b_g_norm, b_w_o,
           ln_mix_g, ln_mix_b, mlp_w_up, mlp_w_down, ln_mlp_g, ln_mlp_b):
    x = np.asarray(x, np.float32)
    B, SEQ, _ = x.shape
    NC = 2 * B
    T = SEQ // 2
    NTL = T // 128
    NQ = NTL
    cst = _consts()
    f32 = lambda a: np.ascontiguousarray(np.asarray(a, np.float32))
    tok = []
    for c in range(NC):
        r = c % 2
        tiles = np.arange(r, SEQ // 128, 2)
        tok.append((tiles[:, None] * 128 + np.arange(128)[None, :]).reshape(-1))
    h = [np.ascontiguousarray(x[c // 2][tok[c]]) for c in range(NC)]
    pos_pt = [np.ascontiguousarray(np.asarray(positions)[c // 2][tok[c]].astype(np.int32).reshape(NTL, 128).T) for c in range(NC)]
    depth = ln_mix_g.shape[0]
    for i in range(depth):
        j = i // 2
        if i % 2 == 0:
            w_in = f32(a_w_in[j])
            ins = [dict(h=h[c], pos=pos_pt[c], w=w_in, invf=cst["invf"], ident=cst["ident"]) for c in range(NC)]
            outs = {"qT": ([1024, T], BF16), "kT": ([256, T], BF16), "v": ([T, 256], BF16), "iqT": ([512, T], BF16),
                    "ikT": ([64, T], BF16), "iw": ([T, 8], F32)}

            def prog(S, a):
                C = Consts(S, a["ident"])
                stage_proj_A(S, C, T, a["h"], a["pos"], a["w"], a["invf"], a["qT"], a["kT"], a["v"], a["iqT"], a["ikT"], a["iw"])
            pr = launch("projA", prog, ins, outs)
            ins = []
            for c in range(NC):
                b0 = (c // 2) * 2
                kT = np.empty((256, SEQ), NPBF16)
                ikT = np.empty((64, SEQ), NPBF16)
                vf = np.empty((SEQ, 256), NPBF16)
                for r in range(2):
                    kT[:, tok[b0 + r]] = pr[b0 + r]["kT"]
                    ikT[:, tok[b0 + r]] = pr[b0 + r]["ikT"]
                    vf[tok[b0 + r]] = pr[b0 + r]["v"]
                ins.append(dict(ident=cst["ident"], kT=kT, v=vf, ikT=ikT, qT=pr[c]["qT"], iqT=pr[c]["iqT"], iw=pr[c]["iw"],
                                maskc=cst["maskc"][c % 2], negI=cst["negI"]))

            def prog(S, a):
                C = Consts(S, a["ident"])
                stage_attn(S, C, NQ, a["kT"].rearrange("(g d) s -> g d s", d=64), a["v"], a["ikT"],
                           a["qT"].rearrange("(h d) t -> h d t", d=64), a["iqT"].rearrange("(h d) t -> h d t", d=64),
                           a["iw"], a["maskc"], a["negI"], a["o"])
            ar = launch("attn", prog, ins, {"o": ([T, 1024], BF16)})
            w_o = f32(a_w_o[j])
            ins = [dict(ident=cst["ident"], o=ar[c]["o"], h=h[c], w=w_o, g=f32(ln_mix_g[i]), b=f32(ln_mix_b[i])) for c in range(NC)]

            def prog(S, a):
                C = Consts(S, a["ident"])
                stage_wo_ln(S, C, T, a["o"], a["h"], a["w"], a["g"], a["b"], a["h1"])
            wr = launch("wo", prog, ins, {"h1": ([T, D], F32)})
        else:
            ins = [dict(h=h[c], w=f32(b_w_in[j]), wa2=f32(b_w_a2[j]), ba=f32(b_b_a[j]), U=cst["cmU"], W=cst["cmW"], ident=cst["ident"])
                   for c in range(NC)]
            outs = {"qgT": ([512, T], BF16), "kgT": ([512, T], BF16), "kh": ([T, 512], BF16), "vb": ([T, 1024], BF16),
                    "dec": ([512, T // 64], F32), "sr": ([T, 1024], F32)}

            def prog(S, a):
                C = Consts(S, a["ident"])
                stage_proj_B(S, C, T, a["h"], a["w"], a["wa2"], a["ba"], a["U"], a["W"],
                             a["qgT"].rearrange("(h d) t -> h d t", d=128), a["kgT"].rearrange("(h d) t -> h d t", d=128),
                             a["kh"], a["vb"], a["dec"].rearrange("(h d) c -> h d c", d=128), a["sr"])
            pr = launch("projB", prog, ins, outs)
            ins = []
            ctok = [t_.reshape(-1, 128)[:, ::64].reshape(-1) // 64 for t_ in tok]
            for c in range(NC):
                b0 = (c // 2) * 2
                hs = slice((c % 2) * 256, (c % 2) * 256 + 256)
                vs = slice((c % 2) * 512, (c % 2) * 512 + 512)
                qg = np.empty((256, SEQ), NPBF16)
                kg = np.empty((256, SEQ), NPBF16)
                khh = np.empty((SEQ, 256), NPBF16)
                vv = np.empty((SEQ, 512), NPBF16)
                dd = np.empty((256, SEQ // 64), np.float32)
                for r in range(2):
                    qg[:, tok[b0 + r]] = pr[b0 + r]["qgT"][hs]
                    kg[:, tok[b0 + r]] = pr[b0 + r]["kgT"][hs]
                    khh[tok[b0 + r]] = pr[b0 + r]["kh"][:, hs]
                    vv[tok[b0 + r]] = pr[b0 + r]["vb"][:, vs]
                    dd[:, ctok[b0 + r]] = pr[b0 + r]["dec"][hs]
                ins.append(dict(ident=cst["ident"], qgT=qg, kgT=kg, kh=khh, vb=vv, dec=dd, gn=f32(b_g_norm[j]), tri=cst["tri64"]))

            def prog(S, a):
                C = Consts(S, a["ident"])

                def acc(kind, hd, blk):
                    t0, t1 = blk * 1024, (blk + 1) * 1024
                    if kind == "qg":
                        return a["qgT"][hd * 128:(hd + 1) * 128, t0:t1]
                    if kind == "kg":
                        return a["kgT"][hd * 128:(hd + 1) * 128, t0:t1]
                    if kind == "kh":
                        return a["kh"][t0:t1, hd * 128:(hd + 1) * 128].rearrange("(c j) d -> j c d", j=64)
                    if kind == "v":
                        return a["vb"][t0:t1, hd * 256:(hd + 1) * 256].rearrange("(c j) e -> j c e", j=64)
                    return a["on"][t0:t1, hd * 256:(hd + 1) * 256].rearrange("(c j) e -> j c e", j=64)
                stage_gla(S, C, SEQ, 2, acc, lambda hd: a["dec"][hd * 128:(hd + 1) * 128, :], a["gn"], a["tri"])
            gr = launch("gla", prog, ins, {"on": ([SEQ, 512], F32)})
            w_o = f32(b_w_o[j])
            ins = []
            for c in range(NC):
                b0 = (c // 2) * 2
                on = np.concatenate([gr[b0]["on"][tok[c]], gr[b0 + 1]["on"][tok[c]]], axis=1)
                ins.append(dict(ident=cst["ident"], on=np.ascontiguousarray(on), sr=pr[c]["sr"], h=h[c], w=w_o,
                                g=f32(ln_mix_g[i]), b=f32(ln_mix_b[i])))

            def prog(S, a):
                C = Consts(S, a["ident"])
                stage_wo_ln(S, C, T, None, a["h"], a["w"], a["g"], a["b"], a["h1"], gate=(a["on"], a["sr"]))
            wr = launch("wog", prog, ins, {"h1": ([T, D], F32)})
        ins = [dict(ident=cst["ident"], h1=wr[c]["h1"], wu=f32(mlp_w_up[i]), wd=f32(mlp_w_down[i]), g=f32(ln_mlp_g[i]), b=f32(ln_mlp_b[i]))
               for c in range(NC)]

        def prog(S, a):
            C = Consts(S, a["ident"])
            stage_mlp(S, C, T, a["h1"], a["wu"], a["wd"], a["g"], a["b"], a["h2"])
        mr = launch("mlp", prog, ins, {"h2": ([T, D], F32)})
        h = [mr[c]["h2"] for c in range(NC)]
    out = np.empty((B, SEQ, D), np.float32)
    for c in range(NC):
        out[c // 2][tok[c]] = h[c]
    return out


def build_fused(SEQ, depth):
    nc = bass.Bass("TRN2", target_bir_lowering=False)
    T = SEQ
    NTL = T // 128

    def din(name, shape, dt=F32):
        return nc.dram_tensor(name, list(shape), dt, kind="ExternalInput").ap()

    def scr(name, shape, dt):
        return nc.dram_tensor("scr_" + name, list(shape), dt).ap()

    NA, NB = (depth + 1) // 2, depth // 2
    a = dict(
        x=din("x", [T, D]), pos=din("pos", [128, NTL], I32), invf=din("invf", [8]), ident=din("ident", [128, 128]),
        maskc=din("maskc", [128, 256]), negI=din("negI", [128, 128]), cmU=din("cmU", [128, 128]), cmW=din("cmW", [128, 128]),
        tri64=din("tri64", [64, 64]),
        a_w_in=din("a_w_in", [NA, D, A_IN]), a_w_o=din("a_w_o", [NA, D, D]),
        b_w_in=din("b_w_in", [max(NB, 1), D, B_IN]), b_w_a2=din("b_w_a2", [max(NB, 1), 16, 512]), b_b_a=din("b_b_a", [max(NB, 1), 512]),
        b_g_norm=din("b_g_norm", [max(NB, 1), 256]), b_w_o=din("b_w_o", [max(NB, 1), D, D]),
        ln_mix_g=din("ln_mix_g", [depth, D]), ln_mix_b=din("ln_mix_b", [depth, D]),
        mlp_w_up=din("mlp_w_up", [depth, D, DFF]), mlp_w_down=din("mlp_w_down", [depth, DFF, D]),
        ln_mlp_g=din("ln_mlp_g", [depth, D]), ln_mlp_b=din("ln_mlp_b", [depth, D]),
    )
    out = nc.dram_tensor("out", [T, D], F32, kind="ExternalOutput").ap()
    hA = scr("hA", [T, D], F32)
    hB = scr("hB", [T, D], F32)
    qT = scr("qT", [1024, T], BF16)
    kT = scr("kT", [256, T], BF16)
    vv = scr("v", [T, 256], BF16)
    iqT = scr("iqT", [512, T], BF16)
    ikT = scr("ikT", [64, T], BF16)
    iw = scr("iw", [T, 8], F32)
    o = scr("o", [T, 1024], BF16)
    qgT = scr("qgT", [512, T], BF16)
    kgT = scr("kgT", [512, T], BF16)
    kh = scr("kh", [T, 512], BF16)
    vb = scr("vb", [T, 1024], BF16)
    dec = scr("dec", [512, T // 64], F32)
    sr = scr("sr", [T, 1024], F32)
    on = scr("on", [T, 1024], F32)

    S = Sched(nc)
    CB = 8
    h_in = a["x"]
    for i in range(depth):
        j = i // 2
        last = i == depth - 1
        if i % 2 == 0:
            S.stage_begin()
            C = Consts(S, a["ident"])
            stage_proj_A(S, C, T, h_in, a["pos"], a["a_w_in"][j], a["invf"], qT, kT, vv, iqT, ikT, iw)
            S.stage_end()
            S.stage_begin()
            C = Consts(S, a["ident"])
            stage_attn(S, C, NTL, kT.rearrange("(g d) s -> g d s", d=64), vv, ikT, qT.rearrange("(h d) t -> h d t", d=64),
                       iqT.rearrange("(h d) t -> h d t", d=64), iw, a["maskc"], a["negI"], o, solo=True)
            S.stage_end()
            S.stage_begin()
            C = Consts(S, a["ident"])
            stage_wo_ln(S, C, T, o, h_in, a["a_w_o"][j], a["ln_mix_g"][i], a["ln_mix_b"][i], hB)
            S.stage_end()
        else:
            S.stage_begin()
            C = Consts(S, a["ident"])
            stage_proj_B(S, C, T, h_in, a["b_w_in"][j], a["b_w_a2"][j], a["b_b_a"][j], a["cmU"], a["cmW"],
                         qgT.rearrange("(h d) t -> h d t", d=128), kgT.rearrange("(h d) t -> h d t", d=128), kh, vb,
                         dec.rearrange("(h d) c -> h d c", d=128), sr)
            S.stage_end()
            S.stage_begin()
            C = Consts(S, a["ident"])

            def acc(kind, hd, blk):
                t0, t1 = blk * 64 * CB, (blk + 1) * 64 * CB
                if kind == "qg":
                    return qgT[hd * 128:(hd + 1) * 128, t0:t1]
                if kind == "kg":
                    return kgT[hd * 128:(hd + 1) * 128, t0:t1]
                if kind == "kh":
                    return kh[t0:t1, hd * 128:(hd + 1) * 128].rearrange("(c j) d -> j c d", j=64)
                if kind == "v":
                    return vb[t0:t1, hd * 256:(hd + 1) * 256].rearrange("(c j) e -> j c e", j=64)
                return on[t0:t1, hd * 256:(hd + 1) * 256].rearrange("(c j) e -> j c e", j=64)
            stage_gla(S, C, SEQ, 4, acc, lambda hd: dec[hd * 128:(hd + 1) * 128, :], a["b_g_norm"][j], a["tri64"], CB=CB)
            S.stage_end()
            S.stage_begin()
            C = Consts(S, a["ident"])
            stage_wo_ln(S, C, T, None, h_in, a["b_w_o"][j], a["ln_mix_g"][i], a["ln_mix_b"][i], hB, gate=(on, sr))
            S.stage_end()
        S.stage_begin()
        C = Consts(S, a["ident"])
        h_out = out if last else hA
        stage_mlp(S, C, T, hB, a["mlp_w_up"][i], a["mlp_w_down"][i], a["ln_mlp_g"][i], a["ln_mlp_b"][i], h_out)
        S.stage_end(last=last)
        h_in = hA
    S.stack.close()
    return nc


_FUSED = {}


def kernel(x, positions, a_w_in, a_w_o, b_w_in, b_w_a2, b_b_a, b_g_norm, b_w_o,
           ln_mix_g, ln_mix_b, mlp_w_up, mlp_w_down, ln_mlp_g, ln_mlp_b):
    x = np.asarray(x, np.float32)
    B, SEQ, _ = x.shape
    depth = int(np.asarray(ln_mix_g).shape[0])
    key = (SEQ, depth)
    if key not in _FUSED:
        _FUSED[key] = build_fused(SEQ, depth)
    nc = _FUSED[key]
    cst = _consts()
    f32 = lambda t: np.ascontiguousarray(np.asarray(t, np.float32))
    shared = dict(invf=cst["invf"], ident=cst["ident"], maskc=cst["maskc"][1], negI=cst["negI"], cmU=cst["cmU"], cmW=cst["cmW"],
                  tri64=cst["tri64"], a_w_in=f32(a_w_in), a_w_o=f32(a_w_o), b_w_in=f32(b_w_in), b_w_a2=f32(b_w_a2), b_b_a=f32(b_b_a),
                  b_g_norm=f32(b_g_norm), b_w_o=f32(b_w_o), ln_mix_g=f32(ln_mix_g), ln_mix_b=f32(ln_mix_b),
                  mlp_w_up=f32(mlp_w_up), mlp_w_down=f32(mlp_w_down), ln_mlp_g=f32(ln_mlp_g), ln_mlp_b=f32(ln_mlp_b))
    in_maps = []
    for c in range(B):
        pos_pt = np.ascontiguousarray(np.asarray(positions)[c].astype(np.int32).reshape(SEQ // 128, 128).T)
        m = dict(shared)
        m["x"] = np.ascontiguousarray(x[c])
        m["pos"] = pos_pt
        in_maps.append(m)
    res = run_bass_kernel_spmd(nc, in_maps, core_ids=list(range(B)))
    return np.stack([res.results[c]["out"] for c in range(B)], axis=0)
```
